# Optimizing a Trainium2 kernel written in Bass

```python
import jax, jax.numpy as jnp
from jax import lax
import numpy as np

D_MODEL = 1024
BATCH = 16
SEQ = 2048
DEPTH = 4

CHUNK = 64
N_MIXERS = 3
N_RGLRU_LAYERS = (DEPTH + 2) // 3
N_RWKV_LAYERS = (DEPTH + 1) // 3
N_ATTN_LAYERS = DEPTH // 3

DEEPNORM_ALPHA = (2 * DEPTH) ** 0.25
DEEPNORM_BETA = (8 * DEPTH) ** -0.25
LN_EPS = 1e-5

D_RNN = 1344
LRU_BLOCKS = 16
LRU_BLOCK_SIZE = D_RNN // LRU_BLOCKS
CONV_WIDTH = 4
RG_LRU_C = 8.0

RW_HEAD_SIZE = 64
RW_HEADS = D_MODEL // RW_HEAD_SIZE
RW_DECAY_LORA = 64
RW_AAA_LORA = 64
RW_GATE_LORA = 128
RW_GN_EPS = 64e-5

ATT_HEADS = 16
ATT_HEAD_DIM = D_MODEL // ATT_HEADS
BAND_CHUNKS = 9
BAND = BAND_CHUNKS * CHUNK
MAX_REL = 2 * CHUNK
NEG_INF = -1e30

MEM_TOKENS = 256
MEM_HEADS = 4
MEM_HEAD_DIM = D_MODEL // MEM_HEADS

D_FF = 4 * D_MODEL

kernel_name = "hybrid_rglru_rwkv7_chunkattn_deepnorm"


def _layer_norm(x, g, b):
    xf = x.astype(jnp.float32)
    mu = jnp.mean(xf, axis=-1, keepdims=True)
    var = jnp.mean(jnp.square(xf - mu), axis=-1, keepdims=True)
    return ((xf - mu) * lax.rsqrt(var + LN_EPS) * g + b).astype(x.dtype)


def _causal_depthwise_conv(x, w, b):
    k, c = w.shape
    y = lax.conv_general_dilated(x, w[:, None, :], window_strides=(1,), padding=[(k - 1, 0)],
                                 dimension_numbers=('NWC', 'WIO', 'NWC'), feature_group_count=c)
    return y + b


def _linear_recurrence_combine(left, right):
    a_l, b_l = left
    a_r, b_r = right
    return a_l * a_r, a_r * b_l + b_r


def rglru_block(x, w_in, conv_w, conv_b, gate_w, gate_b, lam, w_out):
    bsz, seq, _ = x.shape
    u = x @ w_in
    gate_branch = jax.nn.gelu(u[..., :D_RNN])
    xr = _causal_depthwise_conv(u[..., D_RNN:], conv_w, conv_b)
    xb = xr.reshape(bsz, seq, LRU_BLOCKS, LRU_BLOCK_SIZE)
    gates = jnp.einsum('bsnc,gncd->bsgnd', xb, gate_w).reshape(bsz, seq, 2, D_RNN) + gate_b
    gates = jax.nn.sigmoid(gates.astype(jnp.float32))
    r_gate, i_gate = gates[:, :, 0], gates[:, :, 1]
    log_a = -RG_LRU_C * r_gate * jax.nn.softplus(-lam.astype(jnp.float32))
    a = jnp.exp(log_a)
    bterm = jnp.sqrt(-jnp.expm1(2.0 * log_a)) * (i_gate * xr.astype(jnp.float32))
    _, h = lax.associative_scan(_linear_recurrence_combine, (a, bterm), axis=1)
    return (h.astype(x.dtype) * gate_branch) @ w_out


def rwkv7_time_mix(x, mu, w_r, w_k, w_v, w0, w1, w2, a0, a1, a2, g1, g2, k_k, k_a, r_k,
                   lnx_g, lnx_b, w_o):
    bsz, seq, d = x.shape
    f32 = jnp.float32
    xx = jnp.pad(x, ((0, 0), (1, 0), (0, 0)))[:, :-1] - x
    xr, xw, xk, xv, xa, xg = [x + xx * mu[c] for c in range(6)]
    r = (xr @ w_r).astype(f32)
    k = (xk @ w_k).astype(f32)
    v = (xv @ w_v).astype(f32)
    w_log = -jax.nn.softplus(-(w0 + jnp.tanh(xw @ w1) @ w2).astype(f32)) - 0.5
    decay = jnp.exp(-jnp.exp(w_log))
    a = jax.nn.sigmoid((a0 + (xa @ a1) @ a2).astype(f32))
    g = (jax.nn.sigmoid(xg @ g1) @ g2).astype(f32)
    kk = (k * k_k).reshape(bsz, seq, RW_HEADS, RW_HEAD_SIZE)
    kk = kk / jnp.maximum(jnp.linalg.norm(kk, axis=-1, keepdims=True), 1e-12)
    k = k * (1.0 + (a - 1.0) * k_a)

    heads = lambda t: jnp.moveaxis(t.reshape(bsz, seq, RW_HEADS, RW_HEAD_SIZE), 1, 0)
    a_h = a.reshape(bsz, seq, RW_HEADS, RW_HEAD_SIZE)
    seq_inputs = (heads(r), heads(decay), heads(k), heads(v),
                  jnp.moveaxis(-kk, 1, 0), jnp.moveaxis(kk * a_h, 1, 0))

    def step(state, inp):
        r_t, w_t, k_t, v_t, aa_t, bb_t = inp
        sa = jnp.einsum('bhij,bhj->bhi', state, aa_t)
        state = (state * w_t[:, :, None, :] + sa[..., None] * bb_t[:, :, None, :]
                 + v_t[..., None] * k_t[:, :, None, :])
        return state, jnp.einsum('bhij,bhj->bhi', state, r_t)

    state0 = jnp.zeros((bsz, RW_HEADS, RW_HEAD_SIZE, RW_HEAD_SIZE), f32)
    _, y = lax.scan(step, state0, seq_inputs)
    y = jnp.moveaxis(y, 0, 1)
    mu_y = jnp.mean(y, axis=-1, keepdims=True)
    var_y = jnp.mean(jnp.square(y - mu_y), axis=-1, keepdims=True)
    y = ((y - mu_y) * lax.rsqrt(var_y + RW_GN_EPS)).reshape(bsz, seq, d) * lnx_g + lnx_b
    rh = r.reshape(bsz, seq, RW_HEADS, RW_HEAD_SIZE)
    kh = k.reshape(bsz, seq, RW_HEADS, RW_HEAD_SIZE)
    vh = v.reshape(bsz, seq, RW_HEADS, RW_HEAD_SIZE)
    bonus = (jnp.sum(rh * kh * r_k, axis=-1, keepdims=True) * vh).reshape(bsz, seq, d)
    return ((y + bonus) * g).astype(x.dtype) @ w_o


def chunk_relpos_attention(x, w_qkv, rel_bias, w_o):
    bsz, seq, d = x.shape
    n_chunks = seq // CHUNK
    left = (BAND_CHUNKS - 1) * CHUNK
    qkv = (x @ w_qkv).reshape(bsz, seq, 3, ATT_HEADS, ATT_HEAD_DIM)
    q = qkv[:, :, 0] * (ATT_HEAD_DIM ** -0.5)
    kp = jnp.pad(qkv[:, :, 1], ((0, 0), (left, 0), (0, 0), (0, 0)))
    vp = jnp.pad(qkv[:, :, 2], ((0, 0), (left, 0), (0, 0), (0, 0)))
    rel = (left + jnp.arange(CHUNK))[:, None] - jnp.arange(BAND)[None, :]
    bias = rel_bias[:, jnp.clip(rel, -MAX_REL, MAX_REL) + MAX_REL].astype(jnp.float32)

    def one_chunk(c):
        start = c * CHUNK
        qc = lax.dynamic_slice_in_dim(q, start, CHUNK, axis=1)
        kc = lax.dynamic_slice_in_dim(kp, start, BAND, axis=1)
        vc = lax.dynamic_slice_in_dim(vp, start, BAND, axis=1)
        s = jnp.einsum('bqhd,bkhd->bhqk', qc, kc).astype(jnp.float32) + bias
        valid = (start - left + jnp.arange(BAND)) >= 0
        s = jnp.where(valid[None, None, None, :], s, NEG_INF)
        p = jax.nn.softmax(s, axis=-1).astype(vc.dtype)
        return jnp.einsum('bhqk,bkhd->bqhd', p, vc)

    out = lax.map(one_chunk, jnp.arange(n_chunks))
    out = jnp.transpose(out, (1, 0, 2, 3, 4)).reshape(bsz, seq, d)
    return out @ w_o


def memory_cross_attention(x, mem, w_q, w_kv, w_o):
    bsz, seq, d = x.shape
    q = (x @ w_q).reshape(bsz, seq, MEM_HEADS, MEM_HEAD_DIM)
    kv = (mem @ w_kv).reshape(bsz, mem.shape[1], 2, MEM_HEADS, MEM_HEAD_DIM)
    s = jnp.einsum('bqhd,bkhd->bhqk', q, kv[:, :, 0]).astype(jnp.float32) * (MEM_HEAD_DIM ** -0.5)
    p = jax.nn.softmax(s, axis=-1).astype(x.dtype)
    o = jnp.einsum('bhqk,bkhd->bqhd', p, kv[:, :, 1]).reshape(bsz, seq, d)
    return o @ w_o


def squared_relu_mlp(x, w1, w2):
    return jnp.square(jax.nn.relu(x @ w1)) @ w2


def _normal(key, shape, scale):
    return scale * jax.random.normal(key, shape, jnp.float32)


def setup_inputs(seed: int = 0) -> dict:
    key = jax.random.key(seed)
    ks = iter(jax.random.split(key, 40))
    d = D_MODEL
    n_a, n_b, n_c = N_RGLRU_LAYERS, N_RWKV_LAYERS, N_ATTN_LAYERS
    beta = DEEPNORM_BETA
    lam_u = jax.random.uniform(next(ks), (n_a, D_RNN), jnp.float32, minval=0.9, maxval=0.999)
    lam_s = lam_u ** (1.0 / RG_LRU_C)
    return {
        "x": _normal(next(ks), (BATCH, SEQ, d), 1.0),
        "mem": _normal(next(ks), (BATCH, MEM_TOKENS, d), 1.0),
        "ln_g": 1.0 + _normal(next(ks), (DEPTH, 3, d), 0.02),
        "ln_b": _normal(next(ks), (DEPTH, 3, d), 0.02),
        "lru_w_in": _normal(next(ks), (n_a, d, 2 * D_RNN), d ** -0.5),
        "lru_conv_w": _normal(next(ks), (n_a, CONV_WIDTH, D_RNN), CONV_WIDTH ** -0.5),
        "lru_conv_b": _normal(next(ks), (n_a, D_RNN), 0.02),
        "lru_gate_w": _normal(next(ks), (n_a, 2, LRU_BLOCKS, LRU_BLOCK_SIZE, LRU_BLOCK_SIZE), LRU_BLOCK_SIZE ** -0.5),
        "lru_gate_b": _normal(next(ks), (n_a, 2, D_RNN), 0.02),
        "lru_lambda": jnp.log(lam_s) - jnp.log1p(-lam_s),
        "lru_w_out": _normal(next(ks), (n_a, D_RNN, d), beta * D_RNN ** -0.5),
        "rw_mu": jax.random.uniform(next(ks), (n_b, 6, d), jnp.float32),
        "rw_w_r": _normal(next(ks), (n_b, d, d), d ** -0.5),
        "rw_w_k": _normal(next(ks), (n_b, d, d), d ** -0.5),
        "rw_w_v": _normal(next(ks), (n_b, d, d), d ** -0.5),
        "rw_w0": jax.random.uniform(next(ks), (n_b, d), jnp.float32, minval=-6.0, maxval=-1.0),
        "rw_w1": _normal(next(ks), (n_b, d, RW_DECAY_LORA), d ** -0.5),
        "rw_w2": _normal(next(ks), (n_b, RW_DECAY_LORA, d), 0.5 * RW_DECAY_LORA ** -0.5),
        "rw_a0": _normal(next(ks), (n_b, d), 0.1),
        "rw_a1": _normal(next(ks), (n_b, d, RW_AAA_LORA), d ** -0.5),
        "rw_a2": _normal(next(ks), (n_b, RW_AAA_LORA, d), RW_AAA_LORA ** -0.5),
        "rw_g1": _normal(next(ks), (n_b, d, RW_GATE_LORA), d ** -0.5),
        "rw_g2": _normal(next(ks), (n_b, RW_GATE_LORA, d), RW_GATE_LORA ** -0.5),
        "rw_k_k": 0.85 + _normal(next(ks), (n_b, d), 0.02),
        "rw_k_a": 1.0 + _normal(next(ks), (n_b, d), 0.02),
        "rw_r_k": _normal(next(ks), (n_b, RW_HEADS, RW_HEAD_SIZE), 0.1),
        "rw_lnx_g": 1.0 + _normal(next(ks), (n_b, d), 0.02),
        "rw_lnx_b": _normal(next(ks), (n_b, d), 0.02),
        "rw_w_o": _normal(next(ks), (n_b, d, d), beta * d ** -0.5),
        "ca_w_qkv": _normal(next(ks), (n_c, d, 3 * d), d ** -0.5),
        "ca_rel_bias": _normal(next(ks), (n_c, ATT_HEADS, 2 * MAX_REL + 1), 0.2),
        "ca_w_o": _normal(next(ks), (n_c, d, d), beta * d ** -0.5),
        "mx_w_q": _normal(next(ks), (DEPTH, d, d), d ** -0.5),
        "mx_w_kv": _normal(next(ks), (DEPTH, d, 2 * d), d ** -0.5),
        "mx_w_o": _normal(next(ks), (DEPTH, d, d), beta * d ** -0.5),
        "mlp_w1": _normal(next(ks), (DEPTH, d, D_FF), d ** -0.5),
        "mlp_w2": _normal(next(ks), (DEPTH, D_FF, d), beta * D_FF ** -0.5),
    }


def reference(x, mem, ln_g, ln_b,
              lru_w_in, lru_conv_w, lru_conv_b, lru_gate_w, lru_gate_b, lru_lambda, lru_w_out,
              rw_mu, rw_w_r, rw_w_k, rw_w_v, rw_w0, rw_w1, rw_w2, rw_a0, rw_a1, rw_a2, rw_g1, rw_g2,
              rw_k_k, rw_k_a, rw_r_k, rw_lnx_g, rw_lnx_b, rw_w_o,
              ca_w_qkv, ca_rel_bias, ca_w_o,
              mx_w_q, mx_w_kv, mx_w_o, mlp_w1, mlp_w2):
    h = x
    for i in range(DEPTH):
        kind, j = i % N_MIXERS, i // N_MIXERS
        if kind == 0:
            y = rglru_block(h, lru_w_in[j], lru_conv_w[j], lru_conv_b[j], lru_gate_w[j],
                            lru_gate_b[j], lru_lambda[j], lru_w_out[j])
        elif kind == 1:
            y = rwkv7_time_mix(h, rw_mu[j], rw_w_r[j], rw_w_k[j], rw_w_v[j], rw_w0[j], rw_w1[j],
                               rw_w2[j], rw_a0[j], rw_a1[j], rw_a2[j], rw_g1[j], rw_g2[j],
                               rw_k_k[j], rw_k_a[j], rw_r_k[j], rw_lnx_g[j], rw_lnx_b[j], rw_w_o[j])
        else:
            y = chunk_relpos_attention(h, ca_w_qkv[j], ca_rel_bias[j], ca_w_o[j])
        h = _layer_norm(DEEPNORM_ALPHA * h + y, ln_g[i, 0], ln_b[i, 0])
        y = memory_cross_attention(h, mem, mx_w_q[i], mx_w_kv[i], mx_w_o[i])
        h = _layer_norm(DEEPNORM_ALPHA * h + y, ln_g[i, 1], ln_b[i, 1])
        y = squared_relu_mlp(h, mlp_w1[i], mlp_w2[i])
        h = _layer_norm(DEEPNORM_ALPHA * h + y, ln_g[i, 2], ln_b[i, 2])
    return h
```

```python
import numpy as np
import concourse.bass as bass
import concourse.mybir as mybir
from concourse.bass_utils import run_bass_kernel_spmd
from contextlib import ExitStack

F32 = mybir.dt.float32
BF16 = mybir.dt.bfloat16
AF = mybir.ActivationFunctionType
ALU = mybir.AluOpType
AX = mybir.AxisListType

ENGS = ("pe", "act", "dve", "pool", "sp")
SEM_EPOCH = 30000

D = 1024
SEQ = 2048
NSEQ = 2
DEPTH = 4
ALPHA = (2 * DEPTH) ** 0.25
LN_EPS = 1e-5
D_RNN = 1344
NBLK = 16
BS = 84
D_FF = 4096
MEMT = 256
NEG = -30000.0


class Sched:
    def __init__(self, nc, stack, n_dma_sems=8):
        self.nc = nc
        self.stack = stack
        self.prog = {e: [] for e in ENGS}
        self.cnt = {e: 0 for e in ENGS}
        self.nsem = 0
        self.sem_owner = {}
        self.sem = {}
        for e in ENGS:
            self.sem[e] = self._newsem()
            self.sem_owner[id(self.sem[e])] = e
        self.known = {e: {} for e in ENGS}
        self.lastw = {}
        self.readers = {}
        self.dma_sems = {q: [[self._newsem(), 0] for _ in range(n_dma_sems)] for q in ("sp", "act", "pool")}
        self.dma_rr = {q: 0 for q in ("sp", "act", "pool")}
        self.ninstr = 0
        self.nwait = 0

    def _newsem(self):
        self.nsem += 1
        return self.stack.enter_context(self.nc.semaphore(f"s{self.nsem}"))

    def _deps(self, eng, reads, writes):
        deps = {}

        def add(ev, raw):
            if ev is None:
                return
            s, v = ev
            own = self.sem_owner.get(id(s))
            if own == eng:
                if eng == "pe":
                    return
            k = id(s)
            if k not in deps or deps[k][1] < v:
                deps[k] = (s, v)

        for k in reads:
            add(self.lastw.get(k), True)
        for k in writes:
            add(self.lastw.get(k), False)
            rd = self.readers.get(k)
            if rd:
                for ev in rd.values():
                    add(ev, False)
        return deps

    def _filter(self, eng, deps):
        waits = []
        kn = self.known[eng]
        for k, (s, v) in deps.items():
            if kn.get(k, 0) < v:
                kn[k] = v
                waits.append((s, v))
        self.nwait += len(waits)
        return waits

    def _record(self, ev, reads, writes):
        for k in writes:
            self.lastw[k] = ev
            self.readers[k] = {}
        for k in reads:
            r = self.readers.setdefault(k, {})
            r[id(ev[0])] = ev

    def op(self, eng, fn, reads=(), writes=()):
        deps = self._deps(eng, reads, writes)
        waits = self._filter(eng, deps)
        if self.cnt[eng] >= SEM_EPOCH:
            self.sem[eng] = self._newsem()
            self.sem_owner[id(self.sem[eng])] = eng
            self.cnt[eng] = 0
        self.cnt[eng] += 1
        ev = (self.sem[eng], self.cnt[eng])
        self.prog[eng].append((waits, fn, (self.sem[eng], 1)))
        self._record(ev, reads, writes)
        self.ninstr += 1
        return ev

    def dma(self, q, out, in_, reads=(), writes=()):
        slots = self.dma_sems[q]
        si = self.dma_rr[q]
        self.dma_rr[q] = (si + 1) % len(slots)
        slot = slots[si]
        deps = self._deps(q, reads, writes)
        if slot[1] > 0:
            deps[id(slot[0])] = (slot[0], 16 * slot[1])
        waits = self._filter(q, deps)
        slot[1] += 1
        ev = (slot[0], 16 * slot[1])
        self.prog[q].append((waits, (lambda e: e.dma_start(out=out, in_=in_)), (slot[0], 16)))
        self._record(ev, reads, writes)
        self.ninstr += 1
        return ev

    def all_events(self):
        evs = []
        for e in ENGS:
            if self.cnt[e] > 0:
                evs.append((self.sem[e], self.cnt[e]))
        for q in self.dma_sems:
            for s, n in self.dma_sems[q]:
                if n > 0:
                    evs.append((s, 16 * n))
        return evs

    def barrier(self):
        evs = self.all_events()
        for e in ENGS:
            deps = {}
            for s, v in evs:
                if self.sem_owner.get(id(s)) == e:
                    continue
                deps[id(s)] = (s, v)
            waits = self._filter(e, deps)
            if waits:
                self.prog[e].append((waits, None, None))
        self.lastw = {}
        self.readers = {}

    def emit(self):
        nc = self.nc
        prog = self.prog

        def run(name, e):
            for waits, fn, inc in prog[name]:
                for s, v in waits:
                    e.wait_ge(s, v)
                if fn is not None:
                    ins = fn(e)
                    if inc is not None:
                        ins.then_inc(inc[0], inc[1])

        with nc.Block() as block:
            @block.tensor
            def _(e):
                run("pe", e)

            @block.scalar
            def _(e):
                run("act", e)

            @block.vector
            def _(e):
                run("dve", e)

            @block.gpsimd
            def _(e):
                run("pool", e)

            @block.sync
            def _(e):
                run("sp", e)


class Arena:
    def __init__(self, t, nwords):
        self.t = t
        self.n = nwords
        self.off = 0

    def mark(self):
        return self.off

    def reset(self, m):
        self.off = m

    def alloc(self, shape, dtype, parts=128):
        free = int(np.prod(shape))
        words = free if dtype == F32 else (free + 1) // 2
        words = (words + 1) // 2 * 2
        assert self.off + words <= self.n, f"arena overflow {self.off}+{words}>{self.n}"
        ap = self.t[0:parts, self.off:self.off + words]
        self.off += words
        if dtype != F32:
            ap = ap.bitcast(dtype)
        ap = ap[:, 0:free]
        if len(shape) == 2:
            ap = ap.rearrange("p (a b) -> p a b", a=shape[0])
        elif len(shape) == 3:
            ap = ap.rearrange("p (a b c) -> p a b c", a=shape[0], b=shape[1])
        return ap


_uid = [0]


def uid(p="k"):
    _uid[0] += 1
    return f"{p}{_uid[0]}"


class Builder:
    def __init__(self, n_layers=DEPTH, dbg=None):
        self.layers = list(range(n_layers)) if isinstance(n_layers, int) else list(n_layers)
        self.dbg = dbg
        nc = self.nc = bass.Bass("TRN2", target_bir_lowering=False)
        self.st = ExitStack()
        self.S = Sched(nc, self.st)

    def din(self, name, shape):
        return self.nc.dram_tensor(name, list(shape), F32, kind="ExternalInput").ap()

    def build(self):
        nc, st, S = self.nc, self.st, self.S
        P = self.P = {}
        P["x"] = self.din("x", [NSEQ, SEQ, D])
        P["mem"] = self.din("mem", [NSEQ, MEMT, D])
        P["ln_g"] = self.din("ln_g", [DEPTH * 3, D])
        P["ln_b"] = self.din("ln_b", [DEPTH * 3, D])
        P["lru_w_in"] = self.din("lru_w_in", [2, D, 2 * D_RNN])
        P["lru_vec"] = self.din("lru_vec", [2, BS, NBLK, 8])
        P["lru_gate_w"] = self.din("lru_gate_w", [2, BS, 2 * NBLK, BS])
        P["lru_w_out"] = self.din("lru_w_out", [2, D_RNN, D])
        P["mx_w_q"] = self.din("mx_w_q", [DEPTH, D, D])
        P["mx_w_kv"] = self.din("mx_w_kv", [DEPTH, D, 2 * D])
        P["mx_w_o"] = self.din("mx_w_o", [DEPTH, D, D])
        P["mlp_w1"] = self.din("mlp_w1", [DEPTH, D, D_FF])
        P["mlp_w2"] = self.din("mlp_w2", [DEPTH, D_FF, D])
        P["ca_w_qkv"] = self.din("ca_w_qkv", [1, D, 3 * D])
        P["ca_w_o"] = self.din("ca_w_o", [1, D, D])
        P["ca_bias"] = self.din("ca_bias", [128, 16, 640])
        P["consts"] = self.din("consts", [128, 256])
        for nm in ("rw_w_r", "rw_w_k", "rw_w_v", "rw_w_o"):
            P[nm] = self.din(nm, [1, D, D])
        P["rw_w1"] = self.din("rw_w1", [1, D, 64])
        P["rw_a1"] = self.din("rw_a1", [1, D, 64])
        P["rw_g1"] = self.din("rw_g1", [1, D, 128])
        P["rw_w2"] = self.din("rw_w2", [1, 64, D])
        P["rw_a2"] = self.din("rw_a2", [1, 64, D])
        P["rw_g2"] = self.din("rw_g2", [1, 128, D])
        P["rw_vec"] = self.din("rw_vec", [1, 128, 8, 16])
        P["rw_lnx"] = self.din("rw_lnx", [1, 2, D])
        P["rwc"] = self.din("rwc", [128, 648])
        self.out = nc.dram_tensor("out", [NSEQ, SEQ, D], F32, kind="ExternalOutput").ap()
        self.h32 = nc.dram_tensor("h32", [NSEQ, SEQ, D], F32, kind="Internal").ap()
        if self.dbg:
            self.dbg_out = {n: nc.dram_tensor(n, list(shp), F32, kind="ExternalOutput").ap() for n, shp in self.dbg.items()}

        sb = lambda n, s, d: st.enter_context(nc.sbuf_tensor(n, s, d))
        self.hT = sb("hT", [128, 8, SEQ + 2], BF16)
        self.ident = sb("ident", [128, 128], BF16)
        self.identf = sb("identf", [128, 128], F32)
        self.cst = sb("cst", [128, 256], F32)
        self.lng = sb("lng", [128, D], F32)
        self.lnb = sb("lnb", [128, D], F32)
        self.lnz = [sb("lnz0", [128, D], F32)] * 2
        self.lnh = [sb(f"lnh{i}", [128, D], F32) for i in range(2)]
        self.lnhb = [sb(f"lnhb{i}", [128, D], BF16) for i in range(2)]
        self.lnst = [sb(f"lnst{i}", [128, 16], F32) for i in range(2)]
        self.lni = 0
        ARW = 37600
        self.arena_t = sb("arena", [128, ARW], F32)
        self.A = Arena(self.arena_t, ARW)
        self.PS = [st.enter_context(nc.psum_tensor(f"ps{i}", [128, 1024], F32)) for i in range(4)]
        self.out_events = []
        self.gi = 0
        self.gen_list = [(0, 0), (0, 1)]

        S.dma("sp", self.cst[:], P["consts"], writes=["cst"])
        S.op("dve", lambda e: e.tensor_copy(out=self.ident[:], in_=self.cst[:, 0:128]), reads=["cst"], writes=["ident"])
        S.op("dve", lambda e: e.tensor_copy(out=self.identf[:], in_=self.cst[:, 0:128]), reads=["cst"], writes=["identf"])
        S.op("pool", lambda e: e.memset(self.hT[:, :, 0:2], 0.0), writes=["hTpad"])

        for seq in range(NSEQ):
            self.load_x(seq)
            for li, layer in enumerate(self.layers):
                kind, j = layer % 3, layer // 3
                first = (li == 0)
                if kind == 0:
                    self.stage_lru(seq, layer, j, first)
                elif kind == 1:
                    self.stage_rwkv(seq, layer, j, first)
                else:
                    self.stage_attn(seq, layer, j, first)
                self.stage_memattn(seq, layer)
                self.stage_mlp(seq, layer, last=(li == len(self.layers) - 1))
        S.barrier()
        S.emit()
        self.st.close()
        return nc

    def stage_begin(self):
        self.S.barrier()
        self.A.reset(0)

    def load_w(self, dst, src, key, q="pool"):
        K, nk, ncols = dst.shape
        step = max(1, 2048 // K) if ncols * 4 >= 2048 else nk
        step = min(nk, max(1, (1 << 21) // (K * ncols * 4)))
        for k0 in range(0, nk, step):
            k1 = min(nk, k0 + step)
            self.S.dma(q, dst[:, k0:k1, :], src[k0 * K:k1 * K, :].rearrange("(k p) n -> p k n", p=K), writes=[key])

    def load_x(self, seq):
        S = self.S
        self.stage_begin()
        xb = [self.A.alloc([D], BF16) for _ in range(2)]
        for sub in range(SEQ // 128):
            b = xb[sub % 2]
            kb = f"xb{sub % 2}"
            S.dma("pool", b, self.P["x"][seq, sub * 128:(sub + 1) * 128, :], writes=[kb])
            self.to_hT(b, kb, sub)

    def to_hT(self, hb, kb, sub):
        S = self.S
        pi = self.lni % 2 if getattr(self, "force_pi", None) is None else self.force_pi
        ps = self.PS[1][:, pi * 512:(pi + 1) * 512].bitcast(BF16)
        pk = ("PS1", pi)
        for c in range(8):
            S.op("pe", (lambda e, c=c: e.transpose(ps[:, c * 128:(c + 1) * 128], hb[:, c * 128:(c + 1) * 128], self.ident[:])),
                 reads=[kb, "ident"], writes=[pk])
        dst = self.hT[:, :, 2 + sub * 128: 2 + (sub + 1) * 128]
        src = ps.rearrange("p (c t) -> p c t", c=8)
        S.op("dve", lambda e: e.tensor_copy(out=dst, in_=src), reads=[pk], writes=[("hT", sub)])
        self.lni += 1

    def load_ln(self, li):
        S = self.S
        S.dma("sp", self.lng[:], self.P["ln_g"][li, :].partition_broadcast(128), writes=["lng"])
        S.dma("sp", self.lnb[:], self.P["ln_b"][li, :].partition_broadcast(128), writes=["lnb"])

    def ln_finish(self, seq, sub, y, ykeys, first, last):
        S = self.S
        i = self.lni % 2
        z, hn, hb, stt = self.lnz[i], self.lnh[i], self.lnhb[i], self.lnst[i]
        kz, kh, khb, kst = "lnz0", f"lnh{i}", f"lnhb{i}", f"lnst{i}"
        src = (self.P["x"] if first else self.h32)[seq, sub * 128:(sub + 1) * 128, :]
        hk = ("h32", seq, sub)
        S.dma("sp", hn[:], src, reads=[hk], writes=[kh])
        S.op("dve", lambda e: e.scalar_tensor_tensor(out=z[:], in0=hn[:], scalar=float(ALPHA), in1=y, op0=ALU.mult, op1=ALU.add),
             reads=[kh] + list(ykeys), writes=[kz])
        S.op("dve", lambda e: e.bn_stats(out=stt[:, 0:6], in_=z[:, 0:512]), reads=[kz], writes=[kst + "a"])
        S.op("dve", lambda e: e.bn_stats(out=stt[:, 6:12], in_=z[:, 512:1024]), reads=[kz], writes=[kst + "b"])
        S.op("dve", lambda e: e.bn_aggr(out=stt[:, 12:14], in_=stt[:, 0:12]), reads=[kst + "a", kst + "b"], writes=[kst + "c"])
        S.op("act", lambda e: e.activation(out=stt[:, 14:15], in_=stt[:, 13:14], func=AF.Sqrt, bias=self.cst[:, 128:129], scale=1.0),
             reads=[kst + "c"], writes=[kst + "d0"])
        S.op("dve", lambda e: e.reciprocal(out=stt[:, 14:15], in_=stt[:, 14:15]), reads=[kst + "d0"], writes=[kst + "d"])
        S.op("dve", lambda e: e.tensor_scalar(out=stt[:, 15:16], in0=stt[:, 12:13], scalar1=stt[:, 14:15], scalar2=-1.0, op0=ALU.mult, op1=ALU.mult),
             reads=[kst + "c", kst + "d"], writes=[kst + "e"])
        S.op("act", lambda e: e.activation(out=z[:], in_=z[:], func=AF.Identity, bias=stt[:, 15:16], scale=stt[:, 14:15]),
             reads=[kz, kst + "d", kst + "e"], writes=[kz])
        S.op("pool", lambda e: e.tensor_tensor(out=z[:], in0=z[:], in1=self.lng[:], op=ALU.mult), reads=[kz, "lng"], writes=[kz])
        S.op("dve", lambda e: e.tensor_tensor(out=hn[:], in0=z[:], in1=self.lnb[:], op=ALU.add), reads=[kz, "lnb"], writes=[kh])
        dst = (self.out if last else self.h32)[seq, sub * 128:(sub + 1) * 128, :]
        S.dma("sp", dst, hn[:], reads=[kh], writes=[hk])
        if not last:
            S.op("act", lambda e: e.copy(out=hb[:], in_=hn[:]), reads=[kh], writes=[khb])
            self.to_hT(hb, khb, sub)
        else:
            self.lni += 1

    def gps(self, parts=128, n=512):
        i = self.gi
        self.gi += 1
        b = self.gen_list[i % len(self.gen_list)]
        return self.PS[b[0]][0:parts, b[1] * 512: b[1] * 512 + n], (f"PS{b[0]}", b[1])

    def hTs(self, k, t0, n):
        return self.hT[:, k, 2 + t0: 2 + t0 + n]

    def hTkeys(self, t0, n):
        return [("hT", s) for s in range(t0 // 128, (t0 + n + 127) // 128)]

    def dbg_store(self, name, ap, keys, dst_slice=None):
        if self.dbg and name in self.dbg:
            d = self.dbg_out[name] if dst_slice is None else dst_slice(self.dbg_out[name])
            self.S.dma("sp", d, ap, reads=keys)

    def stage_mlp(self, seq, layer, last):
        S, A, PS = self.S, self.A, self.PS
        self.stage_begin()
        self.load_ln(layer * 3 + 2)
        acc = A.alloc([16, D], F32)
        w1g = [A.alloc([8, 512], BF16) for _ in range(2)]
        w2g = [A.alloc([4, D], BF16) for _ in range(2)]
        hid = [A.alloc([4, 512], BF16) for _ in range(2)]
        rl = [A.alloc([512], F32) for _ in range(2)]
        w1 = self.P["mlp_w1"][layer]
        w2 = self.P["mlp_w2"][layer]
        nb = 0
        ny = 0
        for g in range(8):
            gi = g % 2
            self.load_w(w1g[gi], w1[:, g * 512:(g + 1) * 512], f"w1g{gi}")
            self.load_w(w2g[gi], w2[g * 512:(g + 1) * 512, :], f"w2g{gi}")
            for t in range(4):
                hi = (g * 4 + t) % 2
                for fc in range(4):
                    nb += 1
                    ps, pk = self.gps()
                    for k in range(8):
                        S.op("pe", (lambda e, k=k, fc=fc, ps=ps, gi=gi, t=t: e.matmul(ps, lhsT=w1g[gi][:, k, fc * 128:(fc + 1) * 128], rhs=self.hTs(k, t * 512, 512), start=(k == 0), stop=(k == 7))),
                             reads=[f"w1g{gi}"] + self.hTkeys(t * 512, 512), writes=[pk])
                    ri = nb % 2
                    S.op("act", (lambda e, ps=ps, ri=ri: e.activation(out=rl[ri], in_=ps, func=AF.Relu)), reads=[pk], writes=[f"rl{ri}"])
                    S.op("dve", (lambda e, ri=ri, hi=hi, fc=fc: e.tensor_tensor(out=hid[hi][:, fc, :], in0=rl[ri], in1=rl[ri], op=ALU.mult)),
                         reads=[f"rl{ri}"], writes=[(f"hid{hi}", fc)])
                for sub in range(4):
                    yi = ny % 2
                    ny += 1
                    py = PS[2 + yi]
                    for half in range(2):
                        for fc in range(4):
                            S.op("pe", (lambda e, fc=fc, half=half, py=py, hi=hi, gi=gi, sub=sub: e.matmul(py[:, half * 512:(half + 1) * 512], lhsT=hid[hi][:, fc, sub * 128:(sub + 1) * 128], rhs=w2g[gi][:, fc, half * 512:(half + 1) * 512], start=(fc == 0), stop=(fc == 3))),
                                 reads=[(f"hid{hi}", fc), f"w2g{gi}"], writes=[(f"PS{2 + yi}", half)])
                    a = acc[:, t * 4 + sub, :]
                    ak = ("acc", t * 4 + sub)
                    pkeys = [(f"PS{2 + yi}", 0), (f"PS{2 + yi}", 1)]
                    if g == 0:
                        S.op("act", (lambda e, a=a, py=py: e.copy(out=a, in_=py[:, :])), reads=pkeys, writes=[ak])
                    else:
                        S.op("dve", (lambda e, a=a, py=py: e.tensor_tensor(out=a, in0=a, in1=py[:, :], op=ALU.add)), reads=pkeys + [ak], writes=[ak])
        for s16 in range(16):
            self.ln_finish(seq, s16, acc[:, s16, :], [("acc", s16)], first=False, last=last)

    def stage_memattn(self, seq, layer):
        S, A, PS = self.S, self.A, self.PS
        self.stage_begin()
        self.load_ln(layer * 3 + 1)
        wb = [A.alloc([8, D], BF16) for _ in range(2)]
        memb = A.alloc([2, D], BF16)
        memT = A.alloc([8, MEMT], BF16)
        KT = A.alloc([8, MEMT], BF16)
        V = A.alloc([2, D], BF16)
        QT = A.alloc([8, 512], BF16)
        Pn = [A.alloc([4, MEMT], BF16) for _ in range(2)]
        PT = A.alloc([8, 512], BF16)
        OT = A.alloc([8, 512], BF16)
        sm = [A.alloc([16], F32) for _ in range(2)]
        wkv = self.P["mx_w_kv"][layer]
        self.load_w(wb[0], wkv[:, 0:D], "wb0")
        self.load_w(wb[1], wkv[:, D:2 * D], "wb1")
        S.dma("pool", memb, self.P["mem"][seq].rearrange("(a p) d -> p a d", p=128), writes=["memb"])
        for mt in range(2):
            ps = PS[0][:, mt * 512:(mt + 1) * 512].bitcast(BF16)
            for c in range(8):
                S.op("pe", (lambda e, c=c, mt=mt, ps=ps: e.transpose(ps[:, c * 128:(c + 1) * 128], memb[:, mt, c * 128:(c + 1) * 128], self.ident[:])),
                     reads=["memb", "ident"], writes=[("PS0", mt)])
            S.op("dve", (lambda e, mt=mt, ps=ps: e.tensor_copy(out=memT[:, :, mt * 128:(mt + 1) * 128], in_=ps.rearrange("p (c t) -> p c t", c=8))),
                 reads=[("PS0", mt)], writes=[("memT", mt)])
        mk = [("memT", 0), ("memT", 1)]
        for oc in range(8):
            ps, pk = self.gps(n=MEMT)
            for k in range(8):
                S.op("pe", (lambda e, k=k, oc=oc, ps=ps: e.matmul(ps, lhsT=wb[0][:, k, oc * 128:(oc + 1) * 128], rhs=memT[:, k, :], start=(k == 0), stop=(k == 7))),
                     reads=["wb0"] + mk, writes=[pk])
            S.op("act", (lambda e, oc=oc, ps=ps: e.copy(out=KT[:, oc, :], in_=ps)), reads=[pk], writes=[("KT", oc)])
        for mt in range(2):
            for half in range(2):
                ps, pk = self.gps()
                for k in range(8):
                    S.op("pe", (lambda e, k=k, mt=mt, half=half, ps=ps: e.matmul(ps, lhsT=memT[:, k, mt * 128:(mt + 1) * 128], rhs=wb[1][:, k, half * 512:(half + 1) * 512], start=(k == 0), stop=(k == 7))),
                         reads=["wb1"] + mk, writes=[pk])
                S.op("dve", (lambda e, mt=mt, half=half, ps=ps: e.tensor_copy(out=V[:, mt, half * 512:(half + 1) * 512], in_=ps)), reads=[pk], writes=[("V", mt, half)])
        vk = [("V", a, b) for a in range(2) for b in range(2)]
        self.load_w(wb[0], self.P["mx_w_q"][layer], "wb0")
        self.load_w(wb[1], self.P["mx_w_o"][layer], "wb1")
        for t in range(4):
            for oc in range(8):
                ps, pk = self.gps()
                for k in range(8):
                    S.op("pe", (lambda e, k=k, oc=oc, ps=ps, t=t: e.matmul(ps, lhsT=wb[0][:, k, oc * 128:(oc + 1) * 128], rhs=self.hTs(k, t * 512, 512), start=(k == 0), stop=(k == 7))),
                         reads=["wb0"] + self.hTkeys(t * 512, 512), writes=[pk])
                S.op("act", (lambda e, oc=oc, ps=ps: e.activation(out=QT[:, oc, :], in_=ps, func=AF.Copy, scale=1.0 / 16.0)), reads=[pk], writes=[("QT", oc)])
            for sub in range(4):
                pi = sub % 2
                ps = PS[0]
                pkeys = [("PS0", 0), ("PS0", 1)]
                for h in range(4):
                    for c in range(2):
                        S.op("pe", (lambda e, h=h, c=c, ps=ps, sub=sub: e.matmul(ps[:, h * 256:(h + 1) * 256], lhsT=QT[:, 2 * h + c, sub * 128:(sub + 1) * 128], rhs=KT[:, 2 * h + c, :], start=(c == 0), stop=(c == 1))),
                             reads=[("QT", 2 * h + c), ("KT", 2 * h + c)], writes=[("PS0", h // 2)])
                smi = sm[pi]
                ks = f"sm{pi}"
                S.op("dve", (lambda e, ps=ps, smi=smi: e.tensor_reduce(out=smi[:, 0:4], in_=ps[:, :].rearrange("p (h m) -> p h m", h=4), axis=AX.X, op=ALU.max, negate=True)),
                     reads=pkeys, writes=[ks + "m"])
                pn = Pn[pi]
                for h in range(4):
                    S.op("act", (lambda e, h=h, ps=ps, smi=smi, pn=pn: e.activation(out=pn[:, h, :], in_=ps[:, h * 256:(h + 1) * 256], func=AF.Exp, bias=smi[:, h:h + 1], scale=1.0, accum_out=smi[:, 4 + h:5 + h])),
                         reads=[("PS0", h // 2), ks + "m"], writes=[(f"Pn{pi}", h), (ks + "s", h)])
                S.op("dve", (lambda e, smi=smi: e.reciprocal(out=smi[:, 8:12], in_=smi[:, 4:8])), reads=[(ks + "s", h) for h in range(4)], writes=[ks + "r"])
                S.op("dve", (lambda e, smi=smi, pn=pn: e.tensor_tensor(out=pn[:, :, :], in0=pn[:, :, :], in1=smi[:, 8:12].unsqueeze(2).to_broadcast([128, 4, MEMT]), op=ALU.mult)),
                     reads=[(f"Pn{pi}", h) for h in range(4)] + [ks + "r"], writes=[(f"Pn{pi}", h) for h in range(4)])
                pt = PS[1][:, pi * 512:(pi + 1) * 512].bitcast(BF16)
                ptk = ("PS1", pi)
                for h in range(4):
                    for mc in range(2):
                        S.op("pe", (lambda e, h=h, mc=mc, pt=pt, pn=pn: e.transpose(pt[:, (h * 2 + mc) * 128:(h * 2 + mc + 1) * 128], pn[:, h, mc * 128:(mc + 1) * 128], self.ident[:])),
                             reads=[(f"Pn{pi}", h), "ident"], writes=[ptk])
                S.op("dve", (lambda e, pt=pt, sub=sub: e.tensor_copy(out=PT[:, :, sub * 128:(sub + 1) * 128], in_=pt.rearrange("p (c t) -> p c t", c=8))),
                     reads=[ptk], writes=[("PT", sub)])
            for oc in range(8):
                h, c = oc // 2, oc % 2
                ps, pk = self.gps()
                for mc in range(2):
                    S.op("pe", (lambda e, h=h, c=c, mc=mc, ps=ps: e.matmul(ps, lhsT=V[:, mc, h * 256 + c * 128: h * 256 + (c + 1) * 128], rhs=PT[:, h * 2 + mc, :], start=(mc == 0), stop=(mc == 1))),
                         reads=vk + [("PT", s) for s in range(4)], writes=[pk])
                S.op("act", (lambda e, oc=oc, ps=ps: e.copy(out=OT[:, oc, :], in_=ps)), reads=[pk], writes=[("OT", oc)])
            self.out_proj(seq, t, lambda k, sub: OT[:, k, sub * 128:(sub + 1) * 128], [("OT", k) for k in range(8)], wb[1], "wb1", 8, first=False)

    def out_proj(self, seq, t, lhs_fn, lkeys, w, wkey, nk, first):
        S, PS = self.S, self.PS
        for sub in range(4):
            yi = sub % 2
            py = PS[2 + yi]
            for half in range(2):
                for k in range(nk):
                    S.op("pe", (lambda e, k=k, half=half, py=py, sub=sub: e.matmul(py[:, half * 512:(half + 1) * 512], lhsT=lhs_fn(k, sub), rhs=w[:, k, half * 512:(half + 1) * 512], start=(k == 0), stop=(k == nk - 1))),
                         reads=list(lkeys) + [wkey], writes=[(f"PS{2 + yi}", half)])
            self.ln_finish(seq, t * 4 + sub, py[:, :], [(f"PS{2 + yi}", 0), (f"PS{2 + yi}", 1)], first=first, last=False)

    def stage_lru(self, seq, layer, j, first):
        S, A, PS = self.S, self.A, self.PS
        self.stage_begin()
        self.load_ln(layer * 3 + 0)
        win = A.alloc([8, 2 * D_RNN], BF16)
        wout = A.alloc([NBLK, D], BF16, parts=BS)
        gw = A.alloc([2 * NBLK, BS], BF16, parts=BS)
        vec = A.alloc([NBLK, 8], F32, parts=BS)
        c8 = A.alloc([NBLK], F32, parts=BS)
        carry = A.alloc([NBLK], F32, parts=BS)
        xpb = [A.alloc([516], F32, parts=BS) for _ in range(2)]
        hist = A.alloc([NBLK, 4], F32, parts=BS)
        mT = A.alloc([NBLK, 512], BF16, parts=BS)
        NB2 = 2
        gb = [A.alloc([512], F32, parts=BS) for _ in range(NB2)]
        xr = [A.alloc([512], F32, parts=BS) for _ in range(NB2)]
        xrb = [A.alloc([512], BF16, parts=BS) for _ in range(NB2)]
        rg = [A.alloc([512], F32, parts=BS) for _ in range(NB2)]
        ig = [A.alloc([512], F32, parts=BS) for _ in range(NB2)]
        aa = [A.alloc([512], F32, parts=BS) for _ in range(NB2)]
        sq = [A.alloc([512], F32, parts=BS) for _ in range(NB2)]
        hs = [A.alloc([512], F32, parts=BS) for _ in range(NB2)]
        self.load_w(win, self.P["lru_w_in"][j], "win")
        S.dma("pool", wout, self.P["lru_w_out"][j].rearrange("(n p) d -> p n d", p=BS), writes=["wout"])
        S.dma("pool", gw, self.P["lru_gate_w"][j], writes=["gw"])
        S.dma("sp", vec, self.P["lru_vec"][j], writes=["vec"])
        tx = A.alloc([NBLK], F32, parts=BS)
        tl = A.alloc([NBLK], F32, parts=BS)
        tu = A.alloc([NBLK], F32, parts=BS)
        S.op("act", lambda e: e.activation(out=tx, in_=vec[:, :, 7], func=AF.Exp, scale=-1.0), reads=["vec"], writes=["tx"])
        S.op("act", lambda e: e.activation(out=tl, in_=tx, func=AF.Ln, bias=1.0, scale=1.0), reads=["tx"], writes=["tl"])
        S.op("dve", lambda e: e.tensor_scalar(out=tu, in0=tx, scalar1=-0.25, scalar2=1.0 / 3.0, op0=ALU.mult, op1=ALU.add), reads=["tx"], writes=["tu"])
        S.op("dve", lambda e: e.tensor_tensor(out=tu, in0=tu, in1=tx, op=ALU.mult), reads=["tu", "tx"], writes=["tu"])
        S.op("dve", lambda e: e.tensor_scalar(out=tu, in0=tu, scalar1=-1.0, scalar2=0.5, op0=ALU.mult, op1=ALU.add), reads=["tu"], writes=["tu"])
        S.op("dve", lambda e: e.tensor_tensor(out=tu, in0=tu, in1=tx, op=ALU.mult), reads=["tu", "tx"], writes=["tu"])
        S.op("dve", lambda e: e.tensor_scalar(out=tu, in0=tu, scalar1=-1.0, scalar2=1.0, op0=ALU.mult, op1=ALU.add), reads=["tu"], writes=["tu"])
        S.op("dve", lambda e: e.tensor_tensor(out=tu, in0=tu, in1=tx, op=ALU.mult), reads=["tu", "tx"], writes=["tu"])
        S.op("dve", lambda e: e.tensor_tensor(out=tu, in0=tu, in1=tl, op=ALU.subtract), reads=["tu", "tl"], writes=["tu"])
        S.op("dve", lambda e: e.tensor_scalar(out=tx, in0=tx, scalar1=0.05, scalar2=None, op0=ALU.is_lt), reads=["tx", "tu"], writes=["tx"])
        S.op("dve", lambda e: e.tensor_tensor(out=tu, in0=tu, in1=tx, op=ALU.mult), reads=["tu", "tx"], writes=["tu"])
        S.op("dve", lambda e: e.tensor_tensor(out=tu, in0=tu, in1=tl, op=ALU.add), reads=["tu", "tl"], writes=["tu"])
        S.op("dve", lambda e: e.tensor_scalar(out=c8, in0=tu, scalar1=-8.0, scalar2=None, op0=ALU.mult), reads=["tu"], writes=["c8"])
        S.op("dve", lambda e: e.memset(carry, 0.0), writes=["carry"])
        S.op("dve", lambda e: e.memset(hist, 0.0), writes=[("hist", n) for n in range(NBLK)])
        nb = 0
        for t in range(4):
            for n in range(NBLK):
                bi = n % NB2
                ps, pk = self.gps(parts=BS)
                for k in range(8):
                    S.op("pe", (lambda e, k=k, n=n, ps=ps, t=t: e.matmul(ps, lhsT=win[:, k, n * BS:(n + 1) * BS], rhs=self.hTs(k, t * 512, 512), start=(k == 0), stop=(k == 7))),
                         reads=["win"] + self.hTkeys(t * 512, 512), writes=[pk])
                S.op("act", (lambda e, ps=ps, bi=bi: e.activation(out=gb[bi], in_=ps, func=AF.Gelu)), reads=[pk], writes=[f"gb{bi}"])
                ps2, pk2 = self.gps(parts=BS)
                for k in range(8):
                    S.op("pe", (lambda e, k=k, n=n, ps2=ps2, t=t: e.matmul(ps2, lhsT=win[:, k, D_RNN + n * BS: D_RNN + (n + 1) * BS], rhs=self.hTs(k, t * 512, 512), start=(k == 0), stop=(k == 7))),
                         reads=["win"] + self.hTkeys(t * 512, 512), writes=[pk2])
                xp = xpb[bi]
                xk = f"xpb{bi}"
                S.op("pool", (lambda e, xp=xp, n=n: e.tensor_copy(out=xp[:, 0:4], in_=hist[:, n, :])), reads=[("hist", n)], writes=[xk + "h"])
                S.op("act", (lambda e, xp=xp, ps2=ps2: e.copy(out=xp[:, 4:516], in_=ps2)), reads=[pk2], writes=[xk])
                x_ = xr[bi]
                kx = f"xr{bi}"
                S.op("dve", (lambda e, xp=xp, x_=x_, n=n: e.tensor_scalar(out=x_, in0=xp[:, 1:513], scalar1=vec[:, n, 0:1], scalar2=vec[:, n, 4:5], op0=ALU.mult, op1=ALU.add)),
                     reads=[xk, xk + "h", "vec"], writes=[kx])
                for kk in range(1, 4):
                    S.op("dve", (lambda e, xp=xp, x_=x_, n=n, kk=kk: e.scalar_tensor_tensor(out=x_, in0=xp[:, 1 + kk:513 + kk], scalar=vec[:, n, kk:kk + 1], in1=x_, op0=ALU.mult, op1=ALU.add)),
                         reads=[xk, xk + "h", "vec", kx], writes=[kx])
                S.op("pool", (lambda e, xp=xp, n=n: e.tensor_copy(out=hist[:, n, :], in_=xp[:, 512:516])), reads=[xk], writes=[("hist", n)])
                S.op("act", (lambda e, x_=x_, bi=bi: e.copy(out=xrb[bi], in_=x_)), reads=[kx], writes=[f"xrb{bi}"])
                pg, pgk = self.gps(parts=BS)
                S.op("pe", (lambda e, n=n, pg=pg, bi=bi: e.matmul(pg, lhsT=gw[:, n, :], rhs=xrb[bi], start=True, stop=True)), reads=["gw", f"xrb{bi}"], writes=[pgk])
                S.op("act", (lambda e, pg=pg, bi=bi, n=n: e.activation(out=rg[bi], in_=pg, func=AF.Sigmoid, bias=vec[:, n, 5:6], scale=1.0)), reads=[pgk, "vec"], writes=[f"rg{bi}"])
                pg2, pgk2 = self.gps(parts=BS)
                S.op("pe", (lambda e, n=n, pg2=pg2, bi=bi: e.matmul(pg2, lhsT=gw[:, NBLK + n, :], rhs=xrb[bi], start=True, stop=True)), reads=["gw", f"xrb{bi}"], writes=[pgk2])
                S.op("act", (lambda e, pg2=pg2, bi=bi, n=n: e.activation(out=ig[bi], in_=pg2, func=AF.Sigmoid, bias=vec[:, n, 6:7], scale=1.0)), reads=[pgk2, "vec"], writes=[f"ig{bi}"])
                S.op("act", (lambda e, bi=bi, n=n: e.activation(out=aa[bi], in_=rg[bi], func=AF.Exp, scale=c8[:, n:n + 1])), reads=[f"rg{bi}", "c8"], writes=[f"aa{bi}"])
                S.op("act", (lambda e, bi=bi: e.activation(out=sq[bi], in_=aa[bi], func=AF.Square)), reads=[f"aa{bi}"], writes=[f"sq{bi}"])
                S.op("act", (lambda e, bi=bi: e.activation(out=sq[bi], in_=sq[bi], func=AF.Sqrt, bias=1.0, scale=-1.0)), reads=[f"sq{bi}"], writes=[f"sq{bi}"])
                S.op("pool", (lambda e, bi=bi: e.tensor_tensor(out=ig[bi], in0=ig[bi], in1=xr[bi], op=ALU.mult)), reads=[f"ig{bi}", kx], writes=[f"ig{bi}"])
                S.op("dve", (lambda e, bi=bi: e.tensor_tensor(out=ig[bi], in0=ig[bi], in1=sq[bi], op=ALU.mult)), reads=[f"ig{bi}", f"sq{bi}"], writes=[f"ig{bi}"])
                S.op("dve", (lambda e, bi=bi, n=n: e.tensor_tensor_scan(out=hs[bi], data0=aa[bi], data1=ig[bi], initial=carry[:, n:n + 1], op0=ALU.mult, op1=ALU.add)),
                     reads=[f"aa{bi}", f"ig{bi}", ("carry", n)], writes=[f"hs{bi}"])
                S.op("act", (lambda e, bi=bi, n=n: e.copy(out=carry[:, n:n + 1], in_=hs[bi][:, 511:512])), reads=[f"hs{bi}"], writes=[("carry", n)])
                S.op("dve", (lambda e, bi=bi, n=n: e.tensor_tensor(out=mT[:, n, :], in0=hs[bi], in1=gb[bi], op=ALU.mult)), reads=[f"hs{bi}", f"gb{bi}"], writes=[("mT", n)])
            self.out_proj(seq, t, lambda k, sub: mT[:, k, sub * 128:(sub + 1) * 128], [("mT", n) for n in range(NBLK)], wout, "wout", NBLK, first=first)


    def stage_rwkv(self, seq, layer, j, first):
        S, A, PS = self.S, self.A, self.PS
        self.stage_begin()
        self.load_ln(layer * 3 + 0)
        RW = F32
        C0 = float(np.exp(-0.5))
        wr = A.alloc([8, D], BF16)
        wk = A.alloc([8, D], BF16)
        wv = A.alloc([8, D], BF16)
        wo = A.alloc([8, D], BF16)
        w1b = A.alloc([8, 64], BF16)
        a1b = A.alloc([8, 64], BF16)
        g1b = A.alloc([8, 128], BF16)
        w2b = A.alloc([D], BF16, parts=64)
        a2b = A.alloc([D], BF16, parts=64)
        g2b = A.alloc([D], BF16)
        lxg = A.alloc([D], F32)
        lxb = A.alloc([D], F32)
        vec = A.alloc([8, 16], F32)
        omu = A.alloc([8, 6], F32)
        rwc = A.alloc([648], F32)
        mask1 = rwc[:, 0:256]
        masksl = rwc[:, 256:384]
        blk = rwc[:, 384:512]
        rmask = rwc[:, 512:640]
        ind2 = rwc[:, 640:642]
        gneps = rwc[:, 642:643]
        xs = [A.alloc([8, 128], BF16) for _ in range(2)]
        xprev = A.alloc([8, 128], BF16)
        tmpx = A.alloc([8, 128], BF16)
        xlast = A.alloc([8], BF16)
        thw = A.alloc([128], BF16, parts=64)
        la = A.alloc([128], BF16, parts=64)
        sgl = A.alloc([128], BF16)
        V = A.alloc([D], F32)
        Y = A.alloc([D], F32)
        Hst = A.alloc([8, 2, 64], F32)
        Hd = [A.alloc([64], F32) for _ in range(2)]
        scr = A.alloc([D], F32)
        ob = A.alloc([D], BF16)
        OT = A.alloc([8, 128], BF16)
        stt = A.alloc([112], F32)
        names = ["sg", "a", "kq", "kk", "k", "r", "rn", "cum", "G", "Gi", "ex", "ab", "t1", "km", "E", "BhT", "KhT", "rkr"]
        PBs = []
        tmps = {n: A.alloc([128], F32) for n in names if n != "G"}
        for i in range(2):
            pb = dict(tmps)
            pb["G"] = A.alloc([128], F32)
            pb["ATRT"] = A.alloc([256], RW)
            pb["BT"] = A.alloc([128], RW)
            pb["KT"] = A.alloc([128], RW)
            pb["BK"] = A.alloc([256], RW)
            PBs.append(pb)
        HB = []
        for i in range(2):
            hb = {"Mk": A.alloc([256], RW), "Mb": A.alloc([256], RW), "X0": A.alloc([128], RW),
                  "XX": [A.alloc([256], RW) for _ in range(2)], "Z": [A.alloc([128], RW) for _ in range(2)],
                  "Gs": A.alloc([64], RW), "Us": A.alloc([64], RW)}
            S.op("dve", (lambda e, hb=hb: e.memset(hb["Us"], 0.0)), writes=[(f"hb{i}", "Us")])
            S.op("dve", (lambda e, hb=hb: e.memset(hb["Gs"], 0.0)), writes=[(f"hb{i}", "Gs")])
            HB.append(hb)

        self.load_w(wr, self.P["rw_w_r"][j], "wr")
        self.load_w(wk, self.P["rw_w_k"][j], "wk")
        self.load_w(wv, self.P["rw_w_v"][j], "wv")
        self.load_w(wo, self.P["rw_w_o"][j], "wo")
        self.load_w(w1b, self.P["rw_w1"][j], "w1b")
        self.load_w(a1b, self.P["rw_a1"][j], "a1b")
        self.load_w(g1b, self.P["rw_g1"][j], "g1b")
        S.dma("pool", w2b, self.P["rw_w2"][j], writes=["w2b"])
        S.dma("pool", a2b, self.P["rw_a2"][j], writes=["a2b"])
        S.dma("pool", g2b, self.P["rw_g2"][j], writes=["g2b"])
        S.dma("sp", vec, self.P["rw_vec"][j], writes=["vec"])
        S.dma("sp", rwc, self.P["rwc"], writes=["rwc"])
        S.dma("sp", lxg, self.P["rw_lnx"][j, 0, :].partition_broadcast(128), writes=["lxg"])
        S.dma("sp", lxb, self.P["rw_lnx"][j, 1, :].partition_broadcast(128), writes=["lxb"])
        S.op("dve", lambda e: e.tensor_scalar(out=omu, in0=vec[:, :, 0:6], scalar1=-1.0, scalar2=1.0, op0=ALU.mult, op1=ALU.add), reads=["vec"], writes=["omu"])
        S.op("dve", lambda e: e.memset(Hst, 0.0), writes=[("H", h) for h in range(16)])
        S.op("dve", lambda e: e.memset(xlast, 0.0), writes=["xlast"])
        self.force_pi = 0
        coefps = PS[1][:, 512:528]
        ckey = ("PS1", 1)

        def mix(m, bi, hk, t0):
            S.op("dve", (lambda e: e.tensor_tensor(out=tmpx, in0=xprev, in1=vec[:, :, m:m + 1].to_broadcast([128, 8, 128]), op=ALU.mult)),
                 reads=["xprev", "vec"], writes=["tmpx"])
            S.op("dve", (lambda e: e.tensor_tensor(out=xs[bi], in0=self.hT[:, :, 2 + t0:2 + t0 + 128], in1=omu[:, :, m:m + 1].to_broadcast([128, 8, 128]), op=ALU.mult)),
                 reads=hk + ["omu"], writes=[f"xs{bi}"])
            S.op("pool", (lambda e: e.tensor_tensor(out=xs[bi], in0=xs[bi], in1=tmpx, op=ALU.add)), reads=[f"xs{bi}", "tmpx"], writes=[f"xs{bi}"])

        for tt in range(16):
            t0 = tt * 128
            hk = [("hT", tt)]
            S.op("pool", (lambda e, t0=t0: e.tensor_copy(out=xprev[:, :, 1:128], in_=self.hT[:, :, 2 + t0:2 + t0 + 127])), reads=hk, writes=["xprev"])
            S.op("pool", (lambda e: e.tensor_copy(out=xprev[:, :, 0:1], in_=xlast.unsqueeze(2))), reads=["xlast", "xprev"], writes=["xprev"])
            S.op("pool", (lambda e, t0=t0: e.tensor_copy(out=xlast.unsqueeze(2), in_=self.hT[:, :, 2 + t0 + 127:2 + t0 + 128])), reads=hk + ["xprev"], writes=["xlast"])
            mix(1, 0, hk, t0)
            ps, pk = self.gps(parts=64, n=128)
            for k in range(8):
                S.op("pe", (lambda e, k=k, ps=ps: e.matmul(ps, lhsT=w1b[:, k, :], rhs=xs[0][:, k, :], start=(k == 0), stop=(k == 7))), reads=["w1b", "xs0"], writes=[pk])
            S.op("act", (lambda e, ps=ps: e.activation(out=thw, in_=ps, func=AF.Tanh)), reads=[pk], writes=["thw"])
            mix(4, 1, hk, t0)
            ps, pk = self.gps(parts=64, n=128)
            for k in range(8):
                S.op("pe", (lambda e, k=k, ps=ps: e.matmul(ps, lhsT=a1b[:, k, :], rhs=xs[1][:, k, :], start=(k == 0), stop=(k == 7))), reads=["a1b", "xs1"], writes=[pk])
            S.op("act", (lambda e, ps=ps: e.copy(out=la, in_=ps)), reads=[pk], writes=["la"])
            mix(5, 0, hk, t0)
            ps, pk = self.gps(n=128)
            for k in range(8):
                S.op("pe", (lambda e, k=k, ps=ps: e.matmul(ps, lhsT=g1b[:, k, :], rhs=xs[0][:, k, :], start=(k == 0), stop=(k == 7))), reads=["g1b", "xs0"], writes=[pk])
            S.op("act", (lambda e, ps=ps: e.activation(out=sgl, in_=ps, func=AF.Sigmoid)), reads=[pk], writes=["sgl"])
            mix(3, 1, hk, t0)
            for half in range(2):
                ps, pk = self.gps()
                for k in range(8):
                    S.op("pe", (lambda e, k=k, ps=ps, half=half: e.matmul(ps, lhsT=xs[1][:, k, :], rhs=wv[:, k, half * 512:(half + 1) * 512], start=(k == 0), stop=(k == 7))), reads=["wv", "xs1"], writes=[pk])
                S.op("act", (lambda e, ps=ps, half=half: e.copy(out=V[:, half * 512:(half + 1) * 512], in_=ps)), reads=[pk], writes=[("V", half)])
            mix(0, 0, hk, t0)
            mix(2, 1, hk, t0)
            import os
            RWD = int(os.environ.get("RW_DBG", "9"))
            if RWD < 9:
                S.op("dve", (lambda e: e.memset(Y, 0.0)), reads=[("Ydone", 0), ("Ydone", 1)], writes=[("Y", h // 8, q, h) for h in range(16) for q in range(2)])
                S.op("pe", (lambda e: e.matmul(coefps, lhsT=sgl, rhs=g2b[:, 0:16], start=True, stop=True)), reads=["sgl", "g2b"], writes=[ckey])
            RWS = int(os.environ.get("RW_SUB", "999"))
            for c in range(8 if RWD >= 2 else 0):
                if RWS < 999:
                    class _F:
                        def __init__(s_, S0):
                            s_.S0, s_.n = S0, 0
                        def op(s_, *a, **k):
                            s_.n += 1
                            if s_.n <= RWS:
                                return s_.S0.op(*a, **k)
                        def __getattr__(s_, nm):
                            return getattr(s_.S0, nm)
                    S = _F(self.S)
                pb = PBs[c % 2]
                pn = f"pb{c % 2}"
                K_ = lambda n, pn=pn: (pn, n) if n in ("G", "AT", "RT", "BT", "KT", "BK") else ("pbt", n)
                ps, pk = self.gps()
                for k in range(8):
                    S.op("pe", (lambda e, k=k, ps=ps, c=c: e.matmul(ps[:, 0:128], lhsT=wr[:, k, c * 128:(c + 1) * 128], rhs=xs[0][:, k, :], start=(k == 0), stop=(k == 7))), reads=["wr", "xs0"], writes=[pk])
                for k in range(8):
                    S.op("pe", (lambda e, k=k, ps=ps, c=c: e.matmul(ps[:, 128:256], lhsT=wk[:, k, c * 128:(c + 1) * 128], rhs=xs[1][:, k, :], start=(k == 0), stop=(k == 7))), reads=["wk", "xs1"], writes=[pk])
                S.op("pe", (lambda e, ps=ps, c=c: e.matmul(ps[:, 256:384], lhsT=w2b[:, c * 128:(c + 1) * 128], rhs=thw, start=True, stop=True)), reads=["w2b", "thw"], writes=[pk])
                S.op("pe", (lambda e, ps=ps, c=c: e.matmul(ps[:, 384:512], lhsT=a2b[:, c * 128:(c + 1) * 128], rhs=la, start=True, stop=True)), reads=["a2b", "la"], writes=[pk])
                vc = lambda i, c=c: vec[:, c, i:i + 1]
                S.op("act", (lambda e, ps=ps, pb=pb, vc=vc: e.activation(out=pb["sg"], in_=ps[:, 256:384], func=AF.Sigmoid, bias=vc(6), scale=1.0)), reads=[pk, "vec"], writes=[K_("sg")])
                S.op("act", (lambda e, ps=ps, pb=pb, vc=vc: e.activation(out=pb["a"], in_=ps[:, 384:512], func=AF.Sigmoid, bias=vc(7), scale=1.0)), reads=[pk, "vec"], writes=[K_("a")])
                S.op("act", (lambda e, ps=ps, pb=pb: e.copy(out=pb["k"], in_=ps[:, 128:256])), reads=[pk], writes=[K_("k")])
                S.op("dve", (lambda e, pb=pb, vc=vc: e.tensor_scalar(out=pb["kk"], in0=pb["k"], scalar1=vc(8), scalar2=None, op0=ALU.mult)), reads=[K_("k"), "vec"], writes=[K_("kk")])
                S.op("act", (lambda e, pb=pb: e.activation(out=pb["kq"], in_=pb["kk"], func=AF.Square)), reads=[K_("kk")], writes=[K_("kq")])
                S.op("dve", (lambda e, ps=ps, pb=pb: e.tensor_copy(out=pb["r"], in_=ps[:, 0:128])), reads=[pk], writes=[K_("r")])
                ps2, pk2 = self.gps(n=128)
                S.op("pe", (lambda e, ps2=ps2, pb=pb: e.matmul(ps2, lhsT=blk, rhs=pb["kq"], start=True, stop=True)), reads=["rwc", K_("kq")], writes=[pk2])
                S.op("act", (lambda e, ps2=ps2, pb=pb: e.activation(out=pb["rn"], in_=ps2, func=AF.Sqrt)), reads=[pk2], writes=[K_("rn")])
                S.op("dve", (lambda e, pb=pb: e.tensor_scalar(out=pb["rn"], in0=pb["rn"], scalar1=1e-12, scalar2=None, op0=ALU.max)), reads=[K_("rn")], writes=[K_("rn")])
                S.op("dve", (lambda e, pb=pb: e.reciprocal(out=pb["rn"], in_=pb["rn"])), reads=[K_("rn")], writes=[K_("rn")])
                S.op("dve", (lambda e, pb=pb: e.tensor_tensor(out=pb["kk"], in0=pb["kk"], in1=pb["rn"], op=ALU.mult)), reads=[K_("kk"), K_("rn")], writes=[K_("kk")])
                S.op("dve", (lambda e, pb=pb: e.tensor_tensor_scan(out=pb["cum"], data0=rmask, data1=pb["sg"], initial=0.0, op0=ALU.mult, op1=ALU.add)), reads=["rwc", K_("sg")], writes=[K_("cum")])
                S.op("act", (lambda e, pb=pb: e.activation(out=pb["G"], in_=pb["cum"], func=AF.Exp, scale=-C0)), reads=[K_("cum")], writes=[K_("G")])
                S.op("act", (lambda e, pb=pb: e.activation(out=pb["Gi"], in_=pb["cum"], func=AF.Exp, scale=C0)), reads=[K_("cum")], writes=[K_("Gi")])
                S.op("dve", (lambda e, pb=pb: e.tensor_tensor(out=pb["ex"], in0=pb["cum"], in1=pb["sg"], op=ALU.subtract)), reads=[K_("cum"), K_("sg")], writes=[K_("ex")])
                S.op("act", (lambda e, pb=pb: e.activation(out=pb["ex"], in_=pb["ex"], func=AF.Exp, scale=-C0)), reads=[K_("ex")], writes=[K_("ex")])
                S.op("dve", (lambda e, pb=pb: e.scalar_tensor_tensor(out=pb["ATRT"][:, 0:128], in0=pb["kk"], scalar=-1.0, in1=pb["ex"], op0=ALU.mult, op1=ALU.mult)), reads=[K_("kk"), K_("ex")], writes=[K_("AT")])
                S.op("dve", (lambda e, pb=pb: e.tensor_tensor(out=pb["ab"], in0=pb["kk"], in1=pb["a"], op=ALU.mult)), reads=[K_("kk"), K_("a")], writes=[K_("ab")])
                S.op("dve", (lambda e, pb=pb: e.tensor_tensor(out=pb["BT"], in0=pb["ab"], in1=pb["Gi"], op=ALU.mult)), reads=[K_("ab"), K_("Gi")], writes=[K_("BT")])
                S.op("dve", (lambda e, pb=pb, vc=vc: e.tensor_scalar(out=pb["t1"], in0=pb["a"], scalar1=-1.0, scalar2=vc(9), op0=ALU.add, op1=ALU.mult)), reads=[K_("a"), "vec"], writes=[K_("t1")])
                S.op("dve", (lambda e, pb=pb: e.scalar_tensor_tensor(out=pb["km"], in0=pb["t1"], scalar=1.0, in1=pb["k"], op0=ALU.add, op1=ALU.mult)), reads=[K_("t1"), K_("k")], writes=[K_("km")])
                S.op("dve", (lambda e, pb=pb: e.tensor_tensor(out=pb["KT"], in0=pb["km"], in1=pb["Gi"], op=ALU.mult)), reads=[K_("km"), K_("Gi")], writes=[K_("KT")])
                S.op("dve", (lambda e, pb=pb: e.tensor_tensor(out=pb["ATRT"][:, 128:256], in0=pb["r"], in1=pb["G"], op=ALU.mult)), reads=[K_("r"), K_("G")], writes=[K_("RT")])
                for q in range(2):
                    S.op("dve", (lambda e, pb=pb, q=q: e.tensor_scalar(out=pb["E"][:, q * 64:(q + 1) * 64], in0=pb["cum"][:, q * 64:(q + 1) * 64], scalar1=pb["cum"][:, q * 64 + 63:q * 64 + 64], scalar2=None, op0=ALU.subtract)),
                         reads=[K_("cum")], writes=[K_("E")])
                S.op("act", (lambda e, pb=pb: e.activation(out=pb["E"], in_=pb["E"], func=AF.Exp, scale=C0)), reads=[K_("E")], writes=[K_("E")])
                S.op("dve", (lambda e, pb=pb: e.tensor_tensor(out=pb["BhT"], in0=pb["ab"], in1=pb["E"], op=ALU.mult)), reads=[K_("ab"), K_("E")], writes=[K_("BhT")])
                S.op("dve", (lambda e, pb=pb: e.tensor_tensor(out=pb["KhT"], in0=pb["km"], in1=pb["E"], op=ALU.mult)), reads=[K_("km"), K_("E")], writes=[K_("KhT")])
                ps3, pk3 = self.gps(n=256)
                S.op("pe", (lambda e, ps3=ps3, pb=pb: e.transpose(ps3[:, 0:128], pb["BhT"], self.identf[:])), reads=[K_("BhT"), "identf"], writes=[pk3])
                S.op("pe", (lambda e, ps3=ps3, pb=pb: e.transpose(ps3[:, 128:256], pb["KhT"], self.identf[:])), reads=[K_("KhT"), "identf"], writes=[pk3])
                S.op("act", (lambda e, ps3=ps3, pb=pb: e.copy(out=pb["BK"], in_=ps3)), reads=[pk3], writes=[K_("BK")])
                S.op("dve", (lambda e, pb=pb, vc=vc: e.scalar_tensor_tensor(out=pb["rkr"], in0=pb["km"], scalar=vc(10), in1=pb["r"], op0=ALU.mult, op1=ALU.mult)), reads=[K_("km"), K_("r"), "vec"], writes=[K_("rkr")])
                S.op("pe", (lambda e, pb=pb, c=c: e.matmul(coefps[:, 2 * c:2 * c + 2], lhsT=pb["rkr"], rhs=ind2, start=True, stop=True)), reads=[K_("rkr"), "rwc"], writes=[ckey])
                AT = pb["ATRT"][:, 0:128]
                RT = pb["ATRT"][:, 128:256]
                S = self.S
                for hh in range(2 if RWD >= 3 else 0):
                    po = hh * 64
                    h = 2 * c + hh
                    hb = HB[hh]
                    hn = f"hb{hh}"
                    H_ = lambda n: (hn, n)
                    pg = PS[2][:, 0:512]
                    pgk = ("PS2", 0)
                    pg2 = PS[2][:, 512:1024]
                    pgk2 = ("PS2", 1)
                    S.op("pe", (lambda e, pb=pb, po=po, pg=pg: e.matmul(pg[:, 0:256], lhsT=pb["KT"][po:po + 64, :], rhs=pb["ATRT"][po:po + 64, :], start=True, stop=True)),
                         reads=[K_("KT"), K_("AT"), K_("RT")], writes=[pgk])
                    S.op("pe", (lambda e, pb=pb, po=po, pg=pg: e.matmul(pg[:, 256:384], lhsT=pb["ATRT"][po:po + 64, 0:128], rhs=pb["BT"][po:po + 64, :], start=True, stop=True)),
                         reads=[K_("BT"), K_("AT")], writes=[pgk])
                    S.op("dve", (lambda e, hb=hb, pg=pg: e.tensor_tensor(out=hb["Mk"], in0=pg[:, 0:256], in1=mask1, op=ALU.mult)), reads=[pgk, "rwc"], writes=[H_("Mk")])
                    S.op("dve", (lambda e, hb=hb, pg=pg: e.tensor_tensor(out=hb["X0"], in0=pg[:, 256:384], in1=masksl, op=ALU.mult)), reads=[pgk, "rwc"], writes=[H_("X0")])
                    S.op("pe", (lambda e, pb=pb, po=po, pg2=pg2: e.matmul(pg2[:, 0:256], lhsT=pb["BT"][po:po + 64, :], rhs=pb["ATRT"][po:po + 64, :], start=True, stop=True)),
                         reads=[K_("BT"), K_("AT"), K_("RT")], writes=[pgk2])
                    S.op("dve", (lambda e, hb=hb, pg2=pg2: e.tensor_tensor(out=hb["Mb"], in0=pg2[:, 0:256], in1=mask1, op=ALU.mult)), reads=[pgk2, "rwc"], writes=[H_("Mb")])
                    S.op("dve", (lambda e, hb=hb: e.tensor_tensor(out=hb["Z"][0], in0=hb["Mb"][:, 0:128], in1=self.identf[:], op=ALU.add)), reads=[H_("Mb"), "identf"], writes=[H_("Z0")])
                    XT_ap, X_ap = hb["Mb"][:, 0:128], hb["X0"]
                    xk_keys = [H_("Mb"), H_("X0")]
                    zi = 0
                    for lvl in range(5):
                        bank = pg if lvl % 2 == 0 else pg2
                        bkey = pgk if lvl % 2 == 0 else pgk2
                        xx = hb["XX"][lvl % 2]
                        xxk = H_(f"XX{lvl % 2}")
                        if lvl < 4:
                            S.op("pe", (lambda e, bank=bank, X_ap=X_ap, XT_ap=XT_ap: e.matmul(bank[:, 0:128], lhsT=X_ap, rhs=XT_ap, start=True, stop=True)), reads=xk_keys, writes=[bkey])
                        S.op("pe", (lambda e, bank=bank, X_ap=X_ap, XT_ap=XT_ap: e.matmul(bank[:, 128:256], lhsT=XT_ap, rhs=X_ap, start=True, stop=True)), reads=xk_keys, writes=[bkey])
                        if lvl < 4:
                            S.op("act", (lambda e, bank=bank, xx=xx: e.copy(out=xx, in_=bank[:, 0:256])), reads=[bkey], writes=[xxk])
                        else:
                            S.op("act", (lambda e, bank=bank, xx=xx: e.copy(out=xx[:, 128:256], in_=bank[:, 128:256])), reads=[bkey], writes=[xxk])
                        XT_ap, X_ap = xx[:, 0:128], xx[:, 128:256]
                        xk_keys = [xxk]
                        zo = hb["Z"][zi]
                        zn = hb["Z"][1 - zi]
                        S.op("pe", (lambda e, bank=bank, X_ap=X_ap, zo=zo: e.matmul(bank[:, 256:384], lhsT=X_ap, rhs=zo, start=True, stop=True)), reads=[xxk, H_(f"Z{zi}")], writes=[bkey])
                        S.op("dve", (lambda e, bank=bank, zo=zo, zn=zn: e.tensor_tensor(out=zn, in0=bank[:, 256:384], in1=zo, op=ALU.add)), reads=[bkey, H_(f"Z{zi}")], writes=[H_(f"Z{1 - zi}")])
                        zi = 1 - zi
                    hb["TT"] = hb["Z"][zi]
                    hb["TTk"] = H_(f"Z{zi}")
                for q in range(2 if RWD >= 4 else 0):
                    ph = q * 64
                    for hh in range(2):
                        po, h, hb, hn = hh * 64, 2 * c + hh, HB[hh], f"hb{hh}"
                        pq = PS[3][:, hh * 512:(hh + 1) * 512]
                        pqk = ("PS3", hh)
                        S.op("pe", (lambda e, pq=pq, hh=hh, c=c, pb=pb: e.matmul(pq[:, 0:64], lhsT=pb["ATRT"][:, 0:128], rhs=Hst[:, c, hh, :], start=True, stop=False)),
                             reads=[K_("AT"), ("H", h)], writes=[pqk])
                        S.op("pe", (lambda e, pq=pq, hb=hb, h=h: e.matmul(pq[:, 0:64], lhsT=hb["Mk"][:, 0:128], rhs=V[:, h * 64:(h + 1) * 64], start=False, stop=True)),
                             reads=[(hn, "Mk"), ("V", h // 8)], writes=[pqk])
                        S.op("act", (lambda e, pq=pq, ph=ph, hb=hb: e.copy(out=hb["Gs"][ph:ph + 64, :], in_=pq[ph:ph + 64, 0:64])), reads=[pqk], writes=[(hn, "Gs")])
                    for hh in range(2):
                        po, h, hb, hn = hh * 64, 2 * c + hh, HB[hh], f"hb{hh}"
                        pq = PS[3][:, hh * 512:(hh + 1) * 512]
                        pqk = ("PS3", hh)
                        S.op("pe", (lambda e, pq=pq, ph=ph, hb=hb: e.matmul(pq[:, 64:128], lhsT=hb["TT"][ph:ph + 64, :], rhs=hb["Gs"][ph:ph + 64, :], start=True, stop=True)),
                             reads=[hb["TTk"], (hn, "Gs")], writes=[pqk])
                        S.op("dve", (lambda e, pq=pq, ph=ph, hb=hb: e.tensor_copy(out=hb["Us"][ph:ph + 64, :], in_=pq[ph:ph + 64, 64:128])), reads=[pqk], writes=[(hn, "Us")])
                    for hh in range(2):
                        po, h, hb, hn = hh * 64, 2 * c + hh, HB[hh], f"hb{hh}"
                        pq = PS[3][:, hh * 512:(hh + 1) * 512]
                        pqk = ("PS3", hh)
                        vh = V[ph:ph + 64, h * 64:(h + 1) * 64]
                        S.op("pe", (lambda e, pq=pq, hh=hh, c=c, pb=pb: e.matmul(pq[:, 128:192], lhsT=pb["ATRT"][:, 128:256], rhs=Hst[:, c, hh, :], start=True, stop=False)),
                             reads=[K_("RT"), ("H", h)], writes=[pqk])
                        S.op("pe", (lambda e, pq=pq, hb=hb: e.matmul(pq[:, 128:192], lhsT=hb["Mb"][:, 128:256], rhs=hb["Us"][:, :], start=False, stop=False)),
                             reads=[(hn, "Mb"), (hn, "Us")], writes=[pqk])
                        S.op("pe", (lambda e, pq=pq, hb=hb, h=h: e.matmul(pq[:, 128:192], lhsT=hb["Mk"][:, 128:256], rhs=V[:, h * 64:(h + 1) * 64], start=False, stop=True)),
                             reads=[(hn, "Mk"), ("V", h // 8)], writes=[pqk])
                        S.op("pe", (lambda e, pq=pq, ph=ph, hb=hb, pb=pb: e.matmul(pq[:, 192:256], lhsT=pb["BK"][ph:ph + 64, 0:128], rhs=hb["Us"][ph:ph + 64, :], start=True, stop=False)),
                             reads=[K_("BK"), (hn, "Us")], writes=[pqk])
                        S.op("pe", (lambda e, pq=pq, ph=ph, pb=pb, vh=vh: e.matmul(pq[:, 192:256], lhsT=pb["BK"][ph:ph + 64, 128:256], rhs=vh, start=False, stop=True)),
                             reads=[K_("BK"), ("V", h // 8)], writes=[pqk])
                        S.op("act", (lambda e, pq=pq, ph=ph, h=h: e.copy(out=Y[ph:ph + 64, h * 64:(h + 1) * 64], in_=pq[ph:ph + 64, 128:192])), reads=[pqk, ("Ydone", 0), ("Ydone", 1)], writes=[("Y", h // 8, q, h)])
                        S.op("act", (lambda e, pq=pq, po=po, hh=hh: e.copy(out=Hd[hh][po:po + 64, :], in_=pq[po:po + 64, 192:256])), reads=[pqk], writes=[f"Hd{hh}"])
                        S.op("dve", (lambda e, po=po, c=c, pb=pb, q=q, hh=hh: e.scalar_tensor_tensor(out=Hst[po:po + 64, c, hh, :], in0=Hst[po:po + 64, c, hh, :], scalar=pb["G"][po:po + 64, q * 64 + 63:q * 64 + 64], in1=Hd[hh][po:po + 64, :], op0=ALU.mult, op1=ALU.add)),
                             reads=[f"Hd{hh}", ("H", h), K_("G")], writes=[("H", h)])
            ykeys = [("Y", h // 8, q, h) for h in range(16) for q in range(2)]
            Y3 = Y.rearrange("p (h d) -> p h d", h=16)
            bc = lambda ap: ap.unsqueeze(2).to_broadcast([128, 16, 64])
            S.op("act", (lambda e: e.copy(out=stt[:, 96:112], in_=coefps)), reads=[ckey], writes=["coef"])
            S.op("dve", (lambda e: e.tensor_reduce(out=stt[:, 0:16], in_=Y3, axis=AX.X, op=ALU.add)), reads=ykeys, writes=["st_s1"])
            S.op("act", (lambda e: e.activation(out=scr, in_=Y, func=AF.Square)), reads=ykeys, writes=["scr"])
            S.op("dve", (lambda e: e.tensor_reduce(out=stt[:, 16:32], in_=scr.rearrange("p (h d) -> p h d", h=16), axis=AX.X, op=ALU.add)), reads=["scr"], writes=["st_s2"])
            S.op("dve", (lambda e: e.tensor_scalar(out=stt[:, 32:48], in0=stt[:, 0:16], scalar1=1.0 / 64.0, scalar2=None, op0=ALU.mult)), reads=["st_s1"], writes=["st_mean"])
            S.op("dve", (lambda e: e.tensor_tensor(out=stt[:, 48:64], in0=stt[:, 32:48], in1=stt[:, 32:48], op=ALU.mult)), reads=["st_mean"], writes=["st_msq"])
            S.op("dve", (lambda e: e.scalar_tensor_tensor(out=stt[:, 64:80], in0=stt[:, 16:32], scalar=1.0 / 64.0, in1=stt[:, 48:64], op0=ALU.mult, op1=ALU.subtract)), reads=["st_s2", "st_msq"], writes=["st_var"])
            S.op("act", (lambda e: e.activation(out=stt[:, 80:96], in_=stt[:, 64:80], func=AF.Sqrt, bias=gneps, scale=1.0)), reads=["st_var", "rwc"], writes=["st_sd"])
            S.op("dve", (lambda e: e.reciprocal(out=stt[:, 80:96], in_=stt[:, 80:96])), reads=["st_sd"], writes=["st_rstd"])
            S.op("dve", (lambda e: e.tensor_tensor(out=Y3, in0=Y3, in1=bc(stt[:, 32:48]), op=ALU.subtract)), reads=ykeys + ["st_mean", "scr"], writes=["Yn"])
            S.op("dve", (lambda e: e.tensor_tensor(out=Y3, in0=Y3, in1=bc(stt[:, 80:96]), op=ALU.mult)), reads=["Yn", "st_rstd"], writes=["Yn"])
            S.op("pool", (lambda e: e.tensor_tensor(out=Y, in0=Y, in1=lxg, op=ALU.mult)), reads=["Yn", "lxg"], writes=["Yn"])
            S.op("dve", (lambda e: e.tensor_tensor(out=Y, in0=Y, in1=lxb, op=ALU.add)), reads=["Yn", "lxb"], writes=["Yn"])
            S.op("dve", (lambda e: e.tensor_tensor(out=scr.rearrange("p (h d) -> p h d", h=16), in0=V.rearrange("p (h d) -> p h d", h=16), in1=bc(stt[:, 96:112]), op=ALU.mult)),
                 reads=[("V", 0), ("V", 1), "coef", "st_s2"], writes=["scr"])
            S.op("dve", (lambda e: e.tensor_tensor(out=Y, in0=Y, in1=scr, op=ALU.add)), reads=["Yn", "scr"], writes=["Yn"])
            for half in range(2):
                ps, pk = self.gps()
                S.op("pe", (lambda e, ps=ps, half=half: e.matmul(ps, lhsT=sgl, rhs=g2b[:, half * 512:(half + 1) * 512], start=True, stop=True)), reads=["sgl", "g2b"], writes=[pk])
                S.op("dve", (lambda e, ps=ps, half=half: e.tensor_tensor(out=ob[:, half * 512:(half + 1) * 512], in0=Y[:, half * 512:(half + 1) * 512], in1=ps, op=ALU.mult)), reads=[pk, "Yn"], writes=[("ob", half), ("Ydone", half)])
            ps, pk = self.gps()
            pst = ps.bitcast(BF16)
            for cc in range(8):
                S.op("pe", (lambda e, cc=cc, pst=pst: e.transpose(pst[:, cc * 128:(cc + 1) * 128], ob[:, cc * 128:(cc + 1) * 128], self.ident[:])), reads=[("ob", cc // 4), "ident"], writes=[pk])
            S.op("act", (lambda e, pst=pst: e.copy(out=OT, in_=pst.rearrange("p (c t) -> p c t", c=8))), reads=[pk], writes=["OT"])
            py = PS[0]
            for half in range(2):
                for k in range(8):
                    S.op("pe", (lambda e, k=k, half=half: e.matmul(py[:, half * 512:(half + 1) * 512], lhsT=OT[:, k, :], rhs=wo[:, k, half * 512:(half + 1) * 512], start=(k == 0), stop=(k == 7))),
                         reads=["OT", "wo"], writes=[("PS0", half)])
            self.ln_finish(seq, tt, py[:, :], [("PS0", 0), ("PS0", 1)], first=first, last=False)
        self.force_pi = None


    def stage_attn(self, seq, layer, j, first):
        S, A, PS = self.S, self.A, self.PS
        self.stage_begin()
        self.load_ln(layer * 3 + 0)
        wq = A.alloc([8, D], BF16)
        wk = A.alloc([8, D], BF16)
        wv = A.alloc([8, D], BF16)
        wo = A.alloc([8, D], BF16)
        biasb = A.alloc([16, 640], BF16)
        KT = A.alloc([8, 1024], BF16)
        V = A.alloc([8, D], BF16)
        QT = A.alloc([8, 512], BF16)
        OT = A.alloc([8, 512], BF16)
        Pb = [A.alloc([640], BF16) for _ in range(2)]
        PTs = [A.alloc([5, 128], BF16) for _ in range(2)]
        Ob = [A.alloc([D], BF16) for _ in range(2)]
        sm = [A.alloc([64], F32) for _ in range(2)]
        wqkv = self.P["ca_w_qkv"][j]
        self.load_w(wq, wqkv[:, 0:D], "wq")
        self.load_w(wk, wqkv[:, D:2 * D], "wk")
        self.load_w(wv, wqkv[:, 2 * D:3 * D], "wv")
        self.load_w(wo, self.P["ca_w_o"][j], "wo")
        S.dma("pool", biasb, self.P["ca_bias"], writes=["biasb"])
        nh = 0
        for t in range(4):
            hk = self.hTkeys(t * 512, 512)
            r0 = (t % 2) * 512
            for oc in range(8):
                ps, pk = self.gps()
                for k in range(8):
                    S.op("pe", (lambda e, k=k, oc=oc, ps=ps, t=t: e.matmul(ps, lhsT=wq[:, k, oc * 128:(oc + 1) * 128], rhs=self.hTs(k, t * 512, 512), start=(k == 0), stop=(k == 7))),
                         reads=["wq"] + hk, writes=[pk])
                S.op("act", (lambda e, oc=oc, ps=ps: e.activation(out=QT[:, oc, :], in_=ps, func=AF.Copy, scale=0.125)), reads=[pk], writes=[("QT", oc)])
                ps, pk = self.gps()
                for k in range(8):
                    S.op("pe", (lambda e, k=k, oc=oc, ps=ps, t=t: e.matmul(ps, lhsT=wk[:, k, oc * 128:(oc + 1) * 128], rhs=self.hTs(k, t * 512, 512), start=(k == 0), stop=(k == 7))),
                         reads=["wk"] + hk, writes=[pk])
                S.op("dve", (lambda e, oc=oc, ps=ps, r0=r0: e.tensor_copy(out=KT[:, oc, r0:r0 + 512], in_=ps)), reads=[pk], writes=[("KT", oc, t % 2)])
            for sub in range(4):
                slot = (4 * t + sub) % 8
                for half in range(2):
                    ps, pk = self.gps()
                    for k in range(8):
                        S.op("pe", (lambda e, k=k, half=half, ps=ps, t=t, sub=sub: e.matmul(ps, lhsT=self.hTs(k, t * 512 + sub * 128, 128), rhs=wv[:, k, half * 512:(half + 1) * 512], start=(k == 0), stop=(k == 7))),
                             reads=["wv"] + hk, writes=[pk])
                    S.op("act", (lambda e, half=half, ps=ps, slot=slot: e.copy(out=V[:, slot, half * 512:(half + 1) * 512], in_=ps)), reads=[pk], writes=[("V", slot, half)])
            for sub in range(4):
                qb = 4 * t + sub
                kbs = list(range(max(0, qb - 4), qb + 1))
                j0 = kbs[0] - (qb - 4)
                c0 = j0 * 128
                oi = qb % 2
                po_ps = PS[2]
                smi = sm[oi]
                ksm = f"sm{oi}"
                for h in range(16):
                    c, po = h // 2, (h % 2) * 64
                    si = nh % 2
                    nh += 1
                    sps = PS[0] if si == 0 else PS[3]
                    spn = "PS0" if si == 0 else "PS3"
                    for kb in kbs:
                        jj = kb - (qb - 4)
                        slot = kb % 8
                        S.op("pe", (lambda e, jj=jj, slot=slot, c=c, po=po, sps=sps, sub=sub: e.matmul(sps[:, jj * 128:(jj + 1) * 128], lhsT=QT[po:po + 64, c, sub * 128:(sub + 1) * 128], rhs=KT[po:po + 64, c, slot * 128:(slot + 1) * 128], start=True, stop=False)),
                             reads=[("QT", c), ("KT", c, slot // 4)], writes=[(spn, jj // 4)])
                        S.op("pe", (lambda e, jj=jj, h=h, sps=sps: e.matmul(sps[:, jj * 128:(jj + 1) * 128], lhsT=self.ident[:], rhs=biasb[:, h, jj * 128:(jj + 1) * 128], start=False, stop=True)),
                             reads=["biasb", "ident"], writes=[(spn, jj // 4)])
                    skeys = [(spn, 0), (spn, 1)]
                    S.op("dve", (lambda e, sps=sps, smi=smi, h=h, c0=c0: e.tensor_reduce(out=smi[:, 32 + h:33 + h], in_=sps[:, c0:640], axis=AX.X, op=ALU.max, negate=True)),
                         reads=skeys, writes=[(ksm + "m", h)])
                    pb_ = Pb[si]
                    S.op("act", (lambda e, sps=sps, smi=smi, h=h, c0=c0, pb_=pb_: e.activation(out=pb_[:, c0:640], in_=sps[:, c0:640], func=AF.Exp, bias=smi[:, 32 + h:33 + h], scale=1.0, accum_out=smi[:, h:h + 1])),
                         reads=skeys + [(ksm + "m", h)], writes=[f"Pb{si}", (ksm + "s", h)])
                    ptp = PS[1][:, 0:512].bitcast(BF16)
                    for kb in kbs:
                        jj = kb - (qb - 4)
                        S.op("pe", (lambda e, jj=jj, ptp=ptp, pb_=pb_: e.transpose(ptp[:, jj * 128:(jj + 1) * 128], pb_[:, jj * 128:(jj + 1) * 128], self.ident[:])),
                             reads=[f"Pb{si}", "ident"], writes=[("PS1", 0)])
                    pts = PTs[si]
                    S.op("dve", (lambda e, ptp=ptp, pts=pts, j0=j0: e.tensor_copy(out=pts[:, j0:5, :], in_=ptp[:, j0 * 128:640].rearrange("p (a b) -> p a b", b=128))),
                         reads=[("PS1", 0)], writes=[f"PTs{si}"])
                    for kb in kbs:
                        jj = kb - (qb - 4)
                        slot = kb % 8
                        S.op("pe", (lambda e, jj=jj, slot=slot, h=h, pts=pts, po_ps=po_ps, kbs=kbs, kb=kb: e.matmul(po_ps[:, h * 64:(h + 1) * 64], lhsT=pts[:, jj, :], rhs=V[:, slot, h * 64:(h + 1) * 64], start=(kb == kbs[0]), stop=(kb == kbs[-1]))),
                             reads=[f"PTs{si}", ("V", slot, h // 8)], writes=[("PS2", h // 8)])
                S.op("dve", (lambda e, smi=smi: e.reciprocal(out=smi[:, 16:32], in_=smi[:, 0:16])), reads=[(ksm + "s", h) for h in range(16)], writes=[ksm + "r"])
                ob = Ob[oi]
                S.op("dve", (lambda e, smi=smi, ob=ob, po_ps=po_ps: e.tensor_tensor(out=ob.rearrange("p (h d) -> p h d", h=16), in0=po_ps[:, :].rearrange("p (h d) -> p h d", h=16), in1=smi[:, 16:32].unsqueeze(2).to_broadcast([128, 16, 64]), op=ALU.mult)),
                     reads=[("PS2", 0), ("PS2", 1), ksm + "r"], writes=[f"Ob{oi}"])
                otp = PS[1][:, 512:1024].bitcast(BF16)
                for cc in range(8):
                    S.op("pe", (lambda e, cc=cc, otp=otp, ob=ob: e.transpose(otp[:, cc * 128:(cc + 1) * 128], ob[:, cc * 128:(cc + 1) * 128], self.ident[:])),
                         reads=[f"Ob{oi}", "ident"], writes=[("PS1", 1)])
                S.op("act", (lambda e, otp=otp, sub=sub: e.copy(out=OT[:, :, sub * 128:(sub + 1) * 128], in_=otp.rearrange("p (c t) -> p c t", c=8))),
                     reads=[("PS1", 1)], writes=[("OT", sub)])
            self.out_proj(seq, t, lambda k, sub: OT[:, k, sub * 128:(sub + 1) * 128], [("OT", s_) for s_ in range(4)], wo, "wo", 8, first=first)


def make_consts():
    c = np.zeros((128, 256), np.float32)
    c[:, 0:128] = np.eye(128, dtype=np.float32)
    c[:, 128] = LN_EPS
    return c


def make_rwc():
    c = np.zeros((128, 648), np.float32)
    s_ = np.arange(128)[:, None]
    t_ = np.arange(128)[None, :]
    same = (s_ // 64) == (t_ // 64)
    c[:, 0:128] = (same & (s_ < t_))
    c[:, 128:256] = (same & (s_ <= t_))
    c[:, 256:384] = (same & (s_ > t_))
    c[:, 384:512] = same
    c[:, 512:640] = (t_ % 64 != 0)
    c[0:64, 640] = 1.0
    c[64:128, 641] = 1.0
    c[:, 642] = 64e-5
    return c


def prep_shared(inp):
    sh = {}
    sh["ln_g"] = np.ascontiguousarray(inp["ln_g"].reshape(DEPTH * 3, D))
    sh["ln_b"] = np.ascontiguousarray(inp["ln_b"].reshape(DEPTH * 3, D))
    sh["lru_w_in"] = inp["lru_w_in"]
    na = inp["lru_w_in"].shape[0]
    vec = np.zeros((na, BS, NBLK, 8), np.float32)
    cw = inp["lru_conv_w"].reshape(na, 4, NBLK, BS)
    for k in range(4):
        vec[:, :, :, k] = cw[:, k].transpose(0, 2, 1)
    vec[:, :, :, 4] = inp["lru_conv_b"].reshape(na, NBLK, BS).transpose(0, 2, 1)
    gbv = inp["lru_gate_b"].reshape(na, 2, NBLK, BS)
    vec[:, :, :, 5] = gbv[:, 0].transpose(0, 2, 1)
    vec[:, :, :, 6] = gbv[:, 1].transpose(0, 2, 1)
    vec[:, :, :, 7] = inp["lru_lambda"].reshape(na, NBLK, BS).transpose(0, 2, 1)
    sh["lru_vec"] = vec
    sh["lru_gate_w"] = np.ascontiguousarray(inp["lru_gate_w"].transpose(0, 3, 1, 2, 4).reshape(na, BS, 2 * NBLK, BS))
    sh["lru_w_out"] = inp["lru_w_out"]
    for k in ("mx_w_q", "mx_w_kv", "mx_w_o", "mlp_w1", "mlp_w2", "ca_w_qkv", "ca_w_o"):
        sh[k] = inp[k]
    rb = inp["ca_rel_bias"][0]
    q = np.arange(64)[:, None]
    kk = np.arange(576)[None, :]
    idx = np.clip(512 + q - kk, -128, 128) + 128
    band = rb[:, idx]
    b2 = np.full((128, 16, 640), NEG, np.float32)
    b2[0:64, :, 0:576] = band.transpose(1, 0, 2)
    b2[64:128, :, 64:640] = band.transpose(1, 0, 2)
    sh["ca_bias"] = b2
    sh["consts"] = make_consts()
    for k in ("rw_w_r", "rw_w_k", "rw_w_v", "rw_w_o", "rw_w1", "rw_a1", "rw_g1", "rw_w2", "rw_a2", "rw_g2"):
        sh[k] = inp[k]
    nb = inp["rw_mu"].shape[0]
    rv = np.zeros((nb, 128, 8, 16), np.float32)
    fm = lambda v: v.reshape(nb, 8, 128).transpose(0, 2, 1)
    for m in range(6):
        rv[:, :, :, m] = fm(inp["rw_mu"][:, m])
    rv[:, :, :, 6] = fm(inp["rw_w0"])
    rv[:, :, :, 7] = fm(inp["rw_a0"])
    rv[:, :, :, 8] = fm(inp["rw_k_k"])
    rv[:, :, :, 9] = fm(inp["rw_k_a"])
    rv[:, :, :, 10] = fm(inp["rw_r_k"].reshape(nb, D))
    sh["rw_vec"] = rv
    sh["rw_lnx"] = np.ascontiguousarray(np.stack([inp["rw_lnx_g"], inp["rw_lnx_b"]], axis=1))
    sh["rwc"] = make_rwc()
    return sh


_NC_CACHE = {}


def get_nc(n_layers=DEPTH, dbg=None):
    key = (n_layers if isinstance(n_layers, int) else tuple(n_layers), tuple(sorted(dbg.items())) if dbg else None)
    if key not in _NC_CACHE:
        b = Builder(n_layers, dbg)
        nc = b.build()
        _NC_CACHE[key] = nc
    return _NC_CACHE[key]


def kernel(**inputs):
    inp = {k: np.ascontiguousarray(np.asarray(v, dtype=np.float32)) for k, v in inputs.items()}
    sh = prep_shared(inp)
    nc = get_nc()
    ncores = 8
    in_maps = []
    for c in range(ncores):
        m = dict(sh)
        m["x"] = np.ascontiguousarray(inp["x"][c * NSEQ:(c + 1) * NSEQ])
        m["mem"] = np.ascontiguousarray(inp["mem"][c * NSEQ:(c + 1) * NSEQ])
        in_maps.append(m)
    res = run_bass_kernel_spmd(nc, in_maps, core_ids=list(range(ncores)))
    out = np.concatenate([np.asarray(r["out"]).reshape(NSEQ, SEQ, D) for r in res.results], axis=0)
    return out.astype(np.float32)
```

```python
import numpy as np
import concourse.bass as bass
import concourse.mybir as mybir
from concourse.bass_utils import run_bass_kernel_spmd
from contextlib import ExitStack

F32 = mybir.dt.float32
BF16 = mybir.dt.bfloat16
AF = mybir.ActivationFunctionType
ALU = mybir.AluOpType
AX = mybir.AxisListType

ENGS = ("pe", "act", "dve", "pool", "sp")
SEM_EPOCH = 30000

D = 1024
SEQ = 2048
NSEQ = 2
DEPTH = 4
ALPHA = (2 * DEPTH) ** 0.25
LN_EPS = 1e-5
D_RNN = 1344
NBLK = 16
BS = 84
D_FF = 4096
MEMT = 256
NEG = -30000.0


class Sched:
    def __init__(self, nc, stack, n_dma_sems=8):
        self.nc = nc
        self.stack = stack
        self.prog = {e: [] for e in ENGS}
        self.cnt = {e: 0 for e in ENGS}
        self.nsem = 0
        self.sem_owner = {}
        self.sem = {}
        for e in ENGS:
            self.sem[e] = self._newsem()
            self.sem_owner[id(self.sem[e])] = e
        self.known = {e: {} for e in ENGS}
        self.lastw = {}
        self.readers = {}
        self.dma_sems = {q: [[self._newsem(), 0] for _ in range(n_dma_sems)] for q in ("sp", "act", "pool")}
        self.dma_rr = {q: 0 for q in ("sp", "act", "pool")}
        self.ninstr = 0
        self.nwait = 0

    def _newsem(self):
        self.nsem += 1
        return self.stack.enter_context(self.nc.semaphore(f"s{self.nsem}"))

    def _deps(self, eng, reads, writes):
        deps = {}

        def add(ev, raw):
            if ev is None:
                return
            s, v = ev
            own = self.sem_owner.get(id(s))
            if own == eng:
                if eng == "pe":
                    return
            k = id(s)
            if k not in deps or deps[k][1] < v:
                deps[k] = (s, v)

        for k in reads:
            add(self.lastw.get(k), True)
        for k in writes:
            add(self.lastw.get(k), False)
            rd = self.readers.get(k)
            if rd:
                for ev in rd.values():
                    add(ev, False)
        return deps

    def _filter(self, eng, deps):
        waits = []
        kn = self.known[eng]
        for k, (s, v) in deps.items():
            if kn.get(k, 0) < v:
                kn[k] = v
                waits.append((s, v))
        self.nwait += len(waits)
        return waits

    def _record(self, ev, reads, writes):
        for k in writes:
            self.lastw[k] = ev
            self.readers[k] = {}
        for k in reads:
            r = self.readers.setdefault(k, {})
            r[id(ev[0])] = ev

    def op(self, eng, fn, reads=(), writes=()):
        deps = self._deps(eng, reads, writes)
        waits = self._filter(eng, deps)
        if self.cnt[eng] >= SEM_EPOCH:
            self.sem[eng] = self._newsem()
            self.sem_owner[id(self.sem[eng])] = eng
            self.cnt[eng] = 0
        self.cnt[eng] += 1
        ev = (self.sem[eng], self.cnt[eng])
        self.prog[eng].append((waits, fn, (self.sem[eng], 1)))
        self._record(ev, reads, writes)
        self.ninstr += 1
        return ev

    def dma(self, q, out, in_, reads=(), writes=()):
        slots = self.dma_sems[q]
        si = self.dma_rr[q]
        self.dma_rr[q] = (si + 1) % len(slots)
        slot = slots[si]
        deps = self._deps(q, reads, writes)
        if slot[1] > 0:
            deps[id(slot[0])] = (slot[0], 16 * slot[1])
        waits = self._filter(q, deps)
        slot[1] += 1
        ev = (slot[0], 16 * slot[1])
        self.prog[q].append((waits, (lambda e: e.dma_start(out=out, in_=in_)), (slot[0], 16)))
        self._record(ev, reads, writes)
        self.ninstr += 1
        return ev

    def all_events(self):
        evs = []
        for e in ENGS:
            if self.cnt[e] > 0:
                evs.append((self.sem[e], self.cnt[e]))
        for q in self.dma_sems:
            for s, n in self.dma_sems[q]:
                if n > 0:
                    evs.append((s, 16 * n))
        return evs

    def barrier(self):
        evs = self.all_events()
        for e in ENGS:
            deps = {}
            for s, v in evs:
                if self.sem_owner.get(id(s)) == e:
                    continue
                deps[id(s)] = (s, v)
            waits = self._filter(e, deps)
            if waits:
                self.prog[e].append((waits, None, None))
        self.lastw = {}
        self.readers = {}

    def emit(self):
        nc = self.nc
        prog = self.prog

        def run(name, e):
            for waits, fn, inc in prog[name]:
                for s, v in waits:
                    e.wait_ge(s, v)
                if fn is not None:
                    ins = fn(e)
                    if inc is not None:
                        ins.then_inc(inc[0], inc[1])

        with nc.Block() as block:
            @block.tensor
            def _(e):
                run("pe", e)

            @block.scalar
            def _(e):
                run("act", e)

            @block.vector
            def _(e):
                run("dve", e)

            @block.gpsimd
            def _(e):
                run("pool", e)

            @block.sync
            def _(e):
                run("sp", e)


class Arena:
    def __init__(self, t, nwords):
        self.t = t
        self.n = nwords
        self.off = 0

    def mark(self):
        return self.off

    def reset(self, m):
        self.off = m

    def alloc(self, shape, dtype, parts=128):
        free = int(np.prod(shape))
        words = free if dtype == F32 else (free + 1) // 2
        words = (words + 1) // 2 * 2
        assert self.off + words <= self.n, f"arena overflow {self.off}+{words}>{self.n}"
        ap = self.t[0:parts, self.off:self.off + words]
        self.off += words
        if dtype != F32:
            ap = ap.bitcast(dtype)
        ap = ap[:, 0:free]
        if len(shape) == 2:
            ap = ap.rearrange("p (a b) -> p a b", a=shape[0])
        elif len(shape) == 3:
            ap = ap.rearrange("p (a b c) -> p a b c", a=shape[0], b=shape[1])
        return ap


_uid = [0]


def uid(p="k"):
    _uid[0] += 1
    return f"{p}{_uid[0]}"


class Builder:
    def __init__(self, n_layers=DEPTH, dbg=None):
        self.layers = list(range(n_layers)) if isinstance(n_layers, int) else list(n_layers)
        self.dbg = dbg
        nc = self.nc = bass.Bass("TRN2", target_bir_lowering=False)
        self.st = ExitStack()
        self.S = Sched(nc, self.st)

    def din(self, name, shape):
        return self.nc.dram_tensor(name, list(shape), F32, kind="ExternalInput").ap()

    def build(self):
        nc, st, S = self.nc, self.st, self.S
        P = self.P = {}
        P["x"] = self.din("x", [NSEQ, SEQ, D])
        P["mem"] = self.din("mem", [NSEQ, MEMT, D])
        P["ln_g"] = self.din("ln_g", [DEPTH * 3, D])
        P["ln_b"] = self.din("ln_b", [DEPTH * 3, D])
        P["lru_w_in"] = self.din("lru_w_in", [2, D, 2 * D_RNN])
        P["lru_vec"] = self.din("lru_vec", [2, BS, NBLK, 8])
        P["lru_gate_w"] = self.din("lru_gate_w", [2, BS, 2 * NBLK, BS])
        P["lru_w_out"] = self.din("lru_w_out", [2, D_RNN, D])
        P["mx_w_q"] = self.din("mx_w_q", [DEPTH, D, D])
        P["mx_w_kv"] = self.din("mx_w_kv", [DEPTH, D, 2 * D])
        P["mx_w_o"] = self.din("mx_w_o", [DEPTH, D, D])
        P["mlp_w1"] = self.din("mlp_w1", [DEPTH, D, D_FF])
        P["mlp_w2"] = self.din("mlp_w2", [DEPTH, D_FF, D])
        P["ca_w_qkv"] = self.din("ca_w_qkv", [1, D, 3 * D])
        P["ca_w_o"] = self.din("ca_w_o", [1, D, D])
        P["ca_bias"] = self.din("ca_bias", [128, 16, 640])
        P["consts"] = self.din("consts", [128, 256])
        for nm in ("rw_w_r", "rw_w_k", "rw_w_v", "rw_w_o"):
            P[nm] = self.din(nm, [1, D, D])
        P["rw_w1"] = self.din("rw_w1", [1, D, 64])
        P["rw_a1"] = self.din("rw_a1", [1, D, 64])
        P["rw_g1"] = self.din("rw_g1", [1, D, 128])
        P["rw_w2"] = self.din("rw_w2", [1, 64, D])
        P["rw_a2"] = self.din("rw_a2", [1, 64, D])
        P["rw_g2"] = self.din("rw_g2", [1, 128, D])
        P["rw_vec"] = self.din("rw_vec", [1, 128, 8, 16])
        P["rw_lnx"] = self.din("rw_lnx", [1, 2, D])
        P["rwc"] = self.din("rwc", [128, 648])
        self.out = nc.dram_tensor("out", [NSEQ, SEQ, D], F32, kind="ExternalOutput").ap()
        self.h32 = nc.dram_tensor("h32", [NSEQ, SEQ, D], F32, kind="Internal").ap()
        if self.dbg:
            self.dbg_out = {n: nc.dram_tensor(n, list(shp), F32, kind="ExternalOutput").ap() for n, shp in self.dbg.items()}

        sb = lambda n, s, d: st.enter_context(nc.sbuf_tensor(n, s, d))
        self.hT = sb("hT", [128, 8, SEQ + 2], BF16)
        self.ident = sb("ident", [128, 128], BF16)
        self.identf = sb("identf", [128, 128], F32)
        self.cst = sb("cst", [128, 256], F32)
        self.lng = sb("lng", [128, D], F32)
        self.lnb = sb("lnb", [128, D], F32)
        self.lnz = [sb("lnz0", [128, D], F32)] * 2
        self.lnh = [sb(f"lnh{i}", [128, D], F32) for i in range(2)]
        self.lnhb = [sb(f"lnhb{i}", [128, D], BF16) for i in range(2)]
        self.lnst = [sb(f"lnst{i}", [128, 16], F32) for i in range(2)]
        self.lni = 0
        ARW = 37600
        self.arena_t = sb("arena", [128, ARW], F32)
        self.A = Arena(self.arena_t, ARW)
        self.PS = [st.enter_context(nc.psum_tensor(f"ps{i}", [128, 1024], F32)) for i in range(4)]
        self.out_events = []
        self.gi = 0
        self.gen_list = [(0, 0), (0, 1)]

        S.dma("sp", self.cst[:], P["consts"], writes=["cst"])
        S.op("dve", lambda e: e.tensor_copy(out=self.ident[:], in_=self.cst[:, 0:128]), reads=["cst"], writes=["ident"])
        S.op("dve", lambda e: e.tensor_copy(out=self.identf[:], in_=self.cst[:, 0:128]), reads=["cst"], writes=["identf"])
        S.op("pool", lambda e: e.memset(self.hT[:, :, 0:2], 0.0), writes=["hTpad"])

        for seq in range(NSEQ):
            self.load_x(seq)
            for li, layer in enumerate(self.layers):
                kind, j = layer % 3, layer // 3
                first = (li == 0)
                if kind == 0:
                    self.stage_lru(seq, layer, j, first)
                elif kind == 1:
                    self.stage_rwkv(seq, layer, j, first)
                else:
                    self.stage_attn(seq, layer, j, first)
                self.stage_memattn(seq, layer)
                self.stage_mlp(seq, layer, last=(li == len(self.layers) - 1))
        S.barrier()
        S.emit()
        self.st.close()
        return nc

    def stage_begin(self):
        self.S.barrier()
        self.A.reset(0)

    def load_w(self, dst, src, key, q="pool"):
        K, nk, ncols = dst.shape
        step = max(1, 2048 // K) if ncols * 4 >= 2048 else nk
        step = min(nk, max(1, (1 << 21) // (K * ncols * 4)))
        for k0 in range(0, nk, step):
            k1 = min(nk, k0 + step)
            self.S.dma(q, dst[:, k0:k1, :], src[k0 * K:k1 * K, :].rearrange("(k p) n -> p k n", p=K), writes=[key])

    def load_x(self, seq):
        S = self.S
        self.stage_begin()
        xb = [self.A.alloc([D], BF16) for _ in range(2)]
        for sub in range(SEQ // 128):
            b = xb[sub % 2]
            kb = f"xb{sub % 2}"
            S.dma("pool", b, self.P["x"][seq, sub * 128:(sub + 1) * 128, :], writes=[kb])
            self.to_hT(b, kb, sub)

    def to_hT(self, hb, kb, sub):
        S = self.S
        pi = self.lni % 2 if getattr(self, "force_pi", None) is None else self.force_pi
        ps = self.PS[1][:, pi * 512:(pi + 1) * 512].bitcast(BF16)
        pk = ("PS1", pi)
        for c in range(8):
            S.op("pe", (lambda e, c=c: e.transpose(ps[:, c * 128:(c + 1) * 128], hb[:, c * 128:(c + 1) * 128], self.ident[:])),
                 reads=[kb, "ident"], writes=[pk])
        dst = self.hT[:, :, 2 + sub * 128: 2 + (sub + 1) * 128]
        src = ps.rearrange("p (c t) -> p c t", c=8)
        S.op("dve", lambda e: e.tensor_copy(out=dst, in_=src), reads=[pk], writes=[("hT", sub)])
        self.lni += 1

    def load_ln(self, li):
        S = self.S
        S.dma("sp", self.lng[:], self.P["ln_g"][li, :].partition_broadcast(128), writes=["lng"])
        S.dma("sp", self.lnb[:], self.P["ln_b"][li, :].partition_broadcast(128), writes=["lnb"])

    def ln_finish(self, seq, sub, y, ykeys, first, last):
        S = self.S
        i = self.lni % 2
        z, hn, hb, stt = self.lnz[i], self.lnh[i], self.lnhb[i], self.lnst[i]
        kz, kh, khb, kst = "lnz0", f"lnh{i}", f"lnhb{i}", f"lnst{i}"
        src = (self.P["x"] if first else self.h32)[seq, sub * 128:(sub + 1) * 128, :]
        hk = ("h32", seq, sub)
        S.dma("sp", hn[:], src, reads=[hk], writes=[kh])
        S.op("dve", lambda e: e.scalar_tensor_tensor(out=z[:], in0=hn[:], scalar=float(ALPHA), in1=y, op0=ALU.mult, op1=ALU.add),
             reads=[kh] + list(ykeys), writes=[kz])
        S.op("dve", lambda e: e.bn_stats(out=stt[:, 0:6], in_=z[:, 0:512]), reads=[kz], writes=[kst + "a"])
        S.op("dve", lambda e: e.bn_stats(out=stt[:, 6:12], in_=z[:, 512:1024]), reads=[kz], writes=[kst + "b"])
        S.op("dve", lambda e: e.bn_aggr(out=stt[:, 12:14], in_=stt[:, 0:12]), reads=[kst + "a", kst + "b"], writes=[kst + "c"])
        S.op("act", lambda e: e.activation(out=stt[:, 14:15], in_=stt[:, 13:14], func=AF.Sqrt, bias=self.cst[:, 128:129], scale=1.0),
             reads=[kst + "c"], writes=[kst + "d0"])
        S.op("dve", lambda e: e.reciprocal(out=stt[:, 14:15], in_=stt[:, 14:15]), reads=[kst + "d0"], writes=[kst + "d"])
        S.op("dve", lambda e: e.tensor_scalar(out=stt[:, 15:16], in0=stt[:, 12:13], scalar1=stt[:, 14:15], scalar2=-1.0, op0=ALU.mult, op1=ALU.mult),
             reads=[kst + "c", kst + "d"], writes=[kst + "e"])
        S.op("act", lambda e: e.activation(out=z[:], in_=z[:], func=AF.Identity, bias=stt[:, 15:16], scale=stt[:, 14:15]),
             reads=[kz, kst + "d", kst + "e"], writes=[kz])
        S.op("pool", lambda e: e.tensor_tensor(out=z[:], in0=z[:], in1=self.lng[:], op=ALU.mult), reads=[kz, "lng"], writes=[kz])
        S.op("dve", lambda e: e.tensor_tensor(out=hn[:], in0=z[:], in1=self.lnb[:], op=ALU.add), reads=[kz, "lnb"], writes=[kh])
        dst = (self.out if last else self.h32)[seq, sub * 128:(sub + 1) * 128, :]
        S.dma("sp", dst, hn[:], reads=[kh], writes=[hk])
        if not last:
            S.op("act", lambda e: e.copy(out=hb[:], in_=hn[:]), reads=[kh], writes=[khb])
            self.to_hT(hb, khb, sub)
        else:
            self.lni += 1

    def gps(self, parts=128, n=512):
        i = self.gi
        self.gi += 1
        b = self.gen_list[i % len(self.gen_list)]
        return self.PS[b[0]][0:parts, b[1] * 512: b[1] * 512 + n], (f"PS{b[0]}", b[1])

    def hTs(self, k, t0, n):
        return self.hT[:, k, 2 + t0: 2 + t0 + n]

    def hTkeys(self, t0, n):
        return [("hT", s) for s in range(t0 // 128, (t0 + n + 127) // 128)]

    def dbg_store(self, name, ap, keys, dst_slice=None):
        if self.dbg and name in self.dbg:
            d = self.dbg_out[name] if dst_slice is None else dst_slice(self.dbg_out[name])
            self.S.dma("sp", d, ap, reads=keys)

    def stage_mlp(self, seq, layer, last):
        S, A, PS = self.S, self.A, self.PS
        self.stage_begin()
        self.load_ln(layer * 3 + 2)
        acc = A.alloc([16, D], F32)
        w1g = [A.alloc([8, 512], BF16) for _ in range(2)]
        w2g = [A.alloc([4, D], BF16) for _ in range(2)]
        hid = [A.alloc([4, 512], BF16) for _ in range(2)]
        rl = [A.alloc([512], F32) for _ in range(2)]
        w1 = self.P["mlp_w1"][layer]
        w2 = self.P["mlp_w2"][layer]
        nb = 0
        ny = 0
        for g in range(8):
            gi = g % 2
            self.load_w(w1g[gi], w1[:, g * 512:(g + 1) * 512], f"w1g{gi}")
            self.load_w(w2g[gi], w2[g * 512:(g + 1) * 512, :], f"w2g{gi}")
            for t in range(4):
                hi = (g * 4 + t) % 2
                for fc in range(4):
                    nb += 1
                    ps, pk = self.gps()
                    for k in range(8):
                        S.op("pe", (lambda e, k=k, fc=fc, ps=ps, gi=gi, t=t: e.matmul(ps, lhsT=w1g[gi][:, k, fc * 128:(fc + 1) * 128], rhs=self.hTs(k, t * 512, 512), start=(k == 0), stop=(k == 7))),
                             reads=[f"w1g{gi}"] + self.hTkeys(t * 512, 512), writes=[pk])
                    ri = nb % 2
                    S.op("act", (lambda e, ps=ps, ri=ri: e.activation(out=rl[ri], in_=ps, func=AF.Relu)), reads=[pk], writes=[f"rl{ri}"])
                    S.op("dve", (lambda e, ri=ri, hi=hi, fc=fc: e.tensor_tensor(out=hid[hi][:, fc, :], in0=rl[ri], in1=rl[ri], op=ALU.mult)),
                         reads=[f"rl{ri}"], writes=[(f"hid{hi}", fc)])
                for sub in range(4):
                    yi = ny % 2
                    ny += 1
                    py = PS[2 + yi]
                    for half in range(2):
                        for fc in range(4):
                            S.op("pe", (lambda e, fc=fc, half=half, py=py, hi=hi, gi=gi, sub=sub: e.matmul(py[:, half * 512:(half + 1) * 512], lhsT=hid[hi][:, fc, sub * 128:(sub + 1) * 128], rhs=w2g[gi][:, fc, half * 512:(half + 1) * 512], start=(fc == 0), stop=(fc == 3))),
                                 reads=[(f"hid{hi}", fc), f"w2g{gi}"], writes=[(f"PS{2 + yi}", half)])
                    a = acc[:, t * 4 + sub, :]
                    ak = ("acc", t * 4 + sub)
                    pkeys = [(f"PS{2 + yi}", 0), (f"PS{2 + yi}", 1)]
                    if g == 0:
                        S.op("act", (lambda e, a=a, py=py: e.copy(out=a, in_=py[:, :])), reads=pkeys, writes=[ak])
                    else:
                        S.op("dve", (lambda e, a=a, py=py: e.tensor_tensor(out=a, in0=a, in1=py[:, :], op=ALU.add)), reads=pkeys + [ak], writes=[ak])
        for s16 in range(16):
            self.ln_finish(seq, s16, acc[:, s16, :], [("acc", s16)], first=False, last=last)

    def stage_memattn(self, seq, layer):
        S, A, PS = self.S, self.A, self.PS
        self.stage_begin()
        self.load_ln(layer * 3 + 1)
        wb = [A.alloc([8, D], BF16) for _ in range(2)]
        memb = A.alloc([2, D], BF16)
        memT = A.alloc([8, MEMT], BF16)
        KT = A.alloc([8, MEMT], BF16)
        V = A.alloc([2, D], BF16)
        QT = A.alloc([8, 512], BF16)
        Pn = [A.alloc([4, MEMT], BF16) for _ in range(2)]
        PT = A.alloc([8, 512], BF16)
        OT = A.alloc([8, 512], BF16)
        sm = [A.alloc([16], F32) for _ in range(2)]
        wkv = self.P["mx_w_kv"][layer]
        self.load_w(wb[0], wkv[:, 0:D], "wb0")
        self.load_w(wb[1], wkv[:, D:2 * D], "wb1")
        S.dma("pool", memb, self.P["mem"][seq].rearrange("(a p) d -> p a d", p=128), writes=["memb"])
        for mt in range(2):
            ps = PS[0][:, mt * 512:(mt + 1) * 512].bitcast(BF16)
            for c in range(8):
                S.op("pe", (lambda e, c=c, mt=mt, ps=ps: e.transpose(ps[:, c * 128:(c + 1) * 128], memb[:, mt, c * 128:(c + 1) * 128], self.ident[:])),
                     reads=["memb", "ident"], writes=[("PS0", mt)])
            S.op("dve", (lambda e, mt=mt, ps=ps: e.tensor_copy(out=memT[:, :, mt * 128:(mt + 1) * 128], in_=ps.rearrange("p (c t) -> p c t", c=8))),
                 reads=[("PS0", mt)], writes=[("memT", mt)])
        mk = [("memT", 0), ("memT", 1)]
        for oc in range(8):
            ps, pk = self.gps(n=MEMT)
            for k in range(8):
                S.op("pe", (lambda e, k=k, oc=oc, ps=ps: e.matmul(ps, lhsT=wb[0][:, k, oc * 128:(oc + 1) * 128], rhs=memT[:, k, :], start=(k == 0), stop=(k == 7))),
                     reads=["wb0"] + mk, writes=[pk])
            S.op("act", (lambda e, oc=oc, ps=ps: e.copy(out=KT[:, oc, :], in_=ps)), reads=[pk], writes=[("KT", oc)])
        for mt in range(2):
            for half in range(2):
                ps, pk = self.gps()
                for k in range(8):
                    S.op("pe", (lambda e, k=k, mt=mt, half=half, ps=ps: e.matmul(ps, lhsT=memT[:, k, mt * 128:(mt + 1) * 128], rhs=wb[1][:, k, half * 512:(half + 1) * 512], start=(k == 0), stop=(k == 7))),
                         reads=["wb1"] + mk, writes=[pk])
                S.op("dve", (lambda e, mt=mt, half=half, ps=ps: e.tensor_copy(out=V[:, mt, half * 512:(half + 1) * 512], in_=ps)), reads=[pk], writes=[("V", mt, half)])
        vk = [("V", a, b) for a in range(2) for b in range(2)]
        self.load_w(wb[0], self.P["mx_w_q"][layer], "wb0")
        self.load_w(wb[1], self.P["mx_w_o"][layer], "wb1")
        for t in range(4):
            for oc in range(8):
                ps, pk = self.gps()
                for k in range(8):
                    S.op("pe", (lambda e, k=k, oc=oc, ps=ps, t=t: e.matmul(ps, lhsT=wb[0][:, k, oc * 128:(oc + 1) * 128], rhs=self.hTs(k, t * 512, 512), start=(k == 0), stop=(k == 7))),
                         reads=["wb0"] + self.hTkeys(t * 512, 512), writes=[pk])
                S.op("act", (lambda e, oc=oc, ps=ps: e.activation(out=QT[:, oc, :], in_=ps, func=AF.Copy, scale=1.0 / 16.0)), reads=[pk], writes=[("QT", oc)])
            for sub in range(4):
                pi = sub % 2
                ps = PS[0]
                pkeys = [("PS0", 0), ("PS0", 1)]
                for h in range(4):
                    for c in range(2):
                        S.op("pe", (lambda e, h=h, c=c, ps=ps, sub=sub: e.matmul(ps[:, h * 256:(h + 1) * 256], lhsT=QT[:, 2 * h + c, sub * 128:(sub + 1) * 128], rhs=KT[:, 2 * h + c, :], start=(c == 0), stop=(c == 1))),
                             reads=[("QT", 2 * h + c), ("KT", 2 * h + c)], writes=[("PS0", h // 2)])
                smi = sm[pi]
                ks = f"sm{pi}"
                S.op("dve", (lambda e, ps=ps, smi=smi: e.tensor_reduce(out=smi[:, 0:4], in_=ps[:, :].rearrange("p (h m) -> p h m", h=4), axis=AX.X, op=ALU.max, negate=True)),
                     reads=pkeys, writes=[ks + "m"])
                pn = Pn[pi]
                for h in range(4):
                    S.op("act", (lambda e, h=h, ps=ps, smi=smi, pn=pn: e.activation(out=pn[:, h, :], in_=ps[:, h * 256:(h + 1) * 256], func=AF.Exp, bias=smi[:, h:h + 1], scale=1.0, accum_out=smi[:, 4 + h:5 + h])),
                         reads=[("PS0", h // 2), ks + "m"], writes=[(f"Pn{pi}", h), (ks + "s", h)])
                S.op("dve", (lambda e, smi=smi: e.reciprocal(out=smi[:, 8:12], in_=smi[:, 4:8])), reads=[(ks + "s", h) for h in range(4)], writes=[ks + "r"])
                S.op("dve", (lambda e, smi=smi, pn=pn: e.tensor_tensor(out=pn[:, :, :], in0=pn[:, :, :], in1=smi[:, 8:12].unsqueeze(2).to_broadcast([128, 4, MEMT]), op=ALU.mult)),
                     reads=[(f"Pn{pi}", h) for h in range(4)] + [ks + "r"], writes=[(f"Pn{pi}", h) for h in range(4)])
                pt = PS[1][:, pi * 512:(pi + 1) * 512].bitcast(BF16)
                ptk = ("PS1", pi)
                for h in range(4):
                    for mc in range(2):
                        S.op("pe", (lambda e, h=h, mc=mc, pt=pt, pn=pn: e.transpose(pt[:, (h * 2 + mc) * 128:(h * 2 + mc + 1) * 128], pn[:, h, mc * 128:(mc + 1) * 128], self.ident[:])),
                             reads=[(f"Pn{pi}", h), "ident"], writes=[ptk])
                S.op("dve", (lambda e, pt=pt, sub=sub: e.tensor_copy(out=PT[:, :, sub * 128:(sub + 1) * 128], in_=pt.rearrange("p (c t) -> p c t", c=8))),
                     reads=[ptk], writes=[("PT", sub)])
            for oc in range(8):
                h, c = oc // 2, oc % 2
                ps, pk = self.gps()
                for mc in range(2):
                    S.op("pe", (lambda e, h=h, c=c, mc=mc, ps=ps: e.matmul(ps, lhsT=V[:, mc, h * 256 + c * 128: h * 256 + (c + 1) * 128], rhs=PT[:, h * 2 + mc, :], start=(mc == 0), stop=(mc == 1))),
                         reads=vk + [("PT", s) for s in range(4)], writes=[pk])
                S.op("act", (lambda e, oc=oc, ps=ps: e.copy(out=OT[:, oc, :], in_=ps)), reads=[pk], writes=[("OT", oc)])
            self.out_proj(seq, t, lambda k, sub: OT[:, k, sub * 128:(sub + 1) * 128], [("OT", k) for k in range(8)], wb[1], "wb1", 8, first=False)

    def out_proj(self, seq, t, lhs_fn, lkeys, w, wkey, nk, first):
        S, PS = self.S, self.PS
        for sub in range(4):
            yi = sub % 2
            py = PS[2 + yi]
            for half in range(2):
                for k in range(nk):
                    S.op("pe", (lambda e, k=k, half=half, py=py, sub=sub: e.matmul(py[:, half * 512:(half + 1) * 512], lhsT=lhs_fn(k, sub), rhs=w[:, k, half * 512:(half + 1) * 512], start=(k == 0), stop=(k == nk - 1))),
                         reads=list(lkeys) + [wkey], writes=[(f"PS{2 + yi}", half)])
            self.ln_finish(seq, t * 4 + sub, py[:, :], [(f"PS{2 + yi}", 0), (f"PS{2 + yi}", 1)], first=first, last=False)

    def stage_lru(self, seq, layer, j, first):
        S, A, PS = self.S, self.A, self.PS
        self.stage_begin()
        self.load_ln(layer * 3 + 0)
        win = A.alloc([8, 2 * D_RNN], BF16)
        wout = A.alloc([NBLK, D], BF16, parts=BS)
        gw = A.alloc([2 * NBLK, BS], BF16, parts=BS)
        vec = A.alloc([NBLK, 8], F32, parts=BS)
        c8 = A.alloc([NBLK], F32, parts=BS)
        carry = A.alloc([NBLK], F32, parts=BS)
        xpb = [A.alloc([516], F32, parts=BS) for _ in range(2)]
        hist = A.alloc([NBLK, 4], F32, parts=BS)
        mT = A.alloc([NBLK, 512], BF16, parts=BS)
        NB2 = 2
        gb = [A.alloc([512], F32, parts=BS) for _ in range(NB2)]
        xr = [A.alloc([512], F32, parts=BS) for _ in range(NB2)]
        xrb = [A.alloc([512], BF16, parts=BS) for _ in range(NB2)]
        rg = [A.alloc([512], F32, parts=BS) for _ in range(NB2)]
        ig = [A.alloc([512], F32, parts=BS) for _ in range(NB2)]
        aa = [A.alloc([512], F32, parts=BS) for _ in range(NB2)]
        sq = [A.alloc([512], F32, parts=BS) for _ in range(NB2)]
        hs = [A.alloc([512], F32, parts=BS) for _ in range(NB2)]
        self.load_w(win, self.P["lru_w_in"][j], "win")
        S.dma("pool", wout, self.P["lru_w_out"][j].rearrange("(n p) d -> p n d", p=BS), writes=["wout"])
        S.dma("pool", gw, self.P["lru_gate_w"][j], writes=["gw"])
        S.dma("sp", vec, self.P["lru_vec"][j], writes=["vec"])
        tx = A.alloc([NBLK], F32, parts=BS)
        tl = A.alloc([NBLK], F32, parts=BS)
        tu = A.alloc([NBLK], F32, parts=BS)
        S.op("act", lambda e: e.activation(out=tx, in_=vec[:, :, 7], func=AF.Exp, scale=-1.0), reads=["vec"], writes=["tx"])
        S.op("act", lambda e: e.activation(out=tl, in_=tx, func=AF.Ln, bias=1.0, scale=1.0), reads=["tx"], writes=["tl"])
        S.op("dve", lambda e: e.tensor_scalar(out=tu, in0=tx, scalar1=-0.25, scalar2=1.0 / 3.0, op0=ALU.mult, op1=ALU.add), reads=["tx"], writes=["tu"])
        S.op("dve", lambda e: e.tensor_tensor(out=tu, in0=tu, in1=tx, op=ALU.mult), reads=["tu", "tx"], writes=["tu"])
        S.op("dve", lambda e: e.tensor_scalar(out=tu, in0=tu, scalar1=-1.0, scalar2=0.5, op0=ALU.mult, op1=ALU.add), reads=["tu"], writes=["tu"])
        S.op("dve", lambda e: e.tensor_tensor(out=tu, in0=tu, in1=tx, op=ALU.mult), reads=["tu", "tx"], writes=["tu"])
        S.op("dve", lambda e: e.tensor_scalar(out=tu, in0=tu, scalar1=-1.0, scalar2=1.0, op0=ALU.mult, op1=ALU.add), reads=["tu"], writes=["tu"])
        S.op("dve", lambda e: e.tensor_tensor(out=tu, in0=tu, in1=tx, op=ALU.mult), reads=["tu", "tx"], writes=["tu"])
        S.op("dve", lambda e: e.tensor_tensor(out=tu, in0=tu, in1=tl, op=ALU.subtract), reads=["tu", "tl"], writes=["tu"])
        S.op("dve", lambda e: e.tensor_scalar(out=tx, in0=tx, scalar1=0.05, scalar2=None, op0=ALU.is_lt), reads=["tx", "tu"], writes=["tx"])
        S.op("dve", lambda e: e.tensor_tensor(out=tu, in0=tu, in1=tx, op=ALU.mult), reads=["tu", "tx"], writes=["tu"])
        S.op("dve", lambda e: e.tensor_tensor(out=tu, in0=tu, in1=tl, op=ALU.add), reads=["tu", "tl"], writes=["tu"])
        S.op("dve", lambda e: e.tensor_scalar(out=c8, in0=tu, scalar1=-8.0, scalar2=None, op0=ALU.mult), reads=["tu"], writes=["c8"])
        S.op("dve", lambda e: e.memset(carry, 0.0), writes=["carry"])
        S.op("dve", lambda e: e.memset(hist, 0.0), writes=[("hist", n) for n in range(NBLK)])
        nb = 0
        for t in range(4):
            for n in range(NBLK):
                bi = n % NB2
                ps, pk = self.gps(parts=BS)
                for k in range(8):
                    S.op("pe", (lambda e, k=k, n=n, ps=ps, t=t: e.matmul(ps, lhsT=win[:, k, n * BS:(n + 1) * BS], rhs=self.hTs(k, t * 512, 512), start=(k == 0), stop=(k == 7))),
                         reads=["win"] + self.hTkeys(t * 512, 512), writes=[pk])
                S.op("act", (lambda e, ps=ps, bi=bi: e.activation(out=gb[bi], in_=ps, func=AF.Gelu)), reads=[pk], writes=[f"gb{bi}"])
                ps2, pk2 = self.gps(parts=BS)
                for k in range(8):
                    S.op("pe", (lambda e, k=k, n=n, ps2=ps2, t=t: e.matmul(ps2, lhsT=win[:, k, D_RNN + n * BS: D_RNN + (n + 1) * BS], rhs=self.hTs(k, t * 512, 512), start=(k == 0), stop=(k == 7))),
                         reads=["win"] + self.hTkeys(t * 512, 512), writes=[pk2])
                xp = xpb[bi]
                xk = f"xpb{bi}"
                S.op("pool", (lambda e, xp=xp, n=n: e.tensor_copy(out=xp[:, 0:4], in_=hist[:, n, :])), reads=[("hist", n)], writes=[xk + "h"])
                S.op("act", (lambda e, xp=xp, ps2=ps2: e.copy(out=xp[:, 4:516], in_=ps2)), reads=[pk2], writes=[xk])
                x_ = xr[bi]
                kx = f"xr{bi}"
                S.op("dve", (lambda e, xp=xp, x_=x_, n=n: e.tensor_scalar(out=x_, in0=xp[:, 1:513], scalar1=vec[:, n, 0:1], scalar2=vec[:, n, 4:5], op0=ALU.mult, op1=ALU.add)),
                     reads=[xk, xk + "h", "vec"], writes=[kx])
                for kk in range(1, 4):
                    S.op("dve", (lambda e, xp=xp, x_=x_, n=n, kk=kk: e.scalar_tensor_tensor(out=x_, in0=xp[:, 1 + kk:513 + kk], scalar=vec[:, n, kk:kk + 1], in1=x_, op0=ALU.mult, op1=ALU.add)),
                         reads=[xk, xk + "h", "vec", kx], writes=[kx])
                S.op("pool", (lambda e, xp=xp, n=n: e.tensor_copy(out=hist[:, n, :], in_=xp[:, 512:516])), reads=[xk], writes=[("hist", n)])
                S.op("act", (lambda e, x_=x_, bi=bi: e.copy(out=xrb[bi], in_=x_)), reads=[kx], writes=[f"xrb{bi}"])
                pg, pgk = self.gps(parts=BS)
                S.op("pe", (lambda e, n=n, pg=pg, bi=bi: e.matmul(pg, lhsT=gw[:, n, :], rhs=xrb[bi], start=True, stop=True)), reads=["gw", f"xrb{bi}"], writes=[pgk])
                S.op("act", (lambda e, pg=pg, bi=bi, n=n: e.activation(out=rg[bi], in_=pg, func=AF.Sigmoid, bias=vec[:, n, 5:6], scale=1.0)), reads=[pgk, "vec"], writes=[f"rg{bi}"])
                pg2, pgk2 = self.gps(parts=BS)
                S.op("pe", (lambda e, n=n, pg2=pg2, bi=bi: e.matmul(pg2, lhsT=gw[:, NBLK + n, :], rhs=xrb[bi], start=True, stop=True)), reads=["gw", f"xrb{bi}"], writes=[pgk2])
                S.op("act", (lambda e, pg2=pg2, bi=bi, n=n: e.activation(out=ig[bi], in_=pg2, func=AF.Sigmoid, bias=vec[:, n, 6:7], scale=1.0)), reads=[pgk2, "vec"], writes=[f"ig{bi}"])
                S.op("act", (lambda e, bi=bi, n=n: e.activation(out=aa[bi], in_=rg[bi], func=AF.Exp, scale=c8[:, n:n + 1])), reads=[f"rg{bi}", "c8"], writes=[f"aa{bi}"])
                S.op("act", (lambda e, bi=bi: e.activation(out=sq[bi], in_=aa[bi], func=AF.Square)), reads=[f"aa{bi}"], writes=[f"sq{bi}"])
                S.op("act", (lambda e, bi=bi: e.activation(out=sq[bi], in_=sq[bi], func=AF.Sqrt, bias=1.0, scale=-1.0)), reads=[f"sq{bi}"], writes=[f"sq{bi}"])
                S.op("pool", (lambda e, bi=bi: e.tensor_tensor(out=ig[bi], in0=ig[bi], in1=xr[bi], op=ALU.mult)), reads=[f"ig{bi}", kx], writes=[f"ig{bi}"])
                S.op("dve", (lambda e, bi=bi: e.tensor_tensor(out=ig[bi], in0=ig[bi], in1=sq[bi], op=ALU.mult)), reads=[f"ig{bi}", f"sq{bi}"], writes=[f"ig{bi}"])
                S.op("dve", (lambda e, bi=bi, n=n: e.tensor_tensor_scan(out=hs[bi], data0=aa[bi], data1=ig[bi], initial=carry[:, n:n + 1], op0=ALU.mult, op1=ALU.add)),
                     reads=[f"aa{bi}", f"ig{bi}", ("carry", n)], writes=[f"hs{bi}"])
                S.op("act", (lambda e, bi=bi, n=n: e.copy(out=carry[:, n:n + 1], in_=hs[bi][:, 511:512])), reads=[f"hs{bi}"], writes=[("carry", n)])
                S.op("dve", (lambda e, bi=bi, n=n: e.tensor_tensor(out=mT[:, n, :], in0=hs[bi], in1=gb[bi], op=ALU.mult)), reads=[f"hs{bi}", f"gb{bi}"], writes=[("mT", n)])
            self.out_proj(seq, t, lambda k, sub: mT[:, k, sub * 128:(sub + 1) * 128], [("mT", n) for n in range(NBLK)], wout, "wout", NBLK, first=first)


    def stage_rwkv(self, seq, layer, j, first):
        S, A, PS = self.S, self.A, self.PS
        self.stage_begin()
        self.load_ln(layer * 3 + 0)
        import os
        RW = BF16 if os.environ.get('RW_F32', '0') != '1' else F32
        C0 = float(np.exp(-0.5))
        wr = A.alloc([8, D], BF16)
        wk = A.alloc([8, D], BF16)
        wv = A.alloc([8, D], BF16)
        wo = A.alloc([8, D], BF16)
        w1b = A.alloc([8, 64], BF16)
        a1b = A.alloc([8, 64], BF16)
        g1b = A.alloc([8, 128], BF16)
        w2b = A.alloc([D], BF16, parts=64)
        a2b = A.alloc([D], BF16, parts=64)
        g2b = A.alloc([D], BF16)
        lxg = A.alloc([D], F32)
        lxb = A.alloc([D], F32)
        vec = A.alloc([8, 16], F32)
        omu = A.alloc([8, 6], F32)
        rwc = A.alloc([648], F32)
        mask1 = rwc[:, 0:256]
        masksl = rwc[:, 256:384]
        blk = rwc[:, 384:512]
        rmask = rwc[:, 512:640]
        ind2 = rwc[:, 640:642]
        gneps = rwc[:, 642:643]
        xs = [A.alloc([8, 128], BF16) for _ in range(2)]
        xprev = A.alloc([8, 128], BF16)
        tmpx = A.alloc([8, 128], BF16)
        xlast = A.alloc([8], BF16)
        thw = A.alloc([128], BF16, parts=64)
        la = A.alloc([128], BF16, parts=64)
        sgl = A.alloc([128], BF16)
        V = A.alloc([D], F32)
        Y = A.alloc([D], F32)
        Hst = A.alloc([8, 2, 64], F32)
        Hm = Hst if RW == F32 else A.alloc([8, 2, 64], RW)
        Vm = V if RW == F32 else A.alloc([D], RW)
        Hd = [A.alloc([64], F32) for _ in range(2)]
        scr = A.alloc([D], F32)
        ob = A.alloc([D], BF16)
        OT = A.alloc([8, 128], BF16)
        stt = A.alloc([112], F32)
        names = ["sg", "a", "kq", "kk", "k", "r", "rn", "cum", "G", "Gi", "ex", "ab", "t1", "km", "E", "BhT", "KhT", "rkr"]
        PBs = []
        tmps = {n: A.alloc([128], F32) for n in names if n != "G"}
        for i in range(2):
            pb = dict(tmps)
            pb["G"] = A.alloc([128], F32)
            pb["ATRT"] = A.alloc([256], RW)
            pb["BT"] = A.alloc([128], RW)
            pb["KT"] = A.alloc([128], RW)
            pb["BK"] = A.alloc([256], RW)
            PBs.append(pb)
        HB = []
        for i in range(2):
            hb = {"Mk": A.alloc([256], RW), "Mb": A.alloc([256], RW), "X0": A.alloc([128], RW),
                  "XX": [A.alloc([256], RW) for _ in range(2)], "Z": [A.alloc([128], RW) for _ in range(2)],
                  "Gs": A.alloc([64], RW), "Us": A.alloc([64], RW)}
            S.op("dve", (lambda e, hb=hb: e.memset(hb["Us"], 0.0)), writes=[(f"hb{i}", "Us")])
            S.op("dve", (lambda e, hb=hb: e.memset(hb["Gs"], 0.0)), writes=[(f"hb{i}", "Gs")])
            HB.append(hb)

        self.load_w(w1b, self.P["rw_w1"][j], "w1b")
        self.load_w(a1b, self.P["rw_a1"][j], "a1b")
        self.load_w(g1b, self.P["rw_g1"][j], "g1b")
        self.load_w(wv, self.P["rw_w_v"][j], "wv")
        self.load_w(wr, self.P["rw_w_r"][j], "wr")
        self.load_w(wk, self.P["rw_w_k"][j], "wk")
        self.load_w(wo, self.P["rw_w_o"][j], "wo")
        S.dma("pool", w2b, self.P["rw_w2"][j], writes=["w2b"])
        S.dma("pool", a2b, self.P["rw_a2"][j], writes=["a2b"])
        S.dma("pool", g2b, self.P["rw_g2"][j], writes=["g2b"])
        S.dma("sp", vec, self.P["rw_vec"][j], writes=["vec"])
        S.dma("sp", rwc, self.P["rwc"], writes=["rwc"])
        S.dma("sp", lxg, self.P["rw_lnx"][j, 0, :].partition_broadcast(128), writes=["lxg"])
        S.dma("sp", lxb, self.P["rw_lnx"][j, 1, :].partition_broadcast(128), writes=["lxb"])
        S.op("dve", lambda e: e.tensor_scalar(out=omu, in0=vec[:, :, 0:6], scalar1=-1.0, scalar2=1.0, op0=ALU.mult, op1=ALU.add), reads=["vec"], writes=["omu"])
        S.op("dve", lambda e: e.memset(Hst, 0.0), writes=[("H", h) for h in range(16)])
        if RW != F32:
            S.op("dve", lambda e: e.memset(Hm, 0.0), writes=[("Hm", h) for h in range(16)])
        S.op("dve", lambda e: e.memset(xlast, 0.0), writes=["xlast"])
        self.force_pi = 0
        self.gen_list = [(0, 0), (0, 1)]
        HK = "H" if RW == F32 else "Hm"
        VK = "V" if RW == F32 else "Vm"
        coefps = PS[1][:, 512:528]
        ckey = ("PS1", 1)

        def mix(m, bi, hk, t0):
            S.op("dve", (lambda e: e.tensor_tensor(out=tmpx, in0=xprev, in1=vec[:, :, m:m + 1].to_broadcast([128, 8, 128]), op=ALU.mult)),
                 reads=["xprev", "vec"], writes=["tmpx"])
            S.op("dve", (lambda e: e.tensor_tensor(out=xs[bi], in0=self.hT[:, :, 2 + t0:2 + t0 + 128], in1=omu[:, :, m:m + 1].to_broadcast([128, 8, 128]), op=ALU.mult)),
                 reads=hk + ["omu"], writes=[f"xs{bi}"])
            S.op("pool", (lambda e: e.tensor_tensor(out=xs[bi], in0=xs[bi], in1=tmpx, op=ALU.add)), reads=[f"xs{bi}", "tmpx"], writes=[f"xs{bi}"])

        for tt in range(16):
            t0 = tt * 128
            hk = [("hT", tt)]
            S.op("pool", (lambda e, t0=t0: e.tensor_copy(out=xprev[:, :, 1:128], in_=self.hT[:, :, 2 + t0:2 + t0 + 127])), reads=hk, writes=["xprev"])
            S.op("pool", (lambda e: e.tensor_copy(out=xprev[:, :, 0:1], in_=xlast.unsqueeze(2))), reads=["xlast", "xprev"], writes=["xprev"])
            S.op("pool", (lambda e, t0=t0: e.tensor_copy(out=xlast.unsqueeze(2), in_=self.hT[:, :, 2 + t0 + 127:2 + t0 + 128])), reads=hk + ["xprev"], writes=["xlast"])
            mix(1, 0, hk, t0)
            ps, pk = self.gps(parts=64, n=128)
            for k in range(8):
                S.op("pe", (lambda e, k=k, ps=ps: e.matmul(ps, lhsT=w1b[:, k, :], rhs=xs[0][:, k, :], start=(k == 0), stop=(k == 7))), reads=["w1b", "xs0"], writes=[pk])
            S.op("act", (lambda e, ps=ps: e.activation(out=thw, in_=ps, func=AF.Tanh)), reads=[pk], writes=["thw"])
            mix(4, 1, hk, t0)
            ps, pk = self.gps(parts=64, n=128)
            for k in range(8):
                S.op("pe", (lambda e, k=k, ps=ps: e.matmul(ps, lhsT=a1b[:, k, :], rhs=xs[1][:, k, :], start=(k == 0), stop=(k == 7))), reads=["a1b", "xs1"], writes=[pk])
            S.op("act", (lambda e, ps=ps: e.copy(out=la, in_=ps)), reads=[pk], writes=["la"])
            mix(5, 0, hk, t0)
            ps, pk = self.gps(n=128)
            for k in range(8):
                S.op("pe", (lambda e, k=k, ps=ps: e.matmul(ps, lhsT=g1b[:, k, :], rhs=xs[0][:, k, :], start=(k == 0), stop=(k == 7))), reads=["g1b", "xs0"], writes=[pk])
            S.op("act", (lambda e, ps=ps: e.activation(out=sgl, in_=ps, func=AF.Sigmoid)), reads=[pk], writes=["sgl"])
            mix(3, 1, hk, t0)
            for half in range(2):
                ps, pk = self.gps()
                for k in range(8):
                    S.op("pe", (lambda e, k=k, ps=ps, half=half: e.matmul(ps, lhsT=xs[1][:, k, :], rhs=wv[:, k, half * 512:(half + 1) * 512], start=(k == 0), stop=(k == 7))), reads=["wv", "xs1"], writes=[pk])
                S.op("act", (lambda e, ps=ps, half=half: e.copy(out=V[:, half * 512:(half + 1) * 512], in_=ps)), reads=[pk], writes=[("V", half)])
                if RW != F32:
                    S.op("pool", (lambda e, half=half: e.tensor_copy(out=Vm[:, half * 512:(half + 1) * 512], in_=V[:, half * 512:(half + 1) * 512])), reads=[("V", half)], writes=[("Vm", half)])
            mix(0, 0, hk, t0)
            mix(2, 1, hk, t0)
            import os
            RWD = int(os.environ.get("RW_DBG", "9"))
            if RWD < 9:
                S.op("dve", (lambda e: e.memset(Y, 0.0)), reads=[("Ydone", 0), ("Ydone", 1)], writes=[("Y", h // 8, q, h) for h in range(16) for q in range(2)])
                S.op("pe", (lambda e: e.matmul(coefps, lhsT=sgl, rhs=g2b[:, 0:16], start=True, stop=True)), reads=["sgl", "g2b"], writes=[ckey])
            RWS = int(os.environ.get("RW_SUB", "999"))
            for c in range(8 if RWD >= 2 else 0):
                if RWS < 999:
                    class _F:
                        def __init__(s_, S0):
                            s_.S0, s_.n = S0, 0
                        def op(s_, *a, **k):
                            s_.n += 1
                            if s_.n <= RWS:
                                return s_.S0.op(*a, **k)
                        def __getattr__(s_, nm):
                            return getattr(s_.S0, nm)
                    S = _F(self.S)
                pb = PBs[c % 2]
                pn = f"pb{c % 2}"
                K_ = lambda n, pn=pn: (pn, n) if n in ("G", "AT", "RT", "BT", "KT", "BK") else ("pbt", n)
                ps, pk = self.gps()
                for k in range(8):
                    S.op("pe", (lambda e, k=k, ps=ps, c=c: e.matmul(ps[:, 0:128], lhsT=wr[:, k, c * 128:(c + 1) * 128], rhs=xs[0][:, k, :], start=(k == 0), stop=(k == 7))), reads=["wr", "xs0"], writes=[pk])
                for k in range(8):
                    S.op("pe", (lambda e, k=k, ps=ps, c=c: e.matmul(ps[:, 128:256], lhsT=wk[:, k, c * 128:(c + 1) * 128], rhs=xs[1][:, k, :], start=(k == 0), stop=(k == 7))), reads=["wk", "xs1"], writes=[pk])
                S.op("pe", (lambda e, ps=ps, c=c: e.matmul(ps[:, 256:384], lhsT=w2b[:, c * 128:(c + 1) * 128], rhs=thw, start=True, stop=True)), reads=["w2b", "thw"], writes=[pk])
                S.op("pe", (lambda e, ps=ps, c=c: e.matmul(ps[:, 384:512], lhsT=a2b[:, c * 128:(c + 1) * 128], rhs=la, start=True, stop=True)), reads=["a2b", "la"], writes=[pk])
                vc = lambda i, c=c: vec[:, c, i:i + 1]
                S.op("act", (lambda e, ps=ps, pb=pb, vc=vc: e.activation(out=pb["sg"], in_=ps[:, 256:384], func=AF.Sigmoid, bias=vc(6), scale=1.0)), reads=[pk, "vec"], writes=[K_("sg")])
                S.op("act", (lambda e, ps=ps, pb=pb, vc=vc: e.activation(out=pb["a"], in_=ps[:, 384:512], func=AF.Sigmoid, bias=vc(7), scale=1.0)), reads=[pk, "vec"], writes=[K_("a")])
                S.op("act", (lambda e, ps=ps, pb=pb: e.copy(out=pb["k"], in_=ps[:, 128:256])), reads=[pk], writes=[K_("k")])
                S.op("dve", (lambda e, pb=pb, vc=vc: e.tensor_scalar(out=pb["kk"], in0=pb["k"], scalar1=vc(8), scalar2=None, op0=ALU.mult)), reads=[K_("k"), "vec"], writes=[K_("kk")])
                S.op("act", (lambda e, pb=pb: e.activation(out=pb["kq"], in_=pb["kk"], func=AF.Square)), reads=[K_("kk")], writes=[K_("kq")])
                S.op("dve", (lambda e, ps=ps, pb=pb: e.tensor_copy(out=pb["r"], in_=ps[:, 0:128])), reads=[pk], writes=[K_("r")])
                ps2, pk2 = self.gps(n=128)
                S.op("pe", (lambda e, ps2=ps2, pb=pb: e.matmul(ps2, lhsT=blk, rhs=pb["kq"], start=True, stop=True)), reads=["rwc", K_("kq")], writes=[pk2])
                S.op("act", (lambda e, ps2=ps2, pb=pb: e.activation(out=pb["rn"], in_=ps2, func=AF.Sqrt)), reads=[pk2], writes=[K_("rn")])
                S.op("dve", (lambda e, pb=pb: e.tensor_scalar(out=pb["rn"], in0=pb["rn"], scalar1=1e-12, scalar2=None, op0=ALU.max)), reads=[K_("rn")], writes=[K_("rn")])
                S.op("dve", (lambda e, pb=pb: e.reciprocal(out=pb["rn"], in_=pb["rn"])), reads=[K_("rn")], writes=[K_("rn")])
                S.op("dve", (lambda e, pb=pb: e.tensor_tensor(out=pb["kk"], in0=pb["kk"], in1=pb["rn"], op=ALU.mult)), reads=[K_("kk"), K_("rn")], writes=[K_("kk")])
                S.op("dve", (lambda e, pb=pb: e.tensor_tensor_scan(out=pb["cum"], data0=rmask, data1=pb["sg"], initial=0.0, op0=ALU.mult, op1=ALU.add)), reads=["rwc", K_("sg")], writes=[K_("cum")])
                S.op("act", (lambda e, pb=pb: e.activation(out=pb["G"], in_=pb["cum"], func=AF.Exp, scale=-C0)), reads=[K_("cum")], writes=[K_("G")])
                S.op("act", (lambda e, pb=pb: e.activation(out=pb["Gi"], in_=pb["cum"], func=AF.Exp, scale=C0)), reads=[K_("cum")], writes=[K_("Gi")])
                S.op("dve", (lambda e, pb=pb: e.tensor_tensor(out=pb["ex"], in0=pb["cum"], in1=pb["sg"], op=ALU.subtract)), reads=[K_("cum"), K_("sg")], writes=[K_("ex")])
                S.op("act", (lambda e, pb=pb: e.activation(out=pb["ex"], in_=pb["ex"], func=AF.Exp, scale=-C0)), reads=[K_("ex")], writes=[K_("ex")])
                S.op("dve", (lambda e, pb=pb: e.scalar_tensor_tensor(out=pb["ATRT"][:, 0:128], in0=pb["kk"], scalar=-1.0, in1=pb["ex"], op0=ALU.mult, op1=ALU.mult)), reads=[K_("kk"), K_("ex")], writes=[K_("AT")])
                S.op("dve", (lambda e, pb=pb: e.tensor_tensor(out=pb["ab"], in0=pb["kk"], in1=pb["a"], op=ALU.mult)), reads=[K_("kk"), K_("a")], writes=[K_("ab")])
                S.op("dve", (lambda e, pb=pb: e.tensor_tensor(out=pb["BT"], in0=pb["ab"], in1=pb["Gi"], op=ALU.mult)), reads=[K_("ab"), K_("Gi")], writes=[K_("BT")])
                S.op("dve", (lambda e, pb=pb, vc=vc: e.tensor_scalar(out=pb["t1"], in0=pb["a"], scalar1=-1.0, scalar2=vc(9), op0=ALU.add, op1=ALU.mult)), reads=[K_("a"), "vec"], writes=[K_("t1")])
                S.op("dve", (lambda e, pb=pb: e.scalar_tensor_tensor(out=pb["km"], in0=pb["t1"], scalar=1.0, in1=pb["k"], op0=ALU.add, op1=ALU.mult)), reads=[K_("t1"), K_("k")], writes=[K_("km")])
                S.op("dve", (lambda e, pb=pb: e.tensor_tensor(out=pb["KT"], in0=pb["km"], in1=pb["Gi"], op=ALU.mult)), reads=[K_("km"), K_("Gi")], writes=[K_("KT")])
                S.op("dve", (lambda e, pb=pb: e.tensor_tensor(out=pb["ATRT"][:, 128:256], in0=pb["r"], in1=pb["G"], op=ALU.mult)), reads=[K_("r"), K_("G")], writes=[K_("RT")])
                for q in range(2):
                    S.op("dve", (lambda e, pb=pb, q=q: e.tensor_scalar(out=pb["E"][:, q * 64:(q + 1) * 64], in0=pb["cum"][:, q * 64:(q + 1) * 64], scalar1=pb["cum"][:, q * 64 + 63:q * 64 + 64], scalar2=None, op0=ALU.subtract)),
                         reads=[K_("cum")], writes=[K_("E")])
                S.op("act", (lambda e, pb=pb: e.activation(out=pb["E"], in_=pb["E"], func=AF.Exp, scale=C0)), reads=[K_("E")], writes=[K_("E")])
                S.op("dve", (lambda e, pb=pb: e.tensor_tensor(out=pb["BhT"], in0=pb["ab"], in1=pb["E"], op=ALU.mult)), reads=[K_("ab"), K_("E")], writes=[K_("BhT")])
                S.op("dve", (lambda e, pb=pb: e.tensor_tensor(out=pb["KhT"], in0=pb["km"], in1=pb["E"], op=ALU.mult)), reads=[K_("km"), K_("E")], writes=[K_("KhT")])
                ps3, pk3 = self.gps(n=256)
                S.op("pe", (lambda e, ps3=ps3, pb=pb: e.transpose(ps3[:, 0:128], pb["BhT"], self.identf[:])), reads=[K_("BhT"), "identf"], writes=[pk3])
                S.op("pe", (lambda e, ps3=ps3, pb=pb: e.transpose(ps3[:, 128:256], pb["KhT"], self.identf[:])), reads=[K_("KhT"), "identf"], writes=[pk3])
                S.op("act", (lambda e, ps3=ps3, pb=pb: e.copy(out=pb["BK"], in_=ps3)), reads=[pk3], writes=[K_("BK")])
                S.op("dve", (lambda e, pb=pb, vc=vc: e.scalar_tensor_tensor(out=pb["rkr"], in0=pb["km"], scalar=vc(10), in1=pb["r"], op0=ALU.mult, op1=ALU.mult)), reads=[K_("km"), K_("r"), "vec"], writes=[K_("rkr")])
                S.op("pe", (lambda e, pb=pb, c=c: e.matmul(coefps[:, 2 * c:2 * c + 2], lhsT=pb["rkr"], rhs=ind2, start=True, stop=True)), reads=[K_("rkr"), "rwc"], writes=[ckey])
                AT = pb["ATRT"][:, 0:128]
                RT = pb["ATRT"][:, 128:256]
                S = self.S
                NH = 2 if RWD >= 3 else 0
                st_ = []
                for hh in range(NH):
                    po = hh * 64
                    hb = HB[hh]
                    hn = f"hb{hh}"
                    pg = PS[2][:, hh * 512:(hh + 1) * 512]
                    pgk = ("PS2", hh)
                    S.op("pe", (lambda e, pb=pb, po=po, pg=pg: e.matmul(pg[:, 0:256], lhsT=pb["KT"][po:po + 64, :], rhs=pb["ATRT"][po:po + 64, :], start=True, stop=True)),
                         reads=[K_("KT"), K_("AT"), K_("RT")], writes=[pgk])
                    S.op("pe", (lambda e, pb=pb, po=po, pg=pg: e.matmul(pg[:, 256:512], lhsT=pb["BT"][po:po + 64, :], rhs=pb["ATRT"][po:po + 64, :], start=True, stop=True)),
                         reads=[K_("BT"), K_("AT"), K_("RT")], writes=[pgk])
                    S.op("dve", (lambda e, hb=hb, pg=pg: e.tensor_tensor(out=hb["Mk"], in0=pg[:, 0:256], in1=mask1, op=ALU.mult)), reads=[pgk, "rwc"], writes=[(hn, "Mk")])
                    S.op("dve", (lambda e, hb=hb, pg=pg: e.tensor_tensor(out=hb["Mb"], in0=pg[:, 256:512], in1=mask1, op=ALU.mult)), reads=[pgk, "rwc"], writes=[(hn, "Mb")])
                    S.op("pe", (lambda e, pb=pb, po=po, pg=pg: e.matmul(pg[:, 0:128], lhsT=pb["ATRT"][po:po + 64, 0:128], rhs=pb["BT"][po:po + 64, :], start=True, stop=True)),
                         reads=[K_("BT"), K_("AT")], writes=[pgk])
                    S.op("dve", (lambda e, hb=hb, pg=pg: e.tensor_tensor(out=hb["X0"], in0=pg[:, 0:128], in1=masksl, op=ALU.mult)), reads=[pgk, "rwc"], writes=[(hn, "X0")])
                    S.op("dve", (lambda e, hb=hb: e.tensor_tensor(out=hb["Z"][0], in0=hb["Mb"][:, 0:128], in1=self.identf[:], op=ALU.add)), reads=[(hn, "Mb"), "identf"], writes=[(hn, "Z0")])
                    st_.append({"XT": hb["Mb"][:, 0:128], "X": hb["X0"], "keys": [(hn, "Mb"), (hn, "X0")], "zi": 0})
                for lvl in range(5):
                    for hh in range(NH):
                        hb = HB[hh]
                        hn = f"hb{hh}"
                        bank = PS[2][:, hh * 512:(hh + 1) * 512]
                        bkey = ("PS2", hh)
                        sd = st_[hh]
                        X_ap, XT_ap, xk_keys, zi = sd["X"], sd["XT"], sd["keys"], sd["zi"]
                        xx = hb["XX"][lvl % 2]
                        xxk = (hn, f"XX{lvl % 2}")
                        if lvl < 4:
                            S.op("pe", (lambda e, bank=bank, X_ap=X_ap, XT_ap=XT_ap: e.matmul(bank[:, 0:128], lhsT=X_ap, rhs=XT_ap, start=True, stop=True)), reads=xk_keys, writes=[bkey])
                        S.op("pe", (lambda e, bank=bank, X_ap=X_ap, XT_ap=XT_ap: e.matmul(bank[:, 128:256], lhsT=XT_ap, rhs=X_ap, start=True, stop=True)), reads=xk_keys, writes=[bkey])
                        if lvl < 4:
                            S.op("act", (lambda e, bank=bank, xx=xx: e.copy(out=xx, in_=bank[:, 0:256])), reads=[bkey], writes=[xxk])
                        else:
                            S.op("act", (lambda e, bank=bank, xx=xx: e.copy(out=xx[:, 128:256], in_=bank[:, 128:256])), reads=[bkey], writes=[xxk])
                        XT_ap, X_ap = xx[:, 0:128], xx[:, 128:256]
                        zo = hb["Z"][zi]
                        zn = hb["Z"][1 - zi]
                        S.op("pe", (lambda e, bank=bank, X_ap=X_ap, zo=zo: e.matmul(bank[:, 256:384], lhsT=X_ap, rhs=zo, start=True, stop=True)), reads=[xxk, (hn, f"Z{zi}")], writes=[bkey])
                        S.op("dve", (lambda e, bank=bank, zo=zo, zn=zn: e.tensor_tensor(out=zn, in0=bank[:, 256:384], in1=zo, op=ALU.add)), reads=[bkey, (hn, f"Z{zi}")], writes=[(hn, f"Z{1 - zi}")])
                        sd["X"], sd["XT"], sd["keys"], sd["zi"] = X_ap, XT_ap, [xxk], 1 - zi
                for hh in range(NH):
                    HB[hh]["TT"] = HB[hh]["Z"][st_[hh]["zi"]]
                    HB[hh]["TTk"] = (f"hb{hh}", f"Z{st_[hh]['zi']}")
                for q in range(2 if RWD >= 4 else 0):
                    ph = q * 64
                    for hh in range(2):
                        po, h, hb, hn = hh * 64, 2 * c + hh, HB[hh], f"hb{hh}"
                        pq = PS[3][:, hh * 512:(hh + 1) * 512]
                        pqk = ("PS3", hh)
                        S.op("pe", (lambda e, pq=pq, hh=hh, c=c, pb=pb: e.matmul(pq[:, 0:64], lhsT=pb["ATRT"][:, 0:128], rhs=Hm[:, c, hh, :], start=True, stop=False)),
                             reads=[K_("AT"), (HK, h)], writes=[pqk])
                        S.op("pe", (lambda e, pq=pq, hb=hb, h=h: e.matmul(pq[:, 0:64], lhsT=hb["Mk"][:, 0:128], rhs=Vm[:, h * 64:(h + 1) * 64], start=False, stop=True)),
                             reads=[(hn, "Mk"), (VK, h // 8)], writes=[pqk])
                        S.op("act", (lambda e, pq=pq, ph=ph, hb=hb: e.copy(out=hb["Gs"][ph:ph + 64, :], in_=pq[ph:ph + 64, 0:64])), reads=[pqk], writes=[(hn, "Gs")])
                    for hh in range(2):
                        po, h, hb, hn = hh * 64, 2 * c + hh, HB[hh], f"hb{hh}"
                        pq = PS[3][:, hh * 512:(hh + 1) * 512]
                        pqk = ("PS3", hh)
                        S.op("pe", (lambda e, pq=pq, ph=ph, hb=hb: e.matmul(pq[:, 64:128], lhsT=hb["TT"][ph:ph + 64, :], rhs=hb["Gs"][ph:ph + 64, :], start=True, stop=True)),
                             reads=[hb["TTk"], (hn, "Gs")], writes=[pqk])
                        S.op("dve", (lambda e, pq=pq, ph=ph, hb=hb: e.tensor_copy(out=hb["Us"][ph:ph + 64, :], in_=pq[ph:ph + 64, 64:128])), reads=[pqk], writes=[(hn, "Us")])
                    for hh in range(2):
                        po, h, hb, hn = hh * 64, 2 * c + hh, HB[hh], f"hb{hh}"
                        pq = PS[3][:, hh * 512:(hh + 1) * 512]
                        pqk = ("PS3", hh)
                        vh = Vm[ph:ph + 64, h * 64:(h + 1) * 64]
                        S.op("pe", (lambda e, pq=pq, hh=hh, c=c, pb=pb: e.matmul(pq[:, 128:192], lhsT=pb["ATRT"][:, 128:256], rhs=Hm[:, c, hh, :], start=True, stop=False)),
                             reads=[K_("RT"), (HK, h)], writes=[pqk])
                        S.op("pe", (lambda e, pq=pq, hb=hb: e.matmul(pq[:, 128:192], lhsT=hb["Mb"][:, 128:256], rhs=hb["Us"][:, :], start=False, stop=False)),
                             reads=[(hn, "Mb"), (hn, "Us")], writes=[pqk])
                        S.op("pe", (lambda e, pq=pq, hb=hb, h=h: e.matmul(pq[:, 128:192], lhsT=hb["Mk"][:, 128:256], rhs=Vm[:, h * 64:(h + 1) * 64], start=False, stop=True)),
                             reads=[(hn, "Mk"), (VK, h // 8)], writes=[pqk])
                        S.op("pe", (lambda e, pq=pq, ph=ph, hb=hb, pb=pb: e.matmul(pq[:, 192:256], lhsT=pb["BK"][ph:ph + 64, 0:128], rhs=hb["Us"][ph:ph + 64, :], start=True, stop=False)),
                             reads=[K_("BK"), (hn, "Us")], writes=[pqk])
                        S.op("pe", (lambda e, pq=pq, ph=ph, pb=pb, vh=vh: e.matmul(pq[:, 192:256], lhsT=pb["BK"][ph:ph + 64, 128:256], rhs=vh, start=False, stop=True)),
                             reads=[K_("BK"), (VK, h // 8)], writes=[pqk])
                        S.op("act", (lambda e, pq=pq, ph=ph, h=h: e.copy(out=Y[ph:ph + 64, h * 64:(h + 1) * 64], in_=pq[ph:ph + 64, 128:192])), reads=[pqk, ("Ydone", 0), ("Ydone", 1)], writes=[("Y", h // 8, q, h)])
                        S.op("act", (lambda e, pq=pq, po=po, hh=hh: e.copy(out=Hd[hh][po:po + 64, :], in_=pq[po:po + 64, 192:256])), reads=[pqk], writes=[f"Hd{hh}"])
                        S.op("dve", (lambda e, po=po, c=c, pb=pb, q=q, hh=hh: e.scalar_tensor_tensor(out=Hst[po:po + 64, c, hh, :], in0=Hst[po:po + 64, c, hh, :], scalar=pb["G"][po:po + 64, q * 64 + 63:q * 64 + 64], in1=Hd[hh][po:po + 64, :], op0=ALU.mult, op1=ALU.add)),
                             reads=[f"Hd{hh}", ("H", h), K_("G")], writes=[("H", h)])
                        if RW != F32:
                            S.op("act", (lambda e, po=po, c=c, hh=hh: e.copy(out=Hm[po:po + 64, c, hh, :], in_=Hst[po:po + 64, c, hh, :])), reads=[("H", h)], writes=[("Hm", h)])
            ykeys = [("Y", h // 8, q, h) for h in range(16) for q in range(2)]
            Y3 = Y.rearrange("p (h d) -> p h d", h=16)
            bc = lambda ap: ap.unsqueeze(2).to_broadcast([128, 16, 64])
            S.op("act", (lambda e: e.copy(out=stt[:, 96:112], in_=coefps)), reads=[ckey], writes=["coef"])
            S.op("dve", (lambda e: e.tensor_reduce(out=stt[:, 0:16], in_=Y3, axis=AX.X, op=ALU.add)), reads=ykeys, writes=["st_s1"])
            S.op("act", (lambda e: e.activation(out=scr, in_=Y, func=AF.Square)), reads=ykeys, writes=["scr"])
            S.op("dve", (lambda e: e.tensor_reduce(out=stt[:, 16:32], in_=scr.rearrange("p (h d) -> p h d", h=16), axis=AX.X, op=ALU.add)), reads=["scr"], writes=["st_s2"])
            S.op("dve", (lambda e: e.tensor_scalar(out=stt[:, 32:48], in0=stt[:, 0:16], scalar1=1.0 / 64.0, scalar2=None, op0=ALU.mult)), reads=["st_s1"], writes=["st_mean"])
            S.op("dve", (lambda e: e.tensor_tensor(out=stt[:, 48:64], in0=stt[:, 32:48], in1=stt[:, 32:48], op=ALU.mult)), reads=["st_mean"], writes=["st_msq"])
            S.op("dve", (lambda e: e.scalar_tensor_tensor(out=stt[:, 64:80], in0=stt[:, 16:32], scalar=1.0 / 64.0, in1=stt[:, 48:64], op0=ALU.mult, op1=ALU.subtract)), reads=["st_s2", "st_msq"], writes=["st_var"])
            S.op("act", (lambda e: e.activation(out=stt[:, 80:96], in_=stt[:, 64:80], func=AF.Sqrt, bias=gneps, scale=1.0)), reads=["st_var", "rwc"], writes=["st_sd"])
            S.op("dve", (lambda e: e.reciprocal(out=stt[:, 80:96], in_=stt[:, 80:96])), reads=["st_sd"], writes=["st_rstd"])
            S.op("dve", (lambda e: e.tensor_tensor(out=Y3, in0=Y3, in1=bc(stt[:, 32:48]), op=ALU.subtract)), reads=ykeys + ["st_mean", "scr"], writes=["Yn"])
            S.op("dve", (lambda e: e.tensor_tensor(out=Y3, in0=Y3, in1=bc(stt[:, 80:96]), op=ALU.mult)), reads=["Yn", "st_rstd"], writes=["Yn"])
            S.op("pool", (lambda e: e.tensor_tensor(out=Y, in0=Y, in1=lxg, op=ALU.mult)), reads=["Yn", "lxg"], writes=["Yn"])
            S.op("dve", (lambda e: e.tensor_tensor(out=Y, in0=Y, in1=lxb, op=ALU.add)), reads=["Yn", "lxb"], writes=["Yn"])
            S.op("dve", (lambda e: e.tensor_tensor(out=scr.rearrange("p (h d) -> p h d", h=16), in0=V.rearrange("p (h d) -> p h d", h=16), in1=bc(stt[:, 96:112]), op=ALU.mult)),
                 reads=[("V", 0), ("V", 1), "coef", "st_s2"], writes=["scr"])
            S.op("dve", (lambda e: e.tensor_tensor(out=Y, in0=Y, in1=scr, op=ALU.add)), reads=["Yn", "scr"], writes=["Yn"])
            for half in range(2):
                ps, pk = self.gps()
                S.op("pe", (lambda e, ps=ps, half=half: e.matmul(ps, lhsT=sgl, rhs=g2b[:, half * 512:(half + 1) * 512], start=True, stop=True)), reads=["sgl", "g2b"], writes=[pk])
                S.op("dve", (lambda e, ps=ps, half=half: e.tensor_tensor(out=ob[:, half * 512:(half + 1) * 512], in0=Y[:, half * 512:(half + 1) * 512], in1=ps, op=ALU.mult)), reads=[pk, "Yn"], writes=[("ob", half), ("Ydone", half)])
            ps, pk = self.gps()
            pst = ps.bitcast(BF16)
            for cc in range(8):
                S.op("pe", (lambda e, cc=cc, pst=pst: e.transpose(pst[:, cc * 128:(cc + 1) * 128], ob[:, cc * 128:(cc + 1) * 128], self.ident[:])), reads=[("ob", cc // 4), "ident"], writes=[pk])
            S.op("act", (lambda e, pst=pst: e.copy(out=OT, in_=pst.rearrange("p (c t) -> p c t", c=8))), reads=[pk], writes=["OT"])
            py = PS[0]
            for half in range(2):
                for k in range(8):
                    S.op("pe", (lambda e, k=k, half=half: e.matmul(py[:, half * 512:(half + 1) * 512], lhsT=OT[:, k, :], rhs=wo[:, k, half * 512:(half + 1) * 512], start=(k == 0), stop=(k == 7))),
                         reads=["OT", "wo"], writes=[("PS0", half)])
            self.ln_finish(seq, tt, py[:, :], [("PS0", 0), ("PS0", 1)], first=first, last=False)
        self.force_pi = None
        self.gen_list = [(0, 0), (0, 1)]


    def stage_attn(self, seq, layer, j, first):
        S, A, PS = self.S, self.A, self.PS
        self.stage_begin()
        self.load_ln(layer * 3 + 0)
        wq = A.alloc([8, D], BF16)
        wk = A.alloc([8, D], BF16)
        wv = A.alloc([8, D], BF16)
        wo = A.alloc([8, D], BF16)
        biasb = A.alloc([16, 640], BF16)
        KT = A.alloc([8, 1024], BF16)
        V = A.alloc([8, D], BF16)
        QT = A.alloc([8, 512], BF16)
        OT = A.alloc([8, 512], BF16)
        Pb = [A.alloc([640], BF16) for _ in range(2)]
        PTs = [A.alloc([5, 128], BF16) for _ in range(2)]
        Ob = [A.alloc([D], BF16) for _ in range(2)]
        sm = [A.alloc([64], F32) for _ in range(2)]
        wqkv = self.P["ca_w_qkv"][j]
        self.load_w(wq, wqkv[:, 0:D], "wq")
        self.load_w(wk, wqkv[:, D:2 * D], "wk")
        self.load_w(wv, wqkv[:, 2 * D:3 * D], "wv")
        self.load_w(wo, self.P["ca_w_o"][j], "wo")
        S.dma("pool", biasb, self.P["ca_bias"], writes=["biasb"])
        nh = 0
        for t in range(4):
            hk = self.hTkeys(t * 512, 512)
            r0 = (t % 2) * 512
            for oc in range(8):
                ps, pk = self.gps()
                for k in range(8):
                    S.op("pe", (lambda e, k=k, oc=oc, ps=ps, t=t: e.matmul(ps, lhsT=wq[:, k, oc * 128:(oc + 1) * 128], rhs=self.hTs(k, t * 512, 512), start=(k == 0), stop=(k == 7))),
                         reads=["wq"] + hk, writes=[pk])
                S.op("act", (lambda e, oc=oc, ps=ps: e.activation(out=QT[:, oc, :], in_=ps, func=AF.Copy, scale=0.125)), reads=[pk], writes=[("QT", oc)])
                ps, pk = self.gps()
                for k in range(8):
                    S.op("pe", (lambda e, k=k, oc=oc, ps=ps, t=t: e.matmul(ps, lhsT=wk[:, k, oc * 128:(oc + 1) * 128], rhs=self.hTs(k, t * 512, 512), start=(k == 0), stop=(k == 7))),
                         reads=["wk"] + hk, writes=[pk])
                S.op("dve", (lambda e, oc=oc, ps=ps, r0=r0: e.tensor_copy(out=KT[:, oc, r0:r0 + 512], in_=ps)), reads=[pk], writes=[("KT", oc, t % 2)])
            for sub in range(4):
                slot = (4 * t + sub) % 8
                for half in range(2):
                    ps, pk = self.gps()
                    for k in range(8):
                        S.op("pe", (lambda e, k=k, half=half, ps=ps, t=t, sub=sub: e.matmul(ps, lhsT=self.hTs(k, t * 512 + sub * 128, 128), rhs=wv[:, k, half * 512:(half + 1) * 512], start=(k == 0), stop=(k == 7))),
                             reads=["wv"] + hk, writes=[pk])
                    S.op("act", (lambda e, half=half, ps=ps, slot=slot: e.copy(out=V[:, slot, half * 512:(half + 1) * 512], in_=ps)), reads=[pk], writes=[("V", slot, half)])
            for sub in range(4):
                qb = 4 * t + sub
                kbs = list(range(max(0, qb - 4), qb + 1))
                j0 = kbs[0] - (qb - 4)
                c0 = j0 * 128
                oi = qb % 2
                po_ps = PS[2]
                smi = sm[oi]
                ksm = f"sm{oi}"
                for h in range(16):
                    c, po = h // 2, (h % 2) * 64
                    si = nh % 2
                    nh += 1
                    sps = PS[0] if si == 0 else PS[3]
                    spn = "PS0" if si == 0 else "PS3"
                    for kb in kbs:
                        jj = kb - (qb - 4)
                        slot = kb % 8
                        S.op("pe", (lambda e, jj=jj, slot=slot, c=c, po=po, sps=sps, sub=sub: e.matmul(sps[:, jj * 128:(jj + 1) * 128], lhsT=QT[po:po + 64, c, sub * 128:(sub + 1) * 128], rhs=KT[po:po + 64, c, slot * 128:(slot + 1) * 128], start=True, stop=False)),
                             reads=[("QT", c), ("KT", c, slot // 4)], writes=[(spn, jj // 4)])
                        S.op("pe", (lambda e, jj=jj, h=h, sps=sps: e.matmul(sps[:, jj * 128:(jj + 1) * 128], lhsT=self.ident[:], rhs=biasb[:, h, jj * 128:(jj + 1) * 128], start=False, stop=True)),
                             reads=["biasb", "ident"], writes=[(spn, jj // 4)])
                    skeys = [(spn, 0), (spn, 1)]
                    S.op("dve", (lambda e, sps=sps, smi=smi, h=h, c0=c0: e.tensor_reduce(out=smi[:, 32 + h:33 + h], in_=sps[:, c0:640], axis=AX.X, op=ALU.max, negate=True)),
                         reads=skeys, writes=[(ksm + "m", h)])
                    pb_ = Pb[si]
                    S.op("act", (lambda e, sps=sps, smi=smi, h=h, c0=c0, pb_=pb_: e.activation(out=pb_[:, c0:640], in_=sps[:, c0:640], func=AF.Exp, bias=smi[:, 32 + h:33 + h], scale=1.0, accum_out=smi[:, h:h + 1])),
                         reads=skeys + [(ksm + "m", h)], writes=[f"Pb{si}", (ksm + "s", h)])
                    ptp = PS[1][:, 0:512].bitcast(BF16)
                    for kb in kbs:
                        jj = kb - (qb - 4)
                        S.op("pe", (lambda e, jj=jj, ptp=ptp, pb_=pb_: e.transpose(ptp[:, jj * 128:(jj + 1) * 128], pb_[:, jj * 128:(jj + 1) * 128], self.ident[:])),
                             reads=[f"Pb{si}", "ident"], writes=[("PS1", 0)])
                    pts = PTs[si]
                    S.op("dve", (lambda e, ptp=ptp, pts=pts, j0=j0: e.tensor_copy(out=pts[:, j0:5, :], in_=ptp[:, j0 * 128:640].rearrange("p (a b) -> p a b", b=128))),
                         reads=[("PS1", 0)], writes=[f"PTs{si}"])
                    for kb in kbs:
                        jj = kb - (qb - 4)
                        slot = kb % 8
                        S.op("pe", (lambda e, jj=jj, slot=slot, h=h, pts=pts, po_ps=po_ps, kbs=kbs, kb=kb: e.matmul(po_ps[:, h * 64:(h + 1) * 64], lhsT=pts[:, jj, :], rhs=V[:, slot, h * 64:(h + 1) * 64], start=(kb == kbs[0]), stop=(kb == kbs[-1]))),
                             reads=[f"PTs{si}", ("V", slot, h // 8)], writes=[("PS2", h // 8)])
                S.op("dve", (lambda e, smi=smi: e.reciprocal(out=smi[:, 16:32], in_=smi[:, 0:16])), reads=[(ksm + "s", h) for h in range(16)], writes=[ksm + "r"])
                ob = Ob[oi]
                S.op("dve", (lambda e, smi=smi, ob=ob, po_ps=po_ps: e.tensor_tensor(out=ob.rearrange("p (h d) -> p h d", h=16), in0=po_ps[:, :].rearrange("p (h d) -> p h d", h=16), in1=smi[:, 16:32].unsqueeze(2).to_broadcast([128, 16, 64]), op=ALU.mult)),
                     reads=[("PS2", 0), ("PS2", 1), ksm + "r"], writes=[f"Ob{oi}"])
                otp = PS[1][:, 512:1024].bitcast(BF16)
                for cc in range(8):
                    S.op("pe", (lambda e, cc=cc, otp=otp, ob=ob: e.transpose(otp[:, cc * 128:(cc + 1) * 128], ob[:, cc * 128:(cc + 1) * 128], self.ident[:])),
                         reads=[f"Ob{oi}", "ident"], writes=[("PS1", 1)])
                S.op("act", (lambda e, otp=otp, sub=sub: e.copy(out=OT[:, :, sub * 128:(sub + 1) * 128], in_=otp.rearrange("p (c t) -> p c t", c=8))),
                     reads=[("PS1", 1)], writes=[("OT", sub)])
            self.out_proj(seq, t, lambda k, sub: OT[:, k, sub * 128:(sub + 1) * 128], [("OT", s_) for s_ in range(4)], wo, "wo", 8, first=first)


def make_consts():
    c = np.zeros((128, 256), np.float32)
    c[:, 0:128] = np.eye(128, dtype=np.float32)
    c[:, 128] = LN_EPS
    return c


def make_rwc():
    c = np.zeros((128, 648), np.float32)
    s_ = np.arange(128)[:, None]
    t_ = np.arange(128)[None, :]
    same = (s_ // 64) == (t_ // 64)
    c[:, 0:128] = (same & (s_ < t_))
    c[:, 128:256] = (same & (s_ <= t_))
    c[:, 256:384] = (same & (s_ > t_))
    c[:, 384:512] = same
    c[:, 512:640] = (t_ % 64 != 0)
    c[0:64, 640] = 1.0
    c[64:128, 641] = 1.0
    c[:, 642] = 64e-5
    return c


def prep_shared(inp):
    sh = {}
    sh["ln_g"] = np.ascontiguousarray(inp["ln_g"].reshape(DEPTH * 3, D))
    sh["ln_b"] = np.ascontiguousarray(inp["ln_b"].reshape(DEPTH * 3, D))
    sh["lru_w_in"] = inp["lru_w_in"]
    na = inp["lru_w_in"].shape[0]
    vec = np.zeros((na, BS, NBLK, 8), np.float32)
    cw = inp["lru_conv_w"].reshape(na, 4, NBLK, BS)
    for k in range(4):
        vec[:, :, :, k] = cw[:, k].transpose(0, 2, 1)
    vec[:, :, :, 4] = inp["lru_conv_b"].reshape(na, NBLK, BS).transpose(0, 2, 1)
    gbv = inp["lru_gate_b"].reshape(na, 2, NBLK, BS)
    vec[:, :, :, 5] = gbv[:, 0].transpose(0, 2, 1)
    vec[:, :, :, 6] = gbv[:, 1].transpose(0, 2, 1)
    vec[:, :, :, 7] = inp["lru_lambda"].reshape(na, NBLK, BS).transpose(0, 2, 1)
    sh["lru_vec"] = vec
    sh["lru_gate_w"] = np.ascontiguousarray(inp["lru_gate_w"].transpose(0, 3, 1, 2, 4).reshape(na, BS, 2 * NBLK, BS))
    sh["lru_w_out"] = inp["lru_w_out"]
    for k in ("mx_w_q", "mx_w_kv", "mx_w_o", "mlp_w1", "mlp_w2", "ca_w_qkv", "ca_w_o"):
        sh[k] = inp[k]
    rb = inp["ca_rel_bias"][0]
    q = np.arange(64)[:, None]
    kk = np.arange(576)[None, :]
    idx = np.clip(512 + q - kk, -128, 128) + 128
    band = rb[:, idx]
    b2 = np.full((128, 16, 640), NEG, np.float32)
    b2[0:64, :, 0:576] = band.transpose(1, 0, 2)
    b2[64:128, :, 64:640] = band.transpose(1, 0, 2)
    sh["ca_bias"] = b2
    sh["consts"] = make_consts()
    for k in ("rw_w_r", "rw_w_k", "rw_w_v", "rw_w_o", "rw_w1", "rw_a1", "rw_g1", "rw_w2", "rw_a2", "rw_g2"):
        sh[k] = inp[k]
    nb = inp["rw_mu"].shape[0]
    rv = np.zeros((nb, 128, 8, 16), np.float32)
    fm = lambda v: v.reshape(nb, 8, 128).transpose(0, 2, 1)
    for m in range(6):
        rv[:, :, :, m] = fm(inp["rw_mu"][:, m])
    rv[:, :, :, 6] = fm(inp["rw_w0"])
    rv[:, :, :, 7] = fm(inp["rw_a0"])
    rv[:, :, :, 8] = fm(inp["rw_k_k"])
    rv[:, :, :, 9] = fm(inp["rw_k_a"])
    rv[:, :, :, 10] = fm(inp["rw_r_k"].reshape(nb, D))
    sh["rw_vec"] = rv
    sh["rw_lnx"] = np.ascontiguousarray(np.stack([inp["rw_lnx_g"], inp["rw_lnx_b"]], axis=1))
    sh["rwc"] = make_rwc()
    return sh


_NC_CACHE = {}


def get_nc(n_layers=DEPTH, dbg=None):
    key = (n_layers if isinstance(n_layers, int) else tuple(n_layers), tuple(sorted(dbg.items())) if dbg else None)
    if key not in _NC_CACHE:
        b = Builder(n_layers, dbg)
        nc = b.build()
        _NC_CACHE[key] = nc
    return _NC_CACHE[key]


def kernel(**inputs):
    inp = {k: np.ascontiguousarray(np.asarray(v, dtype=np.float32)) for k, v in inputs.items()}
    sh = prep_shared(inp)
    nc = get_nc()
    ncores = 8
    in_maps = []
    for c in range(ncores):
        m = dict(sh)
        m["x"] = np.ascontiguousarray(inp["x"][c * NSEQ:(c + 1) * NSEQ])
        m["mem"] = np.ascontiguousarray(inp["mem"][c * NSEQ:(c + 1) * NSEQ])
        in_maps.append(m)
    res = run_bass_kernel_spmd(nc, in_maps, core_ids=list(range(ncores)))
    out = np.concatenate([np.asarray(r["out"]).reshape(NSEQ, SEQ, D) for r in res.results], axis=0)
    return out.astype(np.float32)
```

```python
import numpy as np
import concourse.bass as bass
import concourse.mybir as mybir
from concourse.bass_utils import run_bass_kernel_spmd
from contextlib import ExitStack

F32 = mybir.dt.float32
BF16 = mybir.dt.bfloat16
AF = mybir.ActivationFunctionType
ALU = mybir.AluOpType
AX = mybir.AxisListType

ENGS = ("pe", "act", "dve", "pool", "sp")
SEM_EPOCH = 30000

D = 1024
SEQ = 2048
NSEQ = 2
DEPTH = 4
ALPHA = (2 * DEPTH) ** 0.25
LN_EPS = 1e-5
D_RNN = 1344
NBLK = 16
BS = 84
D_FF = 4096
MEMT = 256
NEG = -30000.0


class Sched:
    def __init__(self, nc, stack, n_dma_sems=8):
        self.nc = nc
        self.stack = stack
        self.prog = {e: [] for e in ENGS}
        self.cnt = {e: 0 for e in ENGS}
        self.nsem = 0
        self.sem_owner = {}
        self.sem = {}
        for e in ENGS:
            self.sem[e] = self._newsem()
            self.sem_owner[id(self.sem[e])] = e
        self.known = {e: {} for e in ENGS}
        self.lastw = {}
        self.readers = {}
        self.dma_sems = {q: [[self._newsem(), 0] for _ in range(n_dma_sems)] for q in ("sp", "act", "pool")}
        self.dma_rr = {q: 0 for q in ("sp", "act", "pool")}
        self.ninstr = 0
        self.nwait = 0

    def _newsem(self):
        self.nsem += 1
        return self.stack.enter_context(self.nc.semaphore(f"s{self.nsem}"))

    def _deps(self, eng, reads, writes):
        deps = {}

        def add(ev, raw):
            if ev is None:
                return
            s, v = ev
            own = self.sem_owner.get(id(s))
            if own == eng:
                if eng == "pe":
                    return
            k = id(s)
            if k not in deps or deps[k][1] < v:
                deps[k] = (s, v)

        for k in reads:
            add(self.lastw.get(k), True)
        for k in writes:
            add(self.lastw.get(k), False)
            rd = self.readers.get(k)
            if rd:
                for ev in rd.values():
                    add(ev, False)
        return deps

    def _filter(self, eng, deps):
        waits = []
        kn = self.known[eng]
        for k, (s, v) in deps.items():
            if kn.get(k, 0) < v:
                kn[k] = v
                waits.append((s, v))
        self.nwait += len(waits)
        return waits

    def _record(self, ev, reads, writes):
        for k in writes:
            self.lastw[k] = ev
            self.readers[k] = {}
        for k in reads:
            r = self.readers.setdefault(k, {})
            r[id(ev[0])] = ev

    def op(self, eng, fn, reads=(), writes=()):
        deps = self._deps(eng, reads, writes)
        waits = self._filter(eng, deps)
        if self.cnt[eng] >= SEM_EPOCH:
            self.sem[eng] = self._newsem()
            self.sem_owner[id(self.sem[eng])] = eng
            self.cnt[eng] = 0
        self.cnt[eng] += 1
        ev = (self.sem[eng], self.cnt[eng])
        self.prog[eng].append((waits, fn, (self.sem[eng], 1)))
        self._record(ev, reads, writes)
        self.ninstr += 1
        return ev

    def dma(self, q, out, in_, reads=(), writes=()):
        slots = self.dma_sems[q]
        si = self.dma_rr[q]
        self.dma_rr[q] = (si + 1) % len(slots)
        slot = slots[si]
        deps = self._deps(q, reads, writes)
        if slot[1] > 0:
            deps[id(slot[0])] = (slot[0], 16 * slot[1])
        waits = self._filter(q, deps)
        slot[1] += 1
        ev = (slot[0], 16 * slot[1])
        self.prog[q].append((waits, (lambda e: e.dma_start(out=out, in_=in_)), (slot[0], 16)))
        self._record(ev, reads, writes)
        self.ninstr += 1
        return ev

    def all_events(self):
        evs = []
        for e in ENGS:
            if self.cnt[e] > 0:
                evs.append((self.sem[e], self.cnt[e]))
        for q in self.dma_sems:
            for s, n in self.dma_sems[q]:
                if n > 0:
                    evs.append((s, 16 * n))
        return evs

    def barrier(self):
        evs = self.all_events()
        for e in ENGS:
            deps = {}
            for s, v in evs:
                if self.sem_owner.get(id(s)) == e:
                    continue
                deps[id(s)] = (s, v)
            waits = self._filter(e, deps)
            if waits:
                self.prog[e].append((waits, None, None))
        self.lastw = {}
        self.readers = {}

    def emit(self):
        nc = self.nc
        prog = self.prog

        def run(name, e):
            for waits, fn, inc in prog[name]:
                for s, v in waits:
                    e.wait_ge(s, v)
                if fn is not None:
                    ins = fn(e)
                    if inc is not None:
                        ins.then_inc(inc[0], inc[1])

        with nc.Block() as block:
            @block.tensor
            def _(e):
                run("pe", e)

            @block.scalar
            def _(e):
                run("act", e)

            @block.vector
            def _(e):
                run("dve", e)

            @block.gpsimd
            def _(e):
                run("pool", e)

            @block.sync
            def _(e):
                run("sp", e)


class Arena:
    def __init__(self, t, nwords):
        self.t = t
        self.n = nwords
        self.off = 0

    def mark(self):
        return self.off

    def reset(self, m):
        self.off = m

    def alloc(self, shape, dtype, parts=128):
        free = int(np.prod(shape))
        words = free if dtype == F32 else (free + 1) // 2
        words = (words + 1) // 2 * 2
        assert self.off + words <= self.n, f"arena overflow {self.off}+{words}>{self.n}"
        ap = self.t[0:parts, self.off:self.off + words]
        self.off += words
        if dtype != F32:
            ap = ap.bitcast(dtype)
        ap = ap[:, 0:free]
        if len(shape) == 2:
            ap = ap.rearrange("p (a b) -> p a b", a=shape[0])
        elif len(shape) == 3:
            ap = ap.rearrange("p (a b c) -> p a b c", a=shape[0], b=shape[1])
        return ap


_uid = [0]


def uid(p="k"):
    _uid[0] += 1
    return f"{p}{_uid[0]}"


class Builder:
    def __init__(self, n_layers=DEPTH, dbg=None):
        self.layers = list(range(n_layers)) if isinstance(n_layers, int) else list(n_layers)
        self.dbg = dbg
        nc = self.nc = bass.Bass("TRN2", target_bir_lowering=False)
        self.st = ExitStack()
        self.S = Sched(nc, self.st)

    def din(self, name, shape):
        return self.nc.dram_tensor(name, list(shape), F32, kind="ExternalInput").ap()

    def build(self):
        nc, st, S = self.nc, self.st, self.S
        P = self.P = {}
        P["x"] = self.din("x", [NSEQ, SEQ, D])
        P["mem"] = self.din("mem", [NSEQ, MEMT, D])
        P["ln_g"] = self.din("ln_g", [DEPTH * 3, D])
        P["ln_b"] = self.din("ln_b", [DEPTH * 3, D])
        P["lru_w_in"] = self.din("lru_w_in", [2, D, 2 * D_RNN])
        P["lru_vec"] = self.din("lru_vec", [2, BS, NBLK, 8])
        P["lru_gate_w"] = self.din("lru_gate_w", [2, BS, 2 * NBLK, BS])
        P["lru_w_out"] = self.din("lru_w_out", [2, D_RNN, D])
        P["mx_w_q"] = self.din("mx_w_q", [DEPTH, D, D])
        P["mx_w_kv"] = self.din("mx_w_kv", [DEPTH, D, 2 * D])
        P["mx_w_o"] = self.din("mx_w_o", [DEPTH, D, D])
        P["mlp_w1"] = self.din("mlp_w1", [DEPTH, D, D_FF])
        P["mlp_w2"] = self.din("mlp_w2", [DEPTH, D_FF, D])
        P["ca_w_qkv"] = self.din("ca_w_qkv", [1, D, 3 * D])
        P["ca_w_o"] = self.din("ca_w_o", [1, D, D])
        P["ca_bias"] = self.din("ca_bias", [128, 16, 640])
        P["consts"] = self.din("consts", [128, 256])
        for nm in ("rw_w_r", "rw_w_k", "rw_w_v", "rw_w_o"):
            P[nm] = self.din(nm, [1, D, D])
        P["rw_w1"] = self.din("rw_w1", [1, D, 64])
        P["rw_a1"] = self.din("rw_a1", [1, D, 64])
        P["rw_g1"] = self.din("rw_g1", [1, D, 128])
        P["rw_w2"] = self.din("rw_w2", [1, 64, D])
        P["rw_a2"] = self.din("rw_a2", [1, 64, D])
        P["rw_g2"] = self.din("rw_g2", [1, 128, D])
        P["rw_vec"] = self.din("rw_vec", [1, 128, 8, 16])
        P["rw_lnx"] = self.din("rw_lnx", [1, 2, D])
        P["rwc"] = self.din("rwc", [128, 648])
        self.out = nc.dram_tensor("out", [NSEQ, SEQ, D], F32, kind="ExternalOutput").ap()
        self.h32 = nc.dram_tensor("h32", [NSEQ, SEQ, D], F32, kind="Internal").ap()
        if self.dbg:
            self.dbg_out = {n: nc.dram_tensor(n, list(shp), F32, kind="ExternalOutput").ap() for n, shp in self.dbg.items()}

        sb = lambda n, s, d: st.enter_context(nc.sbuf_tensor(n, s, d))
        self.hT = sb("hT", [128, 8, SEQ + 2], BF16)
        self.ident = sb("ident", [128, 128], BF16)
        self.identf = sb("identf", [128, 128], F32)
        self.cst = sb("cst", [128, 256], F32)
        self.lng = sb("lng", [128, D], F32)
        self.lnb = sb("lnb", [128, D], F32)
        self.lnz = [sb("lnz0", [128, D], F32)] * 2
        self.lnh = [sb(f"lnh{i}", [128, D], F32) for i in range(2)]
        self.lnhb = [sb(f"lnhb{i}", [128, D], BF16) for i in range(2)]
        self.lnst = [sb(f"lnst{i}", [128, 16], F32) for i in range(2)]
        self.lni = 0
        ARW = 37600
        self.arena_t = sb("arena", [128, ARW], F32)
        self.A = Arena(self.arena_t, ARW)
        self.PS = [st.enter_context(nc.psum_tensor(f"ps{i}", [128, 1024], F32)) for i in range(4)]
        self.out_events = []
        self.gi = 0
        self.gen_list = [(0, 0), (0, 1)]

        S.dma("sp", self.cst[:], P["consts"], writes=["cst"])
        S.op("dve", lambda e: e.tensor_copy(out=self.ident[:], in_=self.cst[:, 0:128]), reads=["cst"], writes=["ident"])
        S.op("dve", lambda e: e.tensor_copy(out=self.identf[:], in_=self.cst[:, 0:128]), reads=["cst"], writes=["identf"])
        S.op("pool", lambda e: e.memset(self.hT[:, :, 0:2], 0.0), writes=["hTpad"])

        for seq in range(NSEQ):
            self.load_x(seq)
            for li, layer in enumerate(self.layers):
                kind, j = layer % 3, layer // 3
                first = (li == 0)
                if kind == 0:
                    self.stage_lru(seq, layer, j, first)
                elif kind == 1:
                    self.stage_rwkv(seq, layer, j, first)
                else:
                    self.stage_attn(seq, layer, j, first)
                self.stage_memattn(seq, layer)
                self.stage_mlp(seq, layer, last=(li == len(self.layers) - 1))
        S.barrier()
        S.emit()
        self.st.close()
        return nc

    def stage_begin(self):
        self.S.barrier()
        self.A.reset(0)

    def load_w(self, dst, src, key, q="pool"):
        K, nk, ncols = dst.shape
        step = max(1, 2048 // K) if ncols * 4 >= 2048 else nk
        step = min(nk, max(1, (1 << 21) // (K * ncols * 4)))
        for k0 in range(0, nk, step):
            k1 = min(nk, k0 + step)
            self.S.dma(q, dst[:, k0:k1, :], src[k0 * K:k1 * K, :].rearrange("(k p) n -> p k n", p=K), writes=[key])

    def load_x(self, seq):
        S = self.S
        self.stage_begin()
        xb = [self.A.alloc([D], BF16) for _ in range(2)]
        for sub in range(SEQ // 128):
            b = xb[sub % 2]
            kb = f"xb{sub % 2}"
            S.dma("pool", b, self.P["x"][seq, sub * 128:(sub + 1) * 128, :], writes=[kb])
            self.to_hT(b, kb, sub)

    def to_hT(self, hb, kb, sub):
        S = self.S
        pi = self.lni % 2 if getattr(self, "force_pi", None) is None else self.force_pi
        ps = self.PS[1][:, pi * 512:(pi + 1) * 512].bitcast(BF16)
        pk = ("PS1", pi)
        for c in range(8):
            S.op("pe", (lambda e, c=c: e.transpose(ps[:, c * 128:(c + 1) * 128], hb[:, c * 128:(c + 1) * 128], self.ident[:])),
                 reads=[kb, "ident"], writes=[pk])
        dst = self.hT[:, :, 2 + sub * 128: 2 + (sub + 1) * 128]
        src = ps.rearrange("p (c t) -> p c t", c=8)
        S.op("dve", lambda e: e.tensor_copy(out=dst, in_=src), reads=[pk], writes=[("hT", sub)])
        self.lni += 1

    def load_ln(self, li):
        S = self.S
        S.dma("sp", self.lng[:], self.P["ln_g"][li, :].partition_broadcast(128), writes=["lng"])
        S.dma("sp", self.lnb[:], self.P["ln_b"][li, :].partition_broadcast(128), writes=["lnb"])

    def ln_finish(self, seq, sub, y, ykeys, first, last):
        S = self.S
        i = self.lni % 2
        z, hn, hb, stt = self.lnz[i], self.lnh[i], self.lnhb[i], self.lnst[i]
        kz, kh, khb, kst = "lnz0", f"lnh{i}", f"lnhb{i}", f"lnst{i}"
        src = (self.P["x"] if first else self.h32)[seq, sub * 128:(sub + 1) * 128, :]
        hk = ("h32", seq, sub)
        S.dma("sp", hn[:], src, reads=[hk], writes=[kh])
        S.op("dve", lambda e: e.scalar_tensor_tensor(out=z[:], in0=hn[:], scalar=float(ALPHA), in1=y, op0=ALU.mult, op1=ALU.add),
             reads=[kh] + list(ykeys), writes=[kz])
        S.op("dve", lambda e: e.bn_stats(out=stt[:, 0:6], in_=z[:, 0:512]), reads=[kz], writes=[kst + "a"])
        S.op("dve", lambda e: e.bn_stats(out=stt[:, 6:12], in_=z[:, 512:1024]), reads=[kz], writes=[kst + "b"])
        S.op("dve", lambda e: e.bn_aggr(out=stt[:, 12:14], in_=stt[:, 0:12]), reads=[kst + "a", kst + "b"], writes=[kst + "c"])
        S.op("act", lambda e: e.activation(out=stt[:, 14:15], in_=stt[:, 13:14], func=AF.Sqrt, bias=self.cst[:, 128:129], scale=1.0),
             reads=[kst + "c"], writes=[kst + "d0"])
        S.op("dve", lambda e: e.reciprocal(out=stt[:, 14:15], in_=stt[:, 14:15]), reads=[kst + "d0"], writes=[kst + "d"])
        S.op("dve", lambda e: e.tensor_scalar(out=stt[:, 15:16], in0=stt[:, 12:13], scalar1=stt[:, 14:15], scalar2=-1.0, op0=ALU.mult, op1=ALU.mult),
             reads=[kst + "c", kst + "d"], writes=[kst + "e"])
        S.op("act", lambda e: e.activation(out=z[:], in_=z[:], func=AF.Identity, bias=stt[:, 15:16], scale=stt[:, 14:15]),
             reads=[kz, kst + "d", kst + "e"], writes=[kz])
        S.op("pool", lambda e: e.tensor_tensor(out=z[:], in0=z[:], in1=self.lng[:], op=ALU.mult), reads=[kz, "lng"], writes=[kz])
        S.op("dve", lambda e: e.tensor_tensor(out=hn[:], in0=z[:], in1=self.lnb[:], op=ALU.add), reads=[kz, "lnb"], writes=[kh])
        dst = (self.out if last else self.h32)[seq, sub * 128:(sub + 1) * 128, :]
        S.dma("sp", dst, hn[:], reads=[kh], writes=[hk])
        if not last:
            S.op("act", lambda e: e.copy(out=hb[:], in_=hn[:]), reads=[kh], writes=[khb])
            self.to_hT(hb, khb, sub)
        else:
            self.lni += 1

    def gps(self, parts=128, n=512):
        i = self.gi
        self.gi += 1
        b = self.gen_list[i % len(self.gen_list)]
        return self.PS[b[0]][0:parts, b[1] * 512: b[1] * 512 + n], (f"PS{b[0]}", b[1])

    def hTs(self, k, t0, n):
        return self.hT[:, k, 2 + t0: 2 + t0 + n]

    def hTkeys(self, t0, n):
        return [("hT", s) for s in range(t0 // 128, (t0 + n + 127) // 128)]

    def dbg_store(self, name, ap, keys, dst_slice=None):
        if self.dbg and name in self.dbg:
            d = self.dbg_out[name] if dst_slice is None else dst_slice(self.dbg_out[name])
            self.S.dma("sp", d, ap, reads=keys)

    def stage_mlp(self, seq, layer, last):
        S, A, PS = self.S, self.A, self.PS
        self.stage_begin()
        self.load_ln(layer * 3 + 2)
        acc = A.alloc([16, D], F32)
        w1g = [A.alloc([8, 512], BF16) for _ in range(2)]
        w2g = [A.alloc([4, D], BF16) for _ in range(2)]
        hid = [A.alloc([4, 512], BF16) for _ in range(2)]
        rl = [A.alloc([512], F32) for _ in range(2)]
        w1 = self.P["mlp_w1"][layer]
        w2 = self.P["mlp_w2"][layer]
        nb = 0
        ny = 0
        for g in range(8):
            gi = g % 2
            self.load_w(w1g[gi], w1[:, g * 512:(g + 1) * 512], f"w1g{gi}")
            self.load_w(w2g[gi], w2[g * 512:(g + 1) * 512, :], f"w2g{gi}")
            for t in range(4):
                hi = (g * 4 + t) % 2
                for fc in range(4):
                    nb += 1
                    ps, pk = self.gps()
                    for k in range(8):
                        S.op("pe", (lambda e, k=k, fc=fc, ps=ps, gi=gi, t=t: e.matmul(ps, lhsT=w1g[gi][:, k, fc * 128:(fc + 1) * 128], rhs=self.hTs(k, t * 512, 512), start=(k == 0), stop=(k == 7))),
                             reads=[f"w1g{gi}"] + self.hTkeys(t * 512, 512), writes=[pk])
                    ri = nb % 2
                    S.op("act", (lambda e, ps=ps, ri=ri: e.activation(out=rl[ri], in_=ps, func=AF.Relu)), reads=[pk], writes=[f"rl{ri}"])
                    S.op("dve", (lambda e, ri=ri, hi=hi, fc=fc: e.tensor_tensor(out=hid[hi][:, fc, :], in0=rl[ri], in1=rl[ri], op=ALU.mult)),
                         reads=[f"rl{ri}"], writes=[(f"hid{hi}", fc)])
                for sub in range(4):
                    yi = ny % 2
                    ny += 1
                    py = PS[2 + yi]
                    for half in range(2):
                        for fc in range(4):
                            S.op("pe", (lambda e, fc=fc, half=half, py=py, hi=hi, gi=gi, sub=sub: e.matmul(py[:, half * 512:(half + 1) * 512], lhsT=hid[hi][:, fc, sub * 128:(sub + 1) * 128], rhs=w2g[gi][:, fc, half * 512:(half + 1) * 512], start=(fc == 0), stop=(fc == 3))),
                                 reads=[(f"hid{hi}", fc), f"w2g{gi}"], writes=[(f"PS{2 + yi}", half)])
                    a = acc[:, t * 4 + sub, :]
                    ak = ("acc", t * 4 + sub)
                    pkeys = [(f"PS{2 + yi}", 0), (f"PS{2 + yi}", 1)]
                    if g == 0:
                        S.op("act", (lambda e, a=a, py=py: e.copy(out=a, in_=py[:, :])), reads=pkeys, writes=[ak])
                    else:
                        S.op("dve", (lambda e, a=a, py=py: e.tensor_tensor(out=a, in0=a, in1=py[:, :], op=ALU.add)), reads=pkeys + [ak], writes=[ak])
        for s16 in range(16):
            self.ln_finish(seq, s16, acc[:, s16, :], [("acc", s16)], first=False, last=last)

    def stage_memattn(self, seq, layer):
        S, A, PS = self.S, self.A, self.PS
        self.stage_begin()
        self.load_ln(layer * 3 + 1)
        wb = [A.alloc([8, D], BF16) for _ in range(2)]
        memb = A.alloc([2, D], BF16)
        memT = A.alloc([8, MEMT], BF16)
        KT = A.alloc([8, MEMT], BF16)
        V = A.alloc([2, D], BF16)
        QT = A.alloc([8, 512], BF16)
        Pn = [A.alloc([4, MEMT], BF16) for _ in range(2)]
        PT = A.alloc([8, 512], BF16)
        OT = A.alloc([8, 512], BF16)
        sm = [A.alloc([16], F32) for _ in range(2)]
        wkv = self.P["mx_w_kv"][layer]
        self.load_w(wb[0], wkv[:, 0:D], "wb0")
        self.load_w(wb[1], wkv[:, D:2 * D], "wb1")
        S.dma("pool", memb, self.P["mem"][seq].rearrange("(a p) d -> p a d", p=128), writes=["memb"])
        for mt in range(2):
            ps = PS[0][:, mt * 512:(mt + 1) * 512].bitcast(BF16)
            for c in range(8):
                S.op("pe", (lambda e, c=c, mt=mt, ps=ps: e.transpose(ps[:, c * 128:(c + 1) * 128], memb[:, mt, c * 128:(c + 1) * 128], self.ident[:])),
                     reads=["memb", "ident"], writes=[("PS0", mt)])
            S.op("dve", (lambda e, mt=mt, ps=ps: e.tensor_copy(out=memT[:, :, mt * 128:(mt + 1) * 128], in_=ps.rearrange("p (c t) -> p c t", c=8))),
                 reads=[("PS0", mt)], writes=[("memT", mt)])
        mk = [("memT", 0), ("memT", 1)]
        for oc in range(8):
            ps, pk = self.gps(n=MEMT)
            for k in range(8):
                S.op("pe", (lambda e, k=k, oc=oc, ps=ps: e.matmul(ps, lhsT=wb[0][:, k, oc * 128:(oc + 1) * 128], rhs=memT[:, k, :], start=(k == 0), stop=(k == 7))),
                     reads=["wb0"] + mk, writes=[pk])
            S.op("act", (lambda e, oc=oc, ps=ps: e.copy(out=KT[:, oc, :], in_=ps)), reads=[pk], writes=[("KT", oc)])
        for mt in range(2):
            for half in range(2):
                ps, pk = self.gps()
                for k in range(8):
                    S.op("pe", (lambda e, k=k, mt=mt, half=half, ps=ps: e.matmul(ps, lhsT=memT[:, k, mt * 128:(mt + 1) * 128], rhs=wb[1][:, k, half * 512:(half + 1) * 512], start=(k == 0), stop=(k == 7))),
                         reads=["wb1"] + mk, writes=[pk])
                S.op("dve", (lambda e, mt=mt, half=half, ps=ps: e.tensor_copy(out=V[:, mt, half * 512:(half + 1) * 512], in_=ps)), reads=[pk], writes=[("V", mt, half)])
        vk = [("V", a, b) for a in range(2) for b in range(2)]
        self.load_w(wb[0], self.P["mx_w_q"][layer], "wb0")
        self.load_w(wb[1], self.P["mx_w_o"][layer], "wb1")
        for t in range(4):
            for oc in range(8):
                ps, pk = self.gps()
                for k in range(8):
                    S.op("pe", (lambda e, k=k, oc=oc, ps=ps, t=t: e.matmul(ps, lhsT=wb[0][:, k, oc * 128:(oc + 1) * 128], rhs=self.hTs(k, t * 512, 512), start=(k == 0), stop=(k == 7))),
                         reads=["wb0"] + self.hTkeys(t * 512, 512), writes=[pk])
                S.op("act", (lambda e, oc=oc, ps=ps: e.activation(out=QT[:, oc, :], in_=ps, func=AF.Copy, scale=1.0 / 16.0)), reads=[pk], writes=[("QT", oc)])
            def mem_A(sub):
                pi = sub % 2
                ps = PS[0] if pi == 0 else PS[3]
                spn = "PS0" if pi == 0 else "PS3"
                pkeys = [(spn, 0), (spn, 1)]
                for h in range(4):
                    for c in range(2):
                        S.op("pe", (lambda e, h=h, c=c, ps=ps, sub=sub: e.matmul(ps[:, h * 256:(h + 1) * 256], lhsT=QT[:, 2 * h + c, sub * 128:(sub + 1) * 128], rhs=KT[:, 2 * h + c, :], start=(c == 0), stop=(c == 1))),
                             reads=[("QT", 2 * h + c), ("KT", 2 * h + c)], writes=[(spn, h // 2)])
                smi = sm[pi]
                ks = f"sm{pi}"
                S.op("dve", (lambda e, ps=ps, smi=smi: e.tensor_reduce(out=smi[:, 0:4], in_=ps[:, :].rearrange("p (h m) -> p h m", h=4), axis=AX.X, op=ALU.max, negate=True)),
                     reads=pkeys, writes=[ks + "m"])
                pn = Pn[pi]
                for h in range(4):
                    S.op("act", (lambda e, h=h, ps=ps, smi=smi, pn=pn: e.activation(out=pn[:, h, :], in_=ps[:, h * 256:(h + 1) * 256], func=AF.Exp, bias=smi[:, h:h + 1], scale=1.0, accum_out=smi[:, 4 + h:5 + h])),
                         reads=[(spn, h // 2), ks + "m"], writes=[(f"Pn{pi}", h), (ks + "s", h)])
                S.op("dve", (lambda e, smi=smi: e.reciprocal(out=smi[:, 8:12], in_=smi[:, 4:8])), reads=[(ks + "s", h) for h in range(4)], writes=[ks + "r"])
                S.op("dve", (lambda e, smi=smi, pn=pn: e.tensor_tensor(out=pn[:, :, :], in0=pn[:, :, :], in1=smi[:, 8:12].unsqueeze(2).to_broadcast([128, 4, MEMT]), op=ALU.mult)),
                     reads=[(f"Pn{pi}", h) for h in range(4)] + [ks + "r"], writes=[(f"Pn{pi}", h) for h in range(4)])

            def mem_B(sub):
                pi = sub % 2
                pn = Pn[pi]
                pt = PS[1][:, pi * 512:(pi + 1) * 512].bitcast(BF16)
                ptk = ("PS1", pi)
                for h in range(4):
                    for mc in range(2):
                        S.op("pe", (lambda e, h=h, mc=mc, pt=pt, pn=pn: e.transpose(pt[:, (h * 2 + mc) * 128:(h * 2 + mc + 1) * 128], pn[:, h, mc * 128:(mc + 1) * 128], self.ident[:])),
                             reads=[(f"Pn{pi}", h), "ident"], writes=[ptk])
                S.op("dve", (lambda e, pt=pt, sub=sub: e.tensor_copy(out=PT[:, :, sub * 128:(sub + 1) * 128], in_=pt.rearrange("p (c t) -> p c t", c=8))),
                     reads=[ptk], writes=[("PT", sub)])

            for sq_ in range(5):
                if sq_ < 4:
                    mem_A(sq_)
                if sq_ >= 1:
                    mem_B(sq_ - 1)
            for oc in range(8):
                h, c = oc // 2, oc % 2
                ps, pk = self.gps()
                for mc in range(2):
                    S.op("pe", (lambda e, h=h, c=c, mc=mc, ps=ps: e.matmul(ps, lhsT=V[:, mc, h * 256 + c * 128: h * 256 + (c + 1) * 128], rhs=PT[:, h * 2 + mc, :], start=(mc == 0), stop=(mc == 1))),
                         reads=vk + [("PT", s) for s in range(4)], writes=[pk])
                S.op("act", (lambda e, oc=oc, ps=ps: e.copy(out=OT[:, oc, :], in_=ps)), reads=[pk], writes=[("OT", oc)])
            self.out_proj(seq, t, lambda k, sub: OT[:, k, sub * 128:(sub + 1) * 128], [("OT", k) for k in range(8)], wb[1], "wb1", 8, first=False)

    def out_proj(self, seq, t, lhs_fn, lkeys, w, wkey, nk, first):
        S, PS = self.S, self.PS
        for sub in range(4):
            yi = sub % 2
            py = PS[2 + yi]
            for half in range(2):
                for k in range(nk):
                    S.op("pe", (lambda e, k=k, half=half, py=py, sub=sub: e.matmul(py[:, half * 512:(half + 1) * 512], lhsT=lhs_fn(k, sub), rhs=w[:, k, half * 512:(half + 1) * 512], start=(k == 0), stop=(k == nk - 1))),
                         reads=list(lkeys) + [wkey], writes=[(f"PS{2 + yi}", half)])
            self.ln_finish(seq, t * 4 + sub, py[:, :], [(f"PS{2 + yi}", 0), (f"PS{2 + yi}", 1)], first=first, last=False)

    def stage_lru(self, seq, layer, j, first):
        S, A, PS = self.S, self.A, self.PS
        self.stage_begin()
        self.load_ln(layer * 3 + 0)
        win = A.alloc([8, 2 * D_RNN], BF16)
        wout = A.alloc([NBLK, D], BF16, parts=BS)
        gw = A.alloc([2 * NBLK, BS], BF16, parts=BS)
        vec = A.alloc([NBLK, 8], F32, parts=BS)
        c8 = A.alloc([NBLK], F32, parts=BS)
        carry = A.alloc([NBLK], F32, parts=BS)
        xpb = [A.alloc([516], F32, parts=BS) for _ in range(2)]
        hist = A.alloc([NBLK, 4], F32, parts=BS)
        mT = A.alloc([NBLK, 512], BF16, parts=BS)
        NB2 = 2
        gb = [A.alloc([512], F32, parts=BS) for _ in range(NB2)]
        xr = [A.alloc([512], F32, parts=BS) for _ in range(NB2)]
        xrb = [A.alloc([512], BF16, parts=BS) for _ in range(NB2)]
        rg = [A.alloc([512], F32, parts=BS) for _ in range(NB2)]
        ig = [A.alloc([512], F32, parts=BS) for _ in range(NB2)]
        aa = [A.alloc([512], F32, parts=BS) for _ in range(NB2)]
        sq = [A.alloc([512], F32, parts=BS) for _ in range(NB2)]
        hs = [A.alloc([512], F32, parts=BS) for _ in range(NB2)]
        self.load_w(win, self.P["lru_w_in"][j], "win")
        S.dma("pool", wout, self.P["lru_w_out"][j].rearrange("(n p) d -> p n d", p=BS), writes=["wout"])
        S.dma("pool", gw, self.P["lru_gate_w"][j], writes=["gw"])
        S.dma("sp", vec, self.P["lru_vec"][j], writes=["vec"])
        tx = A.alloc([NBLK], F32, parts=BS)
        tl = A.alloc([NBLK], F32, parts=BS)
        tu = A.alloc([NBLK], F32, parts=BS)
        S.op("act", lambda e: e.activation(out=tx, in_=vec[:, :, 7], func=AF.Exp, scale=-1.0), reads=["vec"], writes=["tx"])
        S.op("act", lambda e: e.activation(out=tl, in_=tx, func=AF.Ln, bias=1.0, scale=1.0), reads=["tx"], writes=["tl"])
        S.op("dve", lambda e: e.tensor_scalar(out=tu, in0=tx, scalar1=-0.25, scalar2=1.0 / 3.0, op0=ALU.mult, op1=ALU.add), reads=["tx"], writes=["tu"])
        S.op("dve", lambda e: e.tensor_tensor(out=tu, in0=tu, in1=tx, op=ALU.mult), reads=["tu", "tx"], writes=["tu"])
        S.op("dve", lambda e: e.tensor_scalar(out=tu, in0=tu, scalar1=-1.0, scalar2=0.5, op0=ALU.mult, op1=ALU.add), reads=["tu"], writes=["tu"])
        S.op("dve", lambda e: e.tensor_tensor(out=tu, in0=tu, in1=tx, op=ALU.mult), reads=["tu", "tx"], writes=["tu"])
        S.op("dve", lambda e: e.tensor_scalar(out=tu, in0=tu, scalar1=-1.0, scalar2=1.0, op0=ALU.mult, op1=ALU.add), reads=["tu"], writes=["tu"])
        S.op("dve", lambda e: e.tensor_tensor(out=tu, in0=tu, in1=tx, op=ALU.mult), reads=["tu", "tx"], writes=["tu"])
        S.op("dve", lambda e: e.tensor_tensor(out=tu, in0=tu, in1=tl, op=ALU.subtract), reads=["tu", "tl"], writes=["tu"])
        S.op("dve", lambda e: e.tensor_scalar(out=tx, in0=tx, scalar1=0.05, scalar2=None, op0=ALU.is_lt), reads=["tx", "tu"], writes=["tx"])
        S.op("dve", lambda e: e.tensor_tensor(out=tu, in0=tu, in1=tx, op=ALU.mult), reads=["tu", "tx"], writes=["tu"])
        S.op("dve", lambda e: e.tensor_tensor(out=tu, in0=tu, in1=tl, op=ALU.add), reads=["tu", "tl"], writes=["tu"])
        S.op("dve", lambda e: e.tensor_scalar(out=c8, in0=tu, scalar1=-8.0, scalar2=None, op0=ALU.mult), reads=["tu"], writes=["c8"])
        S.op("dve", lambda e: e.memset(carry, 0.0), writes=["carry"])
        S.op("dve", lambda e: e.memset(hist, 0.0), writes=[("hist", n) for n in range(NBLK)])
        nb = 0
        for t in range(4):
            for n in range(NBLK):
                bi = n % NB2
                ps, pk = self.gps(parts=BS)
                for k in range(8):
                    S.op("pe", (lambda e, k=k, n=n, ps=ps, t=t: e.matmul(ps, lhsT=win[:, k, n * BS:(n + 1) * BS], rhs=self.hTs(k, t * 512, 512), start=(k == 0), stop=(k == 7))),
                         reads=["win"] + self.hTkeys(t * 512, 512), writes=[pk])
                S.op("act", (lambda e, ps=ps, bi=bi: e.activation(out=gb[bi], in_=ps, func=AF.Gelu)), reads=[pk], writes=[f"gb{bi}"])
                ps2, pk2 = self.gps(parts=BS)
                for k in range(8):
                    S.op("pe", (lambda e, k=k, n=n, ps2=ps2, t=t: e.matmul(ps2, lhsT=win[:, k, D_RNN + n * BS: D_RNN + (n + 1) * BS], rhs=self.hTs(k, t * 512, 512), start=(k == 0), stop=(k == 7))),
                         reads=["win"] + self.hTkeys(t * 512, 512), writes=[pk2])
                xp = xpb[bi]
                xk = f"xpb{bi}"
                S.op("pool", (lambda e, xp=xp, n=n: e.tensor_copy(out=xp[:, 0:4], in_=hist[:, n, :])), reads=[("hist", n)], writes=[xk + "h"])
                S.op("act", (lambda e, xp=xp, ps2=ps2: e.copy(out=xp[:, 4:516], in_=ps2)), reads=[pk2], writes=[xk])
                x_ = xr[bi]
                kx = f"xr{bi}"
                S.op("dve", (lambda e, xp=xp, x_=x_, n=n: e.tensor_scalar(out=x_, in0=xp[:, 1:513], scalar1=vec[:, n, 0:1], scalar2=vec[:, n, 4:5], op0=ALU.mult, op1=ALU.add)),
                     reads=[xk, xk + "h", "vec"], writes=[kx])
                for kk in range(1, 4):
                    S.op("dve", (lambda e, xp=xp, x_=x_, n=n, kk=kk: e.scalar_tensor_tensor(out=x_, in0=xp[:, 1 + kk:513 + kk], scalar=vec[:, n, kk:kk + 1], in1=x_, op0=ALU.mult, op1=ALU.add)),
                         reads=[xk, xk + "h", "vec", kx], writes=[kx])
                S.op("pool", (lambda e, xp=xp, n=n: e.tensor_copy(out=hist[:, n, :], in_=xp[:, 512:516])), reads=[xk], writes=[("hist", n)])
                S.op("act", (lambda e, x_=x_, bi=bi: e.copy(out=xrb[bi], in_=x_)), reads=[kx], writes=[f"xrb{bi}"])
                pg, pgk = self.gps(parts=BS)
                S.op("pe", (lambda e, n=n, pg=pg, bi=bi: e.matmul(pg, lhsT=gw[:, n, :], rhs=xrb[bi], start=True, stop=True)), reads=["gw", f"xrb{bi}"], writes=[pgk])
                S.op("act", (lambda e, pg=pg, bi=bi, n=n: e.activation(out=rg[bi], in_=pg, func=AF.Sigmoid, bias=vec[:, n, 5:6], scale=1.0)), reads=[pgk, "vec"], writes=[f"rg{bi}"])
                pg2, pgk2 = self.gps(parts=BS)
                S.op("pe", (lambda e, n=n, pg2=pg2, bi=bi: e.matmul(pg2, lhsT=gw[:, NBLK + n, :], rhs=xrb[bi], start=True, stop=True)), reads=["gw", f"xrb{bi}"], writes=[pgk2])
                S.op("act", (lambda e, pg2=pg2, bi=bi, n=n: e.activation(out=ig[bi], in_=pg2, func=AF.Sigmoid, bias=vec[:, n, 6:7], scale=1.0)), reads=[pgk2, "vec"], writes=[f"ig{bi}"])
                S.op("act", (lambda e, bi=bi, n=n: e.activation(out=aa[bi], in_=rg[bi], func=AF.Exp, scale=c8[:, n:n + 1])), reads=[f"rg{bi}", "c8"], writes=[f"aa{bi}"])
                S.op("act", (lambda e, bi=bi: e.activation(out=sq[bi], in_=aa[bi], func=AF.Square)), reads=[f"aa{bi}"], writes=[f"sq{bi}"])
                S.op("act", (lambda e, bi=bi: e.activation(out=sq[bi], in_=sq[bi], func=AF.Sqrt, bias=1.0, scale=-1.0)), reads=[f"sq{bi}"], writes=[f"sq{bi}"])
                S.op("pool", (lambda e, bi=bi: e.tensor_tensor(out=ig[bi], in0=ig[bi], in1=xr[bi], op=ALU.mult)), reads=[f"ig{bi}", kx], writes=[f"ig{bi}"])
                S.op("dve", (lambda e, bi=bi: e.tensor_tensor(out=ig[bi], in0=ig[bi], in1=sq[bi], op=ALU.mult)), reads=[f"ig{bi}", f"sq{bi}"], writes=[f"ig{bi}"])
                S.op("dve", (lambda e, bi=bi, n=n: e.tensor_tensor_scan(out=hs[bi], data0=aa[bi], data1=ig[bi], initial=carry[:, n:n + 1], op0=ALU.mult, op1=ALU.add)),
                     reads=[f"aa{bi}", f"ig{bi}", ("carry", n)], writes=[f"hs{bi}"])
                S.op("act", (lambda e, bi=bi, n=n: e.copy(out=carry[:, n:n + 1], in_=hs[bi][:, 511:512])), reads=[f"hs{bi}"], writes=[("carry", n)])
                S.op("dve", (lambda e, bi=bi, n=n: e.tensor_tensor(out=mT[:, n, :], in0=hs[bi], in1=gb[bi], op=ALU.mult)), reads=[f"hs{bi}", f"gb{bi}"], writes=[("mT", n)])
            self.out_proj(seq, t, lambda k, sub: mT[:, k, sub * 128:(sub + 1) * 128], [("mT", n) for n in range(NBLK)], wout, "wout", NBLK, first=first)


    def stage_rwkv(self, seq, layer, j, first):
        S, A, PS = self.S, self.A, self.PS
        self.stage_begin()
        self.load_ln(layer * 3 + 0)
        import os
        RW = BF16 if os.environ.get('RW_F32', '0') != '1' else F32
        C0 = float(np.exp(-0.5))
        wr = A.alloc([8, D], BF16)
        wk = A.alloc([8, D], BF16)
        wv = A.alloc([8, D], BF16)
        wo = A.alloc([8, D], BF16)
        w1b = A.alloc([8, 64], BF16)
        a1b = A.alloc([8, 64], BF16)
        g1b = A.alloc([8, 128], BF16)
        w2b = A.alloc([D], BF16, parts=64)
        a2b = A.alloc([D], BF16, parts=64)
        g2b = A.alloc([D], BF16)
        lxg = A.alloc([D], F32)
        lxb = A.alloc([D], F32)
        vec = A.alloc([8, 16], F32)
        omu = A.alloc([8, 6], F32)
        rwc = A.alloc([648], F32)
        mask1 = rwc[:, 0:256]
        masksl = rwc[:, 256:384]
        blk = rwc[:, 384:512]
        rmask = rwc[:, 512:640]
        ind2 = rwc[:, 640:642]
        gneps = rwc[:, 642:643]
        xs = [A.alloc([8, 128], BF16) for _ in range(2)]
        xprev = A.alloc([8, 128], BF16)
        tmpx = A.alloc([8, 128], BF16)
        xlast = A.alloc([8], BF16)
        thw = A.alloc([128], BF16, parts=64)
        la = A.alloc([128], BF16, parts=64)
        sgl = A.alloc([128], BF16)
        V = A.alloc([D], F32)
        Y = A.alloc([D], F32)
        Hst = A.alloc([8, 2, 64], F32)
        Hm = Hst if RW == F32 else A.alloc([8, 2, 64], RW)
        Vm = V if RW == F32 else A.alloc([D], RW)
        Hd = [A.alloc([64], F32) for _ in range(2)]
        scr = A.alloc([D], F32)
        ob = A.alloc([D], BF16)
        OT = A.alloc([8, 128], BF16)
        stt = A.alloc([112], F32)
        names = ["sg", "a", "kq", "kk", "k", "r", "rn", "cum", "G", "Gi", "ex", "ab", "t1", "km", "E", "BhT", "KhT", "rkr"]
        PBs = []
        tmps = {n: A.alloc([128], F32) for n in names if n != "G"}
        for i in range(2):
            pb = dict(tmps)
            pb["G"] = A.alloc([128], F32)
            pb["ATRT"] = A.alloc([256], RW)
            pb["BT"] = A.alloc([128], RW)
            pb["KT"] = A.alloc([128], RW)
            pb["BK"] = A.alloc([256], RW)
            PBs.append(pb)
        HB = []
        for i in range(2):
            hb = {"Mk": A.alloc([256], RW), "Mb": A.alloc([256], RW), "X0": A.alloc([128], RW),
                  "XX": [A.alloc([256], RW) for _ in range(2)], "Z": [A.alloc([128], RW) for _ in range(2)],
                  "Gs": A.alloc([64], RW), "Us": A.alloc([64], RW)}
            S.op("dve", (lambda e, hb=hb: e.memset(hb["Us"], 0.0)), writes=[(f"hb{i}", "Us")])
            S.op("dve", (lambda e, hb=hb: e.memset(hb["Gs"], 0.0)), writes=[(f"hb{i}", "Gs")])
            HB.append(hb)

        self.load_w(w1b, self.P["rw_w1"][j], "w1b")
        self.load_w(a1b, self.P["rw_a1"][j], "a1b")
        self.load_w(g1b, self.P["rw_g1"][j], "g1b")
        self.load_w(wv, self.P["rw_w_v"][j], "wv")
        self.load_w(wr, self.P["rw_w_r"][j], "wr")
        self.load_w(wk, self.P["rw_w_k"][j], "wk")
        self.load_w(wo, self.P["rw_w_o"][j], "wo")
        S.dma("pool", w2b, self.P["rw_w2"][j], writes=["w2b"])
        S.dma("pool", a2b, self.P["rw_a2"][j], writes=["a2b"])
        S.dma("pool", g2b, self.P["rw_g2"][j], writes=["g2b"])
        S.dma("sp", vec, self.P["rw_vec"][j], writes=["vec"])
        S.dma("sp", rwc, self.P["rwc"], writes=["rwc"])
        S.dma("sp", lxg, self.P["rw_lnx"][j, 0, :].partition_broadcast(128), writes=["lxg"])
        S.dma("sp", lxb, self.P["rw_lnx"][j, 1, :].partition_broadcast(128), writes=["lxb"])
        S.op("dve", lambda e: e.tensor_scalar(out=omu, in0=vec[:, :, 0:6], scalar1=-1.0, scalar2=1.0, op0=ALU.mult, op1=ALU.add), reads=["vec"], writes=["omu"])
        S.op("dve", lambda e: e.memset(Hst, 0.0), writes=[("H", h) for h in range(16)])
        if RW != F32:
            S.op("dve", lambda e: e.memset(Hm, 0.0), writes=[("Hm", h) for h in range(16)])
        S.op("dve", lambda e: e.memset(xlast, 0.0), writes=["xlast"])
        self.force_pi = 0
        self.gen_list = [(0, 0), (0, 1)]
        HK = "H" if RW == F32 else "Hm"
        VK = "V" if RW == F32 else "Vm"
        coefps = PS[1][:, 512:528]
        ckey = ("PS1", 1)

        def mix(m, bi, hk, t0):
            S.op("dve", (lambda e: e.tensor_tensor(out=tmpx, in0=xprev, in1=vec[:, :, m:m + 1].to_broadcast([128, 8, 128]), op=ALU.mult)),
                 reads=["xprev", "vec"], writes=["tmpx"])
            S.op("dve", (lambda e: e.tensor_tensor(out=xs[bi], in0=self.hT[:, :, 2 + t0:2 + t0 + 128], in1=omu[:, :, m:m + 1].to_broadcast([128, 8, 128]), op=ALU.mult)),
                 reads=hk + ["omu"], writes=[f"xs{bi}"])
            S.op("pool", (lambda e: e.tensor_tensor(out=xs[bi], in0=xs[bi], in1=tmpx, op=ALU.add)), reads=[f"xs{bi}", "tmpx"], writes=[f"xs{bi}"])

        for tt in range(16):
            t0 = tt * 128
            hk = [("hT", tt)]
            S.op("pool", (lambda e, t0=t0: e.tensor_copy(out=xprev[:, :, 1:128], in_=self.hT[:, :, 2 + t0:2 + t0 + 127])), reads=hk, writes=["xprev"])
            S.op("pool", (lambda e: e.tensor_copy(out=xprev[:, :, 0:1], in_=xlast.unsqueeze(2))), reads=["xlast", "xprev"], writes=["xprev"])
            S.op("pool", (lambda e, t0=t0: e.tensor_copy(out=xlast.unsqueeze(2), in_=self.hT[:, :, 2 + t0 + 127:2 + t0 + 128])), reads=hk + ["xprev"], writes=["xlast"])
            mix(1, 0, hk, t0)
            ps, pk = self.gps(parts=64, n=128)
            for k in range(8):
                S.op("pe", (lambda e, k=k, ps=ps: e.matmul(ps, lhsT=w1b[:, k, :], rhs=xs[0][:, k, :], start=(k == 0), stop=(k == 7))), reads=["w1b", "xs0"], writes=[pk])
            S.op("act", (lambda e, ps=ps: e.activation(out=thw, in_=ps, func=AF.Tanh)), reads=[pk], writes=["thw"])
            mix(4, 1, hk, t0)
            ps, pk = self.gps(parts=64, n=128)
            for k in range(8):
                S.op("pe", (lambda e, k=k, ps=ps: e.matmul(ps, lhsT=a1b[:, k, :], rhs=xs[1][:, k, :], start=(k == 0), stop=(k == 7))), reads=["a1b", "xs1"], writes=[pk])
            S.op("act", (lambda e, ps=ps: e.copy(out=la, in_=ps)), reads=[pk], writes=["la"])
            mix(5, 0, hk, t0)
            ps, pk = self.gps(n=128)
            for k in range(8):
                S.op("pe", (lambda e, k=k, ps=ps: e.matmul(ps, lhsT=g1b[:, k, :], rhs=xs[0][:, k, :], start=(k == 0), stop=(k == 7))), reads=["g1b", "xs0"], writes=[pk])
            S.op("act", (lambda e, ps=ps: e.activation(out=sgl, in_=ps, func=AF.Sigmoid)), reads=[pk], writes=["sgl"])
            mix(3, 1, hk, t0)
            for half in range(2):
                ps, pk = self.gps()
                for k in range(8):
                    S.op("pe", (lambda e, k=k, ps=ps, half=half: e.matmul(ps, lhsT=xs[1][:, k, :], rhs=wv[:, k, half * 512:(half + 1) * 512], start=(k == 0), stop=(k == 7))), reads=["wv", "xs1"], writes=[pk])
                S.op("act", (lambda e, ps=ps, half=half: e.copy(out=V[:, half * 512:(half + 1) * 512], in_=ps)), reads=[pk], writes=[("V", half)])
                if RW != F32:
                    S.op("pool", (lambda e, half=half: e.tensor_copy(out=Vm[:, half * 512:(half + 1) * 512], in_=V[:, half * 512:(half + 1) * 512])), reads=[("V", half)], writes=[("Vm", half)])
            mix(0, 0, hk, t0)
            mix(2, 1, hk, t0)
            import os
            RWD = int(os.environ.get("RW_DBG", "9"))
            if RWD < 9:
                S.op("dve", (lambda e: e.memset(Y, 0.0)), reads=[("Ydone", 0), ("Ydone", 1)], writes=[("Y", h // 8, q, h) for h in range(16) for q in range(2)])
                S.op("pe", (lambda e: e.matmul(coefps, lhsT=sgl, rhs=g2b[:, 0:16], start=True, stop=True)), reads=["sgl", "g2b"], writes=[ckey])
            RWS = int(os.environ.get("RW_SUB", "999"))
            for c in range(8 if RWD >= 2 else 0):
                if RWS < 999:
                    class _F:
                        def __init__(s_, S0):
                            s_.S0, s_.n = S0, 0
                        def op(s_, *a, **k):
                            s_.n += 1
                            if s_.n <= RWS:
                                return s_.S0.op(*a, **k)
                        def __getattr__(s_, nm):
                            return getattr(s_.S0, nm)
                    S = _F(self.S)
                pb = PBs[c % 2]
                pn = f"pb{c % 2}"
                K_ = lambda n, pn=pn: (pn, n) if n in ("G", "AT", "RT", "BT", "KT", "BK") else ("pbt", n)
                ps, pk = self.gps()
                for k in range(8):
                    S.op("pe", (lambda e, k=k, ps=ps, c=c: e.matmul(ps[:, 0:128], lhsT=wr[:, k, c * 128:(c + 1) * 128], rhs=xs[0][:, k, :], start=(k == 0), stop=(k == 7))), reads=["wr", "xs0"], writes=[pk])
                for k in range(8):
                    S.op("pe", (lambda e, k=k, ps=ps, c=c: e.matmul(ps[:, 128:256], lhsT=wk[:, k, c * 128:(c + 1) * 128], rhs=xs[1][:, k, :], start=(k == 0), stop=(k == 7))), reads=["wk", "xs1"], writes=[pk])
                S.op("pe", (lambda e, ps=ps, c=c: e.matmul(ps[:, 256:384], lhsT=w2b[:, c * 128:(c + 1) * 128], rhs=thw, start=True, stop=True)), reads=["w2b", "thw"], writes=[pk])
                S.op("pe", (lambda e, ps=ps, c=c: e.matmul(ps[:, 384:512], lhsT=a2b[:, c * 128:(c + 1) * 128], rhs=la, start=True, stop=True)), reads=["a2b", "la"], writes=[pk])
                vc = lambda i, c=c: vec[:, c, i:i + 1]
                S.op("act", (lambda e, ps=ps, pb=pb, vc=vc: e.activation(out=pb["sg"], in_=ps[:, 256:384], func=AF.Sigmoid, bias=vc(6), scale=1.0)), reads=[pk, "vec"], writes=[K_("sg")])
                S.op("act", (lambda e, ps=ps, pb=pb, vc=vc: e.activation(out=pb["a"], in_=ps[:, 384:512], func=AF.Sigmoid, bias=vc(7), scale=1.0)), reads=[pk, "vec"], writes=[K_("a")])
                S.op("act", (lambda e, ps=ps, pb=pb: e.copy(out=pb["k"], in_=ps[:, 128:256])), reads=[pk], writes=[K_("k")])
                S.op("dve", (lambda e, pb=pb, vc=vc: e.tensor_scalar(out=pb["kk"], in0=pb["k"], scalar1=vc(8), scalar2=None, op0=ALU.mult)), reads=[K_("k"), "vec"], writes=[K_("kk")])
                S.op("act", (lambda e, pb=pb: e.activation(out=pb["kq"], in_=pb["kk"], func=AF.Square)), reads=[K_("kk")], writes=[K_("kq")])
                S.op("dve", (lambda e, ps=ps, pb=pb: e.tensor_copy(out=pb["r"], in_=ps[:, 0:128])), reads=[pk], writes=[K_("r")])
                ps2, pk2 = self.gps(n=128)
                S.op("pe", (lambda e, ps2=ps2, pb=pb: e.matmul(ps2, lhsT=blk, rhs=pb["kq"], start=True, stop=True)), reads=["rwc", K_("kq")], writes=[pk2])
                S.op("act", (lambda e, ps2=ps2, pb=pb: e.activation(out=pb["rn"], in_=ps2, func=AF.Sqrt)), reads=[pk2], writes=[K_("rn")])
                S.op("dve", (lambda e, pb=pb: e.tensor_scalar(out=pb["rn"], in0=pb["rn"], scalar1=1e-12, scalar2=None, op0=ALU.max)), reads=[K_("rn")], writes=[K_("rn")])
                S.op("dve", (lambda e, pb=pb: e.reciprocal(out=pb["rn"], in_=pb["rn"])), reads=[K_("rn")], writes=[K_("rn")])
                S.op("dve", (lambda e, pb=pb: e.tensor_tensor(out=pb["kk"], in0=pb["kk"], in1=pb["rn"], op=ALU.mult)), reads=[K_("kk"), K_("rn")], writes=[K_("kk")])
                S.op("dve", (lambda e, pb=pb: e.tensor_tensor_scan(out=pb["cum"], data0=rmask, data1=pb["sg"], initial=0.0, op0=ALU.mult, op1=ALU.add)), reads=["rwc", K_("sg")], writes=[K_("cum")])
                S.op("act", (lambda e, pb=pb: e.activation(out=pb["G"], in_=pb["cum"], func=AF.Exp, scale=-C0)), reads=[K_("cum")], writes=[K_("G")])
                S.op("act", (lambda e, pb=pb: e.activation(out=pb["Gi"], in_=pb["cum"], func=AF.Exp, scale=C0)), reads=[K_("cum")], writes=[K_("Gi")])
                S.op("dve", (lambda e, pb=pb: e.tensor_tensor(out=pb["ex"], in0=pb["cum"], in1=pb["sg"], op=ALU.subtract)), reads=[K_("cum"), K_("sg")], writes=[K_("ex")])
                S.op("act", (lambda e, pb=pb: e.activation(out=pb["ex"], in_=pb["ex"], func=AF.Exp, scale=-C0)), reads=[K_("ex")], writes=[K_("ex")])
                S.op("dve", (lambda e, pb=pb: e.scalar_tensor_tensor(out=pb["ATRT"][:, 0:128], in0=pb["kk"], scalar=-1.0, in1=pb["ex"], op0=ALU.mult, op1=ALU.mult)), reads=[K_("kk"), K_("ex")], writes=[K_("AT")])
                S.op("dve", (lambda e, pb=pb: e.tensor_tensor(out=pb["ab"], in0=pb["kk"], in1=pb["a"], op=ALU.mult)), reads=[K_("kk"), K_("a")], writes=[K_("ab")])
                S.op("dve", (lambda e, pb=pb: e.tensor_tensor(out=pb["BT"], in0=pb["ab"], in1=pb["Gi"], op=ALU.mult)), reads=[K_("ab"), K_("Gi")], writes=[K_("BT")])
                S.op("dve", (lambda e, pb=pb, vc=vc: e.tensor_scalar(out=pb["t1"], in0=pb["a"], scalar1=-1.0, scalar2=vc(9), op0=ALU.add, op1=ALU.mult)), reads=[K_("a"), "vec"], writes=[K_("t1")])
                S.op("dve", (lambda e, pb=pb: e.scalar_tensor_tensor(out=pb["km"], in0=pb["t1"], scalar=1.0, in1=pb["k"], op0=ALU.add, op1=ALU.mult)), reads=[K_("t1"), K_("k")], writes=[K_("km")])
                S.op("dve", (lambda e, pb=pb: e.tensor_tensor(out=pb["KT"], in0=pb["km"], in1=pb["Gi"], op=ALU.mult)), reads=[K_("km"), K_("Gi")], writes=[K_("KT")])
                S.op("dve", (lambda e, pb=pb: e.tensor_tensor(out=pb["ATRT"][:, 128:256], in0=pb["r"], in1=pb["G"], op=ALU.mult)), reads=[K_("r"), K_("G")], writes=[K_("RT")])
                for q in range(2):
                    S.op("dve", (lambda e, pb=pb, q=q: e.tensor_scalar(out=pb["E"][:, q * 64:(q + 1) * 64], in0=pb["cum"][:, q * 64:(q + 1) * 64], scalar1=pb["cum"][:, q * 64 + 63:q * 64 + 64], scalar2=None, op0=ALU.subtract)),
                         reads=[K_("cum")], writes=[K_("E")])
                S.op("act", (lambda e, pb=pb: e.activation(out=pb["E"], in_=pb["E"], func=AF.Exp, scale=C0)), reads=[K_("E")], writes=[K_("E")])
                S.op("dve", (lambda e, pb=pb: e.tensor_tensor(out=pb["BhT"], in0=pb["ab"], in1=pb["E"], op=ALU.mult)), reads=[K_("ab"), K_("E")], writes=[K_("BhT")])
                S.op("dve", (lambda e, pb=pb: e.tensor_tensor(out=pb["KhT"], in0=pb["km"], in1=pb["E"], op=ALU.mult)), reads=[K_("km"), K_("E")], writes=[K_("KhT")])
                ps3, pk3 = self.gps(n=256)
                S.op("pe", (lambda e, ps3=ps3, pb=pb: e.transpose(ps3[:, 0:128], pb["BhT"], self.identf[:])), reads=[K_("BhT"), "identf"], writes=[pk3])
                S.op("pe", (lambda e, ps3=ps3, pb=pb: e.transpose(ps3[:, 128:256], pb["KhT"], self.identf[:])), reads=[K_("KhT"), "identf"], writes=[pk3])
                S.op("act", (lambda e, ps3=ps3, pb=pb: e.copy(out=pb["BK"], in_=ps3)), reads=[pk3], writes=[K_("BK")])
                S.op("dve", (lambda e, pb=pb, vc=vc: e.scalar_tensor_tensor(out=pb["rkr"], in0=pb["km"], scalar=vc(10), in1=pb["r"], op0=ALU.mult, op1=ALU.mult)), reads=[K_("km"), K_("r"), "vec"], writes=[K_("rkr")])
                S.op("pe", (lambda e, pb=pb, c=c: e.matmul(coefps[:, 2 * c:2 * c + 2], lhsT=pb["rkr"], rhs=ind2, start=True, stop=True)), reads=[K_("rkr"), "rwc"], writes=[ckey])
                AT = pb["ATRT"][:, 0:128]
                RT = pb["ATRT"][:, 128:256]
                S = self.S
                NH = 2 if RWD >= 3 else 0
                st_ = []
                for hh in range(NH):
                    po = hh * 64
                    hb = HB[hh]
                    hn = f"hb{hh}"
                    pg = PS[2][:, hh * 512:(hh + 1) * 512]
                    pgk = ("PS2", hh)
                    S.op("pe", (lambda e, pb=pb, po=po, pg=pg: e.matmul(pg[:, 0:256], lhsT=pb["KT"][po:po + 64, :], rhs=pb["ATRT"][po:po + 64, :], start=True, stop=True)),
                         reads=[K_("KT"), K_("AT"), K_("RT")], writes=[pgk])
                    S.op("pe", (lambda e, pb=pb, po=po, pg=pg: e.matmul(pg[:, 256:512], lhsT=pb["BT"][po:po + 64, :], rhs=pb["ATRT"][po:po + 64, :], start=True, stop=True)),
                         reads=[K_("BT"), K_("AT"), K_("RT")], writes=[pgk])
                    S.op("dve", (lambda e, hb=hb, pg=pg: e.tensor_tensor(out=hb["Mk"], in0=pg[:, 0:256], in1=mask1, op=ALU.mult)), reads=[pgk, "rwc"], writes=[(hn, "Mk")])
                    S.op("dve", (lambda e, hb=hb, pg=pg: e.tensor_tensor(out=hb["Mb"], in0=pg[:, 256:512], in1=mask1, op=ALU.mult)), reads=[pgk, "rwc"], writes=[(hn, "Mb")])
                    S.op("pe", (lambda e, pb=pb, po=po, pg=pg: e.matmul(pg[:, 0:128], lhsT=pb["ATRT"][po:po + 64, 0:128], rhs=pb["BT"][po:po + 64, :], start=True, stop=True)),
                         reads=[K_("BT"), K_("AT")], writes=[pgk])
                    S.op("dve", (lambda e, hb=hb, pg=pg: e.tensor_tensor(out=hb["X0"], in0=pg[:, 0:128], in1=masksl, op=ALU.mult)), reads=[pgk, "rwc"], writes=[(hn, "X0")])
                    S.op("dve", (lambda e, hb=hb: e.tensor_tensor(out=hb["Z"][0], in0=hb["Mb"][:, 0:128], in1=self.identf[:], op=ALU.add)), reads=[(hn, "Mb"), "identf"], writes=[(hn, "Z0")])
                    st_.append({"XT": hb["Mb"][:, 0:128], "X": hb["X0"], "keys": [(hn, "Mb"), (hn, "X0")], "zi": 0})
                for lvl in range(5):
                    for hh in range(NH):
                        hb = HB[hh]
                        hn = f"hb{hh}"
                        bank = PS[2][:, hh * 512:(hh + 1) * 512]
                        bkey = ("PS2", hh)
                        sd = st_[hh]
                        X_ap, XT_ap, xk_keys, zi = sd["X"], sd["XT"], sd["keys"], sd["zi"]
                        xx = hb["XX"][lvl % 2]
                        xxk = (hn, f"XX{lvl % 2}")
                        if lvl < 4:
                            S.op("pe", (lambda e, bank=bank, X_ap=X_ap, XT_ap=XT_ap: e.matmul(bank[:, 0:128], lhsT=X_ap, rhs=XT_ap, start=True, stop=True)), reads=xk_keys, writes=[bkey])
                        S.op("pe", (lambda e, bank=bank, X_ap=X_ap, XT_ap=XT_ap: e.matmul(bank[:, 128:256], lhsT=XT_ap, rhs=X_ap, start=True, stop=True)), reads=xk_keys, writes=[bkey])
                        if lvl < 4:
                            S.op("act", (lambda e, bank=bank, xx=xx: e.copy(out=xx, in_=bank[:, 0:256])), reads=[bkey], writes=[xxk])
                        else:
                            S.op("act", (lambda e, bank=bank, xx=xx: e.copy(out=xx[:, 128:256], in_=bank[:, 128:256])), reads=[bkey], writes=[xxk])
                        XT_ap, X_ap = xx[:, 0:128], xx[:, 128:256]
                        zo = hb["Z"][zi]
                        zn = hb["Z"][1 - zi]
                        S.op("pe", (lambda e, bank=bank, X_ap=X_ap, zo=zo: e.matmul(bank[:, 256:384], lhsT=X_ap, rhs=zo, start=True, stop=True)), reads=[xxk, (hn, f"Z{zi}")], writes=[bkey])
                        S.op("dve", (lambda e, bank=bank, zo=zo, zn=zn: e.tensor_tensor(out=zn, in0=bank[:, 256:384], in1=zo, op=ALU.add)), reads=[bkey, (hn, f"Z{zi}")], writes=[(hn, f"Z{1 - zi}")])
                        sd["X"], sd["XT"], sd["keys"], sd["zi"] = X_ap, XT_ap, [xxk], 1 - zi
                for hh in range(NH):
                    HB[hh]["TT"] = HB[hh]["Z"][st_[hh]["zi"]]
                    HB[hh]["TTk"] = (f"hb{hh}", f"Z{st_[hh]['zi']}")
                for q in range(2 if RWD >= 4 else 0):
                    ph = q * 64
                    for hh in range(2):
                        po, h, hb, hn = hh * 64, 2 * c + hh, HB[hh], f"hb{hh}"
                        pq = PS[3][:, hh * 512:(hh + 1) * 512]
                        pqk = ("PS3", hh)
                        S.op("pe", (lambda e, pq=pq, hh=hh, c=c, pb=pb: e.matmul(pq[:, 0:64], lhsT=pb["ATRT"][:, 0:128], rhs=Hm[:, c, hh, :], start=True, stop=False)),
                             reads=[K_("AT"), (HK, h)], writes=[pqk])
                        S.op("pe", (lambda e, pq=pq, hb=hb, h=h: e.matmul(pq[:, 0:64], lhsT=hb["Mk"][:, 0:128], rhs=Vm[:, h * 64:(h + 1) * 64], start=False, stop=True)),
                             reads=[(hn, "Mk"), (VK, h // 8)], writes=[pqk])
                        S.op("act", (lambda e, pq=pq, ph=ph, hb=hb: e.copy(out=hb["Gs"][ph:ph + 64, :], in_=pq[ph:ph + 64, 0:64])), reads=[pqk], writes=[(hn, "Gs")])
                    for hh in range(2):
                        po, h, hb, hn = hh * 64, 2 * c + hh, HB[hh], f"hb{hh}"
                        pq = PS[3][:, hh * 512:(hh + 1) * 512]
                        pqk = ("PS3", hh)
                        S.op("pe", (lambda e, pq=pq, ph=ph, hb=hb: e.matmul(pq[:, 64:128], lhsT=hb["TT"][ph:ph + 64, :], rhs=hb["Gs"][ph:ph + 64, :], start=True, stop=True)),
                             reads=[hb["TTk"], (hn, "Gs")], writes=[pqk])
                        S.op("dve", (lambda e, pq=pq, ph=ph, hb=hb: e.tensor_copy(out=hb["Us"][ph:ph + 64, :], in_=pq[ph:ph + 64, 64:128])), reads=[pqk], writes=[(hn, "Us")])
                    for hh in range(2):
                        po, h, hb, hn = hh * 64, 2 * c + hh, HB[hh], f"hb{hh}"
                        pq = PS[3][:, hh * 512:(hh + 1) * 512]
                        pqk = ("PS3", hh)
                        vh = Vm[ph:ph + 64, h * 64:(h + 1) * 64]
                        S.op("pe", (lambda e, pq=pq, hh=hh, c=c, pb=pb: e.matmul(pq[:, 128:192], lhsT=pb["ATRT"][:, 128:256], rhs=Hm[:, c, hh, :], start=True, stop=False)),
                             reads=[K_("RT"), (HK, h)], writes=[pqk])
                        S.op("pe", (lambda e, pq=pq, hb=hb: e.matmul(pq[:, 128:192], lhsT=hb["Mb"][:, 128:256], rhs=hb["Us"][:, :], start=False, stop=False)),
                             reads=[(hn, "Mb"), (hn, "Us")], writes=[pqk])
                        S.op("pe", (lambda e, pq=pq, hb=hb, h=h: e.matmul(pq[:, 128:192], lhsT=hb["Mk"][:, 128:256], rhs=Vm[:, h * 64:(h + 1) * 64], start=False, stop=True)),
                             reads=[(hn, "Mk"), (VK, h // 8)], writes=[pqk])
                        S.op("pe", (lambda e, pq=pq, ph=ph, hb=hb, pb=pb: e.matmul(pq[:, 192:256], lhsT=pb["BK"][ph:ph + 64, 0:128], rhs=hb["Us"][ph:ph + 64, :], start=True, stop=False)),
                             reads=[K_("BK"), (hn, "Us")], writes=[pqk])
                        S.op("pe", (lambda e, pq=pq, ph=ph, pb=pb, vh=vh: e.matmul(pq[:, 192:256], lhsT=pb["BK"][ph:ph + 64, 128:256], rhs=vh, start=False, stop=True)),
                             reads=[K_("BK"), (VK, h // 8)], writes=[pqk])
                        S.op("act", (lambda e, pq=pq, ph=ph, h=h: e.copy(out=Y[ph:ph + 64, h * 64:(h + 1) * 64], in_=pq[ph:ph + 64, 128:192])), reads=[pqk, ("Ydone", 0), ("Ydone", 1)], writes=[("Y", h // 8, q, h)])
                        S.op("act", (lambda e, pq=pq, po=po, hh=hh: e.copy(out=Hd[hh][po:po + 64, :], in_=pq[po:po + 64, 192:256])), reads=[pqk], writes=[f"Hd{hh}"])
                        S.op("dve", (lambda e, po=po, c=c, pb=pb, q=q, hh=hh: e.scalar_tensor_tensor(out=Hst[po:po + 64, c, hh, :], in0=Hst[po:po + 64, c, hh, :], scalar=pb["G"][po:po + 64, q * 64 + 63:q * 64 + 64], in1=Hd[hh][po:po + 64, :], op0=ALU.mult, op1=ALU.add)),
                             reads=[f"Hd{hh}", ("H", h), K_("G")], writes=[("H", h)])
                        if RW != F32:
                            S.op("act", (lambda e, po=po, c=c, hh=hh: e.copy(out=Hm[po:po + 64, c, hh, :], in_=Hst[po:po + 64, c, hh, :])), reads=[("H", h)], writes=[("Hm", h)])
            ykeys = [("Y", h // 8, q, h) for h in range(16) for q in range(2)]
            Y3 = Y.rearrange("p (h d) -> p h d", h=16)
            bc = lambda ap: ap.unsqueeze(2).to_broadcast([128, 16, 64])
            S.op("act", (lambda e: e.copy(out=stt[:, 96:112], in_=coefps)), reads=[ckey], writes=["coef"])
            S.op("dve", (lambda e: e.tensor_reduce(out=stt[:, 0:16], in_=Y3, axis=AX.X, op=ALU.add)), reads=ykeys, writes=["st_s1"])
            S.op("act", (lambda e: e.activation(out=scr, in_=Y, func=AF.Square)), reads=ykeys, writes=["scr"])
            S.op("dve", (lambda e: e.tensor_reduce(out=stt[:, 16:32], in_=scr.rearrange("p (h d) -> p h d", h=16), axis=AX.X, op=ALU.add)), reads=["scr"], writes=["st_s2"])
            S.op("dve", (lambda e: e.tensor_scalar(out=stt[:, 32:48], in0=stt[:, 0:16], scalar1=1.0 / 64.0, scalar2=None, op0=ALU.mult)), reads=["st_s1"], writes=["st_mean"])
            S.op("dve", (lambda e: e.tensor_tensor(out=stt[:, 48:64], in0=stt[:, 32:48], in1=stt[:, 32:48], op=ALU.mult)), reads=["st_mean"], writes=["st_msq"])
            S.op("dve", (lambda e: e.scalar_tensor_tensor(out=stt[:, 64:80], in0=stt[:, 16:32], scalar=1.0 / 64.0, in1=stt[:, 48:64], op0=ALU.mult, op1=ALU.subtract)), reads=["st_s2", "st_msq"], writes=["st_var"])
            S.op("act", (lambda e: e.activation(out=stt[:, 80:96], in_=stt[:, 64:80], func=AF.Sqrt, bias=gneps, scale=1.0)), reads=["st_var", "rwc"], writes=["st_sd"])
            S.op("dve", (lambda e: e.reciprocal(out=stt[:, 80:96], in_=stt[:, 80:96])), reads=["st_sd"], writes=["st_rstd"])
            S.op("dve", (lambda e: e.tensor_tensor(out=Y3, in0=Y3, in1=bc(stt[:, 32:48]), op=ALU.subtract)), reads=ykeys + ["st_mean", "scr"], writes=["Yn"])
            S.op("dve", (lambda e: e.tensor_tensor(out=Y3, in0=Y3, in1=bc(stt[:, 80:96]), op=ALU.mult)), reads=["Yn", "st_rstd"], writes=["Yn"])
            S.op("pool", (lambda e: e.tensor_tensor(out=Y, in0=Y, in1=lxg, op=ALU.mult)), reads=["Yn", "lxg"], writes=["Yn"])
            S.op("dve", (lambda e: e.tensor_tensor(out=Y, in0=Y, in1=lxb, op=ALU.add)), reads=["Yn", "lxb"], writes=["Yn"])
            S.op("dve", (lambda e: e.tensor_tensor(out=scr.rearrange("p (h d) -> p h d", h=16), in0=V.rearrange("p (h d) -> p h d", h=16), in1=bc(stt[:, 96:112]), op=ALU.mult)),
                 reads=[("V", 0), ("V", 1), "coef", "st_s2"], writes=["scr"])
            S.op("dve", (lambda e: e.tensor_tensor(out=Y, in0=Y, in1=scr, op=ALU.add)), reads=["Yn", "scr"], writes=["Yn"])
            for half in range(2):
                ps, pk = self.gps()
                S.op("pe", (lambda e, ps=ps, half=half: e.matmul(ps, lhsT=sgl, rhs=g2b[:, half * 512:(half + 1) * 512], start=True, stop=True)), reads=["sgl", "g2b"], writes=[pk])
                S.op("dve", (lambda e, ps=ps, half=half: e.tensor_tensor(out=ob[:, half * 512:(half + 1) * 512], in0=Y[:, half * 512:(half + 1) * 512], in1=ps, op=ALU.mult)), reads=[pk, "Yn"], writes=[("ob", half), ("Ydone", half)])
            ps, pk = self.gps()
            pst = ps.bitcast(BF16)
            for cc in range(8):
                S.op("pe", (lambda e, cc=cc, pst=pst: e.transpose(pst[:, cc * 128:(cc + 1) * 128], ob[:, cc * 128:(cc + 1) * 128], self.ident[:])), reads=[("ob", cc // 4), "ident"], writes=[pk])
            S.op("act", (lambda e, pst=pst: e.copy(out=OT, in_=pst.rearrange("p (c t) -> p c t", c=8))), reads=[pk], writes=["OT"])
            py = PS[0]
            for half in range(2):
                for k in range(8):
                    S.op("pe", (lambda e, k=k, half=half: e.matmul(py[:, half * 512:(half + 1) * 512], lhsT=OT[:, k, :], rhs=wo[:, k, half * 512:(half + 1) * 512], start=(k == 0), stop=(k == 7))),
                         reads=["OT", "wo"], writes=[("PS0", half)])
            self.ln_finish(seq, tt, py[:, :], [("PS0", 0), ("PS0", 1)], first=first, last=False)
        self.force_pi = None
        self.gen_list = [(0, 0), (0, 1)]


    def stage_attn(self, seq, layer, j, first):
        S, A, PS = self.S, self.A, self.PS
        self.stage_begin()
        self.load_ln(layer * 3 + 0)
        wq = A.alloc([8, D], BF16)
        wk = A.alloc([8, D], BF16)
        wv = A.alloc([8, D], BF16)
        wo = A.alloc([8, D], BF16)
        biasb = A.alloc([16, 640], BF16)
        KT = A.alloc([8, 1024], BF16)
        V = A.alloc([8, D], BF16)
        QT = A.alloc([8, 512], BF16)
        OT = A.alloc([8, 512], BF16)
        Pb = [A.alloc([640], BF16) for _ in range(2)]
        PTs = [A.alloc([5, 128], BF16) for _ in range(2)]
        Ob = [A.alloc([D], BF16) for _ in range(2)]
        sm = [A.alloc([64], F32) for _ in range(2)]
        wqkv = self.P["ca_w_qkv"][j]
        self.load_w(wq, wqkv[:, 0:D], "wq")
        self.load_w(wk, wqkv[:, D:2 * D], "wk")
        self.load_w(wv, wqkv[:, 2 * D:3 * D], "wv")
        self.load_w(wo, self.P["ca_w_o"][j], "wo")
        S.dma("pool", biasb, self.P["ca_bias"], writes=["biasb"])
        nh = 0
        for t in range(4):
            hk = self.hTkeys(t * 512, 512)
            r0 = (t % 2) * 512
            for oc in range(8):
                ps, pk = self.gps()
                for k in range(8):
                    S.op("pe", (lambda e, k=k, oc=oc, ps=ps, t=t: e.matmul(ps, lhsT=wq[:, k, oc * 128:(oc + 1) * 128], rhs=self.hTs(k, t * 512, 512), start=(k == 0), stop=(k == 7))),
                         reads=["wq"] + hk, writes=[pk])
                S.op("act", (lambda e, oc=oc, ps=ps: e.activation(out=QT[:, oc, :], in_=ps, func=AF.Copy, scale=0.125)), reads=[pk], writes=[("QT", oc)])
                ps, pk = self.gps()
                for k in range(8):
                    S.op("pe", (lambda e, k=k, oc=oc, ps=ps, t=t: e.matmul(ps, lhsT=wk[:, k, oc * 128:(oc + 1) * 128], rhs=self.hTs(k, t * 512, 512), start=(k == 0), stop=(k == 7))),
                         reads=["wk"] + hk, writes=[pk])
                S.op("dve", (lambda e, oc=oc, ps=ps, r0=r0: e.tensor_copy(out=KT[:, oc, r0:r0 + 512], in_=ps)), reads=[pk], writes=[("KT", oc, t % 2)])
            for sub in range(4):
                slot = (4 * t + sub) % 8
                for half in range(2):
                    ps, pk = self.gps()
                    for k in range(8):
                        S.op("pe", (lambda e, k=k, half=half, ps=ps, t=t, sub=sub: e.matmul(ps, lhsT=self.hTs(k, t * 512 + sub * 128, 128), rhs=wv[:, k, half * 512:(half + 1) * 512], start=(k == 0), stop=(k == 7))),
                             reads=["wv"] + hk, writes=[pk])
                    S.op("act", (lambda e, half=half, ps=ps, slot=slot: e.copy(out=V[:, slot, half * 512:(half + 1) * 512], in_=ps)), reads=[pk], writes=[("V", slot, half)])
            for sub in range(4):
                qb = 4 * t + sub
                kbs = list(range(max(0, qb - 4), qb + 1))
                j0 = kbs[0] - (qb - 4)
                c0 = j0 * 128
                oi = qb % 2
                po_ps = PS[2]
                smi = sm[oi]
                ksm = f"sm{oi}"
                def emit_A(h, nh):
                        c, po = h // 2, (h % 2) * 64
                        si = nh % 2
                        sps = PS[0] if si == 0 else PS[3]
                        spn = "PS0" if si == 0 else "PS3"
                        for kb in kbs:
                            jj = kb - (qb - 4)
                            slot = kb % 8
                            S.op("pe", (lambda e, jj=jj, slot=slot, c=c, po=po, sps=sps, sub=sub: e.matmul(sps[:, jj * 128:(jj + 1) * 128], lhsT=QT[po:po + 64, c, sub * 128:(sub + 1) * 128], rhs=KT[po:po + 64, c, slot * 128:(slot + 1) * 128], start=True, stop=False)),
                                 reads=[("QT", c), ("KT", c, slot // 4)], writes=[(spn, jj // 4)])
                            S.op("pe", (lambda e, jj=jj, h=h, sps=sps: e.matmul(sps[:, jj * 128:(jj + 1) * 128], lhsT=self.ident[:], rhs=biasb[:, h, jj * 128:(jj + 1) * 128], start=False, stop=True)),
                                 reads=["biasb", "ident"], writes=[(spn, jj // 4)])
                        skeys = [(spn, 0), (spn, 1)]
                        S.op("dve", (lambda e, sps=sps, smi=smi, h=h, c0=c0: e.tensor_reduce(out=smi[:, 32 + h:33 + h], in_=sps[:, c0:640], axis=AX.X, op=ALU.max, negate=True)),
                             reads=skeys, writes=[(ksm + "m", h)])
                        pb_ = Pb[si]
                        S.op("act", (lambda e, sps=sps, smi=smi, h=h, c0=c0, pb_=pb_: e.activation(out=pb_[:, c0:640], in_=sps[:, c0:640], func=AF.Exp, bias=smi[:, 32 + h:33 + h], scale=1.0, accum_out=smi[:, h:h + 1])),
                             reads=skeys + [(ksm + "m", h)], writes=[f"Pb{si}", (ksm + "s", h)])

                def emit_B(h, nh):
                        c, po = h // 2, (h % 2) * 64
                        si = nh % 2
                        pb_ = Pb[si]
                        ptp = PS[1][:, 0:512].bitcast(BF16)
                        for kb in kbs:
                            jj = kb - (qb - 4)
                            S.op("pe", (lambda e, jj=jj, ptp=ptp, pb_=pb_: e.transpose(ptp[:, jj * 128:(jj + 1) * 128], pb_[:, jj * 128:(jj + 1) * 128], self.ident[:])),
                                 reads=[f"Pb{si}", "ident"], writes=[("PS1", 0)])
                        pts = PTs[si]
                        S.op("dve", (lambda e, ptp=ptp, pts=pts, j0=j0: e.tensor_copy(out=pts[:, j0:5, :], in_=ptp[:, j0 * 128:640].rearrange("p (a b) -> p a b", b=128))),
                             reads=[("PS1", 0)], writes=[f"PTs{si}"])
                        for kb in kbs:
                            jj = kb - (qb - 4)
                            slot = kb % 8
                            S.op("pe", (lambda e, jj=jj, slot=slot, h=h, pts=pts, po_ps=po_ps, kbs=kbs, kb=kb: e.matmul(po_ps[:, h * 64:(h + 1) * 64], lhsT=pts[:, jj, :], rhs=V[:, slot, h * 64:(h + 1) * 64], start=(kb == kbs[0]), stop=(kb == kbs[-1]))),
                                 reads=[f"PTs{si}", ("V", slot, h // 8)], writes=[("PS2", h // 8)])

                for hq in range(17):
                    if hq < 16:
                        emit_A(hq, nh + hq)
                    if hq >= 1:
                        emit_B(hq - 1, nh + hq - 1)
                nh += 16
                S.op("dve", (lambda e, smi=smi: e.reciprocal(out=smi[:, 16:32], in_=smi[:, 0:16])), reads=[(ksm + "s", h) for h in range(16)], writes=[ksm + "r"])
                ob = Ob[oi]
                S.op("dve", (lambda e, smi=smi, ob=ob, po_ps=po_ps: e.tensor_tensor(out=ob.rearrange("p (h d) -> p h d", h=16), in0=po_ps[:, :].rearrange("p (h d) -> p h d", h=16), in1=smi[:, 16:32].unsqueeze(2).to_broadcast([128, 16, 64]), op=ALU.mult)),
                     reads=[("PS2", 0), ("PS2", 1), ksm + "r"], writes=[f"Ob{oi}"])
                otp = PS[1][:, 512:1024].bitcast(BF16)
                for cc in range(8):
                    S.op("pe", (lambda e, cc=cc, otp=otp, ob=ob: e.transpose(otp[:, cc * 128:(cc + 1) * 128], ob[:, cc * 128:(cc + 1) * 128], self.ident[:])),
                         reads=[f"Ob{oi}", "ident"], writes=[("PS1", 1)])
                S.op("act", (lambda e, otp=otp, sub=sub: e.copy(out=OT[:, :, sub * 128:(sub + 1) * 128], in_=otp.rearrange("p (c t) -> p c t", c=8))),
                     reads=[("PS1", 1)], writes=[("OT", sub)])
            self.out_proj(seq, t, lambda k, sub: OT[:, k, sub * 128:(sub + 1) * 128], [("OT", s_) for s_ in range(4)], wo, "wo", 8, first=first)


def make_consts():
    c = np.zeros((128, 256), np.float32)
    c[:, 0:128] = np.eye(128, dtype=np.float32)
    c[:, 128] = LN_EPS
    return c


def make_rwc():
    c = np.zeros((128, 648), np.float32)
    s_ = np.arange(128)[:, None]
    t_ = np.arange(128)[None, :]
    same = (s_ // 64) == (t_ // 64)
    c[:, 0:128] = (same & (s_ < t_))
    c[:, 128:256] = (same & (s_ <= t_))
    c[:, 256:384] = (same & (s_ > t_))
    c[:, 384:512] = same
    c[:, 512:640] = (t_ % 64 != 0)
    c[0:64, 640] = 1.0
    c[64:128, 641] = 1.0
    c[:, 642] = 64e-5
    return c


def prep_shared(inp):
    sh = {}
    sh["ln_g"] = np.ascontiguousarray(inp["ln_g"].reshape(DEPTH * 3, D))
    sh["ln_b"] = np.ascontiguousarray(inp["ln_b"].reshape(DEPTH * 3, D))
    sh["lru_w_in"] = inp["lru_w_in"]
    na = inp["lru_w_in"].shape[0]
    vec = np.zeros((na, BS, NBLK, 8), np.float32)
    cw = inp["lru_conv_w"].reshape(na, 4, NBLK, BS)
    for k in range(4):
        vec[:, :, :, k] = cw[:, k].transpose(0, 2, 1)
    vec[:, :, :, 4] = inp["lru_conv_b"].reshape(na, NBLK, BS).transpose(0, 2, 1)
    gbv = inp["lru_gate_b"].reshape(na, 2, NBLK, BS)
    vec[:, :, :, 5] = gbv[:, 0].transpose(0, 2, 1)
    vec[:, :, :, 6] = gbv[:, 1].transpose(0, 2, 1)
    vec[:, :, :, 7] = inp["lru_lambda"].reshape(na, NBLK, BS).transpose(0, 2, 1)
    sh["lru_vec"] = vec
    sh["lru_gate_w"] = np.ascontiguousarray(inp["lru_gate_w"].transpose(0, 3, 1, 2, 4).reshape(na, BS, 2 * NBLK, BS))
    sh["lru_w_out"] = inp["lru_w_out"]
    for k in ("mx_w_q", "mx_w_kv", "mx_w_o", "mlp_w1", "mlp_w2", "ca_w_qkv", "ca_w_o"):
        sh[k] = inp[k]
    rb = inp["ca_rel_bias"][0]
    q = np.arange(64)[:, None]
    kk = np.arange(576)[None, :]
    idx = np.clip(512 + q - kk, -128, 128) + 128
    band = rb[:, idx]
    b2 = np.full((128, 16, 640), NEG, np.float32)
    b2[0:64, :, 0:576] = band.transpose(1, 0, 2)
    b2[64:128, :, 64:640] = band.transpose(1, 0, 2)
    sh["ca_bias"] = b2
    sh["consts"] = make_consts()
    for k in ("rw_w_r", "rw_w_k", "rw_w_v", "rw_w_o", "rw_w1", "rw_a1", "rw_g1", "rw_w2", "rw_a2", "rw_g2"):
        sh[k] = inp[k]
    nb = inp["rw_mu"].shape[0]
    rv = np.zeros((nb, 128, 8, 16), np.float32)
    fm = lambda v: v.reshape(nb, 8, 128).transpose(0, 2, 1)
    for m in range(6):
        rv[:, :, :, m] = fm(inp["rw_mu"][:, m])
    rv[:, :, :, 6] = fm(inp["rw_w0"])
    rv[:, :, :, 7] = fm(inp["rw_a0"])
    rv[:, :, :, 8] = fm(inp["rw_k_k"])
    rv[:, :, :, 9] = fm(inp["rw_k_a"])
    rv[:, :, :, 10] = fm(inp["rw_r_k"].reshape(nb, D))
    sh["rw_vec"] = rv
    sh["rw_lnx"] = np.ascontiguousarray(np.stack([inp["rw_lnx_g"], inp["rw_lnx_b"]], axis=1))
    sh["rwc"] = make_rwc()
    return sh


_NC_CACHE = {}


def get_nc(n_layers=DEPTH, dbg=None):
    key = (n_layers if isinstance(n_layers, int) else tuple(n_layers), tuple(sorted(dbg.items())) if dbg else None)
    if key not in _NC_CACHE:
        b = Builder(n_layers, dbg)
        nc = b.build()
        _NC_CACHE[key] = nc
    return _NC_CACHE[key]


def kernel(**inputs):
    inp = {k: np.ascontiguousarray(np.asarray(v, dtype=np.float32)) for k, v in inputs.items()}
    sh = prep_shared(inp)
    nc = get_nc()
    ncores = 8
    in_maps = []
    for c in range(ncores):
        m = dict(sh)
        m["x"] = np.ascontiguousarray(inp["x"][c * NSEQ:(c + 1) * NSEQ])
        m["mem"] = np.ascontiguousarray(inp["mem"][c * NSEQ:(c + 1) * NSEQ])
        in_maps.append(m)
    res = run_bass_kernel_spmd(nc, in_maps, core_ids=list(range(ncores)))
    out = np.concatenate([np.asarray(r["out"]).reshape(NSEQ, SEQ, D) for r in res.results], axis=0)
    return out.astype(np.float32)
```

```python
import numpy as np
import concourse.bass as bass
import concourse.mybir as mybir
from concourse.bass_utils import run_bass_kernel_spmd
from contextlib import ExitStack

F32 = mybir.dt.float32
BF16 = mybir.dt.bfloat16
AF = mybir.ActivationFunctionType
ALU = mybir.AluOpType
AX = mybir.AxisListType

ENGS = ("pe", "act", "dve", "pool", "sp")
SEM_EPOCH = 30000

D = 1024
SEQ = 2048
NSEQ = 2
DEPTH = 4
ALPHA = (2 * DEPTH) ** 0.25
LN_EPS = 1e-5
D_RNN = 1344
NBLK = 16
BS = 84
D_FF = 4096
MEMT = 256
NEG = -30000.0


class Sched:
    def __init__(self, nc, stack, n_dma_sems=8):
        self.nc = nc
        self.stack = stack
        self.prog = {e: [] for e in ENGS}
        self.cnt = {e: 0 for e in ENGS}
        self.nsem = 0
        self.sem_owner = {}
        self.sem = {}
        for e in ENGS:
            self.sem[e] = self._newsem()
            self.sem_owner[id(self.sem[e])] = e
        self.known = {e: {} for e in ENGS}
        self.lastw = {}
        self.readers = {}
        self.dma_sems = {q: [[self._newsem(), 0] for _ in range(n_dma_sems)] for q in ("sp", "act", "pool")}
        self.dma_rr = {q: 0 for q in ("sp", "act", "pool")}
        self.ninstr = 0
        self.nwait = 0

    def _newsem(self):
        self.nsem += 1
        return self.stack.enter_context(self.nc.semaphore(f"s{self.nsem}"))

    def _deps(self, eng, reads, writes):
        deps = {}

        def add(ev, raw):
            if ev is None:
                return
            s, v = ev
            own = self.sem_owner.get(id(s))
            if own == eng:
                if eng == "pe":
                    return
            k = id(s)
            if k not in deps or deps[k][1] < v:
                deps[k] = (s, v)

        for k in reads:
            add(self.lastw.get(k), True)
        for k in writes:
            add(self.lastw.get(k), False)
            rd = self.readers.get(k)
            if rd:
                for ev in rd.values():
                    add(ev, False)
        return deps

    def _filter(self, eng, deps):
        waits = []
        kn = self.known[eng]
        for k, (s, v) in deps.items():
            if kn.get(k, 0) < v:
                kn[k] = v
                waits.append((s, v))
        self.nwait += len(waits)
        return waits

    def _record(self, ev, reads, writes):
        for k in writes:
            self.lastw[k] = ev
            self.readers[k] = {}
        for k in reads:
            r = self.readers.setdefault(k, {})
            r[id(ev[0])] = ev

    def op(self, eng, fn, reads=(), writes=()):
        deps = self._deps(eng, reads, writes)
        waits = self._filter(eng, deps)
        if self.cnt[eng] >= SEM_EPOCH:
            self.sem[eng] = self._newsem()
            self.sem_owner[id(self.sem[eng])] = eng
            self.cnt[eng] = 0
        self.cnt[eng] += 1
        ev = (self.sem[eng], self.cnt[eng])
        self.prog[eng].append((waits, fn, (self.sem[eng], 1)))
        self._record(ev, reads, writes)
        self.ninstr += 1
        return ev

    def dma(self, q, out, in_, reads=(), writes=()):
        slots = self.dma_sems[q]
        si = self.dma_rr[q]
        self.dma_rr[q] = (si + 1) % len(slots)
        slot = slots[si]
        deps = self._deps(q, reads, writes)
        if slot[1] > 0:
            deps[id(slot[0])] = (slot[0], 16 * slot[1])
        waits = self._filter(q, deps)
        slot[1] += 1
        ev = (slot[0], 16 * slot[1])
        self.prog[q].append((waits, (lambda e: e.dma_start(out=out, in_=in_)), (slot[0], 16)))
        self._record(ev, reads, writes)
        self.ninstr += 1
        return ev

    def all_events(self):
        evs = []
        for e in ENGS:
            if self.cnt[e] > 0:
                evs.append((self.sem[e], self.cnt[e]))
        for q in self.dma_sems:
            for s, n in self.dma_sems[q]:
                if n > 0:
                    evs.append((s, 16 * n))
        return evs

    def barrier(self):
        evs = self.all_events()
        for e in ENGS:
            deps = {}
            for s, v in evs:
                if self.sem_owner.get(id(s)) == e:
                    continue
                deps[id(s)] = (s, v)
            waits = self._filter(e, deps)
            if waits:
                self.prog[e].append((waits, None, None))
        self.lastw = {}
        self.readers = {}

    def emit(self):
        nc = self.nc
        prog = self.prog

        def run(name, e):
            for waits, fn, inc in prog[name]:
                for s, v in waits:
                    e.wait_ge(s, v)
                if fn is not None:
                    ins = fn(e)
                    if inc is not None:
                        ins.then_inc(inc[0], inc[1])

        with nc.Block() as block:
            @block.tensor
            def _(e):
                run("pe", e)

            @block.scalar
            def _(e):
                run("act", e)

            @block.vector
            def _(e):
                run("dve", e)

            @block.gpsimd
            def _(e):
                run("pool", e)

            @block.sync
            def _(e):
                run("sp", e)


class Arena:
    def __init__(self, t, nwords):
        self.t = t
        self.n = nwords
        self.off = 0

    def mark(self):
        return self.off

    def reset(self, m):
        self.off = m

    def alloc(self, shape, dtype, parts=128):
        free = int(np.prod(shape))
        words = free if dtype == F32 else (free + 1) // 2
        words = (words + 1) // 2 * 2
        assert self.off + words <= self.n, f"arena overflow {self.off}+{words}>{self.n}"
        ap = self.t[0:parts, self.off:self.off + words]
        self.off += words
        if dtype != F32:
            ap = ap.bitcast(dtype)
        ap = ap[:, 0:free]
        if len(shape) == 2:
            ap = ap.rearrange("p (a b) -> p a b", a=shape[0])
        elif len(shape) == 3:
            ap = ap.rearrange("p (a b c) -> p a b c", a=shape[0], b=shape[1])
        return ap


_uid = [0]


def uid(p="k"):
    _uid[0] += 1
    return f"{p}{_uid[0]}"


class Builder:
    def __init__(self, n_layers=DEPTH, dbg=None):
        self.layers = list(range(n_layers)) if isinstance(n_layers, int) else list(n_layers)
        self.dbg = dbg
        nc = self.nc = bass.Bass("TRN2", target_bir_lowering=False)
        self.st = ExitStack()
        self.S = Sched(nc, self.st)

    def din(self, name, shape):
        return self.nc.dram_tensor(name, list(shape), F32, kind="ExternalInput").ap()

    def build(self):
        nc, st, S = self.nc, self.st, self.S
        P = self.P = {}
        P["x"] = self.din("x", [NSEQ, SEQ, D])
        P["mem"] = self.din("mem", [NSEQ, MEMT, D])
        P["ln_g"] = self.din("ln_g", [DEPTH * 3, D])
        P["ln_b"] = self.din("ln_b", [DEPTH * 3, D])
        P["lru_w_in"] = self.din("lru_w_in", [2, D, 2 * D_RNN])
        P["lru_vec"] = self.din("lru_vec", [2, BS, NBLK, 8])
        P["lru_gate_w"] = self.din("lru_gate_w", [2, BS, 2 * NBLK, BS])
        P["lru_w_out"] = self.din("lru_w_out", [2, D_RNN, D])
        P["mx_w_q"] = self.din("mx_w_q", [DEPTH, D, D])
        P["mx_w_kv"] = self.din("mx_w_kv", [DEPTH, D, 2 * D])
        P["mx_w_o"] = self.din("mx_w_o", [DEPTH, D, D])
        P["mlp_w1"] = self.din("mlp_w1", [DEPTH, D, D_FF])
        P["mlp_w2"] = self.din("mlp_w2", [DEPTH, D_FF, D])
        P["ca_w_qkv"] = self.din("ca_w_qkv", [1, D, 3 * D])
        P["ca_w_o"] = self.din("ca_w_o", [1, D, D])
        P["ca_bias"] = self.din("ca_bias", [128, 16, 640])
        P["consts"] = self.din("consts", [128, 256])
        for nm in ("rw_w_r", "rw_w_k", "rw_w_v", "rw_w_o"):
            P[nm] = self.din(nm, [1, D, D])
        P["rw_w1"] = self.din("rw_w1", [1, D, 64])
        P["rw_a1"] = self.din("rw_a1", [1, D, 64])
        P["rw_g1"] = self.din("rw_g1", [1, D, 128])
        P["rw_w2"] = self.din("rw_w2", [1, 64, D])
        P["rw_a2"] = self.din("rw_a2", [1, 64, D])
        P["rw_g2"] = self.din("rw_g2", [1, 128, D])
        P["rw_vec"] = self.din("rw_vec", [1, 128, 8, 16])
        P["rw_lnx"] = self.din("rw_lnx", [1, 2, D])
        P["rwc"] = self.din("rwc", [128, 648])
        self.out = nc.dram_tensor("out", [NSEQ, SEQ, D], F32, kind="ExternalOutput").ap()
        self.h32 = nc.dram_tensor("h32", [NSEQ, SEQ, D], F32, kind="Internal").ap()
        if self.dbg:
            self.dbg_out = {n: nc.dram_tensor(n, list(shp), F32, kind="ExternalOutput").ap() for n, shp in self.dbg.items()}

        sb = lambda n, s, d: st.enter_context(nc.sbuf_tensor(n, s, d))
        self.hT = sb("hT", [128, 8, SEQ + 2], BF16)
        self.ident = sb("ident", [128, 128], BF16)
        self.identf = sb("identf", [128, 128], F32)
        self.cst = sb("cst", [128, 256], F32)
        self.lng = sb("lng", [128, D], F32)
        self.lnb = sb("lnb", [128, D], F32)
        self.lnz = [sb("lnz0", [128, D], F32)] * 2
        self.lnh = [sb(f"lnh{i}", [128, D], F32) for i in range(2)]
        self.lnhb = [sb(f"lnhb{i}", [128, D], BF16) for i in range(2)]
        self.lnst = [sb(f"lnst{i}", [128, 16], F32) for i in range(2)]
        self.lni = 0
        ARW = 37600
        self.arena_t = sb("arena", [128, ARW], F32)
        self.A = Arena(self.arena_t, ARW)
        self.PS = [st.enter_context(nc.psum_tensor(f"ps{i}", [128, 1024], F32)) for i in range(4)]
        self.out_events = []
        self.gi = 0
        self.gen_list = [(0, 0), (0, 1)]

        S.dma("sp", self.cst[:], P["consts"], writes=["cst"])
        S.op("dve", lambda e: e.tensor_copy(out=self.ident[:], in_=self.cst[:, 0:128]), reads=["cst"], writes=["ident"])
        S.op("dve", lambda e: e.tensor_copy(out=self.identf[:], in_=self.cst[:, 0:128]), reads=["cst"], writes=["identf"])
        S.op("pool", lambda e: e.memset(self.hT[:, :, 0:2], 0.0), writes=["hTpad"])

        for seq in range(NSEQ):
            self.load_x(seq)
            for li, layer in enumerate(self.layers):
                kind, j = layer % 3, layer // 3
                first = (li == 0)
                if kind == 0:
                    self.stage_lru(seq, layer, j, first)
                elif kind == 1:
                    self.stage_rwkv(seq, layer, j, first)
                else:
                    self.stage_attn(seq, layer, j, first)
                self.stage_memattn(seq, layer)
                self.stage_mlp(seq, layer, last=(li == len(self.layers) - 1))
        S.barrier()
        S.emit()
        self.st.close()
        return nc

    def stage_begin(self):
        self.S.barrier()
        self.A.reset(0)

    def load_w(self, dst, src, key, q="pool"):
        K, nk, ncols = dst.shape
        step = max(1, 2048 // K) if ncols * 4 >= 2048 else nk
        step = min(nk, max(1, (1 << 21) // (K * ncols * 4)))
        for k0 in range(0, nk, step):
            k1 = min(nk, k0 + step)
            self.S.dma(q, dst[:, k0:k1, :], src[k0 * K:k1 * K, :].rearrange("(k p) n -> p k n", p=K), writes=[key])

    def load_x(self, seq):
        S = self.S
        self.stage_begin()
        xb = [self.A.alloc([D], BF16) for _ in range(2)]
        for sub in range(SEQ // 128):
            b = xb[sub % 2]
            kb = f"xb{sub % 2}"
            S.dma("pool", b, self.P["x"][seq, sub * 128:(sub + 1) * 128, :], writes=[kb])
            self.to_hT(b, kb, sub)

    def to_hT(self, hb, kb, sub, pi=None):
        S = self.S
        deferred = pi is not None
        if pi is None:
            pi = self.lni % 2 if getattr(self, "force_pi", None) is None else self.force_pi
        ps = self.PS[1][:, pi * 512:(pi + 1) * 512].bitcast(BF16)
        pk = ("PS1", pi)
        for c in range(8):
            S.op("pe", (lambda e, c=c: e.transpose(ps[:, c * 128:(c + 1) * 128], hb[:, c * 128:(c + 1) * 128], self.ident[:])),
                 reads=[kb, "ident"], writes=[pk])
        dst = self.hT[:, :, 2 + sub * 128: 2 + (sub + 1) * 128]
        src = ps.rearrange("p (c t) -> p c t", c=8)
        S.op("dve", lambda e: e.tensor_copy(out=dst, in_=src), reads=[pk], writes=[("hT", sub)])
        if not deferred:
            self.lni += 1

    def load_ln(self, li):
        S = self.S
        S.dma("sp", self.lng[:], self.P["ln_g"][li, :].partition_broadcast(128), writes=["lng"])
        S.dma("sp", self.lnb[:], self.P["ln_b"][li, :].partition_broadcast(128), writes=["lnb"])

    def ln_finish(self, seq, sub, y, ykeys, first, last, defer=False):
        S = self.S
        i = self.lni % 2
        z, hn, hb, stt = self.lnz[i], self.lnh[i], self.lnhb[i], self.lnst[i]
        kz, kh, khb, kst = "lnz0", f"lnh{i}", f"lnhb{i}", f"lnst{i}"
        src = (self.P["x"] if first else self.h32)[seq, sub * 128:(sub + 1) * 128, :]
        hk = ("h32", seq, sub)
        S.dma("sp", hn[:], src, reads=[hk], writes=[kh])
        S.op("dve", lambda e: e.scalar_tensor_tensor(out=z[:], in0=hn[:], scalar=float(ALPHA), in1=y, op0=ALU.mult, op1=ALU.add),
             reads=[kh] + list(ykeys), writes=[kz])
        S.op("dve", lambda e: e.bn_stats(out=stt[:, 0:6], in_=z[:, 0:512]), reads=[kz], writes=[kst + "a"])
        S.op("dve", lambda e: e.bn_stats(out=stt[:, 6:12], in_=z[:, 512:1024]), reads=[kz], writes=[kst + "b"])
        S.op("dve", lambda e: e.bn_aggr(out=stt[:, 12:14], in_=stt[:, 0:12]), reads=[kst + "a", kst + "b"], writes=[kst + "c"])
        S.op("act", lambda e: e.activation(out=stt[:, 14:15], in_=stt[:, 13:14], func=AF.Sqrt, bias=self.cst[:, 128:129], scale=1.0),
             reads=[kst + "c"], writes=[kst + "d0"])
        S.op("dve", lambda e: e.reciprocal(out=stt[:, 14:15], in_=stt[:, 14:15]), reads=[kst + "d0"], writes=[kst + "d"])
        S.op("dve", lambda e: e.tensor_scalar(out=stt[:, 15:16], in0=stt[:, 12:13], scalar1=stt[:, 14:15], scalar2=-1.0, op0=ALU.mult, op1=ALU.mult),
             reads=[kst + "c", kst + "d"], writes=[kst + "e"])
        S.op("act", lambda e: e.activation(out=z[:], in_=z[:], func=AF.Identity, bias=stt[:, 15:16], scale=stt[:, 14:15]),
             reads=[kz, kst + "d", kst + "e"], writes=[kz])
        S.op("dve", lambda e: e.tensor_tensor(out=z[:], in0=z[:], in1=self.lng[:], op=ALU.mult), reads=[kz, "lng"], writes=[kz])
        S.op("dve", lambda e: e.tensor_tensor(out=hn[:], in0=z[:], in1=self.lnb[:], op=ALU.add), reads=[kz, "lnb"], writes=[kh])
        dst = (self.out if last else self.h32)[seq, sub * 128:(sub + 1) * 128, :]
        S.dma("sp", dst, hn[:], reads=[kh], writes=[hk])
        if not last:
            S.op("act", lambda e: e.copy(out=hb[:], in_=hn[:]), reads=[kh], writes=[khb])
            if defer:
                pi = self.lni % 2 if getattr(self, "force_pi", None) is None else self.force_pi
                self.lni += 1
                return (hb, khb, sub, pi)
            self.to_hT(hb, khb, sub)
        else:
            self.lni += 1
        return None

    def gps(self, parts=128, n=512):
        i = self.gi
        self.gi += 1
        b = self.gen_list[i % len(self.gen_list)]
        return self.PS[b[0]][0:parts, b[1] * 512: b[1] * 512 + n], (f"PS{b[0]}", b[1])

    def hTs(self, k, t0, n):
        return self.hT[:, k, 2 + t0: 2 + t0 + n]

    def hTkeys(self, t0, n):
        return [("hT", s) for s in range(t0 // 128, (t0 + n + 127) // 128)]

    def dbg_store(self, name, ap, keys, dst_slice=None):
        if self.dbg and name in self.dbg:
            d = self.dbg_out[name] if dst_slice is None else dst_slice(self.dbg_out[name])
            self.S.dma("sp", d, ap, reads=keys)

    def stage_mlp(self, seq, layer, last):
        S, A, PS = self.S, self.A, self.PS
        self.stage_begin()
        self.load_ln(layer * 3 + 2)
        acc = A.alloc([16, D], F32)
        w1g = [A.alloc([8, 512], BF16) for _ in range(2)]
        w2g = [A.alloc([4, D], BF16) for _ in range(2)]
        hid = [A.alloc([4, 512], BF16) for _ in range(2)]
        rl = [A.alloc([512], F32) for _ in range(2)]
        w1 = self.P["mlp_w1"][layer]
        w2 = self.P["mlp_w2"][layer]
        nb = 0
        ny = 0
        for g in range(8):
            gi = g % 2
            self.load_w(w1g[gi], w1[:, g * 512:(g + 1) * 512], f"w1g{gi}")
            self.load_w(w2g[gi], w2[g * 512:(g + 1) * 512, :], f"w2g{gi}")
            for t in range(4):
                hi = (g * 4 + t) % 2
                for fc in range(4):
                    nb += 1
                    ps, pk = self.gps()
                    for k in range(8):
                        S.op("pe", (lambda e, k=k, fc=fc, ps=ps, gi=gi, t=t: e.matmul(ps, lhsT=w1g[gi][:, k, fc * 128:(fc + 1) * 128], rhs=self.hTs(k, t * 512, 512), start=(k == 0), stop=(k == 7))),
                             reads=[f"w1g{gi}"] + self.hTkeys(t * 512, 512), writes=[pk])
                    ri = nb % 2
                    S.op("act", (lambda e, ps=ps, ri=ri: e.activation(out=rl[ri], in_=ps, func=AF.Relu)), reads=[pk], writes=[f"rl{ri}"])
                    S.op("dve", (lambda e, ri=ri, hi=hi, fc=fc: e.tensor_tensor(out=hid[hi][:, fc, :], in0=rl[ri], in1=rl[ri], op=ALU.mult)),
                         reads=[f"rl{ri}"], writes=[(f"hid{hi}", fc)])
                for sub in range(4):
                    yi = ny % 2
                    ny += 1
                    py = PS[2 + yi]
                    for half in range(2):
                        for fc in range(4):
                            S.op("pe", (lambda e, fc=fc, half=half, py=py, hi=hi, gi=gi, sub=sub: e.matmul(py[:, half * 512:(half + 1) * 512], lhsT=hid[hi][:, fc, sub * 128:(sub + 1) * 128], rhs=w2g[gi][:, fc, half * 512:(half + 1) * 512], start=(fc == 0), stop=(fc == 3))),
                                 reads=[(f"hid{hi}", fc), f"w2g{gi}"], writes=[(f"PS{2 + yi}", half)])
                    a = acc[:, t * 4 + sub, :]
                    ak = ("acc", t * 4 + sub)
                    pkeys = [(f"PS{2 + yi}", 0), (f"PS{2 + yi}", 1)]
                    if g == 0:
                        S.op("act", (lambda e, a=a, py=py: e.copy(out=a, in_=py[:, :])), reads=pkeys, writes=[ak])
                    else:
                        S.op("dve", (lambda e, a=a, py=py: e.tensor_tensor(out=a, in0=a, in1=py[:, :], op=ALU.add)), reads=pkeys + [ak], writes=[ak])
        for s16 in range(16):
            self.ln_finish(seq, s16, acc[:, s16, :], [("acc", s16)], first=False, last=last)

    def stage_memattn(self, seq, layer):
        S, A, PS = self.S, self.A, self.PS
        self.stage_begin()
        self.load_ln(layer * 3 + 1)
        wb = [A.alloc([8, D], BF16) for _ in range(2)]
        memb = A.alloc([2, D], BF16)
        memT = A.alloc([8, MEMT], BF16)
        KT = A.alloc([8, MEMT], BF16)
        V = A.alloc([2, D], BF16)
        QT = A.alloc([8, 512], BF16)
        Pn = [A.alloc([4, MEMT], BF16) for _ in range(2)]
        PT = A.alloc([8, 512], BF16)
        OT = A.alloc([8, 512], BF16)
        sm = [A.alloc([16], F32) for _ in range(2)]
        wkv = self.P["mx_w_kv"][layer]
        self.load_w(wb[0], wkv[:, 0:D], "wb0")
        self.load_w(wb[1], wkv[:, D:2 * D], "wb1")
        S.dma("pool", memb, self.P["mem"][seq].rearrange("(a p) d -> p a d", p=128), writes=["memb"])
        for mt in range(2):
            ps = PS[0][:, mt * 512:(mt + 1) * 512].bitcast(BF16)
            for c in range(8):
                S.op("pe", (lambda e, c=c, mt=mt, ps=ps: e.transpose(ps[:, c * 128:(c + 1) * 128], memb[:, mt, c * 128:(c + 1) * 128], self.ident[:])),
                     reads=["memb", "ident"], writes=[("PS0", mt)])
            S.op("dve", (lambda e, mt=mt, ps=ps: e.tensor_copy(out=memT[:, :, mt * 128:(mt + 1) * 128], in_=ps.rearrange("p (c t) -> p c t", c=8))),
                 reads=[("PS0", mt)], writes=[("memT", mt)])
        mk = [("memT", 0), ("memT", 1)]
        for oc in range(8):
            ps, pk = self.gps(n=MEMT)
            for k in range(8):
                S.op("pe", (lambda e, k=k, oc=oc, ps=ps: e.matmul(ps, lhsT=wb[0][:, k, oc * 128:(oc + 1) * 128], rhs=memT[:, k, :], start=(k == 0), stop=(k == 7))),
                     reads=["wb0"] + mk, writes=[pk])
            S.op("act", (lambda e, oc=oc, ps=ps: e.copy(out=KT[:, oc, :], in_=ps)), reads=[pk], writes=[("KT", oc)])
        for mt in range(2):
            for half in range(2):
                ps, pk = self.gps()
                for k in range(8):
                    S.op("pe", (lambda e, k=k, mt=mt, half=half, ps=ps: e.matmul(ps, lhsT=memT[:, k, mt * 128:(mt + 1) * 128], rhs=wb[1][:, k, half * 512:(half + 1) * 512], start=(k == 0), stop=(k == 7))),
                         reads=["wb1"] + mk, writes=[pk])
                S.op("dve", (lambda e, mt=mt, half=half, ps=ps: e.tensor_copy(out=V[:, mt, half * 512:(half + 1) * 512], in_=ps)), reads=[pk], writes=[("V", mt, half)])
        vk = [("V", a, b) for a in range(2) for b in range(2)]
        self.load_w(wb[0], self.P["mx_w_q"][layer], "wb0")
        self.load_w(wb[1], self.P["mx_w_o"][layer], "wb1")
        for t in range(4):
            for oc in range(8):
                ps, pk = self.gps()
                for k in range(8):
                    S.op("pe", (lambda e, k=k, oc=oc, ps=ps, t=t: e.matmul(ps, lhsT=wb[0][:, k, oc * 128:(oc + 1) * 128], rhs=self.hTs(k, t * 512, 512), start=(k == 0), stop=(k == 7))),
                         reads=["wb0"] + self.hTkeys(t * 512, 512), writes=[pk])
                S.op("act", (lambda e, oc=oc, ps=ps: e.activation(out=QT[:, oc, :], in_=ps, func=AF.Copy, scale=1.0 / 16.0)), reads=[pk], writes=[("QT", oc)])
            def mem_A(sub):
                pi = sub % 2
                ps = PS[0] if pi == 0 else PS[3]
                spn = "PS0" if pi == 0 else "PS3"
                pkeys = [(spn, 0), (spn, 1)]
                for h in range(4):
                    for c in range(2):
                        S.op("pe", (lambda e, h=h, c=c, ps=ps, sub=sub: e.matmul(ps[:, h * 256:(h + 1) * 256], lhsT=QT[:, 2 * h + c, sub * 128:(sub + 1) * 128], rhs=KT[:, 2 * h + c, :], start=(c == 0), stop=(c == 1))),
                             reads=[("QT", 2 * h + c), ("KT", 2 * h + c)], writes=[(spn, h // 2)])
                smi = sm[pi]
                ks = f"sm{pi}"
                S.op("dve", (lambda e, ps=ps, smi=smi: e.tensor_reduce(out=smi[:, 0:4], in_=ps[:, :].rearrange("p (h m) -> p h m", h=4), axis=AX.X, op=ALU.max, negate=True)),
                     reads=pkeys, writes=[ks + "m"])
                pn = Pn[pi]
                for h in range(4):
                    S.op("act", (lambda e, h=h, ps=ps, smi=smi, pn=pn: e.activation(out=pn[:, h, :], in_=ps[:, h * 256:(h + 1) * 256], func=AF.Exp, bias=smi[:, h:h + 1], scale=1.0, accum_out=smi[:, 4 + h:5 + h])),
                         reads=[(spn, h // 2), ks + "m"], writes=[(f"Pn{pi}", h), (ks + "s", h)])
                S.op("dve", (lambda e, smi=smi: e.reciprocal(out=smi[:, 8:12], in_=smi[:, 4:8])), reads=[(ks + "s", h) for h in range(4)], writes=[ks + "r"])
                S.op("dve", (lambda e, smi=smi, pn=pn: e.tensor_tensor(out=pn[:, :, :], in0=pn[:, :, :], in1=smi[:, 8:12].unsqueeze(2).to_broadcast([128, 4, MEMT]), op=ALU.mult)),
                     reads=[(f"Pn{pi}", h) for h in range(4)] + [ks + "r"], writes=[(f"Pn{pi}", h) for h in range(4)])

            def mem_B(sub):
                pi = sub % 2
                pn = Pn[pi]
                pt = PS[1][:, pi * 512:(pi + 1) * 512].bitcast(BF16)
                ptk = ("PS1", pi)
                for h in range(4):
                    for mc in range(2):
                        S.op("pe", (lambda e, h=h, mc=mc, pt=pt, pn=pn: e.transpose(pt[:, (h * 2 + mc) * 128:(h * 2 + mc + 1) * 128], pn[:, h, mc * 128:(mc + 1) * 128], self.ident[:])),
                             reads=[(f"Pn{pi}", h), "ident"], writes=[ptk])
                S.op("dve", (lambda e, pt=pt, sub=sub: e.tensor_copy(out=PT[:, :, sub * 128:(sub + 1) * 128], in_=pt.rearrange("p (c t) -> p c t", c=8))),
                     reads=[ptk], writes=[("PT", sub)])

            for sq_ in range(5):
                if sq_ < 4:
                    mem_A(sq_)
                if sq_ >= 1:
                    mem_B(sq_ - 1)
            for oc in range(8):
                h, c = oc // 2, oc % 2
                ps, pk = self.gps()
                for mc in range(2):
                    S.op("pe", (lambda e, h=h, c=c, mc=mc, ps=ps: e.matmul(ps, lhsT=V[:, mc, h * 256 + c * 128: h * 256 + (c + 1) * 128], rhs=PT[:, h * 2 + mc, :], start=(mc == 0), stop=(mc == 1))),
                         reads=vk + [("PT", s) for s in range(4)], writes=[pk])
                S.op("act", (lambda e, oc=oc, ps=ps: e.copy(out=OT[:, oc, :], in_=ps)), reads=[pk], writes=[("OT", oc)])
            self.out_proj(seq, t, lambda k, sub: OT[:, k, sub * 128:(sub + 1) * 128], [("OT", k) for k in range(8)], wb[1], "wb1", 8, first=False)

    def out_proj(self, seq, t, lhs_fn, lkeys, w, wkey, nk, first):
        S, PS = self.S, self.PS
        pend = None
        for sub in range(4):
            yi = sub % 2
            py = PS[2 + yi]
            for half in range(2):
                for k in range(nk):
                    S.op("pe", (lambda e, k=k, half=half, py=py, sub=sub: e.matmul(py[:, half * 512:(half + 1) * 512], lhsT=lhs_fn(k, sub), rhs=w[:, k, half * 512:(half + 1) * 512], start=(k == 0), stop=(k == nk - 1))),
                         reads=list(lkeys) + [wkey], writes=[(f"PS{2 + yi}", half)])
            if pend is not None:
                self.to_hT(*pend)
            pend = self.ln_finish(seq, t * 4 + sub, py[:, :], [(f"PS{2 + yi}", 0), (f"PS{2 + yi}", 1)], first=first, last=False, defer=True)
        if pend is not None:
            self.to_hT(*pend)

    def stage_lru(self, seq, layer, j, first):
        S, A, PS = self.S, self.A, self.PS
        self.stage_begin()
        self.load_ln(layer * 3 + 0)
        win = A.alloc([8, 2 * D_RNN], BF16)
        wout = A.alloc([NBLK, D], BF16, parts=BS)
        gw = A.alloc([2 * NBLK, BS], BF16, parts=BS)
        vec = A.alloc([NBLK, 8], F32, parts=BS)
        c8 = A.alloc([NBLK], F32, parts=BS)
        carry = A.alloc([NBLK], F32, parts=BS)
        xpb = [A.alloc([516], F32, parts=BS) for _ in range(2)]
        hist = A.alloc([NBLK, 4], F32, parts=BS)
        mT = A.alloc([NBLK, 512], BF16, parts=BS)
        NB2 = 2
        gb = [A.alloc([512], F32, parts=BS) for _ in range(NB2)]
        xr = [A.alloc([512], F32, parts=BS) for _ in range(NB2)]
        xrb = [A.alloc([512], BF16, parts=BS) for _ in range(NB2)]
        rg = [A.alloc([512], F32, parts=BS) for _ in range(NB2)]
        ig = [A.alloc([512], F32, parts=BS) for _ in range(NB2)]
        aa = [A.alloc([512], F32, parts=BS) for _ in range(NB2)]
        sq = [A.alloc([512], F32, parts=BS) for _ in range(NB2)]
        hs = [A.alloc([512], F32, parts=BS) for _ in range(NB2)]
        self.load_w(win, self.P["lru_w_in"][j], "win")
        S.dma("pool", wout, self.P["lru_w_out"][j].rearrange("(n p) d -> p n d", p=BS), writes=["wout"])
        S.dma("pool", gw, self.P["lru_gate_w"][j], writes=["gw"])
        S.dma("sp", vec, self.P["lru_vec"][j], writes=["vec"])
        tx = A.alloc([NBLK], F32, parts=BS)
        tl = A.alloc([NBLK], F32, parts=BS)
        tu = A.alloc([NBLK], F32, parts=BS)
        S.op("act", lambda e: e.activation(out=tx, in_=vec[:, :, 7], func=AF.Exp, scale=-1.0), reads=["vec"], writes=["tx"])
        S.op("act", lambda e: e.activation(out=tl, in_=tx, func=AF.Ln, bias=1.0, scale=1.0), reads=["tx"], writes=["tl"])
        S.op("dve", lambda e: e.tensor_scalar(out=tu, in0=tx, scalar1=-0.25, scalar2=1.0 / 3.0, op0=ALU.mult, op1=ALU.add), reads=["tx"], writes=["tu"])
        S.op("dve", lambda e: e.tensor_tensor(out=tu, in0=tu, in1=tx, op=ALU.mult), reads=["tu", "tx"], writes=["tu"])
        S.op("dve", lambda e: e.tensor_scalar(out=tu, in0=tu, scalar1=-1.0, scalar2=0.5, op0=ALU.mult, op1=ALU.add), reads=["tu"], writes=["tu"])
        S.op("dve", lambda e: e.tensor_tensor(out=tu, in0=tu, in1=tx, op=ALU.mult), reads=["tu", "tx"], writes=["tu"])
        S.op("dve", lambda e: e.tensor_scalar(out=tu, in0=tu, scalar1=-1.0, scalar2=1.0, op0=ALU.mult, op1=ALU.add), reads=["tu"], writes=["tu"])
        S.op("dve", lambda e: e.tensor_tensor(out=tu, in0=tu, in1=tx, op=ALU.mult), reads=["tu", "tx"], writes=["tu"])
        S.op("dve", lambda e: e.tensor_tensor(out=tu, in0=tu, in1=tl, op=ALU.subtract), reads=["tu", "tl"], writes=["tu"])
        S.op("dve", lambda e: e.tensor_scalar(out=tx, in0=tx, scalar1=0.05, scalar2=None, op0=ALU.is_lt), reads=["tx", "tu"], writes=["tx"])
        S.op("dve", lambda e: e.tensor_tensor(out=tu, in0=tu, in1=tx, op=ALU.mult), reads=["tu", "tx"], writes=["tu"])
        S.op("dve", lambda e: e.tensor_tensor(out=tu, in0=tu, in1=tl, op=ALU.add), reads=["tu", "tl"], writes=["tu"])
        S.op("dve", lambda e: e.tensor_scalar(out=c8, in0=tu, scalar1=-8.0, scalar2=None, op0=ALU.mult), reads=["tu"], writes=["c8"])
        S.op("dve", lambda e: e.memset(carry, 0.0), writes=["carry"])
        S.op("dve", lambda e: e.memset(hist, 0.0), writes=[("hist", n) for n in range(NBLK)])
        nb = 0
        for t in range(4):
            for n in range(NBLK):
                bi = n % NB2
                ps, pk = self.gps(parts=BS)
                for k in range(8):
                    S.op("pe", (lambda e, k=k, n=n, ps=ps, t=t: e.matmul(ps, lhsT=win[:, k, n * BS:(n + 1) * BS], rhs=self.hTs(k, t * 512, 512), start=(k == 0), stop=(k == 7))),
                         reads=["win"] + self.hTkeys(t * 512, 512), writes=[pk])
                S.op("act", (lambda e, ps=ps, bi=bi: e.activation(out=gb[bi], in_=ps, func=AF.Gelu)), reads=[pk], writes=[f"gb{bi}"])
                ps2, pk2 = self.gps(parts=BS)
                for k in range(8):
                    S.op("pe", (lambda e, k=k, n=n, ps2=ps2, t=t: e.matmul(ps2, lhsT=win[:, k, D_RNN + n * BS: D_RNN + (n + 1) * BS], rhs=self.hTs(k, t * 512, 512), start=(k == 0), stop=(k == 7))),
                         reads=["win"] + self.hTkeys(t * 512, 512), writes=[pk2])
                xp = xpb[bi]
                xk = f"xpb{bi}"
                S.op("pool", (lambda e, xp=xp, n=n: e.tensor_copy(out=xp[:, 0:4], in_=hist[:, n, :])), reads=[("hist", n)], writes=[xk + "h"])
                S.op("act", (lambda e, xp=xp, ps2=ps2: e.copy(out=xp[:, 4:516], in_=ps2)), reads=[pk2], writes=[xk])
                x_ = xr[bi]
                kx = f"xr{bi}"
                S.op("dve", (lambda e, xp=xp, x_=x_, n=n: e.tensor_scalar(out=x_, in0=xp[:, 1:513], scalar1=vec[:, n, 0:1], scalar2=vec[:, n, 4:5], op0=ALU.mult, op1=ALU.add)),
                     reads=[xk, xk + "h", "vec"], writes=[kx])
                for kk in range(1, 4):
                    S.op("dve", (lambda e, xp=xp, x_=x_, n=n, kk=kk: e.scalar_tensor_tensor(out=x_, in0=xp[:, 1 + kk:513 + kk], scalar=vec[:, n, kk:kk + 1], in1=x_, op0=ALU.mult, op1=ALU.add)),
                         reads=[xk, xk + "h", "vec", kx], writes=[kx])
                S.op("pool", (lambda e, xp=xp, n=n: e.tensor_copy(out=hist[:, n, :], in_=xp[:, 512:516])), reads=[xk], writes=[("hist", n)])
                S.op("act", (lambda e, x_=x_, bi=bi: e.copy(out=xrb[bi], in_=x_)), reads=[kx], writes=[f"xrb{bi}"])
                pg, pgk = self.gps(parts=BS)
                S.op("pe", (lambda e, n=n, pg=pg, bi=bi: e.matmul(pg, lhsT=gw[:, n, :], rhs=xrb[bi], start=True, stop=True)), reads=["gw", f"xrb{bi}"], writes=[pgk])
                S.op("act", (lambda e, pg=pg, bi=bi, n=n: e.activation(out=rg[bi], in_=pg, func=AF.Sigmoid, bias=vec[:, n, 5:6], scale=1.0)), reads=[pgk, "vec"], writes=[f"rg{bi}"])
                pg2, pgk2 = self.gps(parts=BS)
                S.op("pe", (lambda e, n=n, pg2=pg2, bi=bi: e.matmul(pg2, lhsT=gw[:, NBLK + n, :], rhs=xrb[bi], start=True, stop=True)), reads=["gw", f"xrb{bi}"], writes=[pgk2])
                S.op("act", (lambda e, pg2=pg2, bi=bi, n=n: e.activation(out=ig[bi], in_=pg2, func=AF.Sigmoid, bias=vec[:, n, 6:7], scale=1.0)), reads=[pgk2, "vec"], writes=[f"ig{bi}"])
                S.op("act", (lambda e, bi=bi, n=n: e.activation(out=aa[bi], in_=rg[bi], func=AF.Exp, scale=c8[:, n:n + 1])), reads=[f"rg{bi}", "c8"], writes=[f"aa{bi}"])
                S.op("act", (lambda e, bi=bi: e.activation(out=sq[bi], in_=aa[bi], func=AF.Square)), reads=[f"aa{bi}"], writes=[f"sq{bi}"])
                S.op("act", (lambda e, bi=bi: e.activation(out=sq[bi], in_=sq[bi], func=AF.Sqrt, bias=1.0, scale=-1.0)), reads=[f"sq{bi}"], writes=[f"sq{bi}"])
                S.op("pool", (lambda e, bi=bi: e.tensor_tensor(out=ig[bi], in0=ig[bi], in1=xr[bi], op=ALU.mult)), reads=[f"ig{bi}", kx], writes=[f"ig{bi}"])
                S.op("dve", (lambda e, bi=bi: e.tensor_tensor(out=ig[bi], in0=ig[bi], in1=sq[bi], op=ALU.mult)), reads=[f"ig{bi}", f"sq{bi}"], writes=[f"ig{bi}"])
                S.op("dve", (lambda e, bi=bi, n=n: e.tensor_tensor_scan(out=hs[bi], data0=aa[bi], data1=ig[bi], initial=carry[:, n:n + 1], op0=ALU.mult, op1=ALU.add)),
                     reads=[f"aa{bi}", f"ig{bi}", ("carry", n)], writes=[f"hs{bi}"])
                S.op("act", (lambda e, bi=bi, n=n: e.copy(out=carry[:, n:n + 1], in_=hs[bi][:, 511:512])), reads=[f"hs{bi}"], writes=[("carry", n)])
                S.op("dve", (lambda e, bi=bi, n=n: e.tensor_tensor(out=mT[:, n, :], in0=hs[bi], in1=gb[bi], op=ALU.mult)), reads=[f"hs{bi}", f"gb{bi}"], writes=[("mT", n)])
            self.out_proj(seq, t, lambda k, sub: mT[:, k, sub * 128:(sub + 1) * 128], [("mT", n) for n in range(NBLK)], wout, "wout", NBLK, first=first)


    def stage_rwkv(self, seq, layer, j, first):
        S, A, PS = self.S, self.A, self.PS
        self.stage_begin()
        self.load_ln(layer * 3 + 0)
        import os
        RW = BF16 if os.environ.get('RW_F32', '0') != '1' else F32
        C0 = float(np.exp(-0.5))
        wr = A.alloc([8, D], BF16)
        wk = A.alloc([8, D], BF16)
        wv = A.alloc([8, D], BF16)
        wo = A.alloc([8, D], BF16)
        w1b = A.alloc([8, 64], BF16)
        a1b = A.alloc([8, 64], BF16)
        g1b = A.alloc([8, 128], BF16)
        w2b = A.alloc([D], BF16, parts=64)
        a2b = A.alloc([D], BF16, parts=64)
        g2b = A.alloc([D], BF16)
        lxg = A.alloc([D], F32)
        lxb = A.alloc([D], F32)
        vec = A.alloc([8, 16], F32)
        omu = A.alloc([8, 6], F32)
        rwc = A.alloc([648], F32)
        mask1 = rwc[:, 0:256]
        masksl = rwc[:, 256:384]
        blk = rwc[:, 384:512]
        rmask = rwc[:, 512:640]
        ind2 = rwc[:, 640:642]
        gneps = rwc[:, 642:643]
        xs = [A.alloc([8, 128], BF16) for _ in range(2)]
        xprev = A.alloc([8, 128], BF16)
        tmpx = A.alloc([8, 128], BF16)
        xlast = A.alloc([8], BF16)
        thw = A.alloc([128], BF16, parts=64)
        la = A.alloc([128], BF16, parts=64)
        sgl = A.alloc([128], BF16)
        V = A.alloc([D], F32)
        Y = A.alloc([D], F32)
        Hst = A.alloc([8, 2, 64], F32)
        Hm = Hst if RW == F32 else A.alloc([8, 2, 64], RW)
        Vm = V if RW == F32 else A.alloc([D], RW)
        Hd = [A.alloc([64], F32) for _ in range(2)]
        scr = A.alloc([D], F32)
        ob = A.alloc([D], BF16)
        OT = A.alloc([8, 128], BF16)
        stt = A.alloc([112], F32)
        names = ["sg", "a", "kq", "kk", "k", "r", "rn", "cum", "G", "Gi", "ex", "ab", "t1", "km", "E", "BhT", "KhT", "rkr"]
        PBs = []
        tmps = {n: A.alloc([128], F32) for n in names if n != "G"}
        for i in range(2):
            pb = dict(tmps)
            pb["G"] = A.alloc([128], F32)
            pb["ATRT"] = A.alloc([256], RW)
            pb["BT"] = A.alloc([128], RW)
            pb["KT"] = A.alloc([128], RW)
            pb["BK"] = A.alloc([256], RW)
            PBs.append(pb)
        HB = []
        for i in range(2):
            hb = {"Mk": A.alloc([256], RW), "Mb": A.alloc([256], RW), "X0": A.alloc([128], RW),
                  "XX": [A.alloc([256], RW) for _ in range(2)], "Z": [A.alloc([128], RW) for _ in range(2)],
                  "Gs": A.alloc([64], RW), "Us": A.alloc([64], RW)}
            S.op("dve", (lambda e, hb=hb: e.memset(hb["Us"], 0.0)), writes=[(f"hb{i}", "Us")])
            S.op("dve", (lambda e, hb=hb: e.memset(hb["Gs"], 0.0)), writes=[(f"hb{i}", "Gs")])
            HB.append(hb)

        self.load_w(w1b, self.P["rw_w1"][j], "w1b")
        self.load_w(a1b, self.P["rw_a1"][j], "a1b")
        self.load_w(g1b, self.P["rw_g1"][j], "g1b")
        self.load_w(wv, self.P["rw_w_v"][j], "wv")
        self.load_w(wr, self.P["rw_w_r"][j], "wr")
        self.load_w(wk, self.P["rw_w_k"][j], "wk")
        self.load_w(wo, self.P["rw_w_o"][j], "wo")
        S.dma("pool", w2b, self.P["rw_w2"][j], writes=["w2b"])
        S.dma("pool", a2b, self.P["rw_a2"][j], writes=["a2b"])
        S.dma("pool", g2b, self.P["rw_g2"][j], writes=["g2b"])
        S.dma("sp", vec, self.P["rw_vec"][j], writes=["vec"])
        S.dma("sp", rwc, self.P["rwc"], writes=["rwc"])
        S.dma("sp", lxg, self.P["rw_lnx"][j, 0, :].partition_broadcast(128), writes=["lxg"])
        S.dma("sp", lxb, self.P["rw_lnx"][j, 1, :].partition_broadcast(128), writes=["lxb"])
        S.op("dve", lambda e: e.tensor_scalar(out=omu, in0=vec[:, :, 0:6], scalar1=-1.0, scalar2=1.0, op0=ALU.mult, op1=ALU.add), reads=["vec"], writes=["omu"])
        S.op("dve", lambda e: e.memset(Hst, 0.0), writes=[("H", h) for h in range(16)])
        if RW != F32:
            S.op("dve", lambda e: e.memset(Hm, 0.0), writes=[("Hm", h) for h in range(16)])
        S.op("dve", lambda e: e.memset(xlast, 0.0), writes=["xlast"])
        self.force_pi = 0
        self.gen_list = [(0, 0), (0, 1)]
        HK = "H" if RW == F32 else "Hm"
        VK = "V" if RW == F32 else "Vm"
        coefps = PS[1][:, 512:528]
        ckey = ("PS1", 1)

        def mix(m, bi, hk, t0):
            S.op("dve", (lambda e: e.tensor_tensor(out=tmpx, in0=xprev, in1=vec[:, :, m:m + 1].to_broadcast([128, 8, 128]), op=ALU.mult)),
                 reads=["xprev", "vec"], writes=["tmpx"])
            S.op("dve", (lambda e: e.tensor_tensor(out=xs[bi], in0=self.hT[:, :, 2 + t0:2 + t0 + 128], in1=omu[:, :, m:m + 1].to_broadcast([128, 8, 128]), op=ALU.mult)),
                 reads=hk + ["omu"], writes=[f"xs{bi}"])
            S.op("pool", (lambda e: e.tensor_tensor(out=xs[bi], in0=xs[bi], in1=tmpx, op=ALU.add)), reads=[f"xs{bi}", "tmpx"], writes=[f"xs{bi}"])

        for tt in range(16):
            t0 = tt * 128
            hk = [("hT", tt)]
            S.op("pool", (lambda e, t0=t0: e.tensor_copy(out=xprev[:, :, 1:128], in_=self.hT[:, :, 2 + t0:2 + t0 + 127])), reads=hk, writes=["xprev"])
            S.op("pool", (lambda e: e.tensor_copy(out=xprev[:, :, 0:1], in_=xlast.unsqueeze(2))), reads=["xlast", "xprev"], writes=["xprev"])
            S.op("pool", (lambda e, t0=t0: e.tensor_copy(out=xlast.unsqueeze(2), in_=self.hT[:, :, 2 + t0 + 127:2 + t0 + 128])), reads=hk + ["xprev"], writes=["xlast"])
            mix(1, 0, hk, t0)
            ps, pk = self.gps(parts=64, n=128)
            for k in range(8):
                S.op("pe", (lambda e, k=k, ps=ps: e.matmul(ps, lhsT=w1b[:, k, :], rhs=xs[0][:, k, :], start=(k == 0), stop=(k == 7))), reads=["w1b", "xs0"], writes=[pk])
            S.op("act", (lambda e, ps=ps: e.activation(out=thw, in_=ps, func=AF.Tanh)), reads=[pk], writes=["thw"])
            mix(4, 1, hk, t0)
            ps, pk = self.gps(parts=64, n=128)
            for k in range(8):
                S.op("pe", (lambda e, k=k, ps=ps: e.matmul(ps, lhsT=a1b[:, k, :], rhs=xs[1][:, k, :], start=(k == 0), stop=(k == 7))), reads=["a1b", "xs1"], writes=[pk])
            S.op("act", (lambda e, ps=ps: e.copy(out=la, in_=ps)), reads=[pk], writes=["la"])
            mix(5, 0, hk, t0)
            ps, pk = self.gps(n=128)
            for k in range(8):
                S.op("pe", (lambda e, k=k, ps=ps: e.matmul(ps, lhsT=g1b[:, k, :], rhs=xs[0][:, k, :], start=(k == 0), stop=(k == 7))), reads=["g1b", "xs0"], writes=[pk])
            S.op("act", (lambda e, ps=ps: e.activation(out=sgl, in_=ps, func=AF.Sigmoid)), reads=[pk], writes=["sgl"])
            mix(3, 1, hk, t0)
            for half in range(2):
                ps, pk = self.gps()
                for k in range(8):
                    S.op("pe", (lambda e, k=k, ps=ps, half=half: e.matmul(ps, lhsT=xs[1][:, k, :], rhs=wv[:, k, half * 512:(half + 1) * 512], start=(k == 0), stop=(k == 7))), reads=["wv", "xs1"], writes=[pk])
                S.op("act", (lambda e, ps=ps, half=half: e.copy(out=V[:, half * 512:(half + 1) * 512], in_=ps)), reads=[pk], writes=[("V", half)])
                if RW != F32:
                    S.op("pool", (lambda e, half=half: e.tensor_copy(out=Vm[:, half * 512:(half + 1) * 512], in_=V[:, half * 512:(half + 1) * 512])), reads=[("V", half)], writes=[("Vm", half)])
            mix(0, 0, hk, t0)
            mix(2, 1, hk, t0)
            import os
            RWD = int(os.environ.get("RW_DBG", "9"))
            if RWD < 9:
                S.op("dve", (lambda e: e.memset(Y, 0.0)), reads=[("Ydone", 0), ("Ydone", 1)], writes=[("Y", h // 8, q, h) for h in range(16) for q in range(2)])
                S.op("pe", (lambda e: e.matmul(coefps, lhsT=sgl, rhs=g2b[:, 0:16], start=True, stop=True)), reads=["sgl", "g2b"], writes=[ckey])
            RWS = int(os.environ.get("RW_SUB", "999"))
            for c in range(8 if RWD >= 2 else 0):
                if RWS < 999:
                    class _F:
                        def __init__(s_, S0):
                            s_.S0, s_.n = S0, 0
                        def op(s_, *a, **k):
                            s_.n += 1
                            if s_.n <= RWS:
                                return s_.S0.op(*a, **k)
                        def __getattr__(s_, nm):
                            return getattr(s_.S0, nm)
                    S = _F(self.S)
                pb = PBs[c % 2]
                pn = f"pb{c % 2}"
                K_ = lambda n, pn=pn: (pn, n) if n in ("G", "AT", "RT", "BT", "KT", "BK") else ("pbt", n)
                ps, pk = self.gps()
                for k in range(8):
                    S.op("pe", (lambda e, k=k, ps=ps, c=c: e.matmul(ps[:, 0:128], lhsT=wr[:, k, c * 128:(c + 1) * 128], rhs=xs[0][:, k, :], start=(k == 0), stop=(k == 7))), reads=["wr", "xs0"], writes=[pk])
                for k in range(8):
                    S.op("pe", (lambda e, k=k, ps=ps, c=c: e.matmul(ps[:, 128:256], lhsT=wk[:, k, c * 128:(c + 1) * 128], rhs=xs[1][:, k, :], start=(k == 0), stop=(k == 7))), reads=["wk", "xs1"], writes=[pk])
                S.op("pe", (lambda e, ps=ps, c=c: e.matmul(ps[:, 256:384], lhsT=w2b[:, c * 128:(c + 1) * 128], rhs=thw, start=True, stop=True)), reads=["w2b", "thw"], writes=[pk])
                S.op("pe", (lambda e, ps=ps, c=c: e.matmul(ps[:, 384:512], lhsT=a2b[:, c * 128:(c + 1) * 128], rhs=la, start=True, stop=True)), reads=["a2b", "la"], writes=[pk])
                vc = lambda i, c=c: vec[:, c, i:i + 1]
                S.op("act", (lambda e, ps=ps, pb=pb, vc=vc: e.activation(out=pb["sg"], in_=ps[:, 256:384], func=AF.Sigmoid, bias=vc(6), scale=1.0)), reads=[pk, "vec"], writes=[K_("sg")])
                S.op("act", (lambda e, ps=ps, pb=pb, vc=vc: e.activation(out=pb["a"], in_=ps[:, 384:512], func=AF.Sigmoid, bias=vc(7), scale=1.0)), reads=[pk, "vec"], writes=[K_("a")])
                S.op("act", (lambda e, ps=ps, pb=pb: e.copy(out=pb["k"], in_=ps[:, 128:256])), reads=[pk], writes=[K_("k")])
                S.op("dve", (lambda e, pb=pb, vc=vc: e.tensor_scalar(out=pb["kk"], in0=pb["k"], scalar1=vc(8), scalar2=None, op0=ALU.mult)), reads=[K_("k"), "vec"], writes=[K_("kk")])
                S.op("act", (lambda e, pb=pb: e.activation(out=pb["kq"], in_=pb["kk"], func=AF.Square)), reads=[K_("kk")], writes=[K_("kq")])
                S.op("dve", (lambda e, ps=ps, pb=pb: e.tensor_copy(out=pb["r"], in_=ps[:, 0:128])), reads=[pk], writes=[K_("r")])
                ps2, pk2 = self.gps(n=128)
                S.op("pe", (lambda e, ps2=ps2, pb=pb: e.matmul(ps2, lhsT=blk, rhs=pb["kq"], start=True, stop=True)), reads=["rwc", K_("kq")], writes=[pk2])
                S.op("act", (lambda e, ps2=ps2, pb=pb: e.activation(out=pb["rn"], in_=ps2, func=AF.Sqrt)), reads=[pk2], writes=[K_("rn")])
                S.op("dve", (lambda e, pb=pb: e.tensor_scalar(out=pb["rn"], in0=pb["rn"], scalar1=1e-12, scalar2=None, op0=ALU.max)), reads=[K_("rn")], writes=[K_("rn")])
                S.op("dve", (lambda e, pb=pb: e.reciprocal(out=pb["rn"], in_=pb["rn"])), reads=[K_("rn")], writes=[K_("rn")])
                S.op("dve", (lambda e, pb=pb: e.tensor_tensor(out=pb["kk"], in0=pb["kk"], in1=pb["rn"], op=ALU.mult)), reads=[K_("kk"), K_("rn")], writes=[K_("kk")])
                S.op("dve", (lambda e, pb=pb: e.tensor_tensor_scan(out=pb["cum"], data0=rmask, data1=pb["sg"], initial=0.0, op0=ALU.mult, op1=ALU.add)), reads=["rwc", K_("sg")], writes=[K_("cum")])
                S.op("act", (lambda e, pb=pb: e.activation(out=pb["G"], in_=pb["cum"], func=AF.Exp, scale=-C0)), reads=[K_("cum")], writes=[K_("G")])
                S.op("act", (lambda e, pb=pb: e.activation(out=pb["Gi"], in_=pb["cum"], func=AF.Exp, scale=C0)), reads=[K_("cum")], writes=[K_("Gi")])
                S.op("dve", (lambda e, pb=pb: e.tensor_tensor(out=pb["ex"], in0=pb["cum"], in1=pb["sg"], op=ALU.subtract)), reads=[K_("cum"), K_("sg")], writes=[K_("ex")])
                S.op("act", (lambda e, pb=pb: e.activation(out=pb["ex"], in_=pb["ex"], func=AF.Exp, scale=-C0)), reads=[K_("ex")], writes=[K_("ex")])
                S.op("dve", (lambda e, pb=pb: e.scalar_tensor_tensor(out=pb["ATRT"][:, 0:128], in0=pb["kk"], scalar=-1.0, in1=pb["ex"], op0=ALU.mult, op1=ALU.mult)), reads=[K_("kk"), K_("ex")], writes=[K_("AT")])
                S.op("dve", (lambda e, pb=pb: e.tensor_tensor(out=pb["ab"], in0=pb["kk"], in1=pb["a"], op=ALU.mult)), reads=[K_("kk"), K_("a")], writes=[K_("ab")])
                S.op("dve", (lambda e, pb=pb: e.tensor_tensor(out=pb["BT"], in0=pb["ab"], in1=pb["Gi"], op=ALU.mult)), reads=[K_("ab"), K_("Gi")], writes=[K_("BT")])
                S.op("dve", (lambda e, pb=pb, vc=vc: e.tensor_scalar(out=pb["t1"], in0=pb["a"], scalar1=-1.0, scalar2=vc(9), op0=ALU.add, op1=ALU.mult)), reads=[K_("a"), "vec"], writes=[K_("t1")])
                S.op("dve", (lambda e, pb=pb: e.scalar_tensor_tensor(out=pb["km"], in0=pb["t1"], scalar=1.0, in1=pb["k"], op0=ALU.add, op1=ALU.mult)), reads=[K_("t1"), K_("k")], writes=[K_("km")])
                S.op("dve", (lambda e, pb=pb: e.tensor_tensor(out=pb["KT"], in0=pb["km"], in1=pb["Gi"], op=ALU.mult)), reads=[K_("km"), K_("Gi")], writes=[K_("KT")])
                S.op("dve", (lambda e, pb=pb: e.tensor_tensor(out=pb["ATRT"][:, 128:256], in0=pb["r"], in1=pb["G"], op=ALU.mult)), reads=[K_("r"), K_("G")], writes=[K_("RT")])
                for q in range(2):
                    S.op("dve", (lambda e, pb=pb, q=q: e.tensor_scalar(out=pb["E"][:, q * 64:(q + 1) * 64], in0=pb["cum"][:, q * 64:(q + 1) * 64], scalar1=pb["cum"][:, q * 64 + 63:q * 64 + 64], scalar2=None, op0=ALU.subtract)),
                         reads=[K_("cum")], writes=[K_("E")])
                S.op("act", (lambda e, pb=pb: e.activation(out=pb["E"], in_=pb["E"], func=AF.Exp, scale=C0)), reads=[K_("E")], writes=[K_("E")])
                S.op("dve", (lambda e, pb=pb: e.tensor_tensor(out=pb["BhT"], in0=pb["ab"], in1=pb["E"], op=ALU.mult)), reads=[K_("ab"), K_("E")], writes=[K_("BhT")])
                S.op("dve", (lambda e, pb=pb: e.tensor_tensor(out=pb["KhT"], in0=pb["km"], in1=pb["E"], op=ALU.mult)), reads=[K_("km"), K_("E")], writes=[K_("KhT")])
                ps3, pk3 = self.gps(n=256)
                S.op("pe", (lambda e, ps3=ps3, pb=pb: e.transpose(ps3[:, 0:128], pb["BhT"], self.identf[:])), reads=[K_("BhT"), "identf"], writes=[pk3])
                S.op("pe", (lambda e, ps3=ps3, pb=pb: e.transpose(ps3[:, 128:256], pb["KhT"], self.identf[:])), reads=[K_("KhT"), "identf"], writes=[pk3])
                S.op("act", (lambda e, ps3=ps3, pb=pb: e.copy(out=pb["BK"], in_=ps3)), reads=[pk3], writes=[K_("BK")])
                S.op("dve", (lambda e, pb=pb, vc=vc: e.scalar_tensor_tensor(out=pb["rkr"], in0=pb["km"], scalar=vc(10), in1=pb["r"], op0=ALU.mult, op1=ALU.mult)), reads=[K_("km"), K_("r"), "vec"], writes=[K_("rkr")])
                S.op("pe", (lambda e, pb=pb, c=c: e.matmul(coefps[:, 2 * c:2 * c + 2], lhsT=pb["rkr"], rhs=ind2, start=True, stop=True)), reads=[K_("rkr"), "rwc"], writes=[ckey])
                AT = pb["ATRT"][:, 0:128]
                RT = pb["ATRT"][:, 128:256]
                S = self.S
                NH = 2 if RWD >= 3 else 0
                st_ = []
                for hh in range(NH):
                    po = hh * 64
                    hb = HB[hh]
                    hn = f"hb{hh}"
                    pg = PS[2][:, hh * 512:(hh + 1) * 512]
                    pgk = ("PS2", hh)
                    S.op("pe", (lambda e, pb=pb, po=po, pg=pg: e.matmul(pg[:, 0:256], lhsT=pb["KT"][po:po + 64, :], rhs=pb["ATRT"][po:po + 64, :], start=True, stop=True)),
                         reads=[K_("KT"), K_("AT"), K_("RT")], writes=[pgk])
                    S.op("pe", (lambda e, pb=pb, po=po, pg=pg: e.matmul(pg[:, 256:512], lhsT=pb["BT"][po:po + 64, :], rhs=pb["ATRT"][po:po + 64, :], start=True, stop=True)),
                         reads=[K_("BT"), K_("AT"), K_("RT")], writes=[pgk])
                    S.op("dve", (lambda e, hb=hb, pg=pg: e.tensor_tensor(out=hb["Mk"], in0=pg[:, 0:256], in1=mask1, op=ALU.mult)), reads=[pgk, "rwc"], writes=[(hn, "Mk")])
                    S.op("dve", (lambda e, hb=hb, pg=pg: e.tensor_tensor(out=hb["Mb"], in0=pg[:, 256:512], in1=mask1, op=ALU.mult)), reads=[pgk, "rwc"], writes=[(hn, "Mb")])
                    S.op("pe", (lambda e, pb=pb, po=po, pg=pg: e.matmul(pg[:, 0:128], lhsT=pb["ATRT"][po:po + 64, 0:128], rhs=pb["BT"][po:po + 64, :], start=True, stop=True)),
                         reads=[K_("BT"), K_("AT")], writes=[pgk])
                    S.op("dve", (lambda e, hb=hb, pg=pg: e.tensor_tensor(out=hb["X0"], in0=pg[:, 0:128], in1=masksl, op=ALU.mult)), reads=[pgk, "rwc"], writes=[(hn, "X0")])
                    S.op("dve", (lambda e, hb=hb: e.tensor_tensor(out=hb["Z"][0], in0=hb["Mb"][:, 0:128], in1=self.identf[:], op=ALU.add)), reads=[(hn, "Mb"), "identf"], writes=[(hn, "Z0")])
                    st_.append({"XT": hb["Mb"][:, 0:128], "X": hb["X0"], "keys": [(hn, "Mb"), (hn, "X0")], "zi": 0})
                for lvl in range(5):
                    for hh in range(NH):
                        hb = HB[hh]
                        hn = f"hb{hh}"
                        bank = PS[2][:, hh * 512:(hh + 1) * 512]
                        bkey = ("PS2", hh)
                        sd = st_[hh]
                        X_ap, XT_ap, xk_keys, zi = sd["X"], sd["XT"], sd["keys"], sd["zi"]
                        xx = hb["XX"][lvl % 2]
                        xxk = (hn, f"XX{lvl % 2}")
                        if lvl < 4:
                            S.op("pe", (lambda e, bank=bank, X_ap=X_ap, XT_ap=XT_ap: e.matmul(bank[:, 0:128], lhsT=X_ap, rhs=XT_ap, start=True, stop=True)), reads=xk_keys, writes=[bkey])
                        S.op("pe", (lambda e, bank=bank, X_ap=X_ap, XT_ap=XT_ap: e.matmul(bank[:, 128:256], lhsT=XT_ap, rhs=X_ap, start=True, stop=True)), reads=xk_keys, writes=[bkey])
                        if lvl < 4:
                            S.op("act", (lambda e, bank=bank, xx=xx: e.copy(out=xx, in_=bank[:, 0:256])), reads=[bkey], writes=[xxk])
                        else:
                            S.op("act", (lambda e, bank=bank, xx=xx: e.copy(out=xx[:, 128:256], in_=bank[:, 128:256])), reads=[bkey], writes=[xxk])
                        XT_ap, X_ap = xx[:, 0:128], xx[:, 128:256]
                        zo = hb["Z"][zi]
                        zn = hb["Z"][1 - zi]
                        S.op("pe", (lambda e, bank=bank, X_ap=X_ap, zo=zo: e.matmul(bank[:, 256:384], lhsT=X_ap, rhs=zo, start=True, stop=True)), reads=[xxk, (hn, f"Z{zi}")], writes=[bkey])
                        S.op("dve", (lambda e, bank=bank, zo=zo, zn=zn: e.tensor_tensor(out=zn, in0=bank[:, 256:384], in1=zo, op=ALU.add)), reads=[bkey, (hn, f"Z{zi}")], writes=[(hn, f"Z{1 - zi}")])
                        sd["X"], sd["XT"], sd["keys"], sd["zi"] = X_ap, XT_ap, [xxk], 1 - zi
                for hh in range(NH):
                    HB[hh]["TT"] = HB[hh]["Z"][st_[hh]["zi"]]
                    HB[hh]["TTk"] = (f"hb{hh}", f"Z{st_[hh]['zi']}")
                for q in range(2 if RWD >= 4 else 0):
                    ph = q * 64
                    for hh in range(2):
                        po, h, hb, hn = hh * 64, 2 * c + hh, HB[hh], f"hb{hh}"
                        pq = PS[3][:, hh * 512:(hh + 1) * 512]
                        pqk = ("PS3", hh)
                        S.op("pe", (lambda e, pq=pq, hh=hh, c=c, pb=pb: e.matmul(pq[:, 0:64], lhsT=pb["ATRT"][:, 0:128], rhs=Hm[:, c, hh, :], start=True, stop=False)),
                             reads=[K_("AT"), (HK, h)], writes=[pqk])
                        S.op("pe", (lambda e, pq=pq, hb=hb, h=h: e.matmul(pq[:, 0:64], lhsT=hb["Mk"][:, 0:128], rhs=Vm[:, h * 64:(h + 1) * 64], start=False, stop=True)),
                             reads=[(hn, "Mk"), (VK, h // 8)], writes=[pqk])
                        S.op("act", (lambda e, pq=pq, ph=ph, hb=hb: e.copy(out=hb["Gs"][ph:ph + 64, :], in_=pq[ph:ph + 64, 0:64])), reads=[pqk], writes=[(hn, "Gs")])
                    for hh in range(2):
                        po, h, hb, hn = hh * 64, 2 * c + hh, HB[hh], f"hb{hh}"
                        pq = PS[3][:, hh * 512:(hh + 1) * 512]
                        pqk = ("PS3", hh)
                        S.op("pe", (lambda e, pq=pq, ph=ph, hb=hb: e.matmul(pq[:, 64:128], lhsT=hb["TT"][ph:ph + 64, :], rhs=hb["Gs"][ph:ph + 64, :], start=True, stop=True)),
                             reads=[hb["TTk"], (hn, "Gs")], writes=[pqk])
                        S.op("dve", (lambda e, pq=pq, ph=ph, hb=hb: e.tensor_copy(out=hb["Us"][ph:ph + 64, :], in_=pq[ph:ph + 64, 64:128])), reads=[pqk], writes=[(hn, "Us")])
                    for hh in range(2):
                        po, h, hb, hn = hh * 64, 2 * c + hh, HB[hh], f"hb{hh}"
                        pq = PS[3][:, hh * 512:(hh + 1) * 512]
                        pqk = ("PS3", hh)
                        vh = Vm[ph:ph + 64, h * 64:(h + 1) * 64]
                        S.op("pe", (lambda e, pq=pq, hh=hh, c=c, pb=pb: e.matmul(pq[:, 128:192], lhsT=pb["ATRT"][:, 128:256], rhs=Hm[:, c, hh, :], start=True, stop=False)),
                             reads=[K_("RT"), (HK, h)], writes=[pqk])
                        S.op("pe", (lambda e, pq=pq, hb=hb: e.matmul(pq[:, 128:192], lhsT=hb["Mb"][:, 128:256], rhs=hb["Us"][:, :], start=False, stop=False)),
                             reads=[(hn, "Mb"), (hn, "Us")], writes=[pqk])
                        S.op("pe", (lambda e, pq=pq, hb=hb, h=h: e.matmul(pq[:, 128:192], lhsT=hb["Mk"][:, 128:256], rhs=Vm[:, h * 64:(h + 1) * 64], start=False, stop=True)),
                             reads=[(hn, "Mk"), (VK, h // 8)], writes=[pqk])
                        S.op("pe", (lambda e, pq=pq, ph=ph, hb=hb, pb=pb: e.matmul(pq[:, 192:256], lhsT=pb["BK"][ph:ph + 64, 0:128], rhs=hb["Us"][ph:ph + 64, :], start=True, stop=False)),
                             reads=[K_("BK"), (hn, "Us")], writes=[pqk])
                        S.op("pe", (lambda e, pq=pq, ph=ph, pb=pb, vh=vh: e.matmul(pq[:, 192:256], lhsT=pb["BK"][ph:ph + 64, 128:256], rhs=vh, start=False, stop=True)),
                             reads=[K_("BK"), (VK, h // 8)], writes=[pqk])
                        S.op("act", (lambda e, pq=pq, ph=ph, h=h: e.copy(out=Y[ph:ph + 64, h * 64:(h + 1) * 64], in_=pq[ph:ph + 64, 128:192])), reads=[pqk, ("Ydone", 0), ("Ydone", 1)], writes=[("Y", h // 8, q, h)])
                        S.op("act", (lambda e, pq=pq, po=po, hh=hh: e.copy(out=Hd[hh][po:po + 64, :], in_=pq[po:po + 64, 192:256])), reads=[pqk], writes=[f"Hd{hh}"])
                        S.op("dve", (lambda e, po=po, c=c, pb=pb, q=q, hh=hh: e.scalar_tensor_tensor(out=Hst[po:po + 64, c, hh, :], in0=Hst[po:po + 64, c, hh, :], scalar=pb["G"][po:po + 64, q * 64 + 63:q * 64 + 64], in1=Hd[hh][po:po + 64, :], op0=ALU.mult, op1=ALU.add)),
                             reads=[f"Hd{hh}", ("H", h), K_("G")], writes=[("H", h)])
                        if RW != F32:
                            S.op("act", (lambda e, po=po, c=c, hh=hh: e.copy(out=Hm[po:po + 64, c, hh, :], in_=Hst[po:po + 64, c, hh, :])), reads=[("H", h)], writes=[("Hm", h)])
            ykeys = [("Y", h // 8, q, h) for h in range(16) for q in range(2)]
            Y3 = Y.rearrange("p (h d) -> p h d", h=16)
            bc = lambda ap: ap.unsqueeze(2).to_broadcast([128, 16, 64])
            S.op("act", (lambda e: e.copy(out=stt[:, 96:112], in_=coefps)), reads=[ckey], writes=["coef"])
            S.op("dve", (lambda e: e.tensor_reduce(out=stt[:, 0:16], in_=Y3, axis=AX.X, op=ALU.add)), reads=ykeys, writes=["st_s1"])
            S.op("act", (lambda e: e.activation(out=scr, in_=Y, func=AF.Square)), reads=ykeys, writes=["scr"])
            S.op("dve", (lambda e: e.tensor_reduce(out=stt[:, 16:32], in_=scr.rearrange("p (h d) -> p h d", h=16), axis=AX.X, op=ALU.add)), reads=["scr"], writes=["st_s2"])
            S.op("dve", (lambda e: e.tensor_scalar(out=stt[:, 32:48], in0=stt[:, 0:16], scalar1=1.0 / 64.0, scalar2=None, op0=ALU.mult)), reads=["st_s1"], writes=["st_mean"])
            S.op("dve", (lambda e: e.tensor_tensor(out=stt[:, 48:64], in0=stt[:, 32:48], in1=stt[:, 32:48], op=ALU.mult)), reads=["st_mean"], writes=["st_msq"])
            S.op("dve", (lambda e: e.scalar_tensor_tensor(out=stt[:, 64:80], in0=stt[:, 16:32], scalar=1.0 / 64.0, in1=stt[:, 48:64], op0=ALU.mult, op1=ALU.subtract)), reads=["st_s2", "st_msq"], writes=["st_var"])
            S.op("act", (lambda e: e.activation(out=stt[:, 80:96], in_=stt[:, 64:80], func=AF.Sqrt, bias=gneps, scale=1.0)), reads=["st_var", "rwc"], writes=["st_sd"])
            S.op("dve", (lambda e: e.reciprocal(out=stt[:, 80:96], in_=stt[:, 80:96])), reads=["st_sd"], writes=["st_rstd"])
            S.op("dve", (lambda e: e.tensor_tensor(out=Y3, in0=Y3, in1=bc(stt[:, 32:48]), op=ALU.subtract)), reads=ykeys + ["st_mean", "scr"], writes=["Yn"])
            S.op("dve", (lambda e: e.tensor_tensor(out=Y3, in0=Y3, in1=bc(stt[:, 80:96]), op=ALU.mult)), reads=["Yn", "st_rstd"], writes=["Yn"])
            S.op("dve", (lambda e: e.tensor_tensor(out=Y, in0=Y, in1=lxg, op=ALU.mult)), reads=["Yn", "lxg"], writes=["Yn"])
            S.op("dve", (lambda e: e.tensor_tensor(out=Y, in0=Y, in1=lxb, op=ALU.add)), reads=["Yn", "lxb"], writes=["Yn"])
            S.op("dve", (lambda e: e.tensor_tensor(out=scr.rearrange("p (h d) -> p h d", h=16), in0=V.rearrange("p (h d) -> p h d", h=16), in1=bc(stt[:, 96:112]), op=ALU.mult)),
                 reads=[("V", 0), ("V", 1), "coef", "st_s2"], writes=["scr"])
            S.op("dve", (lambda e: e.tensor_tensor(out=Y, in0=Y, in1=scr, op=ALU.add)), reads=["Yn", "scr"], writes=["Yn"])
            for half in range(2):
                ps, pk = self.gps()
                S.op("pe", (lambda e, ps=ps, half=half: e.matmul(ps, lhsT=sgl, rhs=g2b[:, half * 512:(half + 1) * 512], start=True, stop=True)), reads=["sgl", "g2b"], writes=[pk])
                S.op("dve", (lambda e, ps=ps, half=half: e.tensor_tensor(out=ob[:, half * 512:(half + 1) * 512], in0=Y[:, half * 512:(half + 1) * 512], in1=ps, op=ALU.mult)), reads=[pk, "Yn"], writes=[("ob", half), ("Ydone", half)])
            ps, pk = self.gps()
            pst = ps.bitcast(BF16)
            for cc in range(8):
                S.op("pe", (lambda e, cc=cc, pst=pst: e.transpose(pst[:, cc * 128:(cc + 1) * 128], ob[:, cc * 128:(cc + 1) * 128], self.ident[:])), reads=[("ob", cc // 4), "ident"], writes=[pk])
            S.op("act", (lambda e, pst=pst: e.copy(out=OT, in_=pst.rearrange("p (c t) -> p c t", c=8))), reads=[pk], writes=["OT"])
            py = PS[0]
            for half in range(2):
                for k in range(8):
                    S.op("pe", (lambda e, k=k, half=half: e.matmul(py[:, half * 512:(half + 1) * 512], lhsT=OT[:, k, :], rhs=wo[:, k, half * 512:(half + 1) * 512], start=(k == 0), stop=(k == 7))),
                         reads=["OT", "wo"], writes=[("PS0", half)])
            self.ln_finish(seq, tt, py[:, :], [("PS0", 0), ("PS0", 1)], first=first, last=False)
        self.force_pi = None
        self.gen_list = [(0, 0), (0, 1)]


    def stage_attn(self, seq, layer, j, first):
        S, A, PS = self.S, self.A, self.PS
        self.stage_begin()
        self.load_ln(layer * 3 + 0)
        wq = A.alloc([8, D], BF16)
        wk = A.alloc([8, D], BF16)
        wv = A.alloc([8, D], BF16)
        wo = A.alloc([8, D], BF16)
        biasb = A.alloc([16, 640], BF16)
        KT = A.alloc([8, 1024], BF16)
        V = A.alloc([8, D], BF16)
        QT = A.alloc([8, 512], BF16)
        OT = A.alloc([8, 512], BF16)
        Pb = [A.alloc([640], BF16) for _ in range(2)]
        PTs = [A.alloc([5, 128], BF16) for _ in range(2)]
        Ob = [A.alloc([D], BF16) for _ in range(2)]
        sm = [A.alloc([64], F32) for _ in range(2)]
        wqkv = self.P["ca_w_qkv"][j]
        self.load_w(wq, wqkv[:, 0:D], "wq")
        self.load_w(wk, wqkv[:, D:2 * D], "wk")
        self.load_w(wv, wqkv[:, 2 * D:3 * D], "wv")
        self.load_w(wo, self.P["ca_w_o"][j], "wo")
        S.dma("pool", biasb, self.P["ca_bias"], writes=["biasb"])
        nh = 0
        for t in range(4):
            hk = self.hTkeys(t * 512, 512)
            r0 = (t % 2) * 512
            for oc in range(8):
                ps, pk = self.gps()
                for k in range(8):
                    S.op("pe", (lambda e, k=k, oc=oc, ps=ps, t=t: e.matmul(ps, lhsT=wq[:, k, oc * 128:(oc + 1) * 128], rhs=self.hTs(k, t * 512, 512), start=(k == 0), stop=(k == 7))),
                         reads=["wq"] + hk, writes=[pk])
                S.op("act", (lambda e, oc=oc, ps=ps: e.activation(out=QT[:, oc, :], in_=ps, func=AF.Copy, scale=0.125)), reads=[pk], writes=[("QT", oc)])
                ps, pk = self.gps()
                for k in range(8):
                    S.op("pe", (lambda e, k=k, oc=oc, ps=ps, t=t: e.matmul(ps, lhsT=wk[:, k, oc * 128:(oc + 1) * 128], rhs=self.hTs(k, t * 512, 512), start=(k == 0), stop=(k == 7))),
                         reads=["wk"] + hk, writes=[pk])
                S.op("dve", (lambda e, oc=oc, ps=ps, r0=r0: e.tensor_copy(out=KT[:, oc, r0:r0 + 512], in_=ps)), reads=[pk], writes=[("KT", oc, t % 2)])
            for sub in range(4):
                slot = (4 * t + sub) % 8
                for half in range(2):
                    ps, pk = self.gps()
                    for k in range(8):
                        S.op("pe", (lambda e, k=k, half=half, ps=ps, t=t, sub=sub: e.matmul(ps, lhsT=self.hTs(k, t * 512 + sub * 128, 128), rhs=wv[:, k, half * 512:(half + 1) * 512], start=(k == 0), stop=(k == 7))),
                             reads=["wv"] + hk, writes=[pk])
                    S.op("act", (lambda e, half=half, ps=ps, slot=slot: e.copy(out=V[:, slot, half * 512:(half + 1) * 512], in_=ps)), reads=[pk], writes=[("V", slot, half)])
            for sub in range(4):
                qb = 4 * t + sub
                kbs = list(range(max(0, qb - 4), qb + 1))
                j0 = kbs[0] - (qb - 4)
                c0 = j0 * 128
                oi = qb % 2
                po_ps = PS[2]
                smi = sm[oi]
                ksm = f"sm{oi}"
                def emit_A(h, nh):
                        c, po = h // 2, (h % 2) * 64
                        si = nh % 2
                        sps = PS[0] if si == 0 else PS[3]
                        spn = "PS0" if si == 0 else "PS3"
                        for kb in kbs:
                            jj = kb - (qb - 4)
                            slot = kb % 8
                            S.op("pe", (lambda e, jj=jj, slot=slot, c=c, po=po, sps=sps, sub=sub: e.matmul(sps[:, jj * 128:(jj + 1) * 128], lhsT=QT[po:po + 64, c, sub * 128:(sub + 1) * 128], rhs=KT[po:po + 64, c, slot * 128:(slot + 1) * 128], start=True, stop=False)),
                                 reads=[("QT", c), ("KT", c, slot // 4)], writes=[(spn, jj // 4)])
                            S.op("pe", (lambda e, jj=jj, h=h, sps=sps: e.matmul(sps[:, jj * 128:(jj + 1) * 128], lhsT=self.ident[:], rhs=biasb[:, h, jj * 128:(jj + 1) * 128], start=False, stop=True)),
                                 reads=["biasb", "ident"], writes=[(spn, jj // 4)])
                        skeys = [(spn, 0), (spn, 1)]
                        S.op("dve", (lambda e, sps=sps, smi=smi, h=h, c0=c0: e.tensor_reduce(out=smi[:, 32 + h:33 + h], in_=sps[:, c0:640], axis=AX.X, op=ALU.max, negate=True)),
                             reads=skeys, writes=[(ksm + "m", h)])
                        pb_ = Pb[si]
                        S.op("act", (lambda e, sps=sps, smi=smi, h=h, c0=c0, pb_=pb_: e.activation(out=pb_[:, c0:640], in_=sps[:, c0:640], func=AF.Exp, bias=smi[:, 32 + h:33 + h], scale=1.0, accum_out=smi[:, h:h + 1])),
                             reads=skeys + [(ksm + "m", h)], writes=[f"Pb{si}", (ksm + "s", h)])

                def emit_B(h, nh):
                        c, po = h // 2, (h % 2) * 64
                        si = nh % 2
                        pb_ = Pb[si]
                        ptp = PS[1][:, 0:512].bitcast(BF16)
                        for kb in kbs:
                            jj = kb - (qb - 4)
                            S.op("pe", (lambda e, jj=jj, ptp=ptp, pb_=pb_: e.transpose(ptp[:, jj * 128:(jj + 1) * 128], pb_[:, jj * 128:(jj + 1) * 128], self.ident[:])),
                                 reads=[f"Pb{si}", "ident"], writes=[("PS1", 0)])
                        pts = PTs[si]
                        S.op("dve", (lambda e, ptp=ptp, pts=pts, j0=j0: e.tensor_copy(out=pts[:, j0:5, :], in_=ptp[:, j0 * 128:640].rearrange("p (a b) -> p a b", b=128))),
                             reads=[("PS1", 0)], writes=[f"PTs{si}"])
                        for kb in kbs:
                            jj = kb - (qb - 4)
                            slot = kb % 8
                            S.op("pe", (lambda e, jj=jj, slot=slot, h=h, pts=pts, po_ps=po_ps, kbs=kbs, kb=kb: e.matmul(po_ps[:, h * 64:(h + 1) * 64], lhsT=pts[:, jj, :], rhs=V[:, slot, h * 64:(h + 1) * 64], start=(kb == kbs[0]), stop=(kb == kbs[-1]))),
                                 reads=[f"PTs{si}", ("V", slot, h // 8)], writes=[("PS2", h // 8)])

                for hq in range(17):
                    if hq < 16:
                        emit_A(hq, nh + hq)
                    if hq >= 1:
                        emit_B(hq - 1, nh + hq - 1)
                nh += 16
                S.op("dve", (lambda e, smi=smi: e.reciprocal(out=smi[:, 16:32], in_=smi[:, 0:16])), reads=[(ksm + "s", h) for h in range(16)], writes=[ksm + "r"])
                ob = Ob[oi]
                S.op("dve", (lambda e, smi=smi, ob=ob, po_ps=po_ps: e.tensor_tensor(out=ob.rearrange("p (h d) -> p h d", h=16), in0=po_ps[:, :].rearrange("p (h d) -> p h d", h=16), in1=smi[:, 16:32].unsqueeze(2).to_broadcast([128, 16, 64]), op=ALU.mult)),
                     reads=[("PS2", 0), ("PS2", 1), ksm + "r"], writes=[f"Ob{oi}"])
                otp = PS[1][:, 512:1024].bitcast(BF16)
                for cc in range(8):
                    S.op("pe", (lambda e, cc=cc, otp=otp, ob=ob: e.transpose(otp[:, cc * 128:(cc + 1) * 128], ob[:, cc * 128:(cc + 1) * 128], self.ident[:])),
                         reads=[f"Ob{oi}", "ident"], writes=[("PS1", 1)])
                S.op("act", (lambda e, otp=otp, sub=sub: e.copy(out=OT[:, :, sub * 128:(sub + 1) * 128], in_=otp.rearrange("p (c t) -> p c t", c=8))),
                     reads=[("PS1", 1)], writes=[("OT", sub)])
            self.out_proj(seq, t, lambda k, sub: OT[:, k, sub * 128:(sub + 1) * 128], [("OT", s_) for s_ in range(4)], wo, "wo", 8, first=first)


def make_consts():
    c = np.zeros((128, 256), np.float32)
    c[:, 0:128] = np.eye(128, dtype=np.float32)
    c[:, 128] = LN_EPS
    return c


def make_rwc():
    c = np.zeros((128, 648), np.float32)
    s_ = np.arange(128)[:, None]
    t_ = np.arange(128)[None, :]
    same = (s_ // 64) == (t_ // 64)
    c[:, 0:128] = (same & (s_ < t_))
    c[:, 128:256] = (same & (s_ <= t_))
    c[:, 256:384] = (same & (s_ > t_))
    c[:, 384:512] = same
    c[:, 512:640] = (t_ % 64 != 0)
    c[0:64, 640] = 1.0
    c[64:128, 641] = 1.0
    c[:, 642] = 64e-5
    return c


def prep_shared(inp):
    sh = {}
    sh["ln_g"] = np.ascontiguousarray(inp["ln_g"].reshape(DEPTH * 3, D))
    sh["ln_b"] = np.ascontiguousarray(inp["ln_b"].reshape(DEPTH * 3, D))
    sh["lru_w_in"] = inp["lru_w_in"]
    na = inp["lru_w_in"].shape[0]
    vec = np.zeros((na, BS, NBLK, 8), np.float32)
    cw = inp["lru_conv_w"].reshape(na, 4, NBLK, BS)
    for k in range(4):
        vec[:, :, :, k] = cw[:, k].transpose(0, 2, 1)
    vec[:, :, :, 4] = inp["lru_conv_b"].reshape(na, NBLK, BS).transpose(0, 2, 1)
    gbv = inp["lru_gate_b"].reshape(na, 2, NBLK, BS)
    vec[:, :, :, 5] = gbv[:, 0].transpose(0, 2, 1)
    vec[:, :, :, 6] = gbv[:, 1].transpose(0, 2, 1)
    vec[:, :, :, 7] = inp["lru_lambda"].reshape(na, NBLK, BS).transpose(0, 2, 1)
    sh["lru_vec"] = vec
    sh["lru_gate_w"] = np.ascontiguousarray(inp["lru_gate_w"].transpose(0, 3, 1, 2, 4).reshape(na, BS, 2 * NBLK, BS))
    sh["lru_w_out"] = inp["lru_w_out"]
    for k in ("mx_w_q", "mx_w_kv", "mx_w_o", "mlp_w1", "mlp_w2", "ca_w_qkv", "ca_w_o"):
        sh[k] = inp[k]
    rb = inp["ca_rel_bias"][0]
    q = np.arange(64)[:, None]
    kk = np.arange(576)[None, :]
    idx = np.clip(512 + q - kk, -128, 128) + 128
    band = rb[:, idx]
    b2 = np.full((128, 16, 640), NEG, np.float32)
    b2[0:64, :, 0:576] = band.transpose(1, 0, 2)
    b2[64:128, :, 64:640] = band.transpose(1, 0, 2)
    sh["ca_bias"] = b2
    sh["consts"] = make_consts()
    for k in ("rw_w_r", "rw_w_k", "rw_w_v", "rw_w_o", "rw_w1", "rw_a1", "rw_g1", "rw_w2", "rw_a2", "rw_g2"):
        sh[k] = inp[k]
    nb = inp["rw_mu"].shape[0]
    rv = np.zeros((nb, 128, 8, 16), np.float32)
    fm = lambda v: v.reshape(nb, 8, 128).transpose(0, 2, 1)
    for m in range(6):
        rv[:, :, :, m] = fm(inp["rw_mu"][:, m])
    rv[:, :, :, 6] = fm(inp["rw_w0"])
    rv[:, :, :, 7] = fm(inp["rw_a0"])
    rv[:, :, :, 8] = fm(inp["rw_k_k"])
    rv[:, :, :, 9] = fm(inp["rw_k_a"])
    rv[:, :, :, 10] = fm(inp["rw_r_k"].reshape(nb, D))
    sh["rw_vec"] = rv
    sh["rw_lnx"] = np.ascontiguousarray(np.stack([inp["rw_lnx_g"], inp["rw_lnx_b"]], axis=1))
    sh["rwc"] = make_rwc()
    return sh


_NC_CACHE = {}


def get_nc(n_layers=DEPTH, dbg=None):
    key = (n_layers if isinstance(n_layers, int) else tuple(n_layers), tuple(sorted(dbg.items())) if dbg else None)
    if key not in _NC_CACHE:
        b = Builder(n_layers, dbg)
        nc = b.build()
        _NC_CACHE[key] = nc
    return _NC_CACHE[key]


def kernel(**inputs):
    inp = {k: np.ascontiguousarray(np.asarray(v, dtype=np.float32)) for k, v in inputs.items()}
    sh = prep_shared(inp)
    nc = get_nc()
    ncores = 8
    in_maps = []
    for c in range(ncores):
        m = dict(sh)
        m["x"] = np.ascontiguousarray(inp["x"][c * NSEQ:(c + 1) * NSEQ])
        m["mem"] = np.ascontiguousarray(inp["mem"][c * NSEQ:(c + 1) * NSEQ])
        in_maps.append(m)
    res = run_bass_kernel_spmd(nc, in_maps, core_ids=list(range(ncores)))
    out = np.concatenate([np.asarray(r["out"]).reshape(NSEQ, SEQ, D) for r in res.results], axis=0)
    return out.astype(np.float32)
```

```python
import numpy as np
import concourse.bass as bass
import concourse.mybir as mybir
from concourse.bass_utils import run_bass_kernel_spmd
from contextlib import ExitStack

F32 = mybir.dt.float32
BF16 = mybir.dt.bfloat16
AF = mybir.ActivationFunctionType
ALU = mybir.AluOpType
AX = mybir.AxisListType

ENGS = ("pe", "act", "dve", "pool", "sp")
SEM_EPOCH = 30000

D = 1024
SEQ = 2048
NSEQ = 2
DEPTH = 4
ALPHA = (2 * DEPTH) ** 0.25
LN_EPS = 1e-5
D_RNN = 1344
NBLK = 16
BS = 84
D_FF = 4096
MEMT = 256
NEG = -30000.0


class Sched:
    def __init__(self, nc, stack, n_dma_sems=8):
        self.nc = nc
        self.stack = stack
        self.prog = {e: [] for e in ENGS}
        self.cnt = {e: 0 for e in ENGS}
        self.nsem = 0
        self.sem_owner = {}
        self.sem = {}
        for e in ENGS:
            self.sem[e] = self._newsem()
            self.sem_owner[id(self.sem[e])] = e
        self.known = {e: {} for e in ENGS}
        self.lastw = {}
        self.readers = {}
        self.dma_sems = {q: [[self._newsem(), 0] for _ in range(n_dma_sems)] for q in ("sp", "act", "pool")}
        self.dma_rr = {q: 0 for q in ("sp", "act", "pool")}
        self.ninstr = 0
        self.nwait = 0

    def _newsem(self):
        self.nsem += 1
        return self.stack.enter_context(self.nc.semaphore(f"s{self.nsem}"))

    def _deps(self, eng, reads, writes):
        deps = {}

        def add(ev, raw):
            if ev is None:
                return
            s, v = ev
            own = self.sem_owner.get(id(s))
            if own == eng:
                if eng == "pe":
                    return
            k = id(s)
            if k not in deps or deps[k][1] < v:
                deps[k] = (s, v)

        for k in reads:
            add(self.lastw.get(k), True)
        for k in writes:
            add(self.lastw.get(k), False)
            rd = self.readers.get(k)
            if rd:
                for ev in rd.values():
                    add(ev, False)
        return deps

    def _filter(self, eng, deps):
        waits = []
        kn = self.known[eng]
        for k, (s, v) in deps.items():
            if kn.get(k, 0) < v:
                kn[k] = v
                waits.append((s, v))
        self.nwait += len(waits)
        return waits

    def _record(self, ev, reads, writes):
        for k in writes:
            self.lastw[k] = ev
            self.readers[k] = {}
        for k in reads:
            r = self.readers.setdefault(k, {})
            r[id(ev[0])] = ev

    def op(self, eng, fn, reads=(), writes=()):
        deps = self._deps(eng, reads, writes)
        waits = self._filter(eng, deps)
        if self.cnt[eng] >= SEM_EPOCH:
            self.sem[eng] = self._newsem()
            self.sem_owner[id(self.sem[eng])] = eng
            self.cnt[eng] = 0
        self.cnt[eng] += 1
        ev = (self.sem[eng], self.cnt[eng])
        self.prog[eng].append((waits, fn, (self.sem[eng], 1)))
        self._record(ev, reads, writes)
        self.ninstr += 1
        return ev

    def dma(self, q, out, in_, reads=(), writes=()):
        slots = self.dma_sems[q]
        si = self.dma_rr[q]
        self.dma_rr[q] = (si + 1) % len(slots)
        slot = slots[si]
        deps = self._deps(q, reads, writes)
        if slot[1] > 0:
            deps[id(slot[0])] = (slot[0], 16 * slot[1])
        waits = self._filter(q, deps)
        slot[1] += 1
        ev = (slot[0], 16 * slot[1])
        self.prog[q].append((waits, (lambda e: e.dma_start(out=out, in_=in_)), (slot[0], 16)))
        self._record(ev, reads, writes)
        self.ninstr += 1
        return ev

    def all_events(self):
        evs = []
        for e in ENGS:
            if self.cnt[e] > 0:
                evs.append((self.sem[e], self.cnt[e]))
        for q in self.dma_sems:
            for s, n in self.dma_sems[q]:
                if n > 0:
                    evs.append((s, 16 * n))
        return evs

    def barrier(self):
        evs = self.all_events()
        for e in ENGS:
            deps = {}
            for s, v in evs:
                if self.sem_owner.get(id(s)) == e:
                    continue
                deps[id(s)] = (s, v)
            waits = self._filter(e, deps)
            if waits:
                self.prog[e].append((waits, None, None))
        self.lastw = {}
        self.readers = {}

    def emit(self):
        nc = self.nc
        prog = self.prog

        def run(name, e):
            for waits, fn, inc in prog[name]:
                for s, v in waits:
                    e.wait_ge(s, v)
                if fn is not None:
                    ins = fn(e)
                    if inc is not None:
                        ins.then_inc(inc[0], inc[1])

        with nc.Block() as block:
            @block.tensor
            def _(e):
                run("pe", e)

            @block.scalar
            def _(e):
                run("act", e)

            @block.vector
            def _(e):
                run("dve", e)

            @block.gpsimd
            def _(e):
                run("pool", e)

            @block.sync
            def _(e):
                run("sp", e)


class Arena:
    def __init__(self, t, nwords):
        self.t = t
        self.n = nwords
        self.off = 0

    def mark(self):
        return self.off

    def reset(self, m):
        self.off = m

    def alloc(self, shape, dtype, parts=128):
        free = int(np.prod(shape))
        words = free if dtype == F32 else (free + 1) // 2
        words = (words + 1) // 2 * 2
        assert self.off + words <= self.n, f"arena overflow {self.off}+{words}>{self.n}"
        ap = self.t[0:parts, self.off:self.off + words]
        self.off += words
        if dtype != F32:
            ap = ap.bitcast(dtype)
        ap = ap[:, 0:free]
        if len(shape) == 2:
            ap = ap.rearrange("p (a b) -> p a b", a=shape[0])
        elif len(shape) == 3:
            ap = ap.rearrange("p (a b c) -> p a b c", a=shape[0], b=shape[1])
        return ap


_uid = [0]


def uid(p="k"):
    _uid[0] += 1
    return f"{p}{_uid[0]}"


class Builder:
    def __init__(self, n_layers=DEPTH, dbg=None):
        self.layers = list(range(n_layers)) if isinstance(n_layers, int) else list(n_layers)
        self.dbg = dbg
        nc = self.nc = bass.Bass("TRN2", target_bir_lowering=False)
        self.st = ExitStack()
        self.S = Sched(nc, self.st)

    def din(self, name, shape):
        return self.nc.dram_tensor(name, list(shape), F32, kind="ExternalInput").ap()

    def build(self):
        nc, st, S = self.nc, self.st, self.S
        P = self.P = {}
        P["x"] = self.din("x", [NSEQ, SEQ, D])
        P["mem"] = self.din("mem", [NSEQ, MEMT, D])
        P["ln_g"] = self.din("ln_g", [DEPTH * 3, D])
        P["ln_b"] = self.din("ln_b", [DEPTH * 3, D])
        P["lru_w_in"] = self.din("lru_w_in", [2, D, 2 * D_RNN])
        P["lru_vec"] = self.din("lru_vec", [2, BS, NBLK, 8])
        P["lru_gate_w"] = self.din("lru_gate_w", [2, BS, 2 * NBLK, BS])
        P["lru_w_out"] = self.din("lru_w_out", [2, D_RNN, D])
        P["mx_w_q"] = self.din("mx_w_q", [DEPTH, D, D])
        P["mx_w_kv"] = self.din("mx_w_kv", [DEPTH, D, 2 * D])
        P["mx_w_o"] = self.din("mx_w_o", [DEPTH, D, D])
        P["mlp_w1"] = self.din("mlp_w1", [DEPTH, D, D_FF])
        P["mlp_w2"] = self.din("mlp_w2", [DEPTH, D_FF, D])
        P["ca_w_qkv"] = self.din("ca_w_qkv", [1, D, 3 * D])
        P["ca_w_o"] = self.din("ca_w_o", [1, D, D])
        P["ca_bias"] = self.din("ca_bias", [128, 16, 640])
        P["consts"] = self.din("consts", [128, 256])
        for nm in ("rw_w_r", "rw_w_k", "rw_w_v", "rw_w_o"):
            P[nm] = self.din(nm, [1, D, D])
        P["rw_w1"] = self.din("rw_w1", [1, D, 64])
        P["rw_a1"] = self.din("rw_a1", [1, D, 64])
        P["rw_g1"] = self.din("rw_g1", [1, D, 128])
        P["rw_w2"] = self.din("rw_w2", [1, 64, D])
        P["rw_a2"] = self.din("rw_a2", [1, 64, D])
        P["rw_g2"] = self.din("rw_g2", [1, 128, D])
        P["rw_vec"] = self.din("rw_vec", [1, 128, 8, 16])
        P["rw_lnx"] = self.din("rw_lnx", [1, 2, D])
        P["rwc"] = self.din("rwc", [128, 648])
        self.out = nc.dram_tensor("out", [NSEQ, SEQ, D], F32, kind="ExternalOutput").ap()
        self.h32 = nc.dram_tensor("h32", [NSEQ, SEQ, D], F32, kind="Internal").ap()
        if self.dbg:
            self.dbg_out = {n: nc.dram_tensor(n, list(shp), F32, kind="ExternalOutput").ap() for n, shp in self.dbg.items()}

        sb = lambda n, s, d: st.enter_context(nc.sbuf_tensor(n, s, d))
        self.hT = sb("hT", [128, 8, SEQ + 2], BF16)
        self.ident = sb("ident", [128, 128], BF16)
        self.identf = sb("identf", [128, 128], F32)
        self.cst = sb("cst", [128, 256], F32)
        self.lng = sb("lng", [128, D], F32)
        self.lnb = sb("lnb", [128, D], F32)
        self.lnz = [sb("lnz0", [128, D], F32)] * 2
        self.lnh = [sb(f"lnh{i}", [128, D], F32) for i in range(2)]
        self.lnhb = [sb(f"lnhb{i}", [128, D], BF16) for i in range(2)]
        self.lnst = [sb(f"lnst{i}", [128, 16], F32) for i in range(2)]
        self.lni = 0
        ARW = 37600
        self.arena_t = sb("arena", [128, ARW], F32)
        self.A = Arena(self.arena_t, ARW)
        self.PS = [st.enter_context(nc.psum_tensor(f"ps{i}", [128, 1024], F32)) for i in range(4)]
        self.out_events = []
        self.gi = 0
        self.gen_list = [(0, 0), (0, 1)]

        S.dma("sp", self.cst[:], P["consts"], writes=["cst"])
        S.op("dve", lambda e: e.tensor_copy(out=self.ident[:], in_=self.cst[:, 0:128]), reads=["cst"], writes=["ident"])
        S.op("dve", lambda e: e.tensor_copy(out=self.identf[:], in_=self.cst[:, 0:128]), reads=["cst"], writes=["identf"])
        S.op("pool", lambda e: e.memset(self.hT[:, :, 0:2], 0.0), writes=["hTpad"])

        for seq in range(NSEQ):
            self.load_x(seq)
            for li, layer in enumerate(self.layers):
                kind, j = layer % 3, layer // 3
                first = (li == 0)
                if kind == 0:
                    self.stage_lru(seq, layer, j, first)
                elif kind == 1:
                    self.stage_rwkv(seq, layer, j, first)
                else:
                    self.stage_attn(seq, layer, j, first)
                self.stage_memattn(seq, layer)
                self.stage_mlp(seq, layer, last=(li == len(self.layers) - 1))
        S.barrier()
        S.emit()
        self.st.close()
        return nc

    def stage_begin(self):
        self.S.barrier()
        self.A.reset(0)

    def load_w(self, dst, src, key, q="pool"):
        K, nk, ncols = dst.shape
        step = max(1, 2048 // K) if ncols * 4 >= 2048 else nk
        step = min(nk, max(1, (1 << 21) // (K * ncols * 4)))
        for k0 in range(0, nk, step):
            k1 = min(nk, k0 + step)
            self.S.dma(q, dst[:, k0:k1, :], src[k0 * K:k1 * K, :].rearrange("(k p) n -> p k n", p=K), writes=[key])

    def load_x(self, seq):
        S = self.S
        self.stage_begin()
        xb = [self.A.alloc([D], BF16) for _ in range(2)]
        for sub in range(SEQ // 128):
            b = xb[sub % 2]
            kb = f"xb{sub % 2}"
            S.dma("pool", b, self.P["x"][seq, sub * 128:(sub + 1) * 128, :], writes=[kb])
            self.to_hT(b, kb, sub)

    def to_hT(self, hb, kb, sub, pi=None):
        S = self.S
        deferred = pi is not None
        if pi is None:
            pi = self.lni % 2 if getattr(self, "force_pi", None) is None else self.force_pi
        ps = self.PS[1][:, pi * 512:(pi + 1) * 512].bitcast(BF16)
        pk = ("PS1", pi)
        for c in range(8):
            S.op("pe", (lambda e, c=c: e.transpose(ps[:, c * 128:(c + 1) * 128], hb[:, c * 128:(c + 1) * 128], self.ident[:])),
                 reads=[kb, "ident"], writes=[pk])
        dst = self.hT[:, :, 2 + sub * 128: 2 + (sub + 1) * 128]
        src = ps.rearrange("p (c t) -> p c t", c=8)
        S.op("dve", lambda e: e.tensor_copy(out=dst, in_=src), reads=[pk], writes=[("hT", sub)])
        if not deferred:
            self.lni += 1

    def load_ln(self, li):
        S = self.S
        S.dma("sp", self.lng[:], self.P["ln_g"][li, :].partition_broadcast(128), writes=["lng"])
        S.dma("sp", self.lnb[:], self.P["ln_b"][li, :].partition_broadcast(128), writes=["lnb"])

    def ln_finish(self, seq, sub, y, ykeys, first, last, defer=False):
        S = self.S
        i = self.lni % 2
        z, hn, hb, stt = self.lnz[i], self.lnh[i], self.lnhb[i], self.lnst[i]
        kz, kh, khb, kst = "lnz0", f"lnh{i}", f"lnhb{i}", f"lnst{i}"
        src = (self.P["x"] if first else self.h32)[seq, sub * 128:(sub + 1) * 128, :]
        hk = ("h32", seq, sub)
        S.dma("sp", hn[:], src, reads=[hk], writes=[kh])
        S.op("dve", lambda e: e.scalar_tensor_tensor(out=z[:], in0=hn[:], scalar=float(ALPHA), in1=y, op0=ALU.mult, op1=ALU.add),
             reads=[kh] + list(ykeys), writes=[kz])
        S.op("dve", lambda e: e.bn_stats(out=stt[:, 0:6], in_=z[:, 0:512]), reads=[kz], writes=[kst + "a"])
        S.op("dve", lambda e: e.bn_stats(out=stt[:, 6:12], in_=z[:, 512:1024]), reads=[kz], writes=[kst + "b"])
        S.op("dve", lambda e: e.bn_aggr(out=stt[:, 12:14], in_=stt[:, 0:12]), reads=[kst + "a", kst + "b"], writes=[kst + "c"])
        S.op("act", lambda e: e.activation(out=stt[:, 14:15], in_=stt[:, 13:14], func=AF.Sqrt, bias=self.cst[:, 128:129], scale=1.0),
             reads=[kst + "c"], writes=[kst + "d0"])
        S.op("dve", lambda e: e.reciprocal(out=stt[:, 14:15], in_=stt[:, 14:15]), reads=[kst + "d0"], writes=[kst + "d"])
        S.op("dve", lambda e: e.tensor_scalar(out=stt[:, 15:16], in0=stt[:, 12:13], scalar1=stt[:, 14:15], scalar2=-1.0, op0=ALU.mult, op1=ALU.mult),
             reads=[kst + "c", kst + "d"], writes=[kst + "e"])
        S.op("act", lambda e: e.activation(out=z[:], in_=z[:], func=AF.Identity, bias=stt[:, 15:16], scale=stt[:, 14:15]),
             reads=[kz, kst + "d", kst + "e"], writes=[kz])
        S.op("dve", lambda e: e.tensor_tensor(out=z[:], in0=z[:], in1=self.lng[:], op=ALU.mult), reads=[kz, "lng"], writes=[kz])
        S.op("dve", lambda e: e.tensor_tensor(out=hn[:], in0=z[:], in1=self.lnb[:], op=ALU.add), reads=[kz, "lnb"], writes=[kh])
        dst = (self.out if last else self.h32)[seq, sub * 128:(sub + 1) * 128, :]
        S.dma("sp", dst, hn[:], reads=[kh], writes=[hk])
        if not last:
            S.op("act", lambda e: e.copy(out=hb[:], in_=hn[:]), reads=[kh], writes=[khb])
            if defer:
                pi = self.lni % 2 if getattr(self, "force_pi", None) is None else self.force_pi
                self.lni += 1
                return (hb, khb, sub, pi)
            self.to_hT(hb, khb, sub)
        else:
            self.lni += 1
        return None

    def gps(self, parts=128, n=512):
        i = self.gi
        self.gi += 1
        b = self.gen_list[i % len(self.gen_list)]
        return self.PS[b[0]][0:parts, b[1] * 512: b[1] * 512 + n], (f"PS{b[0]}", b[1])

    def hTs(self, k, t0, n):
        return self.hT[:, k, 2 + t0: 2 + t0 + n]

    def hTkeys(self, t0, n):
        return [("hT", s) for s in range(t0 // 128, (t0 + n + 127) // 128)]

    def dbg_store(self, name, ap, keys, dst_slice=None):
        if self.dbg and name in self.dbg:
            d = self.dbg_out[name] if dst_slice is None else dst_slice(self.dbg_out[name])
            self.S.dma("sp", d, ap, reads=keys)

    def stage_mlp(self, seq, layer, last):
        S, A, PS = self.S, self.A, self.PS
        self.stage_begin()
        self.load_ln(layer * 3 + 2)
        acc = A.alloc([16, D], F32)
        w1g = [A.alloc([8, 512], BF16) for _ in range(2)]
        w2g = [A.alloc([4, D], BF16) for _ in range(2)]
        hid = [A.alloc([4, 512], BF16) for _ in range(2)]
        rl = [A.alloc([512], F32) for _ in range(2)]
        w1 = self.P["mlp_w1"][layer]
        w2 = self.P["mlp_w2"][layer]
        nb = 0
        ny = 0
        for g in range(8):
            gi = g % 2
            self.load_w(w1g[gi], w1[:, g * 512:(g + 1) * 512], f"w1g{gi}")
            self.load_w(w2g[gi], w2[g * 512:(g + 1) * 512, :], f"w2g{gi}")
            for t in range(4):
                hi = (g * 4 + t) % 2
                for fc in range(4):
                    nb += 1
                    ps, pk = self.gps()
                    for k in range(8):
                        S.op("pe", (lambda e, k=k, fc=fc, ps=ps, gi=gi, t=t: e.matmul(ps, lhsT=w1g[gi][:, k, fc * 128:(fc + 1) * 128], rhs=self.hTs(k, t * 512, 512), start=(k == 0), stop=(k == 7))),
                             reads=[f"w1g{gi}"] + self.hTkeys(t * 512, 512), writes=[pk])
                    ri = nb % 2
                    S.op("act", (lambda e, ps=ps, ri=ri: e.activation(out=rl[ri], in_=ps, func=AF.Relu)), reads=[pk], writes=[f"rl{ri}"])
                    S.op("dve", (lambda e, ri=ri, hi=hi, fc=fc: e.tensor_tensor(out=hid[hi][:, fc, :], in0=rl[ri], in1=rl[ri], op=ALU.mult)),
                         reads=[f"rl{ri}"], writes=[(f"hid{hi}", fc)])
                for sub in range(4):
                    yi = ny % 2
                    ny += 1
                    py = PS[2 + yi]
                    for half in range(2):
                        for fc in range(4):
                            S.op("pe", (lambda e, fc=fc, half=half, py=py, hi=hi, gi=gi, sub=sub: e.matmul(py[:, half * 512:(half + 1) * 512], lhsT=hid[hi][:, fc, sub * 128:(sub + 1) * 128], rhs=w2g[gi][:, fc, half * 512:(half + 1) * 512], start=(fc == 0), stop=(fc == 3))),
                                 reads=[(f"hid{hi}", fc), f"w2g{gi}"], writes=[(f"PS{2 + yi}", half)])
                    a = acc[:, t * 4 + sub, :]
                    ak = ("acc", t * 4 + sub)
                    pkeys = [(f"PS{2 + yi}", 0), (f"PS{2 + yi}", 1)]
                    if g == 0:
                        S.op("act", (lambda e, a=a, py=py: e.copy(out=a, in_=py[:, :])), reads=pkeys, writes=[ak])
                    else:
                        S.op("dve", (lambda e, a=a, py=py: e.tensor_tensor(out=a, in0=a, in1=py[:, :], op=ALU.add)), reads=pkeys + [ak], writes=[ak])
        for s16 in range(16):
            self.ln_finish(seq, s16, acc[:, s16, :], [("acc", s16)], first=False, last=last)

    def stage_memattn(self, seq, layer):
        S, A, PS = self.S, self.A, self.PS
        self.stage_begin()
        self.load_ln(layer * 3 + 1)
        wb = [A.alloc([8, D], BF16) for _ in range(2)]
        memb = A.alloc([2, D], BF16)
        memT = A.alloc([8, MEMT], BF16)
        KT = A.alloc([8, MEMT], BF16)
        V = A.alloc([2, D], BF16)
        QT = A.alloc([8, 512], BF16)
        Pn = [A.alloc([4, MEMT], BF16) for _ in range(2)]
        PT = A.alloc([8, 512], BF16)
        OT = A.alloc([8, 512], BF16)
        sm = [A.alloc([16], F32) for _ in range(2)]
        wkv = self.P["mx_w_kv"][layer]
        self.load_w(wb[0], wkv[:, 0:D], "wb0")
        self.load_w(wb[1], wkv[:, D:2 * D], "wb1")
        S.dma("pool", memb, self.P["mem"][seq].rearrange("(a p) d -> p a d", p=128), writes=["memb"])
        for mt in range(2):
            ps = PS[0][:, mt * 512:(mt + 1) * 512].bitcast(BF16)
            for c in range(8):
                S.op("pe", (lambda e, c=c, mt=mt, ps=ps: e.transpose(ps[:, c * 128:(c + 1) * 128], memb[:, mt, c * 128:(c + 1) * 128], self.ident[:])),
                     reads=["memb", "ident"], writes=[("PS0", mt)])
            S.op("dve", (lambda e, mt=mt, ps=ps: e.tensor_copy(out=memT[:, :, mt * 128:(mt + 1) * 128], in_=ps.rearrange("p (c t) -> p c t", c=8))),
                 reads=[("PS0", mt)], writes=[("memT", mt)])
        mk = [("memT", 0), ("memT", 1)]
        for oc in range(8):
            ps, pk = self.gps(n=MEMT)
            for k in range(8):
                S.op("pe", (lambda e, k=k, oc=oc, ps=ps: e.matmul(ps, lhsT=wb[0][:, k, oc * 128:(oc + 1) * 128], rhs=memT[:, k, :], start=(k == 0), stop=(k == 7))),
                     reads=["wb0"] + mk, writes=[pk])
            S.op("act", (lambda e, oc=oc, ps=ps: e.copy(out=KT[:, oc, :], in_=ps)), reads=[pk], writes=[("KT", oc)])
        for mt in range(2):
            for half in range(2):
                ps, pk = self.gps()
                for k in range(8):
                    S.op("pe", (lambda e, k=k, mt=mt, half=half, ps=ps: e.matmul(ps, lhsT=memT[:, k, mt * 128:(mt + 1) * 128], rhs=wb[1][:, k, half * 512:(half + 1) * 512], start=(k == 0), stop=(k == 7))),
                         reads=["wb1"] + mk, writes=[pk])
                S.op("dve", (lambda e, mt=mt, half=half, ps=ps: e.tensor_copy(out=V[:, mt, half * 512:(half + 1) * 512], in_=ps)), reads=[pk], writes=[("V", mt, half)])
        vk = [("V", a, b) for a in range(2) for b in range(2)]
        self.load_w(wb[0], self.P["mx_w_q"][layer], "wb0")
        self.load_w(wb[1], self.P["mx_w_o"][layer], "wb1")
        for t in range(4):
            for oc in range(8):
                ps, pk = self.gps()
                for k in range(8):
                    S.op("pe", (lambda e, k=k, oc=oc, ps=ps, t=t: e.matmul(ps, lhsT=wb[0][:, k, oc * 128:(oc + 1) * 128], rhs=self.hTs(k, t * 512, 512), start=(k == 0), stop=(k == 7))),
                         reads=["wb0"] + self.hTkeys(t * 512, 512), writes=[pk])
                S.op("act", (lambda e, oc=oc, ps=ps: e.activation(out=QT[:, oc, :], in_=ps, func=AF.Copy, scale=1.0 / 16.0)), reads=[pk], writes=[("QT", oc)])
            def mem_A(sub):
                pi = sub % 2
                ps = PS[0] if pi == 0 else PS[3]
                spn = "PS0" if pi == 0 else "PS3"
                pkeys = [(spn, 0), (spn, 1)]
                for h in range(4):
                    for c in range(2):
                        S.op("pe", (lambda e, h=h, c=c, ps=ps, sub=sub: e.matmul(ps[:, h * 256:(h + 1) * 256], lhsT=QT[:, 2 * h + c, sub * 128:(sub + 1) * 128], rhs=KT[:, 2 * h + c, :], start=(c == 0), stop=(c == 1))),
                             reads=[("QT", 2 * h + c), ("KT", 2 * h + c)], writes=[(spn, h // 2)])
                smi = sm[pi]
                ks = f"sm{pi}"
                S.op("dve", (lambda e, ps=ps, smi=smi: e.tensor_reduce(out=smi[:, 0:4], in_=ps[:, :].rearrange("p (h m) -> p h m", h=4), axis=AX.X, op=ALU.max, negate=True)),
                     reads=pkeys, writes=[ks + "m"])
                pn = Pn[pi]
                for h in range(4):
                    S.op("act", (lambda e, h=h, ps=ps, smi=smi, pn=pn: e.activation(out=pn[:, h, :], in_=ps[:, h * 256:(h + 1) * 256], func=AF.Exp, bias=smi[:, h:h + 1], scale=1.0, accum_out=smi[:, 4 + h:5 + h])),
                         reads=[(spn, h // 2), ks + "m"], writes=[(f"Pn{pi}", h), (ks + "s", h)])
                S.op("dve", (lambda e, smi=smi: e.reciprocal(out=smi[:, 8:12], in_=smi[:, 4:8])), reads=[(ks + "s", h) for h in range(4)], writes=[ks + "r"])
                S.op("dve", (lambda e, smi=smi, pn=pn: e.tensor_tensor(out=pn[:, :, :], in0=pn[:, :, :], in1=smi[:, 8:12].unsqueeze(2).to_broadcast([128, 4, MEMT]), op=ALU.mult)),
                     reads=[(f"Pn{pi}", h) for h in range(4)] + [ks + "r"], writes=[(f"Pn{pi}", h) for h in range(4)])

            def mem_B(sub):
                pi = sub % 2
                pn = Pn[pi]
                pt = PS[1][:, pi * 512:(pi + 1) * 512].bitcast(BF16)
                ptk = ("PS1", pi)
                for h in range(4):
                    for mc in range(2):
                        S.op("pe", (lambda e, h=h, mc=mc, pt=pt, pn=pn: e.transpose(pt[:, (h * 2 + mc) * 128:(h * 2 + mc + 1) * 128], pn[:, h, mc * 128:(mc + 1) * 128], self.ident[:])),
                             reads=[(f"Pn{pi}", h), "ident"], writes=[ptk])
                S.op("dve", (lambda e, pt=pt, sub=sub: e.tensor_copy(out=PT[:, :, sub * 128:(sub + 1) * 128], in_=pt.rearrange("p (c t) -> p c t", c=8))),
                     reads=[ptk], writes=[("PT", sub)])

            for sq_ in range(5):
                if sq_ < 4:
                    mem_A(sq_)
                if sq_ >= 1:
                    mem_B(sq_ - 1)
            for oc in range(8):
                h, c = oc // 2, oc % 2
                ps, pk = self.gps()
                for mc in range(2):
                    S.op("pe", (lambda e, h=h, c=c, mc=mc, ps=ps: e.matmul(ps, lhsT=V[:, mc, h * 256 + c * 128: h * 256 + (c + 1) * 128], rhs=PT[:, h * 2 + mc, :], start=(mc == 0), stop=(mc == 1))),
                         reads=vk + [("PT", s) for s in range(4)], writes=[pk])
                S.op("act", (lambda e, oc=oc, ps=ps: e.copy(out=OT[:, oc, :], in_=ps)), reads=[pk], writes=[("OT", oc)])
            self.out_proj(seq, t, lambda k, sub: OT[:, k, sub * 128:(sub + 1) * 128], [("OT", k) for k in range(8)], wb[1], "wb1", 8, first=False)

    def out_proj(self, seq, t, lhs_fn, lkeys, w, wkey, nk, first):
        S, PS = self.S, self.PS
        pend = None
        for sub in range(4):
            yi = sub % 2
            py = PS[2 + yi]
            for half in range(2):
                for k in range(nk):
                    S.op("pe", (lambda e, k=k, half=half, py=py, sub=sub: e.matmul(py[:, half * 512:(half + 1) * 512], lhsT=lhs_fn(k, sub), rhs=w[:, k, half * 512:(half + 1) * 512], start=(k == 0), stop=(k == nk - 1))),
                         reads=list(lkeys) + [wkey], writes=[(f"PS{2 + yi}", half)])
            if pend is not None:
                self.to_hT(*pend)
            pend = self.ln_finish(seq, t * 4 + sub, py[:, :], [(f"PS{2 + yi}", 0), (f"PS{2 + yi}", 1)], first=first, last=False, defer=True)
        if pend is not None:
            self.to_hT(*pend)

    def stage_lru(self, seq, layer, j, first):
        S, A, PS = self.S, self.A, self.PS
        self.stage_begin()
        self.load_ln(layer * 3 + 0)
        win = A.alloc([8, 2 * D_RNN], BF16)
        wout = A.alloc([NBLK, D], BF16, parts=BS)
        gw = A.alloc([2 * NBLK, BS], BF16, parts=BS)
        vec = A.alloc([NBLK, 8], F32, parts=BS)
        c8 = A.alloc([NBLK], F32, parts=BS)
        carry = A.alloc([NBLK], F32, parts=BS)
        xpb = [A.alloc([516], F32, parts=BS) for _ in range(2)]
        hist = A.alloc([NBLK, 4], F32, parts=BS)
        mT = A.alloc([NBLK, 512], BF16, parts=BS)
        NB2 = 2
        gb = [A.alloc([512], F32, parts=BS) for _ in range(NB2)]
        xr = [A.alloc([512], F32, parts=BS) for _ in range(NB2)]
        xrb = [A.alloc([512], BF16, parts=BS) for _ in range(NB2)]
        rg = [A.alloc([512], F32, parts=BS) for _ in range(NB2)]
        ig = [A.alloc([512], F32, parts=BS) for _ in range(NB2)]
        aa = [A.alloc([512], F32, parts=BS) for _ in range(NB2)]
        sq = [A.alloc([512], F32, parts=BS) for _ in range(NB2)]
        hs = [A.alloc([512], F32, parts=BS) for _ in range(NB2)]
        self.load_w(win, self.P["lru_w_in"][j], "win")
        S.dma("pool", wout, self.P["lru_w_out"][j].rearrange("(n p) d -> p n d", p=BS), writes=["wout"])
        S.dma("pool", gw, self.P["lru_gate_w"][j], writes=["gw"])
        S.dma("sp", vec, self.P["lru_vec"][j], writes=["vec"])
        tx = A.alloc([NBLK], F32, parts=BS)
        tl = A.alloc([NBLK], F32, parts=BS)
        tu = A.alloc([NBLK], F32, parts=BS)
        S.op("act", lambda e: e.activation(out=tx, in_=vec[:, :, 7], func=AF.Exp, scale=-1.0), reads=["vec"], writes=["tx"])
        S.op("act", lambda e: e.activation(out=tl, in_=tx, func=AF.Ln, bias=1.0, scale=1.0), reads=["tx"], writes=["tl"])
        S.op("dve", lambda e: e.tensor_scalar(out=tu, in0=tx, scalar1=-0.25, scalar2=1.0 / 3.0, op0=ALU.mult, op1=ALU.add), reads=["tx"], writes=["tu"])
        S.op("dve", lambda e: e.tensor_tensor(out=tu, in0=tu, in1=tx, op=ALU.mult), reads=["tu", "tx"], writes=["tu"])
        S.op("dve", lambda e: e.tensor_scalar(out=tu, in0=tu, scalar1=-1.0, scalar2=0.5, op0=ALU.mult, op1=ALU.add), reads=["tu"], writes=["tu"])
        S.op("dve", lambda e: e.tensor_tensor(out=tu, in0=tu, in1=tx, op=ALU.mult), reads=["tu", "tx"], writes=["tu"])
        S.op("dve", lambda e: e.tensor_scalar(out=tu, in0=tu, scalar1=-1.0, scalar2=1.0, op0=ALU.mult, op1=ALU.add), reads=["tu"], writes=["tu"])
        S.op("dve", lambda e: e.tensor_tensor(out=tu, in0=tu, in1=tx, op=ALU.mult), reads=["tu", "tx"], writes=["tu"])
        S.op("dve", lambda e: e.tensor_tensor(out=tu, in0=tu, in1=tl, op=ALU.subtract), reads=["tu", "tl"], writes=["tu"])
        S.op("dve", lambda e: e.tensor_scalar(out=tx, in0=tx, scalar1=0.05, scalar2=None, op0=ALU.is_lt), reads=["tx", "tu"], writes=["tx"])
        S.op("dve", lambda e: e.tensor_tensor(out=tu, in0=tu, in1=tx, op=ALU.mult), reads=["tu", "tx"], writes=["tu"])
        S.op("dve", lambda e: e.tensor_tensor(out=tu, in0=tu, in1=tl, op=ALU.add), reads=["tu", "tl"], writes=["tu"])
        S.op("dve", lambda e: e.tensor_scalar(out=c8, in0=tu, scalar1=-8.0, scalar2=None, op0=ALU.mult), reads=["tu"], writes=["c8"])
        S.op("dve", lambda e: e.memset(carry, 0.0), writes=["carry"])
        S.op("dve", lambda e: e.memset(hist, 0.0), writes=[("hist", n) for n in range(NBLK)])
        nb = 0
        for t in range(4):
            for n in range(NBLK):
                bi = n % NB2
                ps, pk = self.gps(parts=BS)
                for k in range(8):
                    S.op("pe", (lambda e, k=k, n=n, ps=ps, t=t: e.matmul(ps, lhsT=win[:, k, n * BS:(n + 1) * BS], rhs=self.hTs(k, t * 512, 512), start=(k == 0), stop=(k == 7))),
                         reads=["win"] + self.hTkeys(t * 512, 512), writes=[pk])
                S.op("act", (lambda e, ps=ps, bi=bi: e.activation(out=gb[bi], in_=ps, func=AF.Gelu)), reads=[pk], writes=[f"gb{bi}"])
                ps2, pk2 = self.gps(parts=BS)
                for k in range(8):
                    S.op("pe", (lambda e, k=k, n=n, ps2=ps2, t=t: e.matmul(ps2, lhsT=win[:, k, D_RNN + n * BS: D_RNN + (n + 1) * BS], rhs=self.hTs(k, t * 512, 512), start=(k == 0), stop=(k == 7))),
                         reads=["win"] + self.hTkeys(t * 512, 512), writes=[pk2])
                xp = xpb[bi]
                xk = f"xpb{bi}"
                S.op("pool", (lambda e, xp=xp, n=n: e.tensor_copy(out=xp[:, 0:4], in_=hist[:, n, :])), reads=[("hist", n)], writes=[xk + "h"])
                S.op("act", (lambda e, xp=xp, ps2=ps2: e.copy(out=xp[:, 4:516], in_=ps2)), reads=[pk2], writes=[xk])
                x_ = xr[bi]
                kx = f"xr{bi}"
                S.op("dve", (lambda e, xp=xp, x_=x_, n=n: e.tensor_scalar(out=x_, in0=xp[:, 1:513], scalar1=vec[:, n, 0:1], scalar2=vec[:, n, 4:5], op0=ALU.mult, op1=ALU.add)),
                     reads=[xk, xk + "h", "vec"], writes=[kx])
                for kk in range(1, 4):
                    S.op("dve", (lambda e, xp=xp, x_=x_, n=n, kk=kk: e.scalar_tensor_tensor(out=x_, in0=xp[:, 1 + kk:513 + kk], scalar=vec[:, n, kk:kk + 1], in1=x_, op0=ALU.mult, op1=ALU.add)),
                         reads=[xk, xk + "h", "vec", kx], writes=[kx])
                S.op("pool", (lambda e, xp=xp, n=n: e.tensor_copy(out=hist[:, n, :], in_=xp[:, 512:516])), reads=[xk], writes=[("hist", n)])
                S.op("act", (lambda e, x_=x_, bi=bi: e.copy(out=xrb[bi], in_=x_)), reads=[kx], writes=[f"xrb{bi}"])
                pg, pgk = self.gps(parts=BS)
                S.op("pe", (lambda e, n=n, pg=pg, bi=bi: e.matmul(pg, lhsT=gw[:, n, :], rhs=xrb[bi], start=True, stop=True)), reads=["gw", f"xrb{bi}"], writes=[pgk])
                S.op("act", (lambda e, pg=pg, bi=bi, n=n: e.activation(out=rg[bi], in_=pg, func=AF.Sigmoid, bias=vec[:, n, 5:6], scale=1.0)), reads=[pgk, "vec"], writes=[f"rg{bi}"])
                pg2, pgk2 = self.gps(parts=BS)
                S.op("pe", (lambda e, n=n, pg2=pg2, bi=bi: e.matmul(pg2, lhsT=gw[:, NBLK + n, :], rhs=xrb[bi], start=True, stop=True)), reads=["gw", f"xrb{bi}"], writes=[pgk2])
                S.op("act", (lambda e, pg2=pg2, bi=bi, n=n: e.activation(out=ig[bi], in_=pg2, func=AF.Sigmoid, bias=vec[:, n, 6:7], scale=1.0)), reads=[pgk2, "vec"], writes=[f"ig{bi}"])
                S.op("act", (lambda e, bi=bi, n=n: e.activation(out=aa[bi], in_=rg[bi], func=AF.Exp, scale=c8[:, n:n + 1])), reads=[f"rg{bi}", "c8"], writes=[f"aa{bi}"])
                S.op("act", (lambda e, bi=bi: e.activation(out=sq[bi], in_=aa[bi], func=AF.Square)), reads=[f"aa{bi}"], writes=[f"sq{bi}"])
                S.op("act", (lambda e, bi=bi: e.activation(out=sq[bi], in_=sq[bi], func=AF.Sqrt, bias=1.0, scale=-1.0)), reads=[f"sq{bi}"], writes=[f"sq{bi}"])
                S.op("pool", (lambda e, bi=bi: e.tensor_tensor(out=ig[bi], in0=ig[bi], in1=xr[bi], op=ALU.mult)), reads=[f"ig{bi}", kx], writes=[f"ig{bi}"])
                S.op("dve", (lambda e, bi=bi: e.tensor_tensor(out=ig[bi], in0=ig[bi], in1=sq[bi], op=ALU.mult)), reads=[f"ig{bi}", f"sq{bi}"], writes=[f"ig{bi}"])
                S.op("dve", (lambda e, bi=bi, n=n: e.tensor_tensor_scan(out=hs[bi], data0=aa[bi], data1=ig[bi], initial=carry[:, n:n + 1], op0=ALU.mult, op1=ALU.add)),
                     reads=[f"aa{bi}", f"ig{bi}", ("carry", n)], writes=[f"hs{bi}"])
                S.op("act", (lambda e, bi=bi, n=n: e.copy(out=carry[:, n:n + 1], in_=hs[bi][:, 511:512])), reads=[f"hs{bi}"], writes=[("carry", n)])
                S.op("dve", (lambda e, bi=bi, n=n: e.tensor_tensor(out=mT[:, n, :], in0=hs[bi], in1=gb[bi], op=ALU.mult)), reads=[f"hs{bi}", f"gb{bi}"], writes=[("mT", n)])
            self.out_proj(seq, t, lambda k, sub: mT[:, k, sub * 128:(sub + 1) * 128], [("mT", n) for n in range(NBLK)], wout, "wout", NBLK, first=first)


    def stage_rwkv(self, seq, layer, j, first):
        S, A, PS = self.S, self.A, self.PS
        self.stage_begin()
        self.load_ln(layer * 3 + 0)
        import os
        RW = BF16 if os.environ.get('RW_F32', '0') != '1' else F32
        C0 = float(np.exp(-0.5))
        wr = A.alloc([8, D], BF16)
        wk = A.alloc([8, D], BF16)
        wv = A.alloc([8, D], BF16)
        wo = A.alloc([8, D], BF16)
        w1b = A.alloc([8, 64], BF16)
        a1b = A.alloc([8, 64], BF16)
        g1b = A.alloc([8, 128], BF16)
        w2b = A.alloc([D], BF16, parts=64)
        a2b = A.alloc([D], BF16, parts=64)
        g2b = A.alloc([D], BF16)
        lxg = A.alloc([D], F32)
        lxb = A.alloc([D], F32)
        vec = A.alloc([8, 16], F32)
        omu = A.alloc([8, 6], F32)
        rwc = A.alloc([648], F32)
        mask1 = rwc[:, 0:256]
        masksl = rwc[:, 256:384]
        blk = rwc[:, 384:512]
        rmask = rwc[:, 512:640]
        ind2 = rwc[:, 640:642]
        gneps = rwc[:, 642:643]
        xs = [A.alloc([8, 128], BF16) for _ in range(2)]
        xprev = A.alloc([8, 128], BF16)
        tmpx = A.alloc([8, 128], BF16)
        xlast = A.alloc([8], BF16)
        thw = A.alloc([128], BF16, parts=64)
        la = A.alloc([128], BF16, parts=64)
        sgl = A.alloc([128], BF16)
        V = A.alloc([D], F32)
        Y = A.alloc([D], F32)
        Hst = A.alloc([8, 2, 64], F32)
        Hm = Hst if RW == F32 else A.alloc([8, 2, 64], RW)
        Vm = V if RW == F32 else A.alloc([D], RW)
        Hd = [A.alloc([64], F32) for _ in range(2)]
        scr = A.alloc([D], F32)
        ob = A.alloc([D], BF16)
        OT = A.alloc([8, 128], BF16)
        stt = A.alloc([112], F32)
        names = ["sg", "a", "kq", "kk", "k", "r", "rn", "cum", "G", "Gi", "ex", "ab", "t1", "km", "E", "BhT", "KhT", "rkr"]
        PBs = []
        tmps = {n: A.alloc([128], F32) for n in names if n != "G"}
        for i in range(2):
            pb = dict(tmps)
            pb["G"] = A.alloc([128], F32)
            pb["ATRT"] = A.alloc([256], RW)
            pb["BT"] = A.alloc([128], RW)
            pb["KT"] = A.alloc([128], RW)
            pb["BK"] = A.alloc([256], RW)
            PBs.append(pb)
        HB = []
        for i in range(2):
            hb = {"Mk": A.alloc([256], RW), "Mb": A.alloc([256], RW), "X0": A.alloc([128], RW),
                  "XX": [A.alloc([256], RW) for _ in range(2)], "Z": [A.alloc([128], RW) for _ in range(2)],
                  "Gs": A.alloc([64], RW), "Us": A.alloc([64], RW)}
            S.op("dve", (lambda e, hb=hb: e.memset(hb["Us"], 0.0)), writes=[(f"hb{i}", "Us")])
            S.op("dve", (lambda e, hb=hb: e.memset(hb["Gs"], 0.0)), writes=[(f"hb{i}", "Gs")])
            HB.append(hb)

        self.load_w(w1b, self.P["rw_w1"][j], "w1b")
        self.load_w(a1b, self.P["rw_a1"][j], "a1b")
        self.load_w(g1b, self.P["rw_g1"][j], "g1b")
        self.load_w(wv, self.P["rw_w_v"][j], "wv")
        self.load_w(wr, self.P["rw_w_r"][j], "wr")
        self.load_w(wk, self.P["rw_w_k"][j], "wk")
        self.load_w(wo, self.P["rw_w_o"][j], "wo")
        S.dma("pool", w2b, self.P["rw_w2"][j], writes=["w2b"])
        S.dma("pool", a2b, self.P["rw_a2"][j], writes=["a2b"])
        S.dma("pool", g2b, self.P["rw_g2"][j], writes=["g2b"])
        S.dma("sp", vec, self.P["rw_vec"][j], writes=["vec"])
        S.dma("sp", rwc, self.P["rwc"], writes=["rwc"])
        S.dma("sp", lxg, self.P["rw_lnx"][j, 0, :].partition_broadcast(128), writes=["lxg"])
        S.dma("sp", lxb, self.P["rw_lnx"][j, 1, :].partition_broadcast(128), writes=["lxb"])
        S.op("dve", lambda e: e.tensor_scalar(out=omu, in0=vec[:, :, 0:6], scalar1=-1.0, scalar2=1.0, op0=ALU.mult, op1=ALU.add), reads=["vec"], writes=["omu"])
        S.op("dve", lambda e: e.memset(Hst, 0.0), writes=[("H", h) for h in range(16)])
        if RW != F32:
            S.op("dve", lambda e: e.memset(Hm, 0.0), writes=[("Hm", h) for h in range(16)])
        S.op("dve", lambda e: e.memset(xlast, 0.0), writes=["xlast"])
        self.force_pi = 0
        self.gen_list = [(0, 0), (0, 1)]
        HK = "H" if RW == F32 else "Hm"
        VK = "V" if RW == F32 else "Vm"
        coefps = PS[1][:, 512:528]
        ckey = ("PS1", 1)

        def mix(m, bi, hk, t0):
            S.op("dve", (lambda e: e.tensor_tensor(out=tmpx, in0=xprev, in1=vec[:, :, m:m + 1].to_broadcast([128, 8, 128]), op=ALU.mult)),
                 reads=["xprev", "vec"], writes=["tmpx"])
            S.op("dve", (lambda e: e.tensor_tensor(out=xs[bi], in0=self.hT[:, :, 2 + t0:2 + t0 + 128], in1=omu[:, :, m:m + 1].to_broadcast([128, 8, 128]), op=ALU.mult)),
                 reads=hk + ["omu"], writes=[f"xs{bi}"])
            S.op("pool", (lambda e: e.tensor_tensor(out=xs[bi], in0=xs[bi], in1=tmpx, op=ALU.add)), reads=[f"xs{bi}", "tmpx"], writes=[f"xs{bi}"])

        for tt in range(16):
            t0 = tt * 128
            hk = [("hT", tt)]
            S.op("pool", (lambda e, t0=t0: e.tensor_copy(out=xprev[:, :, 1:128], in_=self.hT[:, :, 2 + t0:2 + t0 + 127])), reads=hk, writes=["xprev"])
            S.op("pool", (lambda e: e.tensor_copy(out=xprev[:, :, 0:1], in_=xlast.unsqueeze(2))), reads=["xlast", "xprev"], writes=["xprev"])
            S.op("pool", (lambda e, t0=t0: e.tensor_copy(out=xlast.unsqueeze(2), in_=self.hT[:, :, 2 + t0 + 127:2 + t0 + 128])), reads=hk + ["xprev"], writes=["xlast"])
            mix(1, 0, hk, t0)
            ps, pk = self.gps(parts=64, n=128)
            for k in range(8):
                S.op("pe", (lambda e, k=k, ps=ps: e.matmul(ps, lhsT=w1b[:, k, :], rhs=xs[0][:, k, :], start=(k == 0), stop=(k == 7))), reads=["w1b", "xs0"], writes=[pk])
            S.op("act", (lambda e, ps=ps: e.activation(out=thw, in_=ps, func=AF.Tanh)), reads=[pk], writes=["thw"])
            mix(4, 1, hk, t0)
            ps, pk = self.gps(parts=64, n=128)
            for k in range(8):
                S.op("pe", (lambda e, k=k, ps=ps: e.matmul(ps, lhsT=a1b[:, k, :], rhs=xs[1][:, k, :], start=(k == 0), stop=(k == 7))), reads=["a1b", "xs1"], writes=[pk])
            S.op("act", (lambda e, ps=ps: e.copy(out=la, in_=ps)), reads=[pk], writes=["la"])
            mix(5, 0, hk, t0)
            ps, pk = self.gps(n=128)
            for k in range(8):
                S.op("pe", (lambda e, k=k, ps=ps: e.matmul(ps, lhsT=g1b[:, k, :], rhs=xs[0][:, k, :], start=(k == 0), stop=(k == 7))), reads=["g1b", "xs0"], writes=[pk])
            S.op("act", (lambda e, ps=ps: e.activation(out=sgl, in_=ps, func=AF.Sigmoid)), reads=[pk], writes=["sgl"])
            mix(3, 1, hk, t0)
            for half in range(2):
                ps, pk = self.gps()
                for k in range(8):
                    S.op("pe", (lambda e, k=k, ps=ps, half=half: e.matmul(ps, lhsT=xs[1][:, k, :], rhs=wv[:, k, half * 512:(half + 1) * 512], start=(k == 0), stop=(k == 7))), reads=["wv", "xs1"], writes=[pk])
                S.op("act", (lambda e, ps=ps, half=half: e.copy(out=V[:, half * 512:(half + 1) * 512], in_=ps)), reads=[pk], writes=[("V", half)])
                if RW != F32:
                    S.op("pool", (lambda e, half=half: e.tensor_copy(out=Vm[:, half * 512:(half + 1) * 512], in_=V[:, half * 512:(half + 1) * 512])), reads=[("V", half)], writes=[("Vm", half)])
            mix(0, 0, hk, t0)
            mix(2, 1, hk, t0)
            import os
            RWD = int(os.environ.get("RW_DBG", "9"))
            if RWD < 9:
                S.op("dve", (lambda e: e.memset(Y, 0.0)), reads=[("Ydone", 0), ("Ydone", 1)], writes=[("Y", h // 8, q, h) for h in range(16) for q in range(2)])
                S.op("pe", (lambda e: e.matmul(coefps, lhsT=sgl, rhs=g2b[:, 0:16], start=True, stop=True)), reads=["sgl", "g2b"], writes=[ckey])
            class _Rec:
                def __init__(s_):
                    s_.ops = []

                def op(s_, *a, **k):
                    s_.ops.append((a, k))

            recs = []
            for c in range(8 if RWD >= 2 else 0):
                recP = _Rec()
                S = recP
                pb = PBs[c % 2]
                pn = f"pb{c % 2}"
                K_ = lambda n, pn=pn: (pn, n) if n in ("G", "AT", "RT", "BT", "KT", "BK") else ("pbt", n)
                ps, pk = self.gps()
                for k in range(8):
                    S.op("pe", (lambda e, k=k, ps=ps, c=c: e.matmul(ps[:, 0:128], lhsT=wr[:, k, c * 128:(c + 1) * 128], rhs=xs[0][:, k, :], start=(k == 0), stop=(k == 7))), reads=["wr", "xs0"], writes=[pk])
                for k in range(8):
                    S.op("pe", (lambda e, k=k, ps=ps, c=c: e.matmul(ps[:, 128:256], lhsT=wk[:, k, c * 128:(c + 1) * 128], rhs=xs[1][:, k, :], start=(k == 0), stop=(k == 7))), reads=["wk", "xs1"], writes=[pk])
                S.op("pe", (lambda e, ps=ps, c=c: e.matmul(ps[:, 256:384], lhsT=w2b[:, c * 128:(c + 1) * 128], rhs=thw, start=True, stop=True)), reads=["w2b", "thw"], writes=[pk])
                S.op("pe", (lambda e, ps=ps, c=c: e.matmul(ps[:, 384:512], lhsT=a2b[:, c * 128:(c + 1) * 128], rhs=la, start=True, stop=True)), reads=["a2b", "la"], writes=[pk])
                vc = lambda i, c=c: vec[:, c, i:i + 1]
                S.op("act", (lambda e, ps=ps, pb=pb, vc=vc: e.activation(out=pb["sg"], in_=ps[:, 256:384], func=AF.Sigmoid, bias=vc(6), scale=1.0)), reads=[pk, "vec"], writes=[K_("sg")])
                S.op("act", (lambda e, ps=ps, pb=pb, vc=vc: e.activation(out=pb["a"], in_=ps[:, 384:512], func=AF.Sigmoid, bias=vc(7), scale=1.0)), reads=[pk, "vec"], writes=[K_("a")])
                S.op("act", (lambda e, ps=ps, pb=pb: e.copy(out=pb["k"], in_=ps[:, 128:256])), reads=[pk], writes=[K_("k")])
                S.op("dve", (lambda e, pb=pb, vc=vc: e.tensor_scalar(out=pb["kk"], in0=pb["k"], scalar1=vc(8), scalar2=None, op0=ALU.mult)), reads=[K_("k"), "vec"], writes=[K_("kk")])
                S.op("act", (lambda e, pb=pb: e.activation(out=pb["kq"], in_=pb["kk"], func=AF.Square)), reads=[K_("kk")], writes=[K_("kq")])
                S.op("dve", (lambda e, ps=ps, pb=pb: e.tensor_copy(out=pb["r"], in_=ps[:, 0:128])), reads=[pk], writes=[K_("r")])
                ps2, pk2 = self.gps(n=128)
                S.op("pe", (lambda e, ps2=ps2, pb=pb: e.matmul(ps2, lhsT=blk, rhs=pb["kq"], start=True, stop=True)), reads=["rwc", K_("kq")], writes=[pk2])
                S.op("act", (lambda e, ps2=ps2, pb=pb: e.activation(out=pb["rn"], in_=ps2, func=AF.Sqrt)), reads=[pk2], writes=[K_("rn")])
                S.op("dve", (lambda e, pb=pb: e.tensor_scalar(out=pb["rn"], in0=pb["rn"], scalar1=1e-12, scalar2=None, op0=ALU.max)), reads=[K_("rn")], writes=[K_("rn")])
                S.op("dve", (lambda e, pb=pb: e.reciprocal(out=pb["rn"], in_=pb["rn"])), reads=[K_("rn")], writes=[K_("rn")])
                S.op("dve", (lambda e, pb=pb: e.tensor_tensor(out=pb["kk"], in0=pb["kk"], in1=pb["rn"], op=ALU.mult)), reads=[K_("kk"), K_("rn")], writes=[K_("kk")])
                S.op("dve", (lambda e, pb=pb: e.tensor_tensor_scan(out=pb["cum"], data0=rmask, data1=pb["sg"], initial=0.0, op0=ALU.mult, op1=ALU.add)), reads=["rwc", K_("sg")], writes=[K_("cum")])
                S.op("act", (lambda e, pb=pb: e.activation(out=pb["G"], in_=pb["cum"], func=AF.Exp, scale=-C0)), reads=[K_("cum")], writes=[K_("G")])
                S.op("act", (lambda e, pb=pb: e.activation(out=pb["Gi"], in_=pb["cum"], func=AF.Exp, scale=C0)), reads=[K_("cum")], writes=[K_("Gi")])
                S.op("dve", (lambda e, pb=pb: e.tensor_tensor(out=pb["ex"], in0=pb["cum"], in1=pb["sg"], op=ALU.subtract)), reads=[K_("cum"), K_("sg")], writes=[K_("ex")])
                S.op("act", (lambda e, pb=pb: e.activation(out=pb["ex"], in_=pb["ex"], func=AF.Exp, scale=-C0)), reads=[K_("ex")], writes=[K_("ex")])
                S.op("dve", (lambda e, pb=pb: e.scalar_tensor_tensor(out=pb["ATRT"][:, 0:128], in0=pb["kk"], scalar=-1.0, in1=pb["ex"], op0=ALU.mult, op1=ALU.mult)), reads=[K_("kk"), K_("ex")], writes=[K_("AT")])
                S.op("dve", (lambda e, pb=pb: e.tensor_tensor(out=pb["ab"], in0=pb["kk"], in1=pb["a"], op=ALU.mult)), reads=[K_("kk"), K_("a")], writes=[K_("ab")])
                S.op("dve", (lambda e, pb=pb: e.tensor_tensor(out=pb["BT"], in0=pb["ab"], in1=pb["Gi"], op=ALU.mult)), reads=[K_("ab"), K_("Gi")], writes=[K_("BT")])
                S.op("dve", (lambda e, pb=pb, vc=vc: e.tensor_scalar(out=pb["t1"], in0=pb["a"], scalar1=-1.0, scalar2=vc(9), op0=ALU.add, op1=ALU.mult)), reads=[K_("a"), "vec"], writes=[K_("t1")])
                S.op("dve", (lambda e, pb=pb: e.scalar_tensor_tensor(out=pb["km"], in0=pb["t1"], scalar=1.0, in1=pb["k"], op0=ALU.add, op1=ALU.mult)), reads=[K_("t1"), K_("k")], writes=[K_("km")])
                S.op("dve", (lambda e, pb=pb: e.tensor_tensor(out=pb["KT"], in0=pb["km"], in1=pb["Gi"], op=ALU.mult)), reads=[K_("km"), K_("Gi")], writes=[K_("KT")])
                S.op("dve", (lambda e, pb=pb: e.tensor_tensor(out=pb["ATRT"][:, 128:256], in0=pb["r"], in1=pb["G"], op=ALU.mult)), reads=[K_("r"), K_("G")], writes=[K_("RT")])
                for q in range(2):
                    S.op("dve", (lambda e, pb=pb, q=q: e.tensor_scalar(out=pb["E"][:, q * 64:(q + 1) * 64], in0=pb["cum"][:, q * 64:(q + 1) * 64], scalar1=pb["cum"][:, q * 64 + 63:q * 64 + 64], scalar2=None, op0=ALU.subtract)),
                         reads=[K_("cum")], writes=[K_("E")])
                S.op("act", (lambda e, pb=pb: e.activation(out=pb["E"], in_=pb["E"], func=AF.Exp, scale=C0)), reads=[K_("E")], writes=[K_("E")])
                S.op("dve", (lambda e, pb=pb: e.tensor_tensor(out=pb["BhT"], in0=pb["ab"], in1=pb["E"], op=ALU.mult)), reads=[K_("ab"), K_("E")], writes=[K_("BhT")])
                S.op("dve", (lambda e, pb=pb: e.tensor_tensor(out=pb["KhT"], in0=pb["km"], in1=pb["E"], op=ALU.mult)), reads=[K_("km"), K_("E")], writes=[K_("KhT")])
                ps3, pk3 = self.gps(n=256)
                S.op("pe", (lambda e, ps3=ps3, pb=pb: e.transpose(ps3[:, 0:128], pb["BhT"], self.identf[:])), reads=[K_("BhT"), "identf"], writes=[pk3])
                S.op("pe", (lambda e, ps3=ps3, pb=pb: e.transpose(ps3[:, 128:256], pb["KhT"], self.identf[:])), reads=[K_("KhT"), "identf"], writes=[pk3])
                S.op("act", (lambda e, ps3=ps3, pb=pb: e.copy(out=pb["BK"], in_=ps3)), reads=[pk3], writes=[K_("BK")])
                S.op("dve", (lambda e, pb=pb, vc=vc: e.scalar_tensor_tensor(out=pb["rkr"], in0=pb["km"], scalar=vc(10), in1=pb["r"], op0=ALU.mult, op1=ALU.mult)), reads=[K_("km"), K_("r"), "vec"], writes=[K_("rkr")])
                S.op("pe", (lambda e, pb=pb, c=c: e.matmul(coefps[:, 2 * c:2 * c + 2], lhsT=pb["rkr"], rhs=ind2, start=True, stop=True)), reads=[K_("rkr"), "rwc"], writes=[ckey])
                recQ = _Rec()
                S = recQ
                NH = 2 if RWD >= 3 else 0
                st_ = []
                for hh in range(NH):
                    po = hh * 64
                    hb = HB[hh]
                    hn = f"hb{hh}"
                    pg = PS[2][:, hh * 512:(hh + 1) * 512]
                    pgk = ("PS2", hh)
                    S.op("pe", (lambda e, pb=pb, po=po, pg=pg: e.matmul(pg[:, 0:256], lhsT=pb["KT"][po:po + 64, :], rhs=pb["ATRT"][po:po + 64, :], start=True, stop=True)),
                         reads=[K_("KT"), K_("AT"), K_("RT")], writes=[pgk])
                    S.op("pe", (lambda e, pb=pb, po=po, pg=pg: e.matmul(pg[:, 256:512], lhsT=pb["BT"][po:po + 64, :], rhs=pb["ATRT"][po:po + 64, :], start=True, stop=True)),
                         reads=[K_("BT"), K_("AT"), K_("RT")], writes=[pgk])
                    S.op("dve", (lambda e, hb=hb, pg=pg: e.tensor_tensor(out=hb["Mk"], in0=pg[:, 0:256], in1=mask1, op=ALU.mult)), reads=[pgk, "rwc"], writes=[(hn, "Mk")])
                    S.op("dve", (lambda e, hb=hb, pg=pg: e.tensor_tensor(out=hb["Mb"], in0=pg[:, 256:512], in1=mask1, op=ALU.mult)), reads=[pgk, "rwc"], writes=[(hn, "Mb")])
                    S.op("pe", (lambda e, pb=pb, po=po, pg=pg: e.matmul(pg[:, 0:128], lhsT=pb["ATRT"][po:po + 64, 0:128], rhs=pb["BT"][po:po + 64, :], start=True, stop=True)),
                         reads=[K_("BT"), K_("AT")], writes=[pgk])
                    S.op("dve", (lambda e, hb=hb, pg=pg: e.tensor_tensor(out=hb["X0"], in0=pg[:, 0:128], in1=masksl, op=ALU.mult)), reads=[pgk, "rwc"], writes=[(hn, "X0")])
                    S.op("dve", (lambda e, hb=hb: e.tensor_tensor(out=hb["Z"][0], in0=hb["Mb"][:, 0:128], in1=self.identf[:], op=ALU.add)), reads=[(hn, "Mb"), "identf"], writes=[(hn, "Z0")])
                    st_.append({"XT": hb["Mb"][:, 0:128], "X": hb["X0"], "keys": [(hn, "Mb"), (hn, "X0")], "zi": 0})
                for lvl in range(5):
                    for hh in range(NH):
                        hb = HB[hh]
                        hn = f"hb{hh}"
                        bank = PS[2][:, hh * 512:(hh + 1) * 512]
                        bkey = ("PS2", hh)
                        sd = st_[hh]
                        X_ap, XT_ap, xk_keys, zi = sd["X"], sd["XT"], sd["keys"], sd["zi"]
                        xx = hb["XX"][lvl % 2]
                        xxk = (hn, f"XX{lvl % 2}")
                        if lvl < 4:
                            S.op("pe", (lambda e, bank=bank, X_ap=X_ap, XT_ap=XT_ap: e.matmul(bank[:, 0:128], lhsT=X_ap, rhs=XT_ap, start=True, stop=True)), reads=xk_keys, writes=[bkey])
                        S.op("pe", (lambda e, bank=bank, X_ap=X_ap, XT_ap=XT_ap: e.matmul(bank[:, 128:256], lhsT=XT_ap, rhs=X_ap, start=True, stop=True)), reads=xk_keys, writes=[bkey])
                        if lvl < 4:
                            S.op("act", (lambda e, bank=bank, xx=xx: e.copy(out=xx, in_=bank[:, 0:256])), reads=[bkey], writes=[xxk])
                        else:
                            S.op("act", (lambda e, bank=bank, xx=xx: e.copy(out=xx[:, 128:256], in_=bank[:, 128:256])), reads=[bkey], writes=[xxk])
                        XT_ap, X_ap = xx[:, 0:128], xx[:, 128:256]
                        zo = hb["Z"][zi]
                        zn = hb["Z"][1 - zi]
                        S.op("pe", (lambda e, bank=bank, X_ap=X_ap, zo=zo: e.matmul(bank[:, 256:384], lhsT=X_ap, rhs=zo, start=True, stop=True)), reads=[xxk, (hn, f"Z{zi}")], writes=[bkey])
                        S.op("dve", (lambda e, bank=bank, zo=zo, zn=zn: e.tensor_tensor(out=zn, in0=bank[:, 256:384], in1=zo, op=ALU.add)), reads=[bkey, (hn, f"Z{zi}")], writes=[(hn, f"Z{1 - zi}")])
                        sd["X"], sd["XT"], sd["keys"], sd["zi"] = X_ap, XT_ap, [xxk], 1 - zi
                for hh in range(NH):
                    HB[hh]["TT"] = HB[hh]["Z"][st_[hh]["zi"]]
                    HB[hh]["TTk"] = (f"hb{hh}", f"Z{st_[hh]['zi']}")
                for q in range(2 if RWD >= 4 else 0):
                    ph = q * 64
                    for hh in range(2):
                        po, h, hb, hn = hh * 64, 2 * c + hh, HB[hh], f"hb{hh}"
                        pq = PS[3][:, hh * 512:(hh + 1) * 512]
                        pqk = ("PS3", hh)
                        S.op("pe", (lambda e, pq=pq, hh=hh, c=c, pb=pb: e.matmul(pq[:, 0:64], lhsT=pb["ATRT"][:, 0:128], rhs=Hm[:, c, hh, :], start=True, stop=False)),
                             reads=[K_("AT"), (HK, h)], writes=[pqk])
                        S.op("pe", (lambda e, pq=pq, hb=hb, h=h: e.matmul(pq[:, 0:64], lhsT=hb["Mk"][:, 0:128], rhs=Vm[:, h * 64:(h + 1) * 64], start=False, stop=True)),
                             reads=[(hn, "Mk"), (VK, h // 8)], writes=[pqk])
                        S.op("act", (lambda e, pq=pq, ph=ph, hb=hb: e.copy(out=hb["Gs"][ph:ph + 64, :], in_=pq[ph:ph + 64, 0:64])), reads=[pqk], writes=[(hn, "Gs")])
                    for hh in range(2):
                        po, h, hb, hn = hh * 64, 2 * c + hh, HB[hh], f"hb{hh}"
                        pq = PS[3][:, hh * 512:(hh + 1) * 512]
                        pqk = ("PS3", hh)
                        S.op("pe", (lambda e, pq=pq, ph=ph, hb=hb: e.matmul(pq[:, 64:128], lhsT=hb["TT"][ph:ph + 64, :], rhs=hb["Gs"][ph:ph + 64, :], start=True, stop=True)),
                             reads=[hb["TTk"], (hn, "Gs")], writes=[pqk])
                        S.op("dve", (lambda e, pq=pq, ph=ph, hb=hb: e.tensor_copy(out=hb["Us"][ph:ph + 64, :], in_=pq[ph:ph + 64, 64:128])), reads=[pqk], writes=[(hn, "Us")])
                    for hh in range(2):
                        po, h, hb, hn = hh * 64, 2 * c + hh, HB[hh], f"hb{hh}"
                        pq = PS[3][:, hh * 512:(hh + 1) * 512]
                        pqk = ("PS3", hh)
                        vh = Vm[ph:ph + 64, h * 64:(h + 1) * 64]
                        S.op("pe", (lambda e, pq=pq, hh=hh, c=c, pb=pb: e.matmul(pq[:, 128:192], lhsT=pb["ATRT"][:, 128:256], rhs=Hm[:, c, hh, :], start=True, stop=False)),
                             reads=[K_("RT"), (HK, h)], writes=[pqk])
                        S.op("pe", (lambda e, pq=pq, hb=hb: e.matmul(pq[:, 128:192], lhsT=hb["Mb"][:, 128:256], rhs=hb["Us"][:, :], start=False, stop=False)),
                             reads=[(hn, "Mb"), (hn, "Us")], writes=[pqk])
                        S.op("pe", (lambda e, pq=pq, hb=hb, h=h: e.matmul(pq[:, 128:192], lhsT=hb["Mk"][:, 128:256], rhs=Vm[:, h * 64:(h + 1) * 64], start=False, stop=True)),
                             reads=[(hn, "Mk"), (VK, h // 8)], writes=[pqk])
                        S.op("pe", (lambda e, pq=pq, ph=ph, hb=hb, pb=pb: e.matmul(pq[:, 192:256], lhsT=pb["BK"][ph:ph + 64, 0:128], rhs=hb["Us"][ph:ph + 64, :], start=True, stop=False)),
                             reads=[K_("BK"), (hn, "Us")], writes=[pqk])
                        S.op("pe", (lambda e, pq=pq, ph=ph, pb=pb, vh=vh: e.matmul(pq[:, 192:256], lhsT=pb["BK"][ph:ph + 64, 128:256], rhs=vh, start=False, stop=True)),
                             reads=[K_("BK"), (VK, h // 8)], writes=[pqk])
                        S.op("act", (lambda e, pq=pq, ph=ph, h=h: e.copy(out=Y[ph:ph + 64, h * 64:(h + 1) * 64], in_=pq[ph:ph + 64, 128:192])), reads=[pqk, ("Ydone", 0), ("Ydone", 1)], writes=[("Y", h // 8, q, h)])
                        S.op("act", (lambda e, pq=pq, po=po, hh=hh: e.copy(out=Hd[hh][po:po + 64, :], in_=pq[po:po + 64, 192:256])), reads=[pqk], writes=[f"Hd{hh}"])
                        S.op("dve", (lambda e, po=po, c=c, pb=pb, q=q, hh=hh: e.scalar_tensor_tensor(out=Hst[po:po + 64, c, hh, :], in0=Hst[po:po + 64, c, hh, :], scalar=pb["G"][po:po + 64, q * 64 + 63:q * 64 + 64], in1=Hd[hh][po:po + 64, :], op0=ALU.mult, op1=ALU.add)),
                             reads=[f"Hd{hh}", ("H", h), K_("G")], writes=[("H", h)])
                        if RW != F32:
                            S.op("act", (lambda e, po=po, c=c, hh=hh: e.copy(out=Hm[po:po + 64, c, hh, :], in_=Hst[po:po + 64, c, hh, :])), reads=[("H", h)], writes=[("Hm", h)])
                recs.append((recP.ops, recQ.ops))
            S = self.S
            if recs:
                for a_, k_ in recs[0][0]:
                    S.op(*a_, **k_)
                for c in range(len(recs)):
                    qa = recs[c][1]
                    pb_ops = recs[c + 1][0] if c + 1 < len(recs) else []
                    ia = ib = 0
                    na, nb_ = len(qa), len(pb_ops)
                    while ia < na or ib < nb_:
                        if ib >= nb_ or (ia < na and ia * max(nb_, 1) <= ib * na):
                            S.op(*qa[ia][0], **qa[ia][1])
                            ia += 1
                        else:
                            S.op(*pb_ops[ib][0], **pb_ops[ib][1])
                            ib += 1
            ykeys = [("Y", h // 8, q, h) for h in range(16) for q in range(2)]
            Y3 = Y.rearrange("p (h d) -> p h d", h=16)
            bc = lambda ap: ap.unsqueeze(2).to_broadcast([128, 16, 64])
            S.op("act", (lambda e: e.copy(out=stt[:, 96:112], in_=coefps)), reads=[ckey], writes=["coef"])
            S.op("dve", (lambda e: e.tensor_reduce(out=stt[:, 0:16], in_=Y3, axis=AX.X, op=ALU.add)), reads=ykeys, writes=["st_s1"])
            S.op("act", (lambda e: e.activation(out=scr, in_=Y, func=AF.Square)), reads=ykeys, writes=["scr"])
            S.op("dve", (lambda e: e.tensor_reduce(out=stt[:, 16:32], in_=scr.rearrange("p (h d) -> p h d", h=16), axis=AX.X, op=ALU.add)), reads=["scr"], writes=["st_s2"])
            S.op("dve", (lambda e: e.tensor_scalar(out=stt[:, 32:48], in0=stt[:, 0:16], scalar1=1.0 / 64.0, scalar2=None, op0=ALU.mult)), reads=["st_s1"], writes=["st_mean"])
            S.op("dve", (lambda e: e.tensor_tensor(out=stt[:, 48:64], in0=stt[:, 32:48], in1=stt[:, 32:48], op=ALU.mult)), reads=["st_mean"], writes=["st_msq"])
            S.op("dve", (lambda e: e.scalar_tensor_tensor(out=stt[:, 64:80], in0=stt[:, 16:32], scalar=1.0 / 64.0, in1=stt[:, 48:64], op0=ALU.mult, op1=ALU.subtract)), reads=["st_s2", "st_msq"], writes=["st_var"])
            S.op("act", (lambda e: e.activation(out=stt[:, 80:96], in_=stt[:, 64:80], func=AF.Sqrt, bias=gneps, scale=1.0)), reads=["st_var", "rwc"], writes=["st_sd"])
            S.op("dve", (lambda e: e.reciprocal(out=stt[:, 80:96], in_=stt[:, 80:96])), reads=["st_sd"], writes=["st_rstd"])
            S.op("dve", (lambda e: e.tensor_tensor(out=Y3, in0=Y3, in1=bc(stt[:, 32:48]), op=ALU.subtract)), reads=ykeys + ["st_mean", "scr"], writes=["Yn"])
            S.op("dve", (lambda e: e.tensor_tensor(out=Y3, in0=Y3, in1=bc(stt[:, 80:96]), op=ALU.mult)), reads=["Yn", "st_rstd"], writes=["Yn"])
            S.op("dve", (lambda e: e.tensor_tensor(out=Y, in0=Y, in1=lxg, op=ALU.mult)), reads=["Yn", "lxg"], writes=["Yn"])
            S.op("dve", (lambda e: e.tensor_tensor(out=Y, in0=Y, in1=lxb, op=ALU.add)), reads=["Yn", "lxb"], writes=["Yn"])
            S.op("dve", (lambda e: e.tensor_tensor(out=scr.rearrange("p (h d) -> p h d", h=16), in0=V.rearrange("p (h d) -> p h d", h=16), in1=bc(stt[:, 96:112]), op=ALU.mult)),
                 reads=[("V", 0), ("V", 1), "coef", "st_s2"], writes=["scr"])
            S.op("dve", (lambda e: e.tensor_tensor(out=Y, in0=Y, in1=scr, op=ALU.add)), reads=["Yn", "scr"], writes=["Yn"])
            for half in range(2):
                ps, pk = self.gps()
                S.op("pe", (lambda e, ps=ps, half=half: e.matmul(ps, lhsT=sgl, rhs=g2b[:, half * 512:(half + 1) * 512], start=True, stop=True)), reads=["sgl", "g2b"], writes=[pk])
                S.op("dve", (lambda e, ps=ps, half=half: e.tensor_tensor(out=ob[:, half * 512:(half + 1) * 512], in0=Y[:, half * 512:(half + 1) * 512], in1=ps, op=ALU.mult)), reads=[pk, "Yn"], writes=[("ob", half), ("Ydone", half)])
            ps, pk = self.gps()
            pst = ps.bitcast(BF16)
            for cc in range(8):
                S.op("pe", (lambda e, cc=cc, pst=pst: e.transpose(pst[:, cc * 128:(cc + 1) * 128], ob[:, cc * 128:(cc + 1) * 128], self.ident[:])), reads=[("ob", cc // 4), "ident"], writes=[pk])
            S.op("act", (lambda e, pst=pst: e.copy(out=OT, in_=pst.rearrange("p (c t) -> p c t", c=8))), reads=[pk], writes=["OT"])
            py = PS[0]
            for half in range(2):
                for k in range(8):
                    S.op("pe", (lambda e, k=k, half=half: e.matmul(py[:, half * 512:(half + 1) * 512], lhsT=OT[:, k, :], rhs=wo[:, k, half * 512:(half + 1) * 512], start=(k == 0), stop=(k == 7))),
                         reads=["OT", "wo"], writes=[("PS0", half)])
            self.ln_finish(seq, tt, py[:, :], [("PS0", 0), ("PS0", 1)], first=first, last=False)
        self.force_pi = None
        self.gen_list = [(0, 0), (0, 1)]


    def stage_attn(self, seq, layer, j, first):
        S, A, PS = self.S, self.A, self.PS
        self.stage_begin()
        self.load_ln(layer * 3 + 0)
        wq = A.alloc([8, D], BF16)
        wk = A.alloc([8, D], BF16)
        wv = A.alloc([8, D], BF16)
        wo = A.alloc([8, D], BF16)
        biasb = A.alloc([16, 640], BF16)
        KT = A.alloc([8, 1024], BF16)
        V = A.alloc([8, D], BF16)
        QT = A.alloc([8, 512], BF16)
        OT = A.alloc([8, 512], BF16)
        Pb = [A.alloc([640], BF16) for _ in range(2)]
        PTs = [A.alloc([5, 128], BF16) for _ in range(2)]
        Ob = [A.alloc([D], BF16) for _ in range(2)]
        sm = [A.alloc([64], F32) for _ in range(2)]
        wqkv = self.P["ca_w_qkv"][j]
        self.load_w(wq, wqkv[:, 0:D], "wq")
        self.load_w(wk, wqkv[:, D:2 * D], "wk")
        self.load_w(wv, wqkv[:, 2 * D:3 * D], "wv")
        self.load_w(wo, self.P["ca_w_o"][j], "wo")
        S.dma("pool", biasb, self.P["ca_bias"], writes=["biasb"])
        nh = 0
        for t in range(4):
            hk = self.hTkeys(t * 512, 512)
            r0 = (t % 2) * 512
            for oc in range(8):
                ps, pk = self.gps()
                for k in range(8):
                    S.op("pe", (lambda e, k=k, oc=oc, ps=ps, t=t: e.matmul(ps, lhsT=wq[:, k, oc * 128:(oc + 1) * 128], rhs=self.hTs(k, t * 512, 512), start=(k == 0), stop=(k == 7))),
                         reads=["wq"] + hk, writes=[pk])
                S.op("act", (lambda e, oc=oc, ps=ps: e.activation(out=QT[:, oc, :], in_=ps, func=AF.Copy, scale=0.125)), reads=[pk], writes=[("QT", oc)])
                ps, pk = self.gps()
                for k in range(8):
                    S.op("pe", (lambda e, k=k, oc=oc, ps=ps, t=t: e.matmul(ps, lhsT=wk[:, k, oc * 128:(oc + 1) * 128], rhs=self.hTs(k, t * 512, 512), start=(k == 0), stop=(k == 7))),
                         reads=["wk"] + hk, writes=[pk])
                S.op("dve", (lambda e, oc=oc, ps=ps, r0=r0: e.tensor_copy(out=KT[:, oc, r0:r0 + 512], in_=ps)), reads=[pk], writes=[("KT", oc, t % 2)])
            for sub in range(4):
                slot = (4 * t + sub) % 8
                for half in range(2):
                    ps, pk = self.gps()
                    for k in range(8):
                        S.op("pe", (lambda e, k=k, half=half, ps=ps, t=t, sub=sub: e.matmul(ps, lhsT=self.hTs(k, t * 512 + sub * 128, 128), rhs=wv[:, k, half * 512:(half + 1) * 512], start=(k == 0), stop=(k == 7))),
                             reads=["wv"] + hk, writes=[pk])
                    S.op("act", (lambda e, half=half, ps=ps, slot=slot: e.copy(out=V[:, slot, half * 512:(half + 1) * 512], in_=ps)), reads=[pk], writes=[("V", slot, half)])
            for sub in range(4):
                qb = 4 * t + sub
                kbs = list(range(max(0, qb - 4), qb + 1))
                j0 = kbs[0] - (qb - 4)
                c0 = j0 * 128
                oi = qb % 2
                po_ps = PS[2]
                smi = sm[oi]
                ksm = f"sm{oi}"
                def emit_A(h, nh):
                        c, po = h // 2, (h % 2) * 64
                        si = nh % 2
                        sps = PS[0] if si == 0 else PS[3]
                        spn = "PS0" if si == 0 else "PS3"
                        for kb in kbs:
                            jj = kb - (qb - 4)
                            slot = kb % 8
                            S.op("pe", (lambda e, jj=jj, slot=slot, c=c, po=po, sps=sps, sub=sub: e.matmul(sps[:, jj * 128:(jj + 1) * 128], lhsT=QT[po:po + 64, c, sub * 128:(sub + 1) * 128], rhs=KT[po:po + 64, c, slot * 128:(slot + 1) * 128], start=True, stop=False)),
                                 reads=[("QT", c), ("KT", c, slot // 4)], writes=[(spn, jj // 4)])
                            S.op("pe", (lambda e, jj=jj, h=h, sps=sps: e.matmul(sps[:, jj * 128:(jj + 1) * 128], lhsT=self.ident[:], rhs=biasb[:, h, jj * 128:(jj + 1) * 128], start=False, stop=True)),
                                 reads=["biasb", "ident"], writes=[(spn, jj // 4)])
                        skeys = [(spn, 0), (spn, 1)]
                        S.op("dve", (lambda e, sps=sps, smi=smi, h=h, c0=c0: e.tensor_reduce(out=smi[:, 32 + h:33 + h], in_=sps[:, c0:640], axis=AX.X, op=ALU.max, negate=True)),
                             reads=skeys, writes=[(ksm + "m", h)])
                        pb_ = Pb[si]
                        S.op("act", (lambda e, sps=sps, smi=smi, h=h, c0=c0, pb_=pb_: e.activation(out=pb_[:, c0:640], in_=sps[:, c0:640], func=AF.Exp, bias=smi[:, 32 + h:33 + h], scale=1.0, accum_out=smi[:, h:h + 1])),
                             reads=skeys + [(ksm + "m", h)], writes=[f"Pb{si}", (ksm + "s", h)])

                def emit_B(h, nh):
                        c, po = h // 2, (h % 2) * 64
                        si = nh % 2
                        pb_ = Pb[si]
                        ptp = PS[1][:, 0:512].bitcast(BF16)
                        for kb in kbs:
                            jj = kb - (qb - 4)
                            S.op("pe", (lambda e, jj=jj, ptp=ptp, pb_=pb_: e.transpose(ptp[:, jj * 128:(jj + 1) * 128], pb_[:, jj * 128:(jj + 1) * 128], self.ident[:])),
                                 reads=[f"Pb{si}", "ident"], writes=[("PS1", 0)])
                        pts = PTs[si]
                        S.op("dve", (lambda e, ptp=ptp, pts=pts, j0=j0: e.tensor_copy(out=pts[:, j0:5, :], in_=ptp[:, j0 * 128:640].rearrange("p (a b) -> p a b", b=128))),
                             reads=[("PS1", 0)], writes=[f"PTs{si}"])
                        for kb in kbs:
                            jj = kb - (qb - 4)
                            slot = kb % 8
                            S.op("pe", (lambda e, jj=jj, slot=slot, h=h, pts=pts, po_ps=po_ps, kbs=kbs, kb=kb: e.matmul(po_ps[:, h * 64:(h + 1) * 64], lhsT=pts[:, jj, :], rhs=V[:, slot, h * 64:(h + 1) * 64], start=(kb == kbs[0]), stop=(kb == kbs[-1]))),
                                 reads=[f"PTs{si}", ("V", slot, h // 8)], writes=[("PS2", h // 8)])

                for hq in range(17):
                    if hq < 16:
                        emit_A(hq, nh + hq)
                    if hq >= 1:
                        emit_B(hq - 1, nh + hq - 1)
                nh += 16
                S.op("dve", (lambda e, smi=smi: e.reciprocal(out=smi[:, 16:32], in_=smi[:, 0:16])), reads=[(ksm + "s", h) for h in range(16)], writes=[ksm + "r"])
                ob = Ob[oi]
                S.op("dve", (lambda e, smi=smi, ob=ob, po_ps=po_ps: e.tensor_tensor(out=ob.rearrange("p (h d) -> p h d", h=16), in0=po_ps[:, :].rearrange("p (h d) -> p h d", h=16), in1=smi[:, 16:32].unsqueeze(2).to_broadcast([128, 16, 64]), op=ALU.mult)),
                     reads=[("PS2", 0), ("PS2", 1), ksm + "r"], writes=[f"Ob{oi}"])
                otp = PS[1][:, 512:1024].bitcast(BF16)
                for cc in range(8):
                    S.op("pe", (lambda e, cc=cc, otp=otp, ob=ob: e.transpose(otp[:, cc * 128:(cc + 1) * 128], ob[:, cc * 128:(cc + 1) * 128], self.ident[:])),
                         reads=[f"Ob{oi}", "ident"], writes=[("PS1", 1)])
                S.op("act", (lambda e, otp=otp, sub=sub: e.copy(out=OT[:, :, sub * 128:(sub + 1) * 128], in_=otp.rearrange("p (c t) -> p c t", c=8))),
                     reads=[("PS1", 1)], writes=[("OT", sub)])
            self.out_proj(seq, t, lambda k, sub: OT[:, k, sub * 128:(sub + 1) * 128], [("OT", s_) for s_ in range(4)], wo, "wo", 8, first=first)


def make_consts():
    c = np.zeros((128, 256), np.float32)
    c[:, 0:128] = np.eye(128, dtype=np.float32)
    c[:, 128] = LN_EPS
    return c


def make_rwc():
    c = np.zeros((128, 648), np.float32)
    s_ = np.arange(128)[:, None]
    t_ = np.arange(128)[None, :]
    same = (s_ // 64) == (t_ // 64)
    c[:, 0:128] = (same & (s_ < t_))
    c[:, 128:256] = (same & (s_ <= t_))
    c[:, 256:384] = (same & (s_ > t_))
    c[:, 384:512] = same
    c[:, 512:640] = (t_ % 64 != 0)
    c[0:64, 640] = 1.0
    c[64:128, 641] = 1.0
    c[:, 642] = 64e-5
    return c


def prep_shared(inp):
    sh = {}
    sh["ln_g"] = np.ascontiguousarray(inp["ln_g"].reshape(DEPTH * 3, D))
    sh["ln_b"] = np.ascontiguousarray(inp["ln_b"].reshape(DEPTH * 3, D))
    sh["lru_w_in"] = inp["lru_w_in"]
    na = inp["lru_w_in"].shape[0]
    vec = np.zeros((na, BS, NBLK, 8), np.float32)
    cw = inp["lru_conv_w"].reshape(na, 4, NBLK, BS)
    for k in range(4):
        vec[:, :, :, k] = cw[:, k].transpose(0, 2, 1)
    vec[:, :, :, 4] = inp["lru_conv_b"].reshape(na, NBLK, BS).transpose(0, 2, 1)
    gbv = inp["lru_gate_b"].reshape(na, 2, NBLK, BS)
    vec[:, :, :, 5] = gbv[:, 0].transpose(0, 2, 1)
    vec[:, :, :, 6] = gbv[:, 1].transpose(0, 2, 1)
    vec[:, :, :, 7] = inp["lru_lambda"].reshape(na, NBLK, BS).transpose(0, 2, 1)
    sh["lru_vec"] = vec
    sh["lru_gate_w"] = np.ascontiguousarray(inp["lru_gate_w"].transpose(0, 3, 1, 2, 4).reshape(na, BS, 2 * NBLK, BS))
    sh["lru_w_out"] = inp["lru_w_out"]
    for k in ("mx_w_q", "mx_w_kv", "mx_w_o", "mlp_w1", "mlp_w2", "ca_w_qkv", "ca_w_o"):
        sh[k] = inp[k]
    rb = inp["ca_rel_bias"][0]
    q = np.arange(64)[:, None]
    kk = np.arange(576)[None, :]
    idx = np.clip(512 + q - kk, -128, 128) + 128
    band = rb[:, idx]
    b2 = np.full((128, 16, 640), NEG, np.float32)
    b2[0:64, :, 0:576] = band.transpose(1, 0, 2)
    b2[64:128, :, 64:640] = band.transpose(1, 0, 2)
    sh["ca_bias"] = b2
    sh["consts"] = make_consts()
    for k in ("rw_w_r", "rw_w_k", "rw_w_v", "rw_w_o", "rw_w1", "rw_a1", "rw_g1", "rw_w2", "rw_a2", "rw_g2"):
        sh[k] = inp[k]
    nb = inp["rw_mu"].shape[0]
    rv = np.zeros((nb, 128, 8, 16), np.float32)
    fm = lambda v: v.reshape(nb, 8, 128).transpose(0, 2, 1)
    for m in range(6):
        rv[:, :, :, m] = fm(inp["rw_mu"][:, m])
    rv[:, :, :, 6] = fm(inp["rw_w0"])
    rv[:, :, :, 7] = fm(inp["rw_a0"])
    rv[:, :, :, 8] = fm(inp["rw_k_k"])
    rv[:, :, :, 9] = fm(inp["rw_k_a"])
    rv[:, :, :, 10] = fm(inp["rw_r_k"].reshape(nb, D))
    sh["rw_vec"] = rv
    sh["rw_lnx"] = np.ascontiguousarray(np.stack([inp["rw_lnx_g"], inp["rw_lnx_b"]], axis=1))
    sh["rwc"] = make_rwc()
    return sh


_NC_CACHE = {}


def get_nc(n_layers=DEPTH, dbg=None):
    key = (n_layers if isinstance(n_layers, int) else tuple(n_layers), tuple(sorted(dbg.items())) if dbg else None)
    if key not in _NC_CACHE:
        b = Builder(n_layers, dbg)
        nc = b.build()
        _NC_CACHE[key] = nc
    return _NC_CACHE[key]


def kernel(**inputs):
    inp = {k: np.ascontiguousarray(np.asarray(v, dtype=np.float32)) for k, v in inputs.items()}
    sh = prep_shared(inp)
    nc = get_nc()
    ncores = 8
    in_maps = []
    for c in range(ncores):
        m = dict(sh)
        m["x"] = np.ascontiguousarray(inp["x"][c * NSEQ:(c + 1) * NSEQ])
        m["mem"] = np.ascontiguousarray(inp["mem"][c * NSEQ:(c + 1) * NSEQ])
        in_maps.append(m)
    res = run_bass_kernel_spmd(nc, in_maps, core_ids=list(range(ncores)))
    out = np.concatenate([np.asarray(r["out"]).reshape(NSEQ, SEQ, D) for r in res.results], axis=0)
    return out.astype(np.float32)
```

```python
import numpy as np
import concourse.bass as bass
import concourse.mybir as mybir
from concourse.bass_utils import run_bass_kernel_spmd
from contextlib import ExitStack

F32 = mybir.dt.float32
BF16 = mybir.dt.bfloat16
AF = mybir.ActivationFunctionType
ALU = mybir.AluOpType
AX = mybir.AxisListType

ENGS = ("pe", "act", "dve", "pool", "sp")
SEM_EPOCH = 30000

D = 1024
SEQ = 2048
NSEQ = 2
DEPTH = 4
ALPHA = (2 * DEPTH) ** 0.25
LN_EPS = 1e-5
D_RNN = 1344
NBLK = 16
BS = 84
D_FF = 4096
MEMT = 256
NEG = -30000.0


class Sched:
    def __init__(self, nc, stack, n_dma_sems=8):
        self.nc = nc
        self.stack = stack
        self.prog = {e: [] for e in ENGS}
        self.cnt = {e: 0 for e in ENGS}
        self.nsem = 0
        self.sem_owner = {}
        self.sem = {}
        for e in ENGS:
            self.sem[e] = self._newsem()
            self.sem_owner[id(self.sem[e])] = e
        self.known = {e: {} for e in ENGS}
        self.lastw = {}
        self.readers = {}
        self.dma_sems = {q: [[self._newsem(), 0] for _ in range(n_dma_sems)] for q in ("sp", "act", "pool")}
        self.dma_rr = {q: 0 for q in ("sp", "act", "pool")}
        self.ninstr = 0
        self.nwait = 0

    def _newsem(self):
        self.nsem += 1
        return self.stack.enter_context(self.nc.semaphore(f"s{self.nsem}"))

    def _deps(self, eng, reads, writes):
        deps = {}

        def add(ev, raw):
            if ev is None:
                return
            s, v = ev
            own = self.sem_owner.get(id(s))
            if own == eng:
                if eng == "pe":
                    return
            k = id(s)
            if k not in deps or deps[k][1] < v:
                deps[k] = (s, v)

        for k in reads:
            add(self.lastw.get(k), True)
        for k in writes:
            add(self.lastw.get(k), False)
            rd = self.readers.get(k)
            if rd:
                for ev in rd.values():
                    add(ev, False)
        return deps

    def _filter(self, eng, deps):
        waits = []
        kn = self.known[eng]
        for k, (s, v) in deps.items():
            if kn.get(k, 0) < v:
                kn[k] = v
                waits.append((s, v))
        self.nwait += len(waits)
        return waits

    def _record(self, ev, reads, writes):
        for k in writes:
            self.lastw[k] = ev
            self.readers[k] = {}
        for k in reads:
            r = self.readers.setdefault(k, {})
            r[id(ev[0])] = ev

    def op(self, eng, fn, reads=(), writes=()):
        deps = self._deps(eng, reads, writes)
        waits = self._filter(eng, deps)
        if self.cnt[eng] >= SEM_EPOCH:
            self.sem[eng] = self._newsem()
            self.sem_owner[id(self.sem[eng])] = eng
            self.cnt[eng] = 0
        self.cnt[eng] += 1
        ev = (self.sem[eng], self.cnt[eng])
        self.prog[eng].append((waits, fn, (self.sem[eng], 1)))
        self._record(ev, reads, writes)
        self.ninstr += 1
        return ev

    def dma(self, q, out, in_, reads=(), writes=()):
        slots = self.dma_sems[q]
        si = self.dma_rr[q]
        self.dma_rr[q] = (si + 1) % len(slots)
        slot = slots[si]
        deps = self._deps(q, reads, writes)
        if slot[1] > 0:
            deps[id(slot[0])] = (slot[0], 16 * slot[1])
        waits = self._filter(q, deps)
        slot[1] += 1
        ev = (slot[0], 16 * slot[1])
        self.prog[q].append((waits, (lambda e: e.dma_start(out=out, in_=in_)), (slot[0], 16)))
        self._record(ev, reads, writes)
        self.ninstr += 1
        return ev

    def all_events(self):
        evs = []
        for e in ENGS:
            if self.cnt[e] > 0:
                evs.append((self.sem[e], self.cnt[e]))
        for q in self.dma_sems:
            for s, n in self.dma_sems[q]:
                if n > 0:
                    evs.append((s, 16 * n))
        return evs

    def barrier(self):
        evs = self.all_events()
        for e in ENGS:
            deps = {}
            for s, v in evs:
                if self.sem_owner.get(id(s)) == e:
                    continue
                deps[id(s)] = (s, v)
            waits = self._filter(e, deps)
            if waits:
                self.prog[e].append((waits, None, None))
        self.lastw = {}
        self.readers = {}

    def emit(self):
        nc = self.nc
        prog = self.prog

        def run(name, e):
            for waits, fn, inc in prog[name]:
                for s, v in waits:
                    e.wait_ge(s, v)
                if fn is not None:
                    ins = fn(e)
                    if inc is not None:
                        ins.then_inc(inc[0], inc[1])

        with nc.Block() as block:
            @block.tensor
            def _(e):
                run("pe", e)

            @block.scalar
            def _(e):
                run("act", e)

            @block.vector
            def _(e):
                run("dve", e)

            @block.gpsimd
            def _(e):
                run("pool", e)

            @block.sync
            def _(e):
                run("sp", e)


class Arena:
    def __init__(self, t, nwords):
        self.t = t
        self.n = nwords
        self.off = 0

    def mark(self):
        return self.off

    def reset(self, m):
        self.off = m

    def alloc(self, shape, dtype, parts=128):
        free = int(np.prod(shape))
        words = free if dtype == F32 else (free + 1) // 2
        words = (words + 1) // 2 * 2
        assert self.off + words <= self.n, f"arena overflow {self.off}+{words}>{self.n}"
        ap = self.t[0:parts, self.off:self.off + words]
        self.off += words
        if dtype != F32:
            ap = ap.bitcast(dtype)
        ap = ap[:, 0:free]
        if len(shape) == 2:
            ap = ap.rearrange("p (a b) -> p a b", a=shape[0])
        elif len(shape) == 3:
            ap = ap.rearrange("p (a b c) -> p a b c", a=shape[0], b=shape[1])
        return ap


_uid = [0]


def uid(p="k"):
    _uid[0] += 1
    return f"{p}{_uid[0]}"


class Builder:
    def __init__(self, n_layers=DEPTH, dbg=None):
        self.layers = list(range(n_layers)) if isinstance(n_layers, int) else list(n_layers)
        self.dbg = dbg
        nc = self.nc = bass.Bass("TRN2", target_bir_lowering=False)
        self.st = ExitStack()
        self.S = Sched(nc, self.st)

    def din(self, name, shape):
        return self.nc.dram_tensor(name, list(shape), F32, kind="ExternalInput").ap()

    def build(self):
        nc, st, S = self.nc, self.st, self.S
        P = self.P = {}
        P["x"] = self.din("x", [NSEQ, SEQ, D])
        P["mem"] = self.din("mem", [NSEQ, MEMT, D])
        P["ln_g"] = self.din("ln_g", [DEPTH * 3, D])
        P["ln_b"] = self.din("ln_b", [DEPTH * 3, D])
        P["lru_w_in"] = self.din("lru_w_in", [2, D, 2 * D_RNN])
        P["lru_vec"] = self.din("lru_vec", [2, BS, NBLK, 8])
        P["lru_gate_w"] = self.din("lru_gate_w", [2, BS, 2 * NBLK, BS])
        P["lru_w_out"] = self.din("lru_w_out", [2, D_RNN, D])
        P["mx_w_q"] = self.din("mx_w_q", [DEPTH, D, D])
        P["mx_w_kv"] = self.din("mx_w_kv", [DEPTH, D, 2 * D])
        P["mx_w_o"] = self.din("mx_w_o", [DEPTH, D, D])
        P["mlp_w1"] = self.din("mlp_w1", [DEPTH, D, D_FF])
        P["mlp_w2"] = self.din("mlp_w2", [DEPTH, D_FF, D])
        P["ca_w_qkv"] = self.din("ca_w_qkv", [1, D, 3 * D])
        P["ca_w_o"] = self.din("ca_w_o", [1, D, D])
        P["ca_bias"] = self.din("ca_bias", [128, 16, 640])
        P["consts"] = self.din("consts", [128, 256])
        for nm in ("rw_w_r", "rw_w_k", "rw_w_v", "rw_w_o"):
            P[nm] = self.din(nm, [1, D, D])
        P["rw_w1"] = self.din("rw_w1", [1, D, 64])
        P["rw_a1"] = self.din("rw_a1", [1, D, 64])
        P["rw_g1"] = self.din("rw_g1", [1, D, 128])
        P["rw_w2"] = self.din("rw_w2", [1, 64, D])
        P["rw_a2"] = self.din("rw_a2", [1, 64, D])
        P["rw_g2"] = self.din("rw_g2", [1, 128, D])
        P["rw_vec"] = self.din("rw_vec", [1, 128, 8, 16])
        P["rw_lnx"] = self.din("rw_lnx", [1, 2, D])
        P["rwc"] = self.din("rwc", [128, 648])
        self.out = nc.dram_tensor("out", [NSEQ, SEQ, D], F32, kind="ExternalOutput").ap()
        self.h32 = nc.dram_tensor("h32", [NSEQ, SEQ, D], F32, kind="Internal").ap()
        if self.dbg:
            self.dbg_out = {n: nc.dram_tensor(n, list(shp), F32, kind="ExternalOutput").ap() for n, shp in self.dbg.items()}

        sb = lambda n, s, d: st.enter_context(nc.sbuf_tensor(n, s, d))
        self.hT = sb("hT", [128, 8, SEQ + 2], BF16)
        self.ident = sb("ident", [128, 128], BF16)
        self.identf = sb("identf", [128, 128], F32)
        self.cst = sb("cst", [128, 256], F32)
        self.lng = sb("lng", [128, D], F32)
        self.lnb = sb("lnb", [128, D], F32)
        self.lnz = [sb("lnz0", [128, D], F32)] * 2
        self.lnh = [sb(f"lnh{i}", [128, D], F32) for i in range(2)]
        self.lnhb = [sb(f"lnhb{i}", [128, D], BF16) for i in range(2)]
        self.lnst = [sb(f"lnst{i}", [128, 16], F32) for i in range(2)]
        self.lni = 0
        ARW = 37600
        self.arena_t = sb("arena", [128, ARW], F32)
        self.A = Arena(self.arena_t, ARW)
        self.PS = [st.enter_context(nc.psum_tensor(f"ps{i}", [128, 1024], F32)) for i in range(4)]
        self.out_events = []
        self.gi = 0
        self.gen_list = [(0, 0), (0, 1)]

        S.dma("sp", self.cst[:], P["consts"], writes=["cst"])
        S.op("dve", lambda e: e.tensor_copy(out=self.ident[:], in_=self.cst[:, 0:128]), reads=["cst"], writes=["ident"])
        S.op("dve", lambda e: e.tensor_copy(out=self.identf[:], in_=self.cst[:, 0:128]), reads=["cst"], writes=["identf"])
        S.op("pool", lambda e: e.memset(self.hT[:, :, 0:2], 0.0), writes=["hTpad"])

        for seq in range(NSEQ):
            self.load_x(seq)
            for li, layer in enumerate(self.layers):
                kind, j = layer % 3, layer // 3
                first = (li == 0)
                if kind == 0:
                    self.stage_lru(seq, layer, j, first)
                elif kind == 1:
                    self.stage_rwkv(seq, layer, j, first)
                else:
                    self.stage_attn(seq, layer, j, first)
                self.stage_memattn(seq, layer)
                self.stage_mlp(seq, layer, last=(li == len(self.layers) - 1))
        S.barrier()
        S.emit()
        self.st.close()
        return nc

    def stage_begin(self):
        self.S.barrier()
        self.A.reset(0)

    def load_w(self, dst, src, key, q="pool"):
        K, nk, ncols = dst.shape
        step = max(1, 2048 // K) if ncols * 4 >= 2048 else nk
        step = min(nk, max(1, (1 << 21) // (K * ncols * 4)))
        for k0 in range(0, nk, step):
            k1 = min(nk, k0 + step)
            self.S.dma(q, dst[:, k0:k1, :], src[k0 * K:k1 * K, :].rearrange("(k p) n -> p k n", p=K), writes=[key])

    def load_x(self, seq):
        S = self.S
        self.stage_begin()
        xb = [self.A.alloc([D], BF16) for _ in range(2)]
        for sub in range(SEQ // 128):
            b = xb[sub % 2]
            kb = f"xb{sub % 2}"
            S.dma("pool", b, self.P["x"][seq, sub * 128:(sub + 1) * 128, :], writes=[kb])
            self.to_hT(b, kb, sub)

    def to_hT(self, hb, kb, sub, pi=None):
        S = self.S
        deferred = pi is not None
        if pi is None:
            pi = self.lni % 2 if getattr(self, "force_pi", None) is None else self.force_pi
        ps = self.PS[1][:, pi * 512:(pi + 1) * 512].bitcast(BF16)
        pk = ("PS1", pi)
        for c in range(8):
            S.op("pe", (lambda e, c=c: e.transpose(ps[:, c * 128:(c + 1) * 128], hb[:, c * 128:(c + 1) * 128], self.ident[:])),
                 reads=[kb, "ident"], writes=[pk])
        dst = self.hT[:, :, 2 + sub * 128: 2 + (sub + 1) * 128]
        src = ps.rearrange("p (c t) -> p c t", c=8)
        S.op("dve", lambda e: e.tensor_copy(out=dst, in_=src), reads=[pk], writes=[("hT", sub)])
        if not deferred:
            self.lni += 1

    def load_ln(self, li):
        S = self.S
        S.dma("sp", self.lng[:], self.P["ln_g"][li, :].partition_broadcast(128), writes=["lng"])
        S.dma("sp", self.lnb[:], self.P["ln_b"][li, :].partition_broadcast(128), writes=["lnb"])

    def ln_finish(self, seq, sub, y, ykeys, first, last, defer=False):
        S = self.S
        i = self.lni % 2
        z, hn, hb, stt = self.lnz[i], self.lnh[i], self.lnhb[i], self.lnst[i]
        kz, kh, khb, kst = "lnz0", f"lnh{i}", f"lnhb{i}", f"lnst{i}"
        src = (self.P["x"] if first else self.h32)[seq, sub * 128:(sub + 1) * 128, :]
        hk = ("h32", seq, sub)
        S.dma("sp", hn[:], src, reads=[hk], writes=[kh])
        S.op("dve", lambda e: e.scalar_tensor_tensor(out=z[:], in0=hn[:], scalar=float(ALPHA), in1=y, op0=ALU.mult, op1=ALU.add),
             reads=[kh] + list(ykeys), writes=[kz])
        S.op("dve", lambda e: e.bn_stats(out=stt[:, 0:6], in_=z[:, 0:512]), reads=[kz], writes=[kst + "a"])
        S.op("dve", lambda e: e.bn_stats(out=stt[:, 6:12], in_=z[:, 512:1024]), reads=[kz], writes=[kst + "b"])
        S.op("dve", lambda e: e.bn_aggr(out=stt[:, 12:14], in_=stt[:, 0:12]), reads=[kst + "a", kst + "b"], writes=[kst + "c"])
        S.op("act", lambda e: e.activation(out=stt[:, 14:15], in_=stt[:, 13:14], func=AF.Sqrt, bias=self.cst[:, 128:129], scale=1.0),
             reads=[kst + "c"], writes=[kst + "d0"])
        S.op("dve", lambda e: e.reciprocal(out=stt[:, 14:15], in_=stt[:, 14:15]), reads=[kst + "d0"], writes=[kst + "d"])
        S.op("dve", lambda e: e.tensor_scalar(out=stt[:, 15:16], in0=stt[:, 12:13], scalar1=stt[:, 14:15], scalar2=-1.0, op0=ALU.mult, op1=ALU.mult),
             reads=[kst + "c", kst + "d"], writes=[kst + "e"])
        S.op("act", lambda e: e.activation(out=z[:], in_=z[:], func=AF.Identity, bias=stt[:, 15:16], scale=stt[:, 14:15]),
             reads=[kz, kst + "d", kst + "e"], writes=[kz])
        S.op("dve", lambda e: e.tensor_tensor(out=z[:], in0=z[:], in1=self.lng[:], op=ALU.mult), reads=[kz, "lng"], writes=[kz])
        S.op("dve", lambda e: e.tensor_tensor(out=hn[:], in0=z[:], in1=self.lnb[:], op=ALU.add), reads=[kz, "lnb"], writes=[kh])
        dst = (self.out if last else self.h32)[seq, sub * 128:(sub + 1) * 128, :]
        S.dma("sp", dst, hn[:], reads=[kh], writes=[hk])
        if not last:
            S.op("act", lambda e: e.copy(out=hb[:], in_=hn[:]), reads=[kh], writes=[khb])
            if defer:
                pi = self.lni % 2 if getattr(self, "force_pi", None) is None else self.force_pi
                self.lni += 1
                return (hb, khb, sub, pi)
            self.to_hT(hb, khb, sub)
        else:
            self.lni += 1
        return None

    def gps(self, parts=128, n=512):
        i = self.gi
        self.gi += 1
        b = self.gen_list[i % len(self.gen_list)]
        return self.PS[b[0]][0:parts, b[1] * 512: b[1] * 512 + n], (f"PS{b[0]}", b[1])

    def hTs(self, k, t0, n):
        return self.hT[:, k, 2 + t0: 2 + t0 + n]

    def hTkeys(self, t0, n):
        return [("hT", s) for s in range(t0 // 128, (t0 + n + 127) // 128)]

    def dbg_store(self, name, ap, keys, dst_slice=None):
        if self.dbg and name in self.dbg:
            d = self.dbg_out[name] if dst_slice is None else dst_slice(self.dbg_out[name])
            self.S.dma("sp", d, ap, reads=keys)

    def stage_mlp(self, seq, layer, last):
        S, A, PS = self.S, self.A, self.PS
        self.stage_begin()
        self.load_ln(layer * 3 + 2)
        acc = A.alloc([16, D], F32)
        w1g = [A.alloc([8, 512], BF16) for _ in range(2)]
        w2g = [A.alloc([4, D], BF16) for _ in range(2)]
        hid = [A.alloc([4, 512], BF16) for _ in range(2)]
        rl = [A.alloc([512], F32) for _ in range(2)]
        w1 = self.P["mlp_w1"][layer]
        w2 = self.P["mlp_w2"][layer]
        nb = 0
        ny = 0
        for g in range(8):
            gi = g % 2
            self.load_w(w1g[gi], w1[:, g * 512:(g + 1) * 512], f"w1g{gi}")
            self.load_w(w2g[gi], w2[g * 512:(g + 1) * 512, :], f"w2g{gi}")
            for t in range(4):
                hi = (g * 4 + t) % 2
                for fc in range(4):
                    nb += 1
                    ps, pk = self.gps()
                    for k in range(8):
                        S.op("pe", (lambda e, k=k, fc=fc, ps=ps, gi=gi, t=t: e.matmul(ps, lhsT=w1g[gi][:, k, fc * 128:(fc + 1) * 128], rhs=self.hTs(k, t * 512, 512), start=(k == 0), stop=(k == 7))),
                             reads=[f"w1g{gi}"] + self.hTkeys(t * 512, 512), writes=[pk])
                    ri = nb % 2
                    S.op("act", (lambda e, ps=ps, ri=ri: e.activation(out=rl[ri], in_=ps, func=AF.Relu)), reads=[pk], writes=[f"rl{ri}"])
                    S.op("dve", (lambda e, ri=ri, hi=hi, fc=fc: e.tensor_tensor(out=hid[hi][:, fc, :], in0=rl[ri], in1=rl[ri], op=ALU.mult)),
                         reads=[f"rl{ri}"], writes=[(f"hid{hi}", fc)])
                for sub in range(4):
                    yi = ny % 2
                    ny += 1
                    py = PS[2 + yi]
                    for half in range(2):
                        for fc in range(4):
                            S.op("pe", (lambda e, fc=fc, half=half, py=py, hi=hi, gi=gi, sub=sub: e.matmul(py[:, half * 512:(half + 1) * 512], lhsT=hid[hi][:, fc, sub * 128:(sub + 1) * 128], rhs=w2g[gi][:, fc, half * 512:(half + 1) * 512], start=(fc == 0), stop=(fc == 3))),
                                 reads=[(f"hid{hi}", fc), f"w2g{gi}"], writes=[(f"PS{2 + yi}", half)])
                    a = acc[:, t * 4 + sub, :]
                    ak = ("acc", t * 4 + sub)
                    pkeys = [(f"PS{2 + yi}", 0), (f"PS{2 + yi}", 1)]
                    if g == 0:
                        S.op("act", (lambda e, a=a, py=py: e.copy(out=a, in_=py[:, :])), reads=pkeys, writes=[ak])
                    else:
                        S.op("dve", (lambda e, a=a, py=py: e.tensor_tensor(out=a, in0=a, in1=py[:, :], op=ALU.add)), reads=pkeys + [ak], writes=[ak])
        for s16 in range(16):
            self.ln_finish(seq, s16, acc[:, s16, :], [("acc", s16)], first=False, last=last)

    def stage_memattn(self, seq, layer):
        S, A, PS = self.S, self.A, self.PS
        self.stage_begin()
        self.load_ln(layer * 3 + 1)
        wb = [A.alloc([8, D], BF16) for _ in range(2)]
        memb = A.alloc([2, D], BF16)
        memT = A.alloc([8, MEMT], BF16)
        KT = A.alloc([8, MEMT], BF16)
        V = A.alloc([2, D], BF16)
        QT = A.alloc([8, 512], BF16)
        Pn = [A.alloc([4, MEMT], BF16) for _ in range(2)]
        PT = A.alloc([8, 512], BF16)
        OT = A.alloc([8, 512], BF16)
        sm = [A.alloc([16], F32) for _ in range(2)]
        wkv = self.P["mx_w_kv"][layer]
        self.load_w(wb[0], wkv[:, 0:D], "wb0")
        self.load_w(wb[1], wkv[:, D:2 * D], "wb1")
        S.dma("pool", memb, self.P["mem"][seq].rearrange("(a p) d -> p a d", p=128), writes=["memb"])
        for mt in range(2):
            ps = PS[0][:, mt * 512:(mt + 1) * 512].bitcast(BF16)
            for c in range(8):
                S.op("pe", (lambda e, c=c, mt=mt, ps=ps: e.transpose(ps[:, c * 128:(c + 1) * 128], memb[:, mt, c * 128:(c + 1) * 128], self.ident[:])),
                     reads=["memb", "ident"], writes=[("PS0", mt)])
            S.op("dve", (lambda e, mt=mt, ps=ps: e.tensor_copy(out=memT[:, :, mt * 128:(mt + 1) * 128], in_=ps.rearrange("p (c t) -> p c t", c=8))),
                 reads=[("PS0", mt)], writes=[("memT", mt)])
        mk = [("memT", 0), ("memT", 1)]
        for oc in range(8):
            ps, pk = self.gps(n=MEMT)
            for k in range(8):
                S.op("pe", (lambda e, k=k, oc=oc, ps=ps: e.matmul(ps, lhsT=wb[0][:, k, oc * 128:(oc + 1) * 128], rhs=memT[:, k, :], start=(k == 0), stop=(k == 7))),
                     reads=["wb0"] + mk, writes=[pk])
            S.op("act", (lambda e, oc=oc, ps=ps: e.copy(out=KT[:, oc, :], in_=ps)), reads=[pk], writes=[("KT", oc)])
        for mt in range(2):
            for half in range(2):
                ps, pk = self.gps()
                for k in range(8):
                    S.op("pe", (lambda e, k=k, mt=mt, half=half, ps=ps: e.matmul(ps, lhsT=memT[:, k, mt * 128:(mt + 1) * 128], rhs=wb[1][:, k, half * 512:(half + 1) * 512], start=(k == 0), stop=(k == 7))),
                         reads=["wb1"] + mk, writes=[pk])
                S.op("dve", (lambda e, mt=mt, half=half, ps=ps: e.tensor_copy(out=V[:, mt, half * 512:(half + 1) * 512], in_=ps)), reads=[pk], writes=[("V", mt, half)])
        vk = [("V", a, b) for a in range(2) for b in range(2)]
        self.load_w(wb[0], self.P["mx_w_q"][layer], "wb0")
        self.load_w(wb[1], self.P["mx_w_o"][layer], "wb1")
        for t in range(4):
            for oc in range(8):
                ps, pk = self.gps()
                for k in range(8):
                    S.op("pe", (lambda e, k=k, oc=oc, ps=ps, t=t: e.matmul(ps, lhsT=wb[0][:, k, oc * 128:(oc + 1) * 128], rhs=self.hTs(k, t * 512, 512), start=(k == 0), stop=(k == 7))),
                         reads=["wb0"] + self.hTkeys(t * 512, 512), writes=[pk])
                S.op("act", (lambda e, oc=oc, ps=ps: e.activation(out=QT[:, oc, :], in_=ps, func=AF.Copy, scale=1.0 / 16.0)), reads=[pk], writes=[("QT", oc)])
            def mem_A(sub):
                pi = sub % 2
                ps = PS[0] if pi == 0 else PS[3]
                spn = "PS0" if pi == 0 else "PS3"
                pkeys = [(spn, 0), (spn, 1)]
                for h in range(4):
                    for c in range(2):
                        S.op("pe", (lambda e, h=h, c=c, ps=ps, sub=sub: e.matmul(ps[:, h * 256:(h + 1) * 256], lhsT=QT[:, 2 * h + c, sub * 128:(sub + 1) * 128], rhs=KT[:, 2 * h + c, :], start=(c == 0), stop=(c == 1))),
                             reads=[("QT", 2 * h + c), ("KT", 2 * h + c)], writes=[(spn, h // 2)])
                smi = sm[pi]
                ks = f"sm{pi}"
                S.op("dve", (lambda e, ps=ps, smi=smi: e.tensor_reduce(out=smi[:, 0:4], in_=ps[:, :].rearrange("p (h m) -> p h m", h=4), axis=AX.X, op=ALU.max, negate=True)),
                     reads=pkeys, writes=[ks + "m"])
                pn = Pn[pi]
                for h in range(4):
                    S.op("act", (lambda e, h=h, ps=ps, smi=smi, pn=pn: e.activation(out=pn[:, h, :], in_=ps[:, h * 256:(h + 1) * 256], func=AF.Exp, bias=smi[:, h:h + 1], scale=1.0, accum_out=smi[:, 4 + h:5 + h])),
                         reads=[(spn, h // 2), ks + "m"], writes=[(f"Pn{pi}", h), (ks + "s", h)])
                S.op("dve", (lambda e, smi=smi: e.reciprocal(out=smi[:, 8:12], in_=smi[:, 4:8])), reads=[(ks + "s", h) for h in range(4)], writes=[ks + "r"])
                S.op("dve", (lambda e, smi=smi, pn=pn: e.tensor_tensor(out=pn[:, :, :], in0=pn[:, :, :], in1=smi[:, 8:12].unsqueeze(2).to_broadcast([128, 4, MEMT]), op=ALU.mult)),
                     reads=[(f"Pn{pi}", h) for h in range(4)] + [ks + "r"], writes=[(f"Pn{pi}", h) for h in range(4)])

            def mem_B(sub):
                pi = sub % 2
                pn = Pn[pi]
                pt = PS[1][:, pi * 512:(pi + 1) * 512].bitcast(BF16)
                ptk = ("PS1", pi)
                for h in range(4):
                    for mc in range(2):
                        S.op("pe", (lambda e, h=h, mc=mc, pt=pt, pn=pn: e.transpose(pt[:, (h * 2 + mc) * 128:(h * 2 + mc + 1) * 128], pn[:, h, mc * 128:(mc + 1) * 128], self.ident[:])),
                             reads=[(f"Pn{pi}", h), "ident"], writes=[ptk])
                S.op("dve", (lambda e, pt=pt, sub=sub: e.tensor_copy(out=PT[:, :, sub * 128:(sub + 1) * 128], in_=pt.rearrange("p (c t) -> p c t", c=8))),
                     reads=[ptk], writes=[("PT", sub)])

            for sq_ in range(5):
                if sq_ < 4:
                    mem_A(sq_)
                if sq_ >= 1:
                    mem_B(sq_ - 1)
            for oc in range(8):
                h, c = oc // 2, oc % 2
                ps, pk = self.gps()
                for mc in range(2):
                    S.op("pe", (lambda e, h=h, c=c, mc=mc, ps=ps: e.matmul(ps, lhsT=V[:, mc, h * 256 + c * 128: h * 256 + (c + 1) * 128], rhs=PT[:, h * 2 + mc, :], start=(mc == 0), stop=(mc == 1))),
                         reads=vk + [("PT", s) for s in range(4)], writes=[pk])
                S.op("act", (lambda e, oc=oc, ps=ps: e.copy(out=OT[:, oc, :], in_=ps)), reads=[pk], writes=[("OT", oc)])
            self.out_proj(seq, t, lambda k, sub: OT[:, k, sub * 128:(sub + 1) * 128], [("OT", k) for k in range(8)], wb[1], "wb1", 8, first=False)

    def out_proj(self, seq, t, lhs_fn, lkeys, w, wkey, nk, first):
        S, PS = self.S, self.PS
        pend = None
        for sub in range(4):
            yi = sub % 2
            py = PS[2 + yi]
            for half in range(2):
                for k in range(nk):
                    S.op("pe", (lambda e, k=k, half=half, py=py, sub=sub: e.matmul(py[:, half * 512:(half + 1) * 512], lhsT=lhs_fn(k, sub), rhs=w[:, k, half * 512:(half + 1) * 512], start=(k == 0), stop=(k == nk - 1))),
                         reads=list(lkeys) + [wkey], writes=[(f"PS{2 + yi}", half)])
            if pend is not None:
                self.to_hT(*pend)
            pend = self.ln_finish(seq, t * 4 + sub, py[:, :], [(f"PS{2 + yi}", 0), (f"PS{2 + yi}", 1)], first=first, last=False, defer=True)
        if pend is not None:
            self.to_hT(*pend)

    def stage_lru(self, seq, layer, j, first):
        S, A, PS = self.S, self.A, self.PS
        self.stage_begin()
        self.load_ln(layer * 3 + 0)
        win = A.alloc([8, 2 * D_RNN], BF16)
        wout = A.alloc([NBLK, D], BF16, parts=BS)
        gw = A.alloc([2 * NBLK, BS], BF16, parts=BS)
        vec = A.alloc([NBLK, 8], F32, parts=BS)
        c8 = A.alloc([NBLK], F32, parts=BS)
        carry = A.alloc([NBLK], F32, parts=BS)
        xpb = [A.alloc([516], F32, parts=BS) for _ in range(2)]
        hist = A.alloc([NBLK, 4], F32, parts=BS)
        mT = A.alloc([NBLK, 512], BF16, parts=BS)
        NB2 = 2
        gb = [A.alloc([512], F32, parts=BS) for _ in range(NB2)]
        xr = [A.alloc([512], F32, parts=BS) for _ in range(NB2)]
        xrb = [A.alloc([512], BF16, parts=BS) for _ in range(NB2)]
        rg = [A.alloc([512], F32, parts=BS) for _ in range(NB2)]
        ig = [A.alloc([512], F32, parts=BS) for _ in range(NB2)]
        aa = [A.alloc([512], F32, parts=BS) for _ in range(NB2)]
        sq = [A.alloc([512], F32, parts=BS) for _ in range(NB2)]
        hs = [A.alloc([512], F32, parts=BS) for _ in range(NB2)]
        self.load_w(win, self.P["lru_w_in"][j], "win")
        S.dma("pool", wout, self.P["lru_w_out"][j].rearrange("(n p) d -> p n d", p=BS), writes=["wout"])
        S.dma("pool", gw, self.P["lru_gate_w"][j], writes=["gw"])
        S.dma("sp", vec, self.P["lru_vec"][j], writes=["vec"])
        tx = A.alloc([NBLK], F32, parts=BS)
        tl = A.alloc([NBLK], F32, parts=BS)
        tu = A.alloc([NBLK], F32, parts=BS)
        S.op("act", lambda e: e.activation(out=tx, in_=vec[:, :, 7], func=AF.Exp, scale=-1.0), reads=["vec"], writes=["tx"])
        S.op("act", lambda e: e.activation(out=tl, in_=tx, func=AF.Ln, bias=1.0, scale=1.0), reads=["tx"], writes=["tl"])
        S.op("dve", lambda e: e.tensor_scalar(out=tu, in0=tx, scalar1=-0.25, scalar2=1.0 / 3.0, op0=ALU.mult, op1=ALU.add), reads=["tx"], writes=["tu"])
        S.op("dve", lambda e: e.tensor_tensor(out=tu, in0=tu, in1=tx, op=ALU.mult), reads=["tu", "tx"], writes=["tu"])
        S.op("dve", lambda e: e.tensor_scalar(out=tu, in0=tu, scalar1=-1.0, scalar2=0.5, op0=ALU.mult, op1=ALU.add), reads=["tu"], writes=["tu"])
        S.op("dve", lambda e: e.tensor_tensor(out=tu, in0=tu, in1=tx, op=ALU.mult), reads=["tu", "tx"], writes=["tu"])
        S.op("dve", lambda e: e.tensor_scalar(out=tu, in0=tu, scalar1=-1.0, scalar2=1.0, op0=ALU.mult, op1=ALU.add), reads=["tu"], writes=["tu"])
        S.op("dve", lambda e: e.tensor_tensor(out=tu, in0=tu, in1=tx, op=ALU.mult), reads=["tu", "tx"], writes=["tu"])
        S.op("dve", lambda e: e.tensor_tensor(out=tu, in0=tu, in1=tl, op=ALU.subtract), reads=["tu", "tl"], writes=["tu"])
        S.op("dve", lambda e: e.tensor_scalar(out=tx, in0=tx, scalar1=0.05, scalar2=None, op0=ALU.is_lt), reads=["tx", "tu"], writes=["tx"])
        S.op("dve", lambda e: e.tensor_tensor(out=tu, in0=tu, in1=tx, op=ALU.mult), reads=["tu", "tx"], writes=["tu"])
        S.op("dve", lambda e: e.tensor_tensor(out=tu, in0=tu, in1=tl, op=ALU.add), reads=["tu", "tl"], writes=["tu"])
        S.op("dve", lambda e: e.tensor_scalar(out=c8, in0=tu, scalar1=-8.0, scalar2=None, op0=ALU.mult), reads=["tu"], writes=["c8"])
        S.op("dve", lambda e: e.memset(carry, 0.0), writes=["carry"])
        S.op("dve", lambda e: e.memset(hist, 0.0), writes=[("hist", n) for n in range(NBLK)])
        nb = 0

        class _Rec:
            def __init__(s_):
                s_.ops = []

            def op(s_, *a, **k):
                s_.ops.append((a, k))

        S0 = self.S
        for t in range(4):
            nrecs = []
            for n in range(NBLK):
                S = _Rec()
                nrecs.append(S)
                self.gen_list = [(0, n % 2)]
                bi = n % NB2
                ps, pk = self.gps(parts=BS)
                for k in range(8):
                    S.op("pe", (lambda e, k=k, n=n, ps=ps, t=t: e.matmul(ps, lhsT=win[:, k, n * BS:(n + 1) * BS], rhs=self.hTs(k, t * 512, 512), start=(k == 0), stop=(k == 7))),
                         reads=["win"] + self.hTkeys(t * 512, 512), writes=[pk])
                S.op("act", (lambda e, ps=ps, bi=bi: e.activation(out=gb[bi], in_=ps, func=AF.Gelu)), reads=[pk], writes=[f"gb{bi}"])
                ps2, pk2 = self.gps(parts=BS)
                for k in range(8):
                    S.op("pe", (lambda e, k=k, n=n, ps2=ps2, t=t: e.matmul(ps2, lhsT=win[:, k, D_RNN + n * BS: D_RNN + (n + 1) * BS], rhs=self.hTs(k, t * 512, 512), start=(k == 0), stop=(k == 7))),
                         reads=["win"] + self.hTkeys(t * 512, 512), writes=[pk2])
                xp = xpb[bi]
                xk = f"xpb{bi}"
                S.op("pool", (lambda e, xp=xp, n=n: e.tensor_copy(out=xp[:, 0:4], in_=hist[:, n, :])), reads=[("hist", n)], writes=[xk + "h"])
                S.op("act", (lambda e, xp=xp, ps2=ps2: e.copy(out=xp[:, 4:516], in_=ps2)), reads=[pk2], writes=[xk])
                x_ = xr[bi]
                kx = f"xr{bi}"
                S.op("dve", (lambda e, xp=xp, x_=x_, n=n: e.tensor_scalar(out=x_, in0=xp[:, 1:513], scalar1=vec[:, n, 0:1], scalar2=vec[:, n, 4:5], op0=ALU.mult, op1=ALU.add)),
                     reads=[xk, xk + "h", "vec"], writes=[kx])
                for kk in range(1, 4):
                    S.op("dve", (lambda e, xp=xp, x_=x_, n=n, kk=kk: e.scalar_tensor_tensor(out=x_, in0=xp[:, 1 + kk:513 + kk], scalar=vec[:, n, kk:kk + 1], in1=x_, op0=ALU.mult, op1=ALU.add)),
                         reads=[xk, xk + "h", "vec", kx], writes=[kx])
                S.op("pool", (lambda e, xp=xp, n=n: e.tensor_copy(out=hist[:, n, :], in_=xp[:, 512:516])), reads=[xk], writes=[("hist", n)])
                S.op("act", (lambda e, x_=x_, bi=bi: e.copy(out=xrb[bi], in_=x_)), reads=[kx], writes=[f"xrb{bi}"])
                pg, pgk = self.gps(parts=BS)
                S.op("pe", (lambda e, n=n, pg=pg, bi=bi: e.matmul(pg, lhsT=gw[:, n, :], rhs=xrb[bi], start=True, stop=True)), reads=["gw", f"xrb{bi}"], writes=[pgk])
                S.op("act", (lambda e, pg=pg, bi=bi, n=n: e.activation(out=rg[bi], in_=pg, func=AF.Sigmoid, bias=vec[:, n, 5:6], scale=1.0)), reads=[pgk, "vec"], writes=[f"rg{bi}"])
                pg2, pgk2 = self.gps(parts=BS)
                S.op("pe", (lambda e, n=n, pg2=pg2, bi=bi: e.matmul(pg2, lhsT=gw[:, NBLK + n, :], rhs=xrb[bi], start=True, stop=True)), reads=["gw", f"xrb{bi}"], writes=[pgk2])
                S.op("act", (lambda e, pg2=pg2, bi=bi, n=n: e.activation(out=ig[bi], in_=pg2, func=AF.Sigmoid, bias=vec[:, n, 6:7], scale=1.0)), reads=[pgk2, "vec"], writes=[f"ig{bi}"])
                S.op("act", (lambda e, bi=bi, n=n: e.activation(out=aa[bi], in_=rg[bi], func=AF.Exp, scale=c8[:, n:n + 1])), reads=[f"rg{bi}", "c8"], writes=[f"aa{bi}"])
                S.op("act", (lambda e, bi=bi: e.activation(out=sq[bi], in_=aa[bi], func=AF.Square)), reads=[f"aa{bi}"], writes=[f"sq{bi}"])
                S.op("act", (lambda e, bi=bi: e.activation(out=sq[bi], in_=sq[bi], func=AF.Sqrt, bias=1.0, scale=-1.0)), reads=[f"sq{bi}"], writes=[f"sq{bi}"])
                S.op("pool", (lambda e, bi=bi: e.tensor_tensor(out=ig[bi], in0=ig[bi], in1=xr[bi], op=ALU.mult)), reads=[f"ig{bi}", kx], writes=[f"ig{bi}"])
                S.op("dve", (lambda e, bi=bi: e.tensor_tensor(out=ig[bi], in0=ig[bi], in1=sq[bi], op=ALU.mult)), reads=[f"ig{bi}", f"sq{bi}"], writes=[f"ig{bi}"])
                S.op("dve", (lambda e, bi=bi, n=n: e.tensor_tensor_scan(out=hs[bi], data0=aa[bi], data1=ig[bi], initial=carry[:, n:n + 1], op0=ALU.mult, op1=ALU.add)),
                     reads=[f"aa{bi}", f"ig{bi}", ("carry", n)], writes=[f"hs{bi}"])
                S.op("act", (lambda e, bi=bi, n=n: e.copy(out=carry[:, n:n + 1], in_=hs[bi][:, 511:512])), reads=[f"hs{bi}"], writes=[("carry", n)])
                S.op("dve", (lambda e, bi=bi, n=n: e.tensor_tensor(out=mT[:, n, :], in0=hs[bi], in1=gb[bi], op=ALU.mult)), reads=[f"hs{bi}", f"gb{bi}"], writes=[("mT", n)])
            S = S0
            self.gen_list = [(0, 0), (0, 1)]
            for n0 in range(0, NBLK, 2):
                la, lb = nrecs[n0].ops, nrecs[n0 + 1].ops
                for i_ in range(max(len(la), len(lb))):
                    if i_ < len(la):
                        S.op(*la[i_][0], **la[i_][1])
                    if i_ < len(lb):
                        S.op(*lb[i_][0], **lb[i_][1])
            self.out_proj(seq, t, lambda k, sub: mT[:, k, sub * 128:(sub + 1) * 128], [("mT", n) for n in range(NBLK)], wout, "wout", NBLK, first=first)


    def stage_rwkv(self, seq, layer, j, first):
        S, A, PS = self.S, self.A, self.PS
        self.stage_begin()
        self.load_ln(layer * 3 + 0)
        import os
        RW = BF16 if os.environ.get('RW_F32', '0') != '1' else F32
        C0 = float(np.exp(-0.5))
        wr = A.alloc([8, D], BF16)
        wk = A.alloc([8, D], BF16)
        wv = A.alloc([8, D], BF16)
        wo = A.alloc([8, D], BF16)
        w1b = A.alloc([8, 64], BF16)
        a1b = A.alloc([8, 64], BF16)
        g1b = A.alloc([8, 128], BF16)
        w2b = A.alloc([D], BF16, parts=64)
        a2b = A.alloc([D], BF16, parts=64)
        g2b = A.alloc([D], BF16)
        lxg = A.alloc([D], F32)
        lxb = A.alloc([D], F32)
        vec = A.alloc([8, 16], F32)
        omu = A.alloc([8, 6], F32)
        rwc = A.alloc([648], F32)
        mask1 = rwc[:, 0:256]
        masksl = rwc[:, 256:384]
        blk = rwc[:, 384:512]
        rmask = rwc[:, 512:640]
        ind2 = rwc[:, 640:642]
        gneps = rwc[:, 642:643]
        xs = [A.alloc([8, 128], BF16) for _ in range(2)]
        xprev = A.alloc([8, 128], BF16)
        tmpx = A.alloc([8, 128], BF16)
        xlast = A.alloc([8], BF16)
        thw = A.alloc([128], BF16, parts=64)
        la = A.alloc([128], BF16, parts=64)
        sgl = A.alloc([128], BF16)
        V = A.alloc([D], F32)
        Y = A.alloc([D], F32)
        Hst = A.alloc([8, 2, 64], F32)
        Hm = Hst if RW == F32 else A.alloc([8, 2, 64], RW)
        Vm = V if RW == F32 else A.alloc([D], RW)
        Hd = [A.alloc([64], F32) for _ in range(2)]
        scr = A.alloc([D], F32)
        ob = A.alloc([D], BF16)
        OT = A.alloc([8, 128], BF16)
        stt = A.alloc([112], F32)
        names = ["sg", "a", "kq", "kk", "k", "r", "rn", "cum", "G", "Gi", "ex", "ab", "t1", "km", "E", "BhT", "KhT", "rkr"]
        PBs = []
        tmps = {n: A.alloc([128], F32) for n in names if n != "G"}
        for i in range(2):
            pb = dict(tmps)
            pb["G"] = A.alloc([128], F32)
            pb["ATRT"] = A.alloc([256], RW)
            pb["BT"] = A.alloc([128], RW)
            pb["KT"] = A.alloc([128], RW)
            pb["BK"] = A.alloc([256], RW)
            PBs.append(pb)
        HB = []
        for i in range(2):
            hb = {"Mk": A.alloc([256], RW), "Mb": A.alloc([256], RW), "X0": A.alloc([128], RW),
                  "XX": [A.alloc([256], RW) for _ in range(2)], "Z": [A.alloc([128], RW) for _ in range(2)],
                  "Gs": A.alloc([64], RW), "Us": A.alloc([64], RW)}
            S.op("dve", (lambda e, hb=hb: e.memset(hb["Us"], 0.0)), writes=[(f"hb{i}", "Us")])
            S.op("dve", (lambda e, hb=hb: e.memset(hb["Gs"], 0.0)), writes=[(f"hb{i}", "Gs")])
            HB.append(hb)

        self.load_w(w1b, self.P["rw_w1"][j], "w1b")
        self.load_w(a1b, self.P["rw_a1"][j], "a1b")
        self.load_w(g1b, self.P["rw_g1"][j], "g1b")
        self.load_w(wv, self.P["rw_w_v"][j], "wv")
        self.load_w(wr, self.P["rw_w_r"][j], "wr")
        self.load_w(wk, self.P["rw_w_k"][j], "wk")
        self.load_w(wo, self.P["rw_w_o"][j], "wo")
        S.dma("pool", w2b, self.P["rw_w2"][j], writes=["w2b"])
        S.dma("pool", a2b, self.P["rw_a2"][j], writes=["a2b"])
        S.dma("pool", g2b, self.P["rw_g2"][j], writes=["g2b"])
        S.dma("sp", vec, self.P["rw_vec"][j], writes=["vec"])
        S.dma("sp", rwc, self.P["rwc"], writes=["rwc"])
        S.dma("sp", lxg, self.P["rw_lnx"][j, 0, :].partition_broadcast(128), writes=["lxg"])
        S.dma("sp", lxb, self.P["rw_lnx"][j, 1, :].partition_broadcast(128), writes=["lxb"])
        S.op("dve", lambda e: e.tensor_scalar(out=omu, in0=vec[:, :, 0:6], scalar1=-1.0, scalar2=1.0, op0=ALU.mult, op1=ALU.add), reads=["vec"], writes=["omu"])
        S.op("dve", lambda e: e.memset(Hst, 0.0), writes=[("H", h) for h in range(16)])
        if RW != F32:
            S.op("dve", lambda e: e.memset(Hm, 0.0), writes=[("Hm", h) for h in range(16)])
        S.op("dve", lambda e: e.memset(xlast, 0.0), writes=["xlast"])
        self.force_pi = 0
        self.gen_list = [(0, 0), (0, 1)]
        HK = "H" if RW == F32 else "Hm"
        VK = "V" if RW == F32 else "Vm"
        coefps = PS[1][:, 512:528]
        ckey = ("PS1", 1)

        def mix(m, bi, hk, t0):
            S.op("dve", (lambda e: e.tensor_tensor(out=tmpx, in0=xprev, in1=vec[:, :, m:m + 1].to_broadcast([128, 8, 128]), op=ALU.mult)),
                 reads=["xprev", "vec"], writes=["tmpx"])
            S.op("dve", (lambda e: e.tensor_tensor(out=xs[bi], in0=self.hT[:, :, 2 + t0:2 + t0 + 128], in1=omu[:, :, m:m + 1].to_broadcast([128, 8, 128]), op=ALU.mult)),
                 reads=hk + ["omu"], writes=[f"xs{bi}"])
            S.op("pool", (lambda e: e.tensor_tensor(out=xs[bi], in0=xs[bi], in1=tmpx, op=ALU.add)), reads=[f"xs{bi}", "tmpx"], writes=[f"xs{bi}"])

        for tt in range(16):
            t0 = tt * 128
            hk = [("hT", tt)]
            S.op("pool", (lambda e, t0=t0: e.tensor_copy(out=xprev[:, :, 1:128], in_=self.hT[:, :, 2 + t0:2 + t0 + 127])), reads=hk, writes=["xprev"])
            S.op("pool", (lambda e: e.tensor_copy(out=xprev[:, :, 0:1], in_=xlast.unsqueeze(2))), reads=["xlast", "xprev"], writes=["xprev"])
            S.op("pool", (lambda e, t0=t0: e.tensor_copy(out=xlast.unsqueeze(2), in_=self.hT[:, :, 2 + t0 + 127:2 + t0 + 128])), reads=hk + ["xprev"], writes=["xlast"])
            mix(1, 0, hk, t0)
            ps, pk = self.gps(parts=64, n=128)
            for k in range(8):
                S.op("pe", (lambda e, k=k, ps=ps: e.matmul(ps, lhsT=w1b[:, k, :], rhs=xs[0][:, k, :], start=(k == 0), stop=(k == 7))), reads=["w1b", "xs0"], writes=[pk])
            S.op("act", (lambda e, ps=ps: e.activation(out=thw, in_=ps, func=AF.Tanh)), reads=[pk], writes=["thw"])
            mix(4, 1, hk, t0)
            ps, pk = self.gps(parts=64, n=128)
            for k in range(8):
                S.op("pe", (lambda e, k=k, ps=ps: e.matmul(ps, lhsT=a1b[:, k, :], rhs=xs[1][:, k, :], start=(k == 0), stop=(k == 7))), reads=["a1b", "xs1"], writes=[pk])
            S.op("act", (lambda e, ps=ps: e.copy(out=la, in_=ps)), reads=[pk], writes=["la"])
            mix(5, 0, hk, t0)
            ps, pk = self.gps(n=128)
            for k in range(8):
                S.op("pe", (lambda e, k=k, ps=ps: e.matmul(ps, lhsT=g1b[:, k, :], rhs=xs[0][:, k, :], start=(k == 0), stop=(k == 7))), reads=["g1b", "xs0"], writes=[pk])
            S.op("act", (lambda e, ps=ps: e.activation(out=sgl, in_=ps, func=AF.Sigmoid)), reads=[pk], writes=["sgl"])
            mix(3, 1, hk, t0)
            for half in range(2):
                ps, pk = self.gps()
                for k in range(8):
                    S.op("pe", (lambda e, k=k, ps=ps, half=half: e.matmul(ps, lhsT=xs[1][:, k, :], rhs=wv[:, k, half * 512:(half + 1) * 512], start=(k == 0), stop=(k == 7))), reads=["wv", "xs1"], writes=[pk])
                S.op("act", (lambda e, ps=ps, half=half: e.copy(out=V[:, half * 512:(half + 1) * 512], in_=ps)), reads=[pk], writes=[("V", half)])
                if RW != F32:
                    S.op("pool", (lambda e, half=half: e.tensor_copy(out=Vm[:, half * 512:(half + 1) * 512], in_=V[:, half * 512:(half + 1) * 512])), reads=[("V", half)], writes=[("Vm", half)])
            mix(0, 0, hk, t0)
            mix(2, 1, hk, t0)
            import os
            RWD = int(os.environ.get("RW_DBG", "9"))
            if RWD < 9:
                S.op("dve", (lambda e: e.memset(Y, 0.0)), reads=[("Ydone", 0), ("Ydone", 1)], writes=[("Y", h // 8, q, h) for h in range(16) for q in range(2)])
                S.op("pe", (lambda e: e.matmul(coefps, lhsT=sgl, rhs=g2b[:, 0:16], start=True, stop=True)), reads=["sgl", "g2b"], writes=[ckey])
            class _Rec:
                def __init__(s_):
                    s_.ops = []

                def op(s_, *a, **k):
                    s_.ops.append((a, k))

            recs = []
            for c in range(8 if RWD >= 2 else 0):
                recP = _Rec()
                S = recP
                pb = PBs[c % 2]
                pn = f"pb{c % 2}"
                K_ = lambda n, pn=pn: (pn, n) if n in ("G", "AT", "RT", "BT", "KT", "BK") else ("pbt", n)
                ps, pk = self.gps()
                for k in range(8):
                    S.op("pe", (lambda e, k=k, ps=ps, c=c: e.matmul(ps[:, 0:128], lhsT=wr[:, k, c * 128:(c + 1) * 128], rhs=xs[0][:, k, :], start=(k == 0), stop=(k == 7))), reads=["wr", "xs0"], writes=[pk])
                for k in range(8):
                    S.op("pe", (lambda e, k=k, ps=ps, c=c: e.matmul(ps[:, 128:256], lhsT=wk[:, k, c * 128:(c + 1) * 128], rhs=xs[1][:, k, :], start=(k == 0), stop=(k == 7))), reads=["wk", "xs1"], writes=[pk])
                S.op("pe", (lambda e, ps=ps, c=c: e.matmul(ps[:, 256:384], lhsT=w2b[:, c * 128:(c + 1) * 128], rhs=thw, start=True, stop=True)), reads=["w2b", "thw"], writes=[pk])
                S.op("pe", (lambda e, ps=ps, c=c: e.matmul(ps[:, 384:512], lhsT=a2b[:, c * 128:(c + 1) * 128], rhs=la, start=True, stop=True)), reads=["a2b", "la"], writes=[pk])
                vc = lambda i, c=c: vec[:, c, i:i + 1]
                S.op("act", (lambda e, ps=ps, pb=pb, vc=vc: e.activation(out=pb["sg"], in_=ps[:, 256:384], func=AF.Sigmoid, bias=vc(6), scale=1.0)), reads=[pk, "vec"], writes=[K_("sg")])
                S.op("act", (lambda e, ps=ps, pb=pb, vc=vc: e.activation(out=pb["a"], in_=ps[:, 384:512], func=AF.Sigmoid, bias=vc(7), scale=1.0)), reads=[pk, "vec"], writes=[K_("a")])
                S.op("act", (lambda e, ps=ps, pb=pb: e.copy(out=pb["k"], in_=ps[:, 128:256])), reads=[pk], writes=[K_("k")])
                S.op("dve", (lambda e, pb=pb, vc=vc: e.tensor_scalar(out=pb["kk"], in0=pb["k"], scalar1=vc(8), scalar2=None, op0=ALU.mult)), reads=[K_("k"), "vec"], writes=[K_("kk")])
                S.op("act", (lambda e, pb=pb: e.activation(out=pb["kq"], in_=pb["kk"], func=AF.Square)), reads=[K_("kk")], writes=[K_("kq")])
                S.op("dve", (lambda e, ps=ps, pb=pb: e.tensor_copy(out=pb["r"], in_=ps[:, 0:128])), reads=[pk], writes=[K_("r")])
                ps2, pk2 = self.gps(n=128)
                S.op("pe", (lambda e, ps2=ps2, pb=pb: e.matmul(ps2, lhsT=blk, rhs=pb["kq"], start=True, stop=True)), reads=["rwc", K_("kq")], writes=[pk2])
                S.op("act", (lambda e, ps2=ps2, pb=pb: e.activation(out=pb["rn"], in_=ps2, func=AF.Sqrt)), reads=[pk2], writes=[K_("rn")])
                S.op("dve", (lambda e, pb=pb: e.tensor_scalar(out=pb["rn"], in0=pb["rn"], scalar1=1e-12, scalar2=None, op0=ALU.max)), reads=[K_("rn")], writes=[K_("rn")])
                S.op("dve", (lambda e, pb=pb: e.reciprocal(out=pb["rn"], in_=pb["rn"])), reads=[K_("rn")], writes=[K_("rn")])
                S.op("dve", (lambda e, pb=pb: e.tensor_tensor(out=pb["kk"], in0=pb["kk"], in1=pb["rn"], op=ALU.mult)), reads=[K_("kk"), K_("rn")], writes=[K_("kk")])
                S.op("dve", (lambda e, pb=pb: e.tensor_tensor_scan(out=pb["cum"], data0=rmask, data1=pb["sg"], initial=0.0, op0=ALU.mult, op1=ALU.add)), reads=["rwc", K_("sg")], writes=[K_("cum")])
                S.op("act", (lambda e, pb=pb: e.activation(out=pb["G"], in_=pb["cum"], func=AF.Exp, scale=-C0)), reads=[K_("cum")], writes=[K_("G")])
                S.op("act", (lambda e, pb=pb: e.activation(out=pb["Gi"], in_=pb["cum"], func=AF.Exp, scale=C0)), reads=[K_("cum")], writes=[K_("Gi")])
                S.op("dve", (lambda e, pb=pb: e.tensor_tensor(out=pb["ex"], in0=pb["cum"], in1=pb["sg"], op=ALU.subtract)), reads=[K_("cum"), K_("sg")], writes=[K_("ex")])
                S.op("act", (lambda e, pb=pb: e.activation(out=pb["ex"], in_=pb["ex"], func=AF.Exp, scale=-C0)), reads=[K_("ex")], writes=[K_("ex")])
                S.op("dve", (lambda e, pb=pb: e.scalar_tensor_tensor(out=pb["ATRT"][:, 0:128], in0=pb["kk"], scalar=-1.0, in1=pb["ex"], op0=ALU.mult, op1=ALU.mult)), reads=[K_("kk"), K_("ex")], writes=[K_("AT")])
                S.op("dve", (lambda e, pb=pb: e.tensor_tensor(out=pb["ab"], in0=pb["kk"], in1=pb["a"], op=ALU.mult)), reads=[K_("kk"), K_("a")], writes=[K_("ab")])
                S.op("dve", (lambda e, pb=pb: e.tensor_tensor(out=pb["BT"], in0=pb["ab"], in1=pb["Gi"], op=ALU.mult)), reads=[K_("ab"), K_("Gi")], writes=[K_("BT")])
                S.op("dve", (lambda e, pb=pb, vc=vc: e.tensor_scalar(out=pb["t1"], in0=pb["a"], scalar1=-1.0, scalar2=vc(9), op0=ALU.add, op1=ALU.mult)), reads=[K_("a"), "vec"], writes=[K_("t1")])
                S.op("dve", (lambda e, pb=pb: e.scalar_tensor_tensor(out=pb["km"], in0=pb["t1"], scalar=1.0, in1=pb["k"], op0=ALU.add, op1=ALU.mult)), reads=[K_("t1"), K_("k")], writes=[K_("km")])
                S.op("dve", (lambda e, pb=pb: e.tensor_tensor(out=pb["KT"], in0=pb["km"], in1=pb["Gi"], op=ALU.mult)), reads=[K_("km"), K_("Gi")], writes=[K_("KT")])
                S.op("dve", (lambda e, pb=pb: e.tensor_tensor(out=pb["ATRT"][:, 128:256], in0=pb["r"], in1=pb["G"], op=ALU.mult)), reads=[K_("r"), K_("G")], writes=[K_("RT")])
                for q in range(2):
                    S.op("dve", (lambda e, pb=pb, q=q: e.tensor_scalar(out=pb["E"][:, q * 64:(q + 1) * 64], in0=pb["cum"][:, q * 64:(q + 1) * 64], scalar1=pb["cum"][:, q * 64 + 63:q * 64 + 64], scalar2=None, op0=ALU.subtract)),
                         reads=[K_("cum")], writes=[K_("E")])
                S.op("act", (lambda e, pb=pb: e.activation(out=pb["E"], in_=pb["E"], func=AF.Exp, scale=C0)), reads=[K_("E")], writes=[K_("E")])
                S.op("dve", (lambda e, pb=pb: e.tensor_tensor(out=pb["BhT"], in0=pb["ab"], in1=pb["E"], op=ALU.mult)), reads=[K_("ab"), K_("E")], writes=[K_("BhT")])
                S.op("dve", (lambda e, pb=pb: e.tensor_tensor(out=pb["KhT"], in0=pb["km"], in1=pb["E"], op=ALU.mult)), reads=[K_("km"), K_("E")], writes=[K_("KhT")])
                ps3, pk3 = self.gps(n=256)
                S.op("pe", (lambda e, ps3=ps3, pb=pb: e.transpose(ps3[:, 0:128], pb["BhT"], self.identf[:])), reads=[K_("BhT"), "identf"], writes=[pk3])
                S.op("pe", (lambda e, ps3=ps3, pb=pb: e.transpose(ps3[:, 128:256], pb["KhT"], self.identf[:])), reads=[K_("KhT"), "identf"], writes=[pk3])
                S.op("act", (lambda e, ps3=ps3, pb=pb: e.copy(out=pb["BK"], in_=ps3)), reads=[pk3], writes=[K_("BK")])
                S.op("dve", (lambda e, pb=pb, vc=vc: e.scalar_tensor_tensor(out=pb["rkr"], in0=pb["km"], scalar=vc(10), in1=pb["r"], op0=ALU.mult, op1=ALU.mult)), reads=[K_("km"), K_("r"), "vec"], writes=[K_("rkr")])
                S.op("pe", (lambda e, pb=pb, c=c: e.matmul(coefps[:, 2 * c:2 * c + 2], lhsT=pb["rkr"], rhs=ind2, start=True, stop=True)), reads=[K_("rkr"), "rwc"], writes=[ckey])
                recQ = _Rec()
                S = recQ
                NH = 2 if RWD >= 3 else 0
                st_ = []
                for hh in range(NH):
                    po = hh * 64
                    hb = HB[hh]
                    hn = f"hb{hh}"
                    pg = PS[2][:, hh * 512:(hh + 1) * 512]
                    pgk = ("PS2", hh)
                    S.op("pe", (lambda e, pb=pb, po=po, pg=pg: e.matmul(pg[:, 0:256], lhsT=pb["KT"][po:po + 64, :], rhs=pb["ATRT"][po:po + 64, :], start=True, stop=True)),
                         reads=[K_("KT"), K_("AT"), K_("RT")], writes=[pgk])
                    S.op("pe", (lambda e, pb=pb, po=po, pg=pg: e.matmul(pg[:, 256:512], lhsT=pb["BT"][po:po + 64, :], rhs=pb["ATRT"][po:po + 64, :], start=True, stop=True)),
                         reads=[K_("BT"), K_("AT"), K_("RT")], writes=[pgk])
                    S.op("dve", (lambda e, hb=hb, pg=pg: e.tensor_tensor(out=hb["Mk"], in0=pg[:, 0:256], in1=mask1, op=ALU.mult)), reads=[pgk, "rwc"], writes=[(hn, "Mk")])
                    S.op("dve", (lambda e, hb=hb, pg=pg: e.tensor_tensor(out=hb["Mb"], in0=pg[:, 256:512], in1=mask1, op=ALU.mult)), reads=[pgk, "rwc"], writes=[(hn, "Mb")])
                    S.op("pe", (lambda e, pb=pb, po=po, pg=pg: e.matmul(pg[:, 0:128], lhsT=pb["ATRT"][po:po + 64, 0:128], rhs=pb["BT"][po:po + 64, :], start=True, stop=True)),
                         reads=[K_("BT"), K_("AT")], writes=[pgk])
                    S.op("dve", (lambda e, hb=hb, pg=pg: e.tensor_tensor(out=hb["X0"], in0=pg[:, 0:128], in1=masksl, op=ALU.mult)), reads=[pgk, "rwc"], writes=[(hn, "X0")])
                    S.op("dve", (lambda e, hb=hb: e.tensor_tensor(out=hb["Z"][0], in0=hb["Mb"][:, 0:128], in1=self.identf[:], op=ALU.add)), reads=[(hn, "Mb"), "identf"], writes=[(hn, "Z0")])
                    st_.append({"XT": hb["Mb"][:, 0:128], "X": hb["X0"], "keys": [(hn, "Mb"), (hn, "X0")], "zi": 0})
                for lvl in range(5):
                    for hh in range(NH):
                        hb = HB[hh]
                        hn = f"hb{hh}"
                        bank = PS[2][:, hh * 512:(hh + 1) * 512]
                        bkey = ("PS2", hh)
                        sd = st_[hh]
                        X_ap, XT_ap, xk_keys, zi = sd["X"], sd["XT"], sd["keys"], sd["zi"]
                        xx = hb["XX"][lvl % 2]
                        xxk = (hn, f"XX{lvl % 2}")
                        if lvl < 4:
                            S.op("pe", (lambda e, bank=bank, X_ap=X_ap, XT_ap=XT_ap: e.matmul(bank[:, 0:128], lhsT=X_ap, rhs=XT_ap, start=True, stop=True)), reads=xk_keys, writes=[bkey])
                        S.op("pe", (lambda e, bank=bank, X_ap=X_ap, XT_ap=XT_ap: e.matmul(bank[:, 128:256], lhsT=XT_ap, rhs=X_ap, start=True, stop=True)), reads=xk_keys, writes=[bkey])
                        if lvl < 4:
                            S.op("act", (lambda e, bank=bank, xx=xx: e.copy(out=xx, in_=bank[:, 0:256])), reads=[bkey], writes=[xxk])
                        else:
                            S.op("act", (lambda e, bank=bank, xx=xx: e.copy(out=xx[:, 128:256], in_=bank[:, 128:256])), reads=[bkey], writes=[xxk])
                        XT_ap, X_ap = xx[:, 0:128], xx[:, 128:256]
                        zo = hb["Z"][zi]
                        zn = hb["Z"][1 - zi]
                        S.op("pe", (lambda e, bank=bank, X_ap=X_ap, zo=zo: e.matmul(bank[:, 256:384], lhsT=X_ap, rhs=zo, start=True, stop=True)), reads=[xxk, (hn, f"Z{zi}")], writes=[bkey])
                        S.op("dve", (lambda e, bank=bank, zo=zo, zn=zn: e.tensor_tensor(out=zn, in0=bank[:, 256:384], in1=zo, op=ALU.add)), reads=[bkey, (hn, f"Z{zi}")], writes=[(hn, f"Z{1 - zi}")])
                        sd["X"], sd["XT"], sd["keys"], sd["zi"] = X_ap, XT_ap, [xxk], 1 - zi
                for hh in range(NH):
                    HB[hh]["TT"] = HB[hh]["Z"][st_[hh]["zi"]]
                    HB[hh]["TTk"] = (f"hb{hh}", f"Z{st_[hh]['zi']}")
                for q in range(2 if RWD >= 4 else 0):
                    ph = q * 64
                    for hh in range(2):
                        po, h, hb, hn = hh * 64, 2 * c + hh, HB[hh], f"hb{hh}"
                        pq = PS[3][:, hh * 512:(hh + 1) * 512]
                        pqk = ("PS3", hh)
                        S.op("pe", (lambda e, pq=pq, hh=hh, c=c, pb=pb: e.matmul(pq[:, 0:64], lhsT=pb["ATRT"][:, 0:128], rhs=Hm[:, c, hh, :], start=True, stop=False)),
                             reads=[K_("AT"), (HK, h)], writes=[pqk])
                        S.op("pe", (lambda e, pq=pq, hb=hb, h=h: e.matmul(pq[:, 0:64], lhsT=hb["Mk"][:, 0:128], rhs=Vm[:, h * 64:(h + 1) * 64], start=False, stop=True)),
                             reads=[(hn, "Mk"), (VK, h // 8)], writes=[pqk])
                        S.op("act", (lambda e, pq=pq, ph=ph, hb=hb: e.copy(out=hb["Gs"][ph:ph + 64, :], in_=pq[ph:ph + 64, 0:64])), reads=[pqk], writes=[(hn, "Gs")])
                    for hh in range(2):
                        po, h, hb, hn = hh * 64, 2 * c + hh, HB[hh], f"hb{hh}"
                        pq = PS[3][:, hh * 512:(hh + 1) * 512]
                        pqk = ("PS3", hh)
                        S.op("pe", (lambda e, pq=pq, ph=ph, hb=hb: e.matmul(pq[:, 64:128], lhsT=hb["TT"][ph:ph + 64, :], rhs=hb["Gs"][ph:ph + 64, :], start=True, stop=True)),
                             reads=[hb["TTk"], (hn, "Gs")], writes=[pqk])
                        S.op("dve", (lambda e, pq=pq, ph=ph, hb=hb: e.tensor_copy(out=hb["Us"][ph:ph + 64, :], in_=pq[ph:ph + 64, 64:128])), reads=[pqk], writes=[(hn, "Us")])
                    for hh in range(2):
                        po, h, hb, hn = hh * 64, 2 * c + hh, HB[hh], f"hb{hh}"
                        pq = PS[3][:, hh * 512:(hh + 1) * 512]
                        pqk = ("PS3", hh)
                        vh = Vm[ph:ph + 64, h * 64:(h + 1) * 64]
                        S.op("pe", (lambda e, pq=pq, hh=hh, c=c, pb=pb: e.matmul(pq[:, 128:192], lhsT=pb["ATRT"][:, 128:256], rhs=Hm[:, c, hh, :], start=True, stop=False)),
                             reads=[K_("RT"), (HK, h)], writes=[pqk])
                        S.op("pe", (lambda e, pq=pq, hb=hb: e.matmul(pq[:, 128:192], lhsT=hb["Mb"][:, 128:256], rhs=hb["Us"][:, :], start=False, stop=False)),
                             reads=[(hn, "Mb"), (hn, "Us")], writes=[pqk])
                        S.op("pe", (lambda e, pq=pq, hb=hb, h=h: e.matmul(pq[:, 128:192], lhsT=hb["Mk"][:, 128:256], rhs=Vm[:, h * 64:(h + 1) * 64], start=False, stop=True)),
                             reads=[(hn, "Mk"), (VK, h // 8)], writes=[pqk])
                        S.op("pe", (lambda e, pq=pq, ph=ph, hb=hb, pb=pb: e.matmul(pq[:, 192:256], lhsT=pb["BK"][ph:ph + 64, 0:128], rhs=hb["Us"][ph:ph + 64, :], start=True, stop=False)),
                             reads=[K_("BK"), (hn, "Us")], writes=[pqk])
                        S.op("pe", (lambda e, pq=pq, ph=ph, pb=pb, vh=vh: e.matmul(pq[:, 192:256], lhsT=pb["BK"][ph:ph + 64, 128:256], rhs=vh, start=False, stop=True)),
                             reads=[K_("BK"), (VK, h // 8)], writes=[pqk])
                        S.op("act", (lambda e, pq=pq, ph=ph, h=h: e.copy(out=Y[ph:ph + 64, h * 64:(h + 1) * 64], in_=pq[ph:ph + 64, 128:192])), reads=[pqk, ("Ydone", 0), ("Ydone", 1)], writes=[("Y", h // 8, q, h)])
                        S.op("act", (lambda e, pq=pq, po=po, hh=hh: e.copy(out=Hd[hh][po:po + 64, :], in_=pq[po:po + 64, 192:256])), reads=[pqk], writes=[f"Hd{hh}"])
                        S.op("dve", (lambda e, po=po, c=c, pb=pb, q=q, hh=hh: e.scalar_tensor_tensor(out=Hst[po:po + 64, c, hh, :], in0=Hst[po:po + 64, c, hh, :], scalar=pb["G"][po:po + 64, q * 64 + 63:q * 64 + 64], in1=Hd[hh][po:po + 64, :], op0=ALU.mult, op1=ALU.add)),
                             reads=[f"Hd{hh}", ("H", h), K_("G")], writes=[("H", h)])
                        if RW != F32:
                            S.op("act", (lambda e, po=po, c=c, hh=hh: e.copy(out=Hm[po:po + 64, c, hh, :], in_=Hst[po:po + 64, c, hh, :])), reads=[("H", h)], writes=[("Hm", h)])
                recs.append((recP.ops, recQ.ops))
            S = self.S
            if recs:
                for a_, k_ in recs[0][0]:
                    S.op(*a_, **k_)
                for c in range(len(recs)):
                    qa = recs[c][1]
                    pb_ops = recs[c + 1][0] if c + 1 < len(recs) else []
                    ia = ib = 0
                    na, nb_ = len(qa), len(pb_ops)
                    while ia < na or ib < nb_:
                        if ib >= nb_ or (ia < na and ia * max(nb_, 1) <= ib * na):
                            S.op(*qa[ia][0], **qa[ia][1])
                            ia += 1
                        else:
                            S.op(*pb_ops[ib][0], **pb_ops[ib][1])
                            ib += 1
            ykeys = [("Y", h // 8, q, h) for h in range(16) for q in range(2)]
            Y3 = Y.rearrange("p (h d) -> p h d", h=16)
            bc = lambda ap: ap.unsqueeze(2).to_broadcast([128, 16, 64])
            S.op("act", (lambda e: e.copy(out=stt[:, 96:112], in_=coefps)), reads=[ckey], writes=["coef"])
            S.op("dve", (lambda e: e.tensor_reduce(out=stt[:, 0:16], in_=Y3, axis=AX.X, op=ALU.add)), reads=ykeys, writes=["st_s1"])
            S.op("act", (lambda e: e.activation(out=scr, in_=Y, func=AF.Square)), reads=ykeys, writes=["scr"])
            S.op("dve", (lambda e: e.tensor_reduce(out=stt[:, 16:32], in_=scr.rearrange("p (h d) -> p h d", h=16), axis=AX.X, op=ALU.add)), reads=["scr"], writes=["st_s2"])
            S.op("dve", (lambda e: e.tensor_scalar(out=stt[:, 32:48], in0=stt[:, 0:16], scalar1=1.0 / 64.0, scalar2=None, op0=ALU.mult)), reads=["st_s1"], writes=["st_mean"])
            S.op("dve", (lambda e: e.tensor_tensor(out=stt[:, 48:64], in0=stt[:, 32:48], in1=stt[:, 32:48], op=ALU.mult)), reads=["st_mean"], writes=["st_msq"])
            S.op("dve", (lambda e: e.scalar_tensor_tensor(out=stt[:, 64:80], in0=stt[:, 16:32], scalar=1.0 / 64.0, in1=stt[:, 48:64], op0=ALU.mult, op1=ALU.subtract)), reads=["st_s2", "st_msq"], writes=["st_var"])
            S.op("act", (lambda e: e.activation(out=stt[:, 80:96], in_=stt[:, 64:80], func=AF.Sqrt, bias=gneps, scale=1.0)), reads=["st_var", "rwc"], writes=["st_sd"])
            S.op("dve", (lambda e: e.reciprocal(out=stt[:, 80:96], in_=stt[:, 80:96])), reads=["st_sd"], writes=["st_rstd"])
            S.op("dve", (lambda e: e.tensor_tensor(out=Y3, in0=Y3, in1=bc(stt[:, 32:48]), op=ALU.subtract)), reads=ykeys + ["st_mean", "scr"], writes=["Yn"])
            S.op("dve", (lambda e: e.tensor_tensor(out=Y3, in0=Y3, in1=bc(stt[:, 80:96]), op=ALU.mult)), reads=["Yn", "st_rstd"], writes=["Yn"])
            S.op("dve", (lambda e: e.tensor_tensor(out=Y, in0=Y, in1=lxg, op=ALU.mult)), reads=["Yn", "lxg"], writes=["Yn"])
            S.op("dve", (lambda e: e.tensor_tensor(out=Y, in0=Y, in1=lxb, op=ALU.add)), reads=["Yn", "lxb"], writes=["Yn"])
            S.op("dve", (lambda e: e.tensor_tensor(out=scr.rearrange("p (h d) -> p h d", h=16), in0=V.rearrange("p (h d) -> p h d", h=16), in1=bc(stt[:, 96:112]), op=ALU.mult)),
                 reads=[("V", 0), ("V", 1), "coef", "st_s2"], writes=["scr"])
            S.op("dve", (lambda e: e.tensor_tensor(out=Y, in0=Y, in1=scr, op=ALU.add)), reads=["Yn", "scr"], writes=["Yn"])
            for half in range(2):
                ps, pk = self.gps()
                S.op("pe", (lambda e, ps=ps, half=half: e.matmul(ps, lhsT=sgl, rhs=g2b[:, half * 512:(half + 1) * 512], start=True, stop=True)), reads=["sgl", "g2b"], writes=[pk])
                S.op("dve", (lambda e, ps=ps, half=half: e.tensor_tensor(out=ob[:, half * 512:(half + 1) * 512], in0=Y[:, half * 512:(half + 1) * 512], in1=ps, op=ALU.mult)), reads=[pk, "Yn"], writes=[("ob", half), ("Ydone", half)])
            ps, pk = self.gps()
            pst = ps.bitcast(BF16)
            for cc in range(8):
                S.op("pe", (lambda e, cc=cc, pst=pst: e.transpose(pst[:, cc * 128:(cc + 1) * 128], ob[:, cc * 128:(cc + 1) * 128], self.ident[:])), reads=[("ob", cc // 4), "ident"], writes=[pk])
            S.op("act", (lambda e, pst=pst: e.copy(out=OT, in_=pst.rearrange("p (c t) -> p c t", c=8))), reads=[pk], writes=["OT"])
            py = PS[0]
            for half in range(2):
                for k in range(8):
                    S.op("pe", (lambda e, k=k, half=half: e.matmul(py[:, half * 512:(half + 1) * 512], lhsT=OT[:, k, :], rhs=wo[:, k, half * 512:(half + 1) * 512], start=(k == 0), stop=(k == 7))),
                         reads=["OT", "wo"], writes=[("PS0", half)])
            self.ln_finish(seq, tt, py[:, :], [("PS0", 0), ("PS0", 1)], first=first, last=False)
        self.force_pi = None
        self.gen_list = [(0, 0), (0, 1)]


    def stage_attn(self, seq, layer, j, first):
        S, A, PS = self.S, self.A, self.PS
        self.stage_begin()
        self.load_ln(layer * 3 + 0)
        wq = A.alloc([8, D], BF16)
        wk = A.alloc([8, D], BF16)
        wv = A.alloc([8, D], BF16)
        wo = A.alloc([8, D], BF16)
        biasb = A.alloc([16, 640], BF16)
        KT = A.alloc([8, 1024], BF16)
        V = A.alloc([8, D], BF16)
        QT = A.alloc([8, 512], BF16)
        OT = A.alloc([8, 512], BF16)
        Pb = [A.alloc([640], BF16) for _ in range(2)]
        PTs = [A.alloc([5, 128], BF16) for _ in range(2)]
        Ob = [A.alloc([D], BF16) for _ in range(2)]
        sm = [A.alloc([64], F32) for _ in range(2)]
        wqkv = self.P["ca_w_qkv"][j]
        self.load_w(wq, wqkv[:, 0:D], "wq")
        self.load_w(wk, wqkv[:, D:2 * D], "wk")
        self.load_w(wv, wqkv[:, 2 * D:3 * D], "wv")
        self.load_w(wo, self.P["ca_w_o"][j], "wo")
        S.dma("pool", biasb, self.P["ca_bias"], writes=["biasb"])
        nh = 0
        for t in range(4):
            hk = self.hTkeys(t * 512, 512)
            r0 = (t % 2) * 512
            for oc in range(8):
                ps, pk = self.gps()
                for k in range(8):
                    S.op("pe", (lambda e, k=k, oc=oc, ps=ps, t=t: e.matmul(ps, lhsT=wq[:, k, oc * 128:(oc + 1) * 128], rhs=self.hTs(k, t * 512, 512), start=(k == 0), stop=(k == 7))),
                         reads=["wq"] + hk, writes=[pk])
                S.op("act", (lambda e, oc=oc, ps=ps: e.activation(out=QT[:, oc, :], in_=ps, func=AF.Copy, scale=0.125)), reads=[pk], writes=[("QT", oc)])
                ps, pk = self.gps()
                for k in range(8):
                    S.op("pe", (lambda e, k=k, oc=oc, ps=ps, t=t: e.matmul(ps, lhsT=wk[:, k, oc * 128:(oc + 1) * 128], rhs=self.hTs(k, t * 512, 512), start=(k == 0), stop=(k == 7))),
                         reads=["wk"] + hk, writes=[pk])
                S.op("dve", (lambda e, oc=oc, ps=ps, r0=r0: e.tensor_copy(out=KT[:, oc, r0:r0 + 512], in_=ps)), reads=[pk], writes=[("KT", oc, t % 2)])
            for sub in range(4):
                slot = (4 * t + sub) % 8
                for half in range(2):
                    ps, pk = self.gps()
                    for k in range(8):
                        S.op("pe", (lambda e, k=k, half=half, ps=ps, t=t, sub=sub: e.matmul(ps, lhsT=self.hTs(k, t * 512 + sub * 128, 128), rhs=wv[:, k, half * 512:(half + 1) * 512], start=(k == 0), stop=(k == 7))),
                             reads=["wv"] + hk, writes=[pk])
                    S.op("act", (lambda e, half=half, ps=ps, slot=slot: e.copy(out=V[:, slot, half * 512:(half + 1) * 512], in_=ps)), reads=[pk], writes=[("V", slot, half)])
            for sub in range(4):
                qb = 4 * t + sub
                kbs = list(range(max(0, qb - 4), qb + 1))
                j0 = kbs[0] - (qb - 4)
                c0 = j0 * 128
                oi = qb % 2
                po_ps = PS[2]
                smi = sm[oi]
                ksm = f"sm{oi}"
                def emit_A(h, nh):
                        c, po = h // 2, (h % 2) * 64
                        si = nh % 2
                        sps = PS[0] if si == 0 else PS[3]
                        spn = "PS0" if si == 0 else "PS3"
                        for kb in kbs:
                            jj = kb - (qb - 4)
                            slot = kb % 8
                            S.op("pe", (lambda e, jj=jj, slot=slot, c=c, po=po, sps=sps, sub=sub: e.matmul(sps[:, jj * 128:(jj + 1) * 128], lhsT=QT[po:po + 64, c, sub * 128:(sub + 1) * 128], rhs=KT[po:po + 64, c, slot * 128:(slot + 1) * 128], start=True, stop=False)),
                                 reads=[("QT", c), ("KT", c, slot // 4)], writes=[(spn, jj // 4)])
                            S.op("pe", (lambda e, jj=jj, h=h, sps=sps: e.matmul(sps[:, jj * 128:(jj + 1) * 128], lhsT=self.ident[:], rhs=biasb[:, h, jj * 128:(jj + 1) * 128], start=False, stop=True)),
                                 reads=["biasb", "ident"], writes=[(spn, jj // 4)])
                        skeys = [(spn, 0), (spn, 1)]
                        S.op("dve", (lambda e, sps=sps, smi=smi, h=h, c0=c0: e.tensor_reduce(out=smi[:, 32 + h:33 + h], in_=sps[:, c0:640], axis=AX.X, op=ALU.max, negate=True)),
                             reads=skeys, writes=[(ksm + "m", h)])
                        pb_ = Pb[si]
                        S.op("act", (lambda e, sps=sps, smi=smi, h=h, c0=c0, pb_=pb_: e.activation(out=pb_[:, c0:640], in_=sps[:, c0:640], func=AF.Exp, bias=smi[:, 32 + h:33 + h], scale=1.0, accum_out=smi[:, h:h + 1])),
                             reads=skeys + [(ksm + "m", h)], writes=[f"Pb{si}", (ksm + "s", h)])

                def emit_B(h, nh):
                        c, po = h // 2, (h % 2) * 64
                        si = nh % 2
                        pb_ = Pb[si]
                        ptp = PS[1][:, 0:512].bitcast(BF16)
                        for kb in kbs:
                            jj = kb - (qb - 4)
                            S.op("pe", (lambda e, jj=jj, ptp=ptp, pb_=pb_: e.transpose(ptp[:, jj * 128:(jj + 1) * 128], pb_[:, jj * 128:(jj + 1) * 128], self.ident[:])),
                                 reads=[f"Pb{si}", "ident"], writes=[("PS1", 0)])
                        pts = PTs[si]
                        S.op("dve", (lambda e, ptp=ptp, pts=pts, j0=j0: e.tensor_copy(out=pts[:, j0:5, :], in_=ptp[:, j0 * 128:640].rearrange("p (a b) -> p a b", b=128))),
                             reads=[("PS1", 0)], writes=[f"PTs{si}"])
                        for kb in kbs:
                            jj = kb - (qb - 4)
                            slot = kb % 8
                            S.op("pe", (lambda e, jj=jj, slot=slot, h=h, pts=pts, po_ps=po_ps, kbs=kbs, kb=kb: e.matmul(po_ps[:, h * 64:(h + 1) * 64], lhsT=pts[:, jj, :], rhs=V[:, slot, h * 64:(h + 1) * 64], start=(kb == kbs[0]), stop=(kb == kbs[-1]))),
                                 reads=[f"PTs{si}", ("V", slot, h // 8)], writes=[("PS2", h // 8)])

                for hq in range(17):
                    if hq < 16:
                        emit_A(hq, nh + hq)
                    if hq >= 1:
                        emit_B(hq - 1, nh + hq - 1)
                nh += 16
                S.op("dve", (lambda e, smi=smi: e.reciprocal(out=smi[:, 16:32], in_=smi[:, 0:16])), reads=[(ksm + "s", h) for h in range(16)], writes=[ksm + "r"])
                ob = Ob[oi]
                S.op("dve", (lambda e, smi=smi, ob=ob, po_ps=po_ps: e.tensor_tensor(out=ob.rearrange("p (h d) -> p h d", h=16), in0=po_ps[:, :].rearrange("p (h d) -> p h d", h=16), in1=smi[:, 16:32].unsqueeze(2).to_broadcast([128, 16, 64]), op=ALU.mult)),
                     reads=[("PS2", 0), ("PS2", 1), ksm + "r"], writes=[f"Ob{oi}"])
                otp = PS[1][:, 512:1024].bitcast(BF16)
                for cc in range(8):
                    S.op("pe", (lambda e, cc=cc, otp=otp, ob=ob: e.transpose(otp[:, cc * 128:(cc + 1) * 128], ob[:, cc * 128:(cc + 1) * 128], self.ident[:])),
                         reads=[f"Ob{oi}", "ident"], writes=[("PS1", 1)])
                S.op("act", (lambda e, otp=otp, sub=sub: e.copy(out=OT[:, :, sub * 128:(sub + 1) * 128], in_=otp.rearrange("p (c t) -> p c t", c=8))),
                     reads=[("PS1", 1)], writes=[("OT", sub)])
            self.out_proj(seq, t, lambda k, sub: OT[:, k, sub * 128:(sub + 1) * 128], [("OT", s_) for s_ in range(4)], wo, "wo", 8, first=first)


def make_consts():
    c = np.zeros((128, 256), np.float32)
    c[:, 0:128] = np.eye(128, dtype=np.float32)
    c[:, 128] = LN_EPS
    return c


def make_rwc():
    c = np.zeros((128, 648), np.float32)
    s_ = np.arange(128)[:, None]
    t_ = np.arange(128)[None, :]
    same = (s_ // 64) == (t_ // 64)
    c[:, 0:128] = (same & (s_ < t_))
    c[:, 128:256] = (same & (s_ <= t_))
    c[:, 256:384] = (same & (s_ > t_))
    c[:, 384:512] = same
    c[:, 512:640] = (t_ % 64 != 0)
    c[0:64, 640] = 1.0
    c[64:128, 641] = 1.0
    c[:, 642] = 64e-5
    return c


def prep_shared(inp):
    sh = {}
    sh["ln_g"] = np.ascontiguousarray(inp["ln_g"].reshape(DEPTH * 3, D))
    sh["ln_b"] = np.ascontiguousarray(inp["ln_b"].reshape(DEPTH * 3, D))
    sh["lru_w_in"] = inp["lru_w_in"]
    na = inp["lru_w_in"].shape[0]
    vec = np.zeros((na, BS, NBLK, 8), np.float32)
    cw = inp["lru_conv_w"].reshape(na, 4, NBLK, BS)
    for k in range(4):
        vec[:, :, :, k] = cw[:, k].transpose(0, 2, 1)
    vec[:, :, :, 4] = inp["lru_conv_b"].reshape(na, NBLK, BS).transpose(0, 2, 1)
    gbv = inp["lru_gate_b"].reshape(na, 2, NBLK, BS)
    vec[:, :, :, 5] = gbv[:, 0].transpose(0, 2, 1)
    vec[:, :, :, 6] = gbv[:, 1].transpose(0, 2, 1)
    vec[:, :, :, 7] = inp["lru_lambda"].reshape(na, NBLK, BS).transpose(0, 2, 1)
    sh["lru_vec"] = vec
    sh["lru_gate_w"] = np.ascontiguousarray(inp["lru_gate_w"].transpose(0, 3, 1, 2, 4).reshape(na, BS, 2 * NBLK, BS))
    sh["lru_w_out"] = inp["lru_w_out"]
    for k in ("mx_w_q", "mx_w_kv", "mx_w_o", "mlp_w1", "mlp_w2", "ca_w_qkv", "ca_w_o"):
        sh[k] = inp[k]
    rb = inp["ca_rel_bias"][0]
    q = np.arange(64)[:, None]
    kk = np.arange(576)[None, :]
    idx = np.clip(512 + q - kk, -128, 128) + 128
    band = rb[:, idx]
    b2 = np.full((128, 16, 640), NEG, np.float32)
    b2[0:64, :, 0:576] = band.transpose(1, 0, 2)
    b2[64:128, :, 64:640] = band.transpose(1, 0, 2)
    sh["ca_bias"] = b2
    sh["consts"] = make_consts()
    for k in ("rw_w_r", "rw_w_k", "rw_w_v", "rw_w_o", "rw_w1", "rw_a1", "rw_g1", "rw_w2", "rw_a2", "rw_g2"):
        sh[k] = inp[k]
    nb = inp["rw_mu"].shape[0]
    rv = np.zeros((nb, 128, 8, 16), np.float32)
    fm = lambda v: v.reshape(nb, 8, 128).transpose(0, 2, 1)
    for m in range(6):
        rv[:, :, :, m] = fm(inp["rw_mu"][:, m])
    rv[:, :, :, 6] = fm(inp["rw_w0"])
    rv[:, :, :, 7] = fm(inp["rw_a0"])
    rv[:, :, :, 8] = fm(inp["rw_k_k"])
    rv[:, :, :, 9] = fm(inp["rw_k_a"])
    rv[:, :, :, 10] = fm(inp["rw_r_k"].reshape(nb, D))
    sh["rw_vec"] = rv
    sh["rw_lnx"] = np.ascontiguousarray(np.stack([inp["rw_lnx_g"], inp["rw_lnx_b"]], axis=1))
    sh["rwc"] = make_rwc()
    return sh


_NC_CACHE = {}


def get_nc(n_layers=DEPTH, dbg=None):
    key = (n_layers if isinstance(n_layers, int) else tuple(n_layers), tuple(sorted(dbg.items())) if dbg else None)
    if key not in _NC_CACHE:
        b = Builder(n_layers, dbg)
        nc = b.build()
        _NC_CACHE[key] = nc
    return _NC_CACHE[key]


def kernel(**inputs):
    inp = {k: np.ascontiguousarray(np.asarray(v, dtype=np.float32)) for k, v in inputs.items()}
    sh = prep_shared(inp)
    nc = get_nc()
    ncores = 8
    in_maps = []
    for c in range(ncores):
        m = dict(sh)
        m["x"] = np.ascontiguousarray(inp["x"][c * NSEQ:(c + 1) * NSEQ])
        m["mem"] = np.ascontiguousarray(inp["mem"][c * NSEQ:(c + 1) * NSEQ])
        in_maps.append(m)
    res = run_bass_kernel_spmd(nc, in_maps, core_ids=list(range(ncores)))
    out = np.concatenate([np.asarray(r["out"]).reshape(NSEQ, SEQ, D) for r in res.results], axis=0)
    return out.astype(np.float32)
```
